# Optimizing a Trainium2 kernel written in Bass

```python
import math
import jax, jax.numpy as jnp
from jax import lax
import numpy as np

D_MODEL = 1024
BATCH = 4
SEQ = 4096
DEPTH = 4

EPS = 1e-6
N_EVEN = (DEPTH + 1) // 2
N_ODD = DEPTH // 2
A_WIDTH = D_MODEL // 2
SGU_HEADS = 4
SGU_HEAD_DIM = A_WIDTH // SGU_HEADS
SGU_CHUNK = 128
B_WIDTH = D_MODEL - A_WIDTH
HGRN_HEADS = 4
HGRN_HEAD_DIM = B_WIDTH // HGRN_HEADS
HGRN_CHUNK = 64
C_WIDTH = D_MODEL // 2
CONV_WIDTH = 3
D_WIDTH = D_MODEL - C_WIDTH
DIFF_HEADS = 4
DIFF_V_DIM = D_WIDTH // DIFF_HEADS
DIFF_QK_DIM = DIFF_V_DIM // 2
ATTN_BLOCK = 128
ROPE_THETA = 10000.0
D_FF = 4 * D_MODEL

EVEN_SIZES = (A_WIDTH, A_WIDTH, B_WIDTH, B_WIDTH, B_WIDTH, B_WIDTH, B_WIDTH)
ODD_SIZES = (C_WIDTH, C_WIDTH, C_WIDTH, D_WIDTH, D_WIDTH, D_WIDTH)
EVEN_IN = sum(EVEN_SIZES)
ODD_IN = sum(ODD_SIZES)

kernel_name = "hybrid_sgu_hgrn2_shortconv_diffattn_encoder"


def split_cols(t, sizes):
    out, start = [], 0
    for s in sizes:
        out.append(t[..., start:start + s])
        start += s
    return out


def rmsnorm(x, g):
    xf = x.astype(jnp.float32)
    y = xf * lax.rsqrt(jnp.mean(xf * xf, axis=-1, keepdims=True) + EPS)
    return (y * g.astype(jnp.float32)).astype(x.dtype)


def rope_tables(seq, dim):
    inv = 1.0 / (ROPE_THETA ** (jnp.arange(0, dim, 2, dtype=jnp.float32) / dim))
    ang = jnp.arange(seq, dtype=jnp.float32)[:, None] * inv[None, :]
    return jnp.cos(ang), jnp.sin(ang)


def apply_rope(x, cos, sin):
    c = cos[None, :, None, None, :]
    s = sin[None, :, None, None, :]
    x1, x2 = jnp.split(x.astype(jnp.float32), 2, axis=-1)
    return jnp.concatenate([x1 * c - x2 * s, x2 * c + x1 * s], axis=-1).astype(x.dtype)


def sgu_mixer(u, v, norm_g, w_s, b_s):
    bsz, s, _ = u.shape
    u = jax.nn.gelu(u)
    v = rmsnorm(jax.nn.gelu(v).reshape(bsz, s, SGU_HEADS, SGU_HEAD_DIM),
                norm_g.reshape(SGU_HEADS, SGU_HEAD_DIM))
    vc = v.reshape(bsz, s // SGU_CHUNK, SGU_CHUNK, SGU_HEADS, SGU_HEAD_DIM)
    mixed = jnp.einsum('hpq,bnqhc->bnphc', w_s.astype(vc.dtype), vc) \
        + b_s.T.astype(vc.dtype)[None, None, :, :, None]
    return u * mixed.reshape(bsz, s, A_WIDTH)


def hgrn2_bidir(q, i, g, f_fwd, f_bwd, lb, norm_g):
    bsz, s, _ = q.shape
    h, dh = HGRN_HEADS, HGRN_HEAD_DIM
    lbf = lb.astype(jnp.float32)[:, None, None, :]
    f = lbf + (1.0 - lbf) * jax.nn.sigmoid(jnp.stack([f_fwd, f_bwd]).astype(jnp.float32))
    logf = jnp.log(f)
    k = 1.0 - f
    qf = q.astype(jnp.float32)
    vf = i.astype(jnp.float32)
    qd = jnp.stack([qf, qf[:, ::-1]])
    vd = jnp.stack([vf, vf[:, ::-1]])
    kd = jnp.stack([k[0], k[1][:, ::-1]])
    gd = jnp.stack([logf[0], logf[1][:, ::-1]])
    n_chunks = s // HGRN_CHUNK

    def to_chunks(t):
        return jnp.moveaxis(t.reshape(2, bsz, n_chunks, HGRN_CHUNK, h, dh), 2, 0)

    tril = jnp.tril(jnp.ones((HGRN_CHUNK, HGRN_CHUNK), dtype=bool))[None, None, :, :, None, None]

    def step(state, inp):
        qc, kc, vc, gc = inp
        b = jnp.cumsum(gc, axis=2)
        o_inter = jnp.einsum('zbthk,zbhkv->zbthv', qc * jnp.exp(b), state)
        diff = b[:, :, :, None] - b[:, :, None, :]
        decay = jnp.exp(jnp.where(tril, diff, -jnp.inf))
        att = jnp.einsum('zbthk,zbtjhk,zbjhk->zbhtj', qc, decay, kc)
        o_intra = jnp.einsum('zbhtj,zbjhv->zbthv', att, vc)
        b_last = b[:, :, -1]
        k_dec = kc * jnp.exp(b_last[:, :, None] - b)
        state = jnp.exp(b_last)[..., None] * state + jnp.einsum('zbjhk,zbjhv->zbhkv', k_dec, vc)
        return state, o_inter + o_intra

    s0 = jnp.zeros((2, bsz, h, dh, dh), jnp.float32)
    _, o = lax.scan(step, s0, (to_chunks(qd), to_chunks(kd), to_chunks(vd), to_chunks(gd)))
    o = jnp.moveaxis(o, 0, 2).reshape(2, bsz, s, h, dh)
    o = o[0] + o[1][:, ::-1]
    o = rmsnorm(o, norm_g.reshape(h, dh)) * jax.nn.silu(g.astype(jnp.float32).reshape(bsz, s, h, dh))
    return o.reshape(bsz, s, B_WIDTH).astype(q.dtype)


def short_conv_mixer(h_in, b_gate, c_gate, conv_w):
    z = c_gate * h_in
    pad = CONV_WIDTH // 2
    y = lax.conv_general_dilated(z, conv_w[:, None, :].astype(z.dtype), window_strides=(1,),
                                 padding=[(pad, pad)], dimension_numbers=('NWC', 'WIO', 'NWC'),
                                 feature_group_count=C_WIDTH)
    return b_gate * y


def diff_attention(q, k, v, q_g, k_g, lq1, lk1, lq2, lk2, sub_g, lambda_init, cos, sin):
    bsz, s, _ = q.shape
    h, dq, dv = DIFF_HEADS, DIFF_QK_DIM, DIFF_V_DIM
    qh = apply_rope(rmsnorm(q.reshape(bsz, s, h, 2, dq), q_g), cos, sin)
    kh = apply_rope(rmsnorm(k.reshape(bsz, s, h, 2, dq), k_g), cos, sin)
    kf = kh.astype(jnp.float32)
    vf = v.reshape(bsz, s, h, dv).astype(jnp.float32)
    lam = (jnp.exp(jnp.sum(lq1.astype(jnp.float32) * lk1.astype(jnp.float32)))
           - jnp.exp(jnp.sum(lq2.astype(jnp.float32) * lk2.astype(jnp.float32))) + lambda_init)
    scale = dq ** -0.5
    qb = jnp.moveaxis(qh.reshape(bsz, s // ATTN_BLOCK, ATTN_BLOCK, h, 2, dq), 1, 0)

    def block(qblk):
        sc = jnp.einsum('bqhcd,bkhcd->bhcqk', qblk.astype(jnp.float32), kf) * scale
        p = jax.nn.softmax(sc, axis=-1)
        a = p[:, :, 0] - lam * p[:, :, 1]
        return jnp.einsum('bhqk,bkhv->bqhv', a, vf)

    o = lax.map(block, qb)
    o = jnp.moveaxis(o, 0, 1).reshape(bsz, s, h, dv)
    o = rmsnorm(o, sub_g) * (1.0 - lambda_init)
    return o.reshape(bsz, s, D_WIDTH).astype(q.dtype)


def setup_inputs(seed: int = 0) -> dict:
    key = jax.random.key(seed)
    ks = jax.random.split(key, 24)

    def nrm(k, shape, scale):
        return jax.random.normal(k, shape, jnp.float32) * scale

    def gain(k, shape):
        return 1.0 + 0.02 * jax.random.normal(k, shape, jnp.float32)

    return {
        "x": nrm(ks[0], (BATCH, SEQ, D_MODEL), 1.0),
        "norm_mix_g": gain(ks[1], (DEPTH, D_MODEL)),
        "norm_mlp_g": gain(ks[2], (DEPTH, D_MODEL)),
        "w_in_even": nrm(ks[3], (N_EVEN, D_MODEL, EVEN_IN), D_MODEL ** -0.5),
        "w_out_even": nrm(ks[4], (N_EVEN, A_WIDTH + B_WIDTH, D_MODEL), (A_WIDTH + B_WIDTH) ** -0.5),
        "sgu_norm_g": gain(ks[5], (N_EVEN, A_WIDTH)),
        "sgu_w": nrm(ks[6], (N_EVEN, SGU_HEADS, SGU_CHUNK, SGU_CHUNK), SGU_CHUNK ** -0.5),
        "sgu_b": 1.0 + 0.01 * jax.random.normal(ks[7], (N_EVEN, SGU_HEADS, SGU_CHUNK), jnp.float32),
        "hgrn_lb_logits": nrm(ks[8], (2, N_EVEN, B_WIDTH), 0.1),
        "hgrn_norm_g": gain(ks[9], (N_EVEN, B_WIDTH)),
        "w_in_odd": nrm(ks[10], (N_ODD, D_MODEL, ODD_IN), D_MODEL ** -0.5),
        "w_out_odd": nrm(ks[11], (N_ODD, C_WIDTH + D_WIDTH, D_MODEL), (C_WIDTH + D_WIDTH) ** -0.5),
        "conv_w": nrm(ks[12], (N_ODD, CONV_WIDTH, C_WIDTH), CONV_WIDTH ** -0.5),
        "q_norm_g": gain(ks[13], (N_ODD, DIFF_QK_DIM)),
        "k_norm_g": gain(ks[14], (N_ODD, DIFF_QK_DIM)),
        "lambda_q1": nrm(ks[15], (N_ODD, DIFF_QK_DIM), 0.1),
        "lambda_k1": nrm(ks[16], (N_ODD, DIFF_QK_DIM), 0.1),
        "lambda_q2": nrm(ks[17], (N_ODD, DIFF_QK_DIM), 0.1),
        "lambda_k2": nrm(ks[18], (N_ODD, DIFF_QK_DIM), 0.1),
        "diff_norm_g": gain(ks[19], (N_ODD, DIFF_V_DIM)),
        "mlp_w1": nrm(ks[20], (DEPTH, D_MODEL, D_FF), D_MODEL ** -0.5),
        "mlp_w2": nrm(ks[21], (DEPTH, D_FF, D_MODEL), D_FF ** -0.5),
    }


def reference(x, norm_mix_g, norm_mlp_g, w_in_even, w_out_even, sgu_norm_g, sgu_w, sgu_b,
              hgrn_lb_logits, hgrn_norm_g, w_in_odd, w_out_odd, conv_w, q_norm_g, k_norm_g,
              lambda_q1, lambda_k1, lambda_q2, lambda_k2, diff_norm_g, mlp_w1, mlp_w2):
    s = x.shape[1]
    cos, sin = rope_tables(s, DIFF_QK_DIM)
    p_lb = jax.nn.softmax(hgrn_lb_logits.astype(jnp.float32), axis=1)
    lower_bounds = jnp.cumsum(p_lb, axis=1) - p_lb[:, :1]
    for l in range(DEPTH):
        h = rmsnorm(x, norm_mix_g[l])
        if l % 2 == 0:
            e = l // 2
            u, v, q, i, g, f_fwd, f_bwd = split_cols(h @ w_in_even[e], EVEN_SIZES)
            out_a = sgu_mixer(u, v, sgu_norm_g[e], sgu_w[e], sgu_b[e])
            out_b = hgrn2_bidir(q, i, g, f_fwd, f_bwd, lower_bounds[:, e], hgrn_norm_g[e])
            mix = jnp.concatenate([out_a, out_b], axis=-1) @ w_out_even[e]
        else:
            o = l // 2
            h_in, b_gate, c_gate, q, k, v = split_cols(h @ w_in_odd[o], ODD_SIZES)
            out_c = short_conv_mixer(h_in, b_gate, c_gate, conv_w[o])
            lambda_init = 0.8 - 0.6 * math.exp(-0.3 * l)
            out_d = diff_attention(q, k, v, q_norm_g[o], k_norm_g[o], lambda_q1[o], lambda_k1[o],
                                   lambda_q2[o], lambda_k2[o], diff_norm_g[o], lambda_init, cos, sin)
            mix = jnp.concatenate([out_c, out_d], axis=-1) @ w_out_odd[o]
        x = x + mix
        hm = rmsnorm(x, norm_mlp_g[l]) @ mlp_w1[l]
        x = x + jnp.square(jax.nn.relu(hm)) @ mlp_w2[l]
    return x
```

```python
import math
import numpy as np
import concourse.bass as bass
import concourse.mybir as mybir
from concourse.bass_utils import run_bass_kernel_spmd

F32 = mybir.dt.float32
BF16 = mybir.dt.bfloat16
AF = mybir.ActivationFunctionType
ALU = mybir.AluOpType
AX = mybir.AxisListType

ENGS = ["sync", "gpsimd", "scalar", "vector", "tensor"]
NCORES = 8


class _Op:
    __slots__ = ("eng", "fn", "deps", "needs_inc", "ticket", "slot", "dval")

    def __init__(self, eng, fn):
        self.eng = eng
        self.fn = fn
        self.deps = []
        self.needs_inc = False
        self.ticket = 0
        self.slot = None
        self.dval = 0


class DmaSlot:
    def __init__(self, sem):
        self.sem = sem
        self.count = 0


class Prog:
    def __init__(self, nc, stack):
        self.nc = nc
        self.stack = stack
        self.ops = {e: [] for e in ENGS}
        self.last_w = {}
        self.readers = {}
        self.esem = {e: stack.enter_context(nc.semaphore("es_" + e)) for e in ENGS}
        self.nslots = 0
        self.store_ops = []
        self.stage_dma = {}

    def slot(self):
        self.nslots += 1
        return DmaSlot(self.stack.enter_context(self.nc.semaphore("ds%d" % self.nslots)))

    def barrier(self):
        deps = []
        for e in ENGS:
            for op in reversed(self.ops[e]):
                if op.slot is None and op.fn is not None:
                    op.needs_inc = True
                    deps.append(op)
                    break
        deps += list(self.stage_dma.values())
        for e in ENGS:
            b = _Op(e, None)
            b.deps = list(deps)
            self.ops[e].append(b)
        self.stage_dma = {}

    def add(self, eng, fn, r=(), w=(), slot=None, ndma=1, dval=None):
        op = _Op(eng, fn)
        deps = []
        seen = set()

        def push(d):
            if d is not None and id(d) not in seen:
                seen.add(id(d))
                deps.append(d)

        for k in r:
            push(self.last_w.get(k))
        for k in w:
            push(self.last_w.get(k))
            for rd in self.readers.get(k, {}).values():
                push(rd)
        for d in deps:
            if d.slot is None:
                if d.eng == eng and eng == "tensor":
                    continue
                d.needs_inc = True
            op.deps.append(d)
        if slot is not None:
            slot.count += (16 * ndma if dval is None else dval)
            op.slot = slot
            op.dval = slot.count
            self.stage_dma[id(slot)] = op
        for k in w:
            self.last_w[k] = op
            self.readers[k] = {}
        for k in r:
            rk = eng if slot is None else ("dma", id(op))
            self.readers.setdefault(k, {})[rk] = op
        self.ops[eng].append(op)
        return op

    def dma(self, eng, slot, out, in_, r=(), w=()):
        def fn(e, out=out, in_=in_, slot=slot):
            return e.dma_start(out=out, in_=in_).then_inc(slot.sem, 16)

        return self.add(eng, fn, r=r, w=w, slot=slot)

    def store(self, eng, slot, out, in_, r=()):
        op = self.dma(eng, slot, out, in_, r=r)
        self.store_ops.append(op)
        return op

    def emit(self):
        nc = self.nc
        fin = _Op("sync", None)
        fin.deps = list(self.store_ops)
        self.ops["sync"].append(fin)
        for e in ENGS:
            t = 0
            for op in self.ops[e]:
                if op.slot is None and op.needs_inc:
                    t += 1
                    op.ticket = t
        with nc.Block() as block:
            for e in ENGS:
                def body(eng, e=e):
                    waited = {}
                    for op in self.ops[e]:
                        for d in op.deps:
                            if d.slot is not None:
                                key, sem, val = ("d", id(d.slot)), d.slot.sem, d.dval
                            else:
                                key, sem, val = ("e", d.eng), self.esem[d.eng], d.ticket
                            if waited.get(key, 0) < val:
                                eng.wait_ge(sem, val)
                                waited[key] = val
                        if op.fn is None:
                            continue
                        inst = op.fn(eng)
                        if op.slot is None and op.needs_inc:
                            inst.then_inc(self.esem[e], 1)

                getattr(block, e)(body)


from contextlib import ExitStack

EPS = 1e-6
D = 1024
TOK = 2048
SEQ = 4096


class Ctx:
    def __init__(self):
        self.nc = bass.Bass("TRN2", target_bir_lowering=False)
        self.st = ExitStack()
        self.P = Prog(self.nc, self.st)
        self.slots = {}
        self.cur = self.st
        self.uid = 0

    def din(self, name, shape, dt=F32):
        return self.nc.dram_tensor(name, list(shape), dt, kind="ExternalInput").ap()

    def dout(self, name, shape, dt=F32):
        return self.nc.dram_tensor(name, list(shape), dt, kind="ExternalOutput").ap()

    def sb(self, name, shape, dt=F32):
        self.uid += 1
        return self.cur.enter_context(self.nc.sbuf_tensor("s%d_%s" % (self.uid, name), list(shape), dt))

    def ps(self, name, shape, dt=F32):
        self.uid += 1
        return self.cur.enter_context(self.nc.psum_tensor("p%d_%s" % (self.uid, name), list(shape), dt))

    def dint(self, name, shape, dt=F32):
        return self.nc.dram_tensor(name, list(shape), dt).ap()

    def stage_begin(self):
        self.stk = getattr(self, "stk", [])
        self.stk.append(self.cur)
        self.cur = ExitStack()

    def stage_end(self):
        self.P.barrier()
        self.cur.close()
        self.cur = self.stk.pop()

    def slot(self, key):
        if key not in self.slots:
            self.slots[key] = self.P.slot()
        return self.slots[key]

    def load(self, key, out, in_, eng="sync"):
        return self.P.dma(eng, self.slot(key), out, in_, w=[key])

    def load_const(self, key, out, in_, eng="sync"):
        op = self.P.dma(eng, self.slot("const_" + eng), out, in_, w=[key])
        self.cpend = getattr(self, "cpend", {})
        self.cpend.setdefault(eng, []).append(key)
        return op

    def const_done(self):
        for eng, keys in getattr(self, "cpend", {}).items():
            ops = [self.P.last_w[k] for k in keys]
            last = max(ops, key=lambda o: o.dval)
            for k in keys:
                self.P.last_w[k] = last
        self.cpend = {}

    def store(self, key, out, in_, eng="sync"):
        return self.P.store(eng, self.slot("st_" + key), out, in_, r=[key])

    def act(self, out, in_, func, r, w, **kw):
        return self.P.add("scalar", lambda e: e.activation(out=out, in_=in_, func=func, **kw), r=r, w=w)

    def vec(self, method, r, w, **kw):
        return self.P.add("vector", lambda e: getattr(e, method)(**kw), r=r, w=w)

    def pe(self, fn, r, w):
        return self.P.add("tensor", fn, r=r, w=w)

    def finish(self):
        self.P.emit()
        self.st.close()
        return self.nc


def rstd_ops(c, ss, n, inv_n, rkeys, key):
    c.vec("tensor_scalar", r=rkeys, w=[key], out=ss[:, 0:n], in0=ss[:, 0:n], scalar1=inv_n, scalar2=EPS,
          op0=ALU.mult, op1=ALU.add)
    c.act(ss[:, 0:n], ss[:, 0:n], AF.Sqrt, r=[key], w=[key])
    c.vec("reciprocal", r=[key], w=[key], out=ss[:, 0:n], in_=ss[:, 0:n])


def norm_transpose(c, i, xtile, xkey, junk, ss, xs, pT, hT_out, hkey, gT, idb, pkey="pT"):
    b = i % 2
    sk, jk, xk, pk = "ss%d" % b, "junk%d" % b, "xs%d" % b, pkey + "%d" % b
    c.act(junk[b][:], xtile, AF.Square, r=[xkey], w=[jk, sk], accum_out=ss[b][:, 0:1])
    rstd_ops(c, ss[b], 1, 1.0 / D, [sk], sk)
    c.act(xs[b][:], xtile, AF.Copy, r=[xkey, sk], w=[xk], scale=ss[b][:, 0:1])

    def tr(e, b=b):
        for k in range(8):
            ins = e.transpose(pT[b][:, k * 128:(k + 1) * 128], xs[b][:, k * 128:(k + 1) * 128], idb[:])
        return ins

    c.pe(tr, r=[xk, "idb"], w=[pk])
    c.vec("tensor_tensor", r=[pk, "gT"], w=[hkey], out=hT_out,
          in0=pT[b][:].rearrange("p (k t) -> p k t", t=128),
          in1=gT[:].rearrange("p (k o) -> p k o", o=1).to_broadcast([128, 8, 128]), op=ALU.mult)


def build_proj(N):
    c = Ctx()
    x = c.din("x", [TOK, D]); gTd = c.din("gT", [128, 8]); w = c.din("w", [D, N]); identd = c.din("ident", [128, 128])
    y = c.dout("y", [TOK, N])
    ncb = N // 512
    wbf = c.sb("wbf", [128, 8, N], BF16)
    idb = c.sb("idb", [128, 128], BF16); gT = c.sb("gT", [128, 8])
    xt = [c.sb("xt%d" % i, [128, D]) for i in range(2)]
    junk = [c.sb("junk%d" % i, [128, D]) for i in range(2)]
    xs = [c.sb("xs%d" % i, [128, D], BF16) for i in range(2)]
    ss = [c.sb("ss%d" % i, [128, 4]) for i in range(2)]
    hT = [c.sb("hT%d" % i, [128, 8, 128], BF16) for i in range(2)]
    yt = [c.sb("yt%d" % i, [128, N]) for i in range(2)]
    pT = [c.ps("pT%d" % i, [128, 1024], BF16) for i in range(2)]
    pm = [c.ps("pm%d" % i, [128, 512]) for i in range(4)]
    c.load("idb", idb[:], identd, eng="gpsimd")
    c.load("gT", gT[:], gTd)
    for k in range(8):
        c.load("wbf%d" % k, wbf[:, k, :], w[k * 128:(k + 1) * 128, :], eng="gpsimd")
    wkeys = ["wbf%d" % k for k in range(8)]
    n = 0
    for i in range(TOK // 128):
        b = i % 2
        c.load("xt%d" % b, xt[b][:], x[i * 128:(i + 1) * 128, :])
        norm_transpose(c, i, xt[b][:], "xt%d" % b, junk, ss, xs, pT, hT[b][:], "hT%d" % b, gT, idb)
        for cb in range(ncb):
            pb = n % 4
            n += 1

            def mm(e, b=b, cb=cb, pb=pb):
                for k in range(8):
                    ins = e.matmul(pm[pb][:], lhsT=hT[b][:, k, :], rhs=wbf[:, k, cb * 512:(cb + 1) * 512],
                                   start=(k == 0), stop=(k == 7))
                return ins

            c.pe(mm, r=["hT%d" % b] + wkeys, w=["pm%d" % pb])
            if cb % 2 == 0:
                c.act(yt[b][:, cb * 512:(cb + 1) * 512], pm[pb][:], AF.Copy, r=["pm%d" % pb], w=["yt%d" % b])
            else:
                c.vec("tensor_copy", r=["pm%d" % pb], w=["yt%d" % b], out=yt[b][:, cb * 512:(cb + 1) * 512], in_=pm[pb][:])
        c.store("yt%d" % b, y[i * 128:(i + 1) * 128, :], yt[b][:])
    return c.finish()


GS = 512


def build_outmlp():
    c = Ctx()
    x = c.din("x", [TOK, D]); cat = c.din("cat", [TOK, D]); wo = c.din("wo", [D, D]); gTd = c.din("gT", [128, 8])
    w1 = c.din("w1", [D, 4 * D]); w2 = c.din("w2", [4 * D, D]); identd = c.din("ident", [128, 128])
    xo = c.dout("xo", [TOK, D])
    NTI = TOK // 128
    xres = c.sb("xres", [128, NTI, D])
    hTa = c.sb("hTa", [128, 8, TOK], BF16)
    wob = c.sb("wob", [128, 8, D], BF16)
    idb = c.sb("idb", [128, 128], BF16); gT = c.sb("gT", [128, 8])
    catb = [c.sb("catb%d" % i, [128, D], BF16) for i in range(2)]
    catT = [c.sb("catT%d" % i, [128, 8, 128], BF16) for i in range(2)]
    junk = [c.sb("junk%d" % i, [128, D]) for i in range(2)]
    xs = [c.sb("xs%d" % i, [128, D], BF16) for i in range(2)]
    ss = [c.sb("ss%d" % i, [128, 4]) for i in range(2)]
    w1g = [c.sb("w1g%d" % i, [128, 8, GS], BF16) for i in range(2)]
    w2g = [c.sb("w2g%d" % i, [128, GS // 128, D], BF16) for i in range(2)]
    rl = [c.sb("rl%d" % i, [128, 512]) for i in range(2)]
    actb = [c.sb("actb%d" % i, [128, GS // 128, 512], BF16) for i in range(2)]
    pT = [c.ps("pT%d" % i, [128, 1024], BF16) for i in range(2)]
    pm = [c.ps("pm%d" % i, [128, 512]) for i in range(6)]
    c.load("idb", idb[:], identd, eng="gpsimd")
    c.load("gT", gT[:], gTd)
    for k in range(8):
        c.load("wob%d" % k, wob[:, k, :], wo[k * 128:(k + 1) * 128, :], eng="gpsimd")
    wokeys = ["wob%d" % k for k in range(8)]
    for i in range(NTI):
        c.load("xres%d" % i, xres[:, i, :], x[i * 128:(i + 1) * 128, :])
    n = 0
    for i in range(NTI):
        b = i % 2
        c.load("catb%d" % b, catb[b][:], cat[i * 128:(i + 1) * 128, :], eng="gpsimd")

        def tr(e, b=b):
            for k in range(8):
                ins = e.transpose(pT[b][:, k * 128:(k + 1) * 128], catb[b][:, k * 128:(k + 1) * 128], idb[:])
            return ins

        c.pe(tr, r=["catb%d" % b, "idb"], w=["pT%d" % b])
        c.vec("tensor_copy", r=["pT%d" % b], w=["catT%d" % b], out=catT[b][:],
              in_=pT[b][:].rearrange("p (k t) -> p k t", t=128))
        for cb in range(2):
            pb = n % 6
            n += 1

            def mm(e, b=b, cb=cb, pb=pb):
                for k in range(8):
                    ins = e.matmul(pm[pb][:], lhsT=catT[b][:, k, :], rhs=wob[:, k, cb * 512:(cb + 1) * 512],
                                   start=(k == 0), stop=(k == 7))
                return ins

            c.pe(mm, r=["catT%d" % b] + wokeys, w=["pm%d" % pb])
            c.vec("tensor_tensor", r=["pm%d" % pb, "xres%d" % i], w=["xres%d" % i],
                  out=xres[:, i, cb * 512:(cb + 1) * 512], in0=pm[pb][:], in1=xres[:, i, cb * 512:(cb + 1) * 512], op=ALU.add)
    for i in range(NTI):
        norm_transpose(c, i, xres[:, i, :], "xres%d" % i, junk, ss, xs, pT, hTa[:, :, i * 128:(i + 1) * 128],
                       "hTa%d" % i, gT, idb)
    NG = 4 * D // GS
    CPG = GS // 128
    m = 0
    for g in range(NG):
        gb = g % 2
        c.load("w1g%d" % gb, w1g[gb][:], w1[:, g * GS:(g + 1) * GS].rearrange("(k p) c -> p k c", p=128), eng="gpsimd")
        c.load("w2g%d" % gb, w2g[gb][:], w2[g * GS:(g + 1) * GS, :].rearrange("(k p) c -> p k c", p=128), eng="gpsimd")
        for tb in range(TOK // 512):
            ab = m % 2
            m += 1
            hkeys = ["hTa%d" % (tb * 4 + j) for j in range(4)]
            for cc in range(CPG):
                pb = n % 6
                n += 1
                rb = n % 2

                def mm1(e, gb=gb, cc=cc, tb=tb, pb=pb):
                    for k in range(8):
                        ins = e.matmul(pm[pb][:], lhsT=w1g[gb][:, k, cc * 128:(cc + 1) * 128],
                                       rhs=hTa[:, k, tb * 512:(tb + 1) * 512], start=(k == 0), stop=(k == 7))
                    return ins

                c.pe(mm1, r=hkeys + ["w1g%d" % gb], w=["pm%d" % pb])
                c.vec("tensor_scalar", r=["pm%d" % pb], w=["rl%d" % rb], out=rl[rb][:], in0=pm[pb][:], scalar1=0.0,
                      scalar2=None, op0=ALU.max)
                c.act(actb[ab][:, cc, :], rl[rb][:], AF.Square, r=["rl%d" % rb], w=["actb%d_%d" % (ab, cc)])
            akeys = ["actb%d_%d" % (ab, cc) for cc in range(CPG)]
            for tt in range(4):
                ti = tb * 4 + tt
                for cb in range(2):
                    pb = n % 6
                    n += 1

                    def mm2(e, gb=gb, ab=ab, tt=tt, cb=cb, pb=pb):
                        for cc in range(CPG):
                            ins = e.matmul(pm[pb][:], lhsT=actb[ab][:, cc, tt * 128:(tt + 1) * 128],
                                           rhs=w2g[gb][:, cc, cb * 512:(cb + 1) * 512], start=(cc == 0), stop=(cc == CPG - 1))
                        return ins

                    c.pe(mm2, r=akeys + ["w2g%d" % gb], w=["pm%d" % pb])
                    c.vec("tensor_tensor", r=["pm%d" % pb, "xres%d" % ti], w=["xres%d" % ti],
                          out=xres[:, ti, cb * 512:(cb + 1) * 512], in0=pm[pb][:], in1=xres[:, ti, cb * 512:(cb + 1) * 512],
                          op=ALU.add)
    for i in range(NTI):
        c.store("xres%d" % i, xo[i * 128:(i + 1) * 128, :], xres[:, i, :])
    return c.finish()


_CACHE = {}


def _prog(key, fn, *a):
    if key not in _CACHE:
        _CACHE[key] = fn(*a)
    return _CACHE[key]


def _run(nc, maps):
    res = run_bass_kernel_spmd(nc, maps, core_ids=list(range(NCORES)))
    return res.results


def _gT(g):
    return np.ascontiguousarray(g.reshape(8, 128).T)


_IDENT = np.eye(128, dtype=np.float32)


def run_proj(xf, g, W):
    N = W.shape[1]
    nc = _prog(("P", N), build_proj, N)
    W = np.ascontiguousarray(W)
    maps = [dict(x=np.ascontiguousarray(xf[c * TOK:(c + 1) * TOK]), gT=_gT(g), w=W, ident=_IDENT) for c in range(NCORES)]
    r = _run(nc, maps)
    return np.concatenate([r[c]["y"] for c in range(NCORES)], 0)


def run_outmlp(xf, catf, wo, g, w1, w2):
    nc = _prog("O", build_outmlp)
    wo, w1, w2 = (np.ascontiguousarray(a) for a in (wo, w1, w2))
    maps = [dict(x=np.ascontiguousarray(xf[c * TOK:(c + 1) * TOK]), cat=np.ascontiguousarray(catf[c * TOK:(c + 1) * TOK]),
                 wo=wo, gT=_gT(g), w1=w1, w2=w2, ident=_IDENT) for c in range(NCORES)]
    r = _run(nc, maps)
    return np.concatenate([r[c]["xo"] for c in range(NCORES)], 0)


def build_meven():
    c = Ctx()
    P = c.P
    u = c.din("u", [SEQ, 256]); v = c.din("v", [SEQ, 256]); sgugd = c.din("sgug", [128, 256])
    wsTd = c.din("wsT", [128, 2, 128]); bsd = c.din("bs", [128, 2])
    aT = c.din("aT", [2, 2, 128, SEQ]); qT = c.din("qT", [2, 128, SEQ]); iv = c.din("iv", [SEQ, 256]); gg = c.din("gg", [SEQ, 256])
    lbld = c.din("lbl", [128, 8]); lbmd = c.din("lbm", [128, 1]); hngd = c.din("hng", [128, 256])
    identd = c.din("ident", [128, 128]); mFd = c.din("maskF", [128, 128]); mBd = c.din("maskB", [128, 128])
    segd = c.din("segm", [128, 512])
    out = c.dout("out", [SEQ, 512])
    NT = SEQ // 128
    idb = c.sb("idb", [128, 128], BF16); c.load("idb", idb[:], identd, eng="gpsimd")
    sgug = c.sb("sgug", [128, 256]); c.load("sgug", sgug[:], sgugd)
    wsb = c.sb("wsb", [128, 2, 128], BF16); c.load("wsb", wsb[:], wsTd, eng="gpsimd")
    bs = c.sb("bs", [128, 2]); c.load("bs", bs[:], bsd)
    lbl = c.sb("lbl", [128, 8]); c.load("lbl", lbl[:], lbld)
    lbm = c.sb("lbm", [128, 1]); c.load("lbm", lbm[:], lbmd)
    hng = c.sb("hng", [128, 256]); c.load("hng", hng[:], hngd)
    mF = c.sb("mF", [128, 128]); c.load("mF", mF[:], mFd)
    mB = c.sb("mB", [128, 128]); c.load("mB", mB[:], mBd)
    segm = c.sb("segm", [128, 512]); c.load("segm", segm[:], segd)
    lb4 = c.sb("lb4", [128, 4]); oml4 = c.sb("oml4", [128, 4]); noml4 = c.sb("noml4", [128, 4])
    l3 = lbl[:].rearrange("p (a e) -> p a e", e=2)
    c.vec("tensor_tensor", r=["lbl"], w=["lb4"], out=lb4[:].rearrange("p (a o) -> p a o", o=1), in0=l3[:, :, 1:2],
          in1=l3[:, :, 0:1], op=ALU.subtract)
    c.act(lb4[:], lb4[:], AF.Sigmoid, r=["lb4"], w=["lb4"])
    c.vec("tensor_scalar", r=["lb4", "lbm"], w=["lb4"], out=lb4[:], in0=lb4[:], scalar1=lbm[:, 0:1], scalar2=None, op0=ALU.mult)
    c.vec("tensor_scalar", r=["lb4"], w=["oml4"], out=oml4[:], in0=lb4[:], scalar1=-1.0, scalar2=1.0, op0=ALU.mult, op1=ALU.add)
    c.vec("tensor_scalar", r=["lb4"], w=["noml4"], out=noml4[:], in0=lb4[:], scalar1=1.0, scalar2=-1.0, op0=ALU.mult, op1=ALU.add)

    pm = c.ps("pm", [128, 512])
    pkt = c.ps("pkt", [128, 1024], BF16)
    pa = [c.ps("pa%d" % i, [128, 512]) for i in range(2)]
    po = [c.ps("po%d" % i, [128, 512]) for i in range(2)]
    pu = [c.ps("pu%d" % i, [128, 512]) for i in range(2)]

    ut = [c.sb("ut%d" % i, [128, 256]) for i in range(2)]
    vt = [c.sb("vt%d" % i, [128, 256]) for i in range(2)]
    vn = [c.sb("vn%d" % i, [128, 256], BF16) for i in range(2)]
    oa = [c.sb("oa%d" % i, [128, 256]) for i in range(2)]
    sj = c.sb("sj", [128, 128]); ssg = c.sb("ssg", [128, 4])
    for n in range(NT):
        b = n % 2
        uk, vk, nk, ok = "ut%d" % b, "vt%d" % b, "vn%d" % b, "oa%d" % b
        c.load(uk, ut[b][:], u[n * 128:(n + 1) * 128, :])
        c.load(vk, vt[b][:], v[n * 128:(n + 1) * 128, :])
        c.act(vt[b][:], vt[b][:], AF.Gelu_apprx_tanh, r=[vk], w=[vk])
        c.act(ut[b][:], ut[b][:], AF.Gelu_apprx_tanh, r=[uk], w=[uk])
        for h in range(2):
            c.act(sj[:], vt[b][:, h * 128:(h + 1) * 128], AF.Square, r=[vk], w=["sj", "ssg"], accum_out=ssg[:, h:h + 1])
        rstd_ops(c, ssg, 2, 1.0 / 128, ["ssg"], "ssg")
        for h in range(2):
            hs = slice(h * 128, (h + 1) * 128)
            c.vec("scalar_tensor_tensor", r=[vk, "ssg", "sgug"], w=[nk], out=vn[b][:, hs], in0=vt[b][:, hs],
                  scalar=ssg[:, h:h + 1], in1=sgug[:, hs], op0=ALU.mult, op1=ALU.mult)

        def mm(e, b=b):
            for h in range(2):
                ins = e.matmul(pm[:, h * 128:(h + 1) * 128], lhsT=wsb[:, h, :], rhs=vn[b][:, h * 128:(h + 1) * 128],
                               start=True, stop=True)
            return ins

        c.pe(mm, r=[nk, "wsb"], w=["pm"])
        for h in range(2):
            hs = slice(h * 128, (h + 1) * 128)
            c.vec("scalar_tensor_tensor", r=["pm", "bs", uk], w=[ok], out=oa[b][:, hs], in0=pm[:, hs],
                  scalar=bs[:, h:h + 1], in1=ut[b][:, hs], op0=ALU.add, op1=ALU.mult)
        c.store(ok, out[n * 128:(n + 1) * 128, 0:256], oa[b][:])

    v3 = lambda t: t[:].rearrange("p (t c) -> p t c", c=128)
    A = [c.sb("A%d" % i, [128, 512]) for i in range(2)]
    Q = [c.sb("Q%d" % i, [128, 512]) for i in range(2)]
    IV = [c.sb("IV%d" % i, [128, 4, 128], BF16) for i in range(2)]
    GG = [c.sb("GG%d" % i, [128, 512]) for i in range(2)]
    L = c.sb("L", [128, 512]); KK = c.sb("KK", [128, 512]); BFW = c.sb("BFW", [128, 512]); BB = c.sb("BB", [128, 512])
    BR = c.sb("BR", [128, 512]); EQ = c.sb("EQ", [128, 512])
    QE = c.sb("QE", [128, 512], BF16); KD = c.sb("KD", [128, 512], BF16); QI = c.sb("QI", [128, 512], BF16)
    KDZ = [c.sb("KDZ%d" % i, [128, 512], BF16) for i in range(2)]
    KDEC = c.sb("KDEC", [128, 512], BF16); kdT = c.sb("kdT", [128, 512], BF16)
    dS = c.sb("dS", [128, 4])
    attT = [c.sb("attT%d" % i, [128, 128], BF16) for i in range(2)]
    S32 = c.sb("S32", [128, 128]); Sbf = c.sb("Sbf", [128, 128], BF16)
    OF = c.sb("OF", [128, NT, 128])
    osum = [c.sb("osum%d" % i, [128, 128]) for i in range(2)]
    obst = [c.sb("obst%d" % i, [128, 128]) for i in range(2)]
    sj2 = c.sb("sj2", [128, 128]); ssh = c.sb("ssh", [128, 4])
    for i in range(2):
        c.vec("memset", r=[], w=["KDZ%d" % i], ap=KDZ[i][:], constant=0.0)
    nt_ = 0
    ntb = 0
    for hh in range(2):
        for dr in range(2):
            col = dr * 2 + hh
            lbc, omlc, nomlc = lb4[:, col:col + 1], oml4[:, col:col + 1], noml4[:, col:col + 1]
            c.vec("memset", r=[], w=["S32"], ap=S32[:], constant=0.0)
            c.vec("memset", r=[], w=["Sbf"], ap=Sbf[:], constant=0.0)
            tbs = range(SEQ // 512) if dr == 0 else range(SEQ // 512 - 1, -1, -1)
            for tb in tbs:
                p2 = ntb % 2
                ntb += 1
                ak, qk, ik, gk = "A%d" % p2, "Q%d" % p2, "IV%d" % p2, "GG%d" % p2
                c.load(ak, A[p2][:], aT[dr, hh, :, tb * 512:(tb + 1) * 512])
                c.load(qk, Q[p2][:], qT[hh, :, tb * 512:(tb + 1) * 512])
                c.load(ik, IV[p2][:], iv[tb * 512:(tb + 1) * 512, hh * 128:(hh + 1) * 128].rearrange("(t p) c -> p t c", p=128),
                       eng="gpsimd")
                if dr == 1:
                    c.load(gk, v3(GG[p2]), gg[tb * 512:(tb + 1) * 512, hh * 128:(hh + 1) * 128].rearrange("(t p) c -> p t c", p=128))
                    c.act(GG[p2][:], GG[p2][:], AF.Silu, r=[gk], w=[gk])
                a_, q_ = A[p2], Q[p2]
                c.act(a_[:], a_[:], AF.Sigmoid, r=[ak], w=[ak])
                c.act(L[:], a_[:], AF.Ln, r=[ak, "oml4", "lb4"], w=["L"], scale=omlc, bias=lbc)
                c.vec("tensor_scalar", r=[ak, "oml4", "noml4"], w=["KK"], out=KK[:], in0=a_[:], scalar1=nomlc, scalar2=omlc,
                      op0=ALU.mult, op1=ALU.add)
                c.vec("tensor_tensor_scan", r=["L", "segm"], w=["BFW"], out=BFW[:], data0=segm[:], data1=L[:], initial=0.0,
                      op0=ALU.mult, op1=ALU.add)
                if dr == 0:
                    Bt, bkey, ri, li = BFW, "BFW", 63, 127
                else:
                    c.vec("tensor_tensor", r=["L", "BFW"], w=["L"], out=L[:], in0=L[:], in1=BFW[:], op=ALU.subtract)
                    c.vec("tensor_tensor", r=["L", "BFW"], w=["BB"], out=v3(BB), in0=v3(L),
                          in1=v3(BFW)[:, :, 127:128].to_broadcast([128, 4, 128]), op=ALU.add)
                    Bt, bkey, ri, li = BB, "BB", 64, 0
                B3 = v3(Bt)
                c.vec("tensor_tensor", r=[bkey], w=["BR"], out=v3(BR), in0=B3,
                      in1=B3[:, :, ri:ri + 1].to_broadcast([128, 4, 128]), op=ALU.subtract)
                c.act(EQ[:], BR[:], AF.Exp, r=["BR"], w=["EQ"])
                c.vec("tensor_tensor", r=[qk, "EQ"], w=["QE"], out=QE[:], in0=q_[:], in1=EQ[:], op=ALU.mult)
                c.act(BR[:], BR[:], AF.Exp, r=["BR"], w=["BR"], scale=-1.0)
                c.vec("tensor_tensor", r=["KK", "BR"], w=["KD"], out=KD[:], in0=KK[:], in1=BR[:], op=ALU.mult)
                hsl = slice(0, 64) if dr == 0 else slice(64, 128)
                c.vec("tensor_tensor", r=["KK", "BR"], w=["KDZ%d" % dr], out=v3(KDZ[dr])[:, :, hsl], in0=v3(KK)[:, :, hsl],
                      in1=v3(BR)[:, :, hsl], op=ALU.mult)
                c.act(EQ[:], Bt[:], AF.Exp, r=[bkey, "QE"], w=["EQ"])
                c.vec("tensor_tensor", r=[qk, "EQ"], w=["QI"], out=QI[:], in0=q_[:], in1=EQ[:], op=ALU.mult)
                c.vec("tensor_tensor", r=[bkey, "KD", "KDZ%d" % dr], w=["BR"], out=v3(BR),
                      in0=B3[:, :, li:li + 1].to_broadcast([128, 4, 128]), in1=B3, op=ALU.subtract)
                c.act(BR[:], BR[:], AF.Exp, r=["BR"], w=["BR"])
                c.vec("tensor_tensor", r=["KK", "BR"], w=["KDEC"], out=KDEC[:], in0=KK[:], in1=BR[:], op=ALU.mult)
                c.act(dS[:].rearrange("p (t o) -> p t o", o=1), B3[:, :, li:li + 1], AF.Exp, r=[bkey], w=["dS"])

                def trk(e):
                    for t in range(4):
                        ins = e.transpose(pkt[:, t * 128:(t + 1) * 128], KDEC[:, t * 128:(t + 1) * 128], idb[:])
                    return ins

                c.pe(trk, r=["KDEC", "idb"], w=["pkt"])
                c.act(kdT[:], pkt[:, 0:512], AF.Copy, r=["pkt"], w=["kdT"])
                tts = range(4) if dr == 0 else range(3, -1, -1)
                for tt in tts:
                    ti = tb * 4 + tt
                    pb = nt_ % 2
                    nt_ += 1
                    ts_ = slice(tt * 128, (tt + 1) * 128)
                    kA, kB = (KDZ[0], KD) if dr == 0 else (KD, KDZ[1])

                    def mma(e, pb=pb, tt=tt, kA=kA, kB=kB):
                        e.matmul(pa[pb][:, 0:64], lhsT=kA[:, tt * 128:(tt + 1) * 128], rhs=QE[:, tt * 128:tt * 128 + 64],
                                 start=True, stop=True)
                        return e.matmul(pa[pb][:, 64:128], lhsT=kB[:, tt * 128:(tt + 1) * 128],
                                        rhs=QE[:, tt * 128 + 64:(tt + 1) * 128], start=True, stop=True)

                    c.pe(mma, r=["KD", "KDZ%d" % dr, "QE"], w=["pa%d" % pb])
                    c.vec("tensor_tensor", r=["pa%d" % pb, "mF", "mB"], w=["attT%d" % pb], out=attT[pb][:], in0=pa[pb][:, 0:128],
                          in1=(mF if dr == 0 else mB)[:], op=ALU.mult)

                    def mmo(e, pb=pb, tt=tt, p2=p2):
                        e.matmul(po[pb][:, 0:128], lhsT=QI[:, tt * 128:(tt + 1) * 128], rhs=Sbf[:], start=True, stop=False)
                        return e.matmul(po[pb][:, 0:128], lhsT=attT[pb][:], rhs=IV[p2][:, tt, :], start=False, stop=True)

                    c.pe(mmo, r=["QI", "Sbf", "attT%d" % pb, ik], w=["po%d" % pb])
                    c.pe(lambda e, pb=pb, tt=tt, p2=p2: e.matmul(pu[pb][:, 0:128], lhsT=kdT[:, tt * 128:(tt + 1) * 128],
                                                                 rhs=IV[p2][:, tt, :], start=True, stop=True),
                         r=["kdT", ik], w=["pu%d" % pb])
                    c.vec("scalar_tensor_tensor", r=["S32", "dS", "pu%d" % pb], w=["S32"], out=S32[:], in0=S32[:],
                          scalar=dS[:, tt:tt + 1], in1=pu[pb][:, 0:128], op0=ALU.mult, op1=ALU.add)
                    c.act(Sbf[:], S32[:], AF.Copy, r=["S32"], w=["Sbf"])
                    if dr == 0:
                        c.act(OF[:, ti, :], po[pb][:, 0:128], AF.Copy, r=["po%d" % pb], w=["OF%d" % ti])
                    else:
                        ob_ = nt_ % 2
                        c.vec("tensor_tensor", r=["po%d" % pb, "OF%d" % ti], w=["osum%d" % ob_], out=osum[ob_][:],
                              in0=po[pb][:, 0:128], in1=OF[:, ti, :], op=ALU.add)
                        c.act(sj2[:], osum[ob_][:], AF.Square, r=["osum%d" % ob_], w=["sj2", "ssh"], accum_out=ssh[:, 0:1])
                        rstd_ops(c, ssh, 1, 1.0 / 128, ["ssh"], "ssh")
                        c.vec("scalar_tensor_tensor", r=["osum%d" % ob_, "ssh", "hng"], w=["osum%d" % ob_], out=osum[ob_][:],
                              in0=osum[ob_][:], scalar=ssh[:, 0:1], in1=hng[:, hh * 128:(hh + 1) * 128], op0=ALU.mult, op1=ALU.mult)
                        c.vec("tensor_tensor", r=["osum%d" % ob_, gk], w=["obst%d" % ob_], out=obst[ob_][:], in0=osum[ob_][:],
                              in1=GG[p2][:, tt * 128:(tt + 1) * 128], op=ALU.mult)
                        c.store("obst%d" % ob_, out[ti * 128:(ti + 1) * 128, 256 + hh * 128:256 + (hh + 1) * 128], obst[ob_][:])
    return c.finish()


_MASKF = np.triu(np.ones((128, 128), np.float32))
_MASKB = np.tril(np.ones((128, 128), np.float32))
_SEGM = np.ones((128, 512), np.float32)
_SEGM[:, ::128] = 0.0


def _bc(vec):
    return np.ascontiguousarray(np.broadcast_to(vec[None, :], (128, vec.shape[0])))


def run_meven(y, e, inp):
    nc = _prog("Me", build_meven)
    maps = []
    for c in range(NCORES):
        b, hp = c // 2, c % 2
        yb = y[b * SEQ:(b + 1) * SEQ]
        cs = slice(hp * 256, hp * 256 + 256)
        u, v, q, iv, g, ff, fb = (yb[:, k * 512:(k + 1) * 512] for k in range(7))
        heads = [2 * hp, 2 * hp + 1]
        aT = np.stack([np.stack([f[:, h * 128:(h + 1) * 128].T for h in heads]) for f in (ff, fb)])
        qT = np.stack([q[:, h * 128:(h + 1) * 128].T for h in heads])
        lbl = np.zeros((128, 8), np.float32)
        for dr in range(2):
            for hi, h in enumerate(heads):
                for le in range(2):
                    lbl[:, (dr * 2 + hi) * 2 + le] = inp["hgrn_lb_logits"][dr, le, h * 128:(h + 1) * 128]
        maps.append(dict(
            u=np.ascontiguousarray(u[:, cs]), v=np.ascontiguousarray(v[:, cs]), sgug=_bc(inp["sgu_norm_g"][e][cs]),
            wsT=np.ascontiguousarray(np.transpose(inp["sgu_w"][e][heads[0]:heads[1] + 1], (2, 0, 1))),
            bs=np.ascontiguousarray(inp["sgu_b"][e][heads[0]:heads[1] + 1].T),
            aT=np.ascontiguousarray(aT), qT=np.ascontiguousarray(qT), iv=np.ascontiguousarray(iv[:, cs]),
            gg=np.ascontiguousarray(g[:, cs]), lbl=lbl, lbm=np.full((128, 1), float(e), np.float32),
            hng=_bc(inp["hgrn_norm_g"][e][cs]), ident=_IDENT, maskF=_MASKF, maskB=_MASKB, segm=_SEGM))
    r = _run(nc, maps)
    cat = np.empty((4 * SEQ, D), np.float32)
    for c in range(NCORES):
        b, hp = c // 2, c % 2
        o = r[c]["out"]
        cat[b * SEQ:(b + 1) * SEQ, hp * 256:hp * 256 + 256] = o[:, 0:256]
        cat[b * SEQ:(b + 1) * SEQ, 512 + hp * 256:512 + hp * 256 + 256] = o[:, 256:512]
    return cat


def build_modd():
    c = Ctx()
    hinT = c.din("hinT", [2, 128, SEQ]); bgT = c.din("bgT", [2, 128, SEQ]); cgT = c.din("cgT", [2, 128, SEQ])
    cwd = c.din("cw", [128, 6])
    qT = c.din("qT", [2, 128, SEQ]); qTp = c.din("qTp", [2, 128, SEQ]); kT = c.din("kT", [2, 128, SEQ]); kTp = c.din("kTp", [2, 128, SEQ])
    vd = c.din("v", [SEQ, 256]); gvd = c.din("gv", [128, 4]); cosd = c.din("cosT", [128, SEQ]); sind = c.din("sinT", [128, SEQ])
    lamd = c.din("lamv", [128, 4, 64]); lcd = c.din("lconst", [128, 2]); subgd = c.din("subg", [128, 128])
    bonesd = c.din("bones", [128, 128])
    oc = c.dout("oc", [2, 128, SEQ]); od = c.dout("od", [SEQ, 256])
    NT = SEQ // 128
    cw = c.sb("cw", [128, 6]); c.load("cw", cw[:], cwd)
    gv = c.sb("gv", [128, 4]); c.load("gv", gv[:], gvd)
    cosT = c.sb("cosT", [128, SEQ]); c.load("cosT", cosT[:], cosd)
    sinT = c.sb("sinT", [128, SEQ]); c.load("sinT", sinT[:], sind)
    lamv = c.sb("lamv", [128, 4, 64]); c.load("lamv", lamv[:], lamd)
    lconst = c.sb("lconst", [128, 2]); c.load("lconst", lconst[:], lcd)
    subg = c.sb("subg", [128, 128]); c.load("subg", subg[:], subgd)
    bones = c.sb("bones", [128, 128]); c.load("bones", bones[:], bonesd)
    epsb = c.sb("epsb", [128, 1]); c.vec("memset", r=[], w=["epsb"], ap=epsb[:], constant=EPS)
    lj = c.sb("lj", [128, 64]); ls = c.sb("ls", [128, 2]); nlam = c.sb("nlam", [128, 1])
    for j in range(2):
        c.vec("tensor_tensor", r=["lamv"], w=["lj"], out=lj[:], in0=lamv[:, 2 * j, :], in1=lamv[:, 2 * j + 1, :], op=ALU.mult)
        c.vec("tensor_reduce", r=["lj"], w=["ls%d" % j], out=ls[:, j:j + 1], in_=lj[:], axis=AX.X, op=ALU.add)
    c.act(ls[:], ls[:], AF.Exp, r=["ls0", "ls1"], w=["ls"])
    c.vec("tensor_tensor", r=["ls"], w=["nlam"], out=nlam[:], in0=ls[:, 1:2], in1=ls[:, 0:1], op=ALU.subtract)
    c.vec("tensor_tensor", r=["nlam", "lconst"], w=["nlam"], out=nlam[:], in0=nlam[:], in1=lconst[:, 0:1], op=ALU.subtract)
    c.vec("tensor_scalar", r=["subg", "lconst"], w=["subg"], out=subg[:], in0=subg[:], scalar1=lconst[:, 1:2], scalar2=None,
          op0=ALU.mult)

    hin = c.sb("hin", [128, SEQ]); cg = c.sb("cg", [128, SEQ]); bg = c.sb("bg", [128, SEQ])
    z = c.sb("z", [128, SEQ + 2]); yy = c.sb("yy", [128, SEQ])
    c.vec("memset", r=[], w=["z"], ap=z[:], constant=0.0)
    for g in range(2):
        c.load("hin", hin[:], hinT[g]); c.load("cg", cg[:], cgT[g]); c.load("bg", bg[:], bgT[g])
        c.vec("tensor_tensor", r=["hin", "cg"], w=["z"], out=z[:, 1:SEQ + 1], in0=cg[:], in1=hin[:], op=ALU.mult)
        c.vec("tensor_scalar", r=["z", "cw"], w=["yy"], out=yy[:], in0=z[:, 1:SEQ + 1], scalar1=cw[:, 3 * g + 1:3 * g + 2],
              scalar2=None, op0=ALU.mult)
        c.vec("scalar_tensor_tensor", r=["z", "cw", "yy"], w=["yy"], out=yy[:], in0=z[:, 0:SEQ], scalar=cw[:, 3 * g:3 * g + 1],
              in1=yy[:], op0=ALU.mult, op1=ALU.add)
        c.vec("scalar_tensor_tensor", r=["z", "cw", "yy"], w=["yy"], out=yy[:], in0=z[:, 2:SEQ + 2],
              scalar=cw[:, 3 * g + 2:3 * g + 3], in1=yy[:], op0=ALU.mult, op1=ALU.add)
        c.vec("tensor_tensor", r=["yy", "bg"], w=["yy"], out=yy[:], in0=yy[:], in1=bg[:], op=ALU.mult)
        c.store("yy", oc[g], yy[:])

    pss = c.ps("pss", [128, 512])
    pS = [c.ps("pS%d" % i, [128, 1024]) for i in range(2)]
    pacc = c.ps("pacc", [128, 3, 512])
    Kr = c.sb("Kr", [128, SEQ], BF16); Qr = c.sb("Qr", [128, SEQ], BF16)
    Vaug = c.sb("Vaug", [128, NT, 132], BF16)
    xt = [c.sb("axt%d" % i, [128, 512]) for i in range(2)]
    xp = [c.sb("axp%d" % i, [128, 512]) for i in range(2)]
    sq = c.sb("sq", [128, 512]); rr = c.sb("rr", [128, 512]); t1 = c.sb("t1", [128, 512]); t2 = c.sb("t2", [128, 512])
    pT = [c.sb("pTs%d" % i, [128, 1024], BF16) for i in range(2)]
    rc = c.sb("rc", [128, 8]); o1 = [c.sb("o1_%d" % i, [128, 128]) for i in range(2)]
    odt = [c.sb("odt%d" % i, [128, 128]) for i in range(2)]
    sj = c.sb("sj", [128, 128]); ssd = c.sb("ssd", [128, 4])
    nb = 0
    it = 0
    ne = 0
    for hh in range(2):
        c.load("Vaug", Vaug[:, :, 0:128], vd[:, hh * 128:(hh + 1) * 128].rearrange("(t p) c -> p t c", p=128), eng="gpsimd")
        c.vec("memset", r=[], w=["Vones"], ap=Vaug[:, :, 128:129], constant=1.0)
        for (src, srcp, dst, dkey, gi) in ((kT, kTp, Kr, "Kr", 2), (qT, qTp, Qr, "Qr", 0)):
            for blk in range(SEQ // 512):
                b = nb % 2
                nb += 1
                bs_ = slice(blk * 512, (blk + 1) * 512)
                xk, pk = "axt%d" % b, "axp%d" % b
                c.load(xk, xt[b][:], src[hh, :, bs_])
                c.load(pk, xp[b][:], srcp[hh, :, bs_])
                c.act(sq[:], xt[b][:], AF.Square, r=[xk], w=["sq"])
                c.pe(lambda e: e.matmul(pss[:], lhsT=bones[:], rhs=sq[:], start=True, stop=True), r=["sq", "bones"], w=["pss"])
                c.act(rr[:], pss[:], AF.Ln, r=["pss", "epsb"], w=["rr"], scale=1.0 / 64, bias=epsb[:, 0:1])
                c.act(rr[:], rr[:], AF.Exp, r=["rr"], w=["rr"], scale=-0.5)
                c.vec("scalar_tensor_tensor", r=[xk, "gv", "cosT"], w=["t1"], out=t1[:], in0=xt[b][:], scalar=gv[:, gi:gi + 1],
                      in1=cosT[:, bs_], op0=ALU.mult, op1=ALU.mult)
                c.vec("scalar_tensor_tensor", r=[pk, "gv", "sinT"], w=["t2"], out=t2[:], in0=xp[b][:], scalar=gv[:, gi + 1:gi + 2],
                      in1=sinT[:, bs_], op0=ALU.mult, op1=ALU.mult)
                c.vec("tensor_tensor", r=["t1", "t2"], w=["t1"], out=t1[:], in0=t1[:], in1=t2[:], op=ALU.add)
                c.vec("tensor_tensor", r=["t1", "rr"], w=[dkey + "%d" % blk], out=dst[:, bs_], in0=t1[:], in1=rr[:], op=ALU.mult)
        kkeys = ["Kr%d" % j for j in range(8)]
        for qb in range(SEQ // 512):
            for kt in range(NT):
                sb_ = it % 2
                it += 1

                def mms(e, sb_=sb_, kt=kt, qb=qb):
                    e.matmul(pS[sb_][:, 0:512], lhsT=Kr[0:64, kt * 128:(kt + 1) * 128], rhs=Qr[0:64, qb * 512:(qb + 1) * 512],
                             start=True, stop=True)
                    return e.matmul(pS[sb_][:, 512:1024], lhsT=Kr[64:128, kt * 128:(kt + 1) * 128],
                                    rhs=Qr[64:128, qb * 512:(qb + 1) * 512], start=True, stop=True)

                c.pe(mms, r=["Kr%d" % (kt // 4), "Qr%d" % qb], w=["pS%d" % sb_])
                c.act(pT[sb_][:], pS[sb_][:], AF.Exp, r=["pS%d" % sb_], w=["pTs%d" % sb_], scale=0.125)

                def mmv(e, sb_=sb_, kt=kt):
                    for a in range(8):
                        bank, off = a // 3, (a % 3) * 132
                        ins = e.matmul(pacc[:, bank, off:off + 129], lhsT=pT[sb_][:, a * 128:(a + 1) * 128], rhs=Vaug[:, kt, 0:129],
                                       start=(kt == 0 and a % 3 == 0), stop=(kt == NT - 1), skip_group_check=True)
                    return ins

                c.pe(mmv, r=["pTs%d" % sb_, "Vaug", "Vones"], w=["pacc"])
            for a in range(8):
                bank, off = a // 3, (a % 3) * 132
                c.vec("reciprocal", r=["pacc"], w=["rc%d" % a], out=rc[:, a:a + 1], in_=pacc[:, bank, off + 128:off + 129])
            c.vec("tensor_scalar", r=["rc%d" % a for a in range(4, 8)] + ["nlam"], w=["rc%d" % a for a in range(4, 8)],
                  out=rc[:, 4:8], in0=rc[:, 4:8], scalar1=nlam[:, 0:1], scalar2=None, op0=ALU.mult)
            for qs in range(4):
                e_ = ne % 2
                ne += 1
                a1, a2 = qs, 4 + qs
                c.vec("tensor_scalar", r=["pacc", "rc%d" % a1], w=["o1_%d" % e_], out=o1[e_][:],
                      in0=pacc[:, a1 // 3, (a1 % 3) * 132:(a1 % 3) * 132 + 128], scalar1=rc[:, a1:a1 + 1], scalar2=None, op0=ALU.mult)
                c.vec("scalar_tensor_tensor", r=["pacc", "rc%d" % a2, "o1_%d" % e_], w=["o1_%d" % e_], out=o1[e_][:],
                      in0=pacc[:, a2 // 3, (a2 % 3) * 132:(a2 % 3) * 132 + 128], scalar=rc[:, a2:a2 + 1], in1=o1[e_][:],
                      op0=ALU.mult, op1=ALU.add)
                c.act(sj[:], o1[e_][:], AF.Square, r=["o1_%d" % e_], w=["sj", "ssd"], accum_out=ssd[:, 0:1])
                rstd_ops(c, ssd, 1, 1.0 / 128, ["ssd"], "ssd")
                c.vec("scalar_tensor_tensor", r=["o1_%d" % e_, "ssd", "subg"], w=["odt%d" % e_], out=odt[e_][:], in0=o1[e_][:],
                      scalar=ssd[:, 0:1], in1=subg[:], op0=ALU.mult, op1=ALU.mult)
                ti = qb * 4 + qs
                c.store("odt%d" % e_, od[ti * 128:(ti + 1) * 128, hh * 128:(hh + 1) * 128], odt[e_][:])
    return c.finish()


def _rope_tables():
    inv = 1.0 / (10000.0 ** (np.arange(0, 64, 2, dtype=np.float32) / 64.0))
    ang = np.arange(SEQ, dtype=np.float32)[:, None] * inv[None, :]
    cos, sin = np.cos(ang).astype(np.float32).T, np.sin(ang).astype(np.float32).T
    cosT = np.concatenate([cos, cos, cos, cos], 0)
    sinT = np.concatenate([-sin, sin, -sin, sin], 0)
    return np.ascontiguousarray(cosT), np.ascontiguousarray(sinT)


def _perm64(a):
    return np.concatenate([a[32:64], a[0:32], a[96:128], a[64:96]], 0)


_BONES = np.kron(np.eye(2, dtype=np.float32), np.ones((64, 64), np.float32))


def run_modd(y, o, layer, inp):
    nc = _prog("Mo", build_modd)
    cosT, sinT = _rope_tables()
    lam_init = 0.8 - 0.6 * math.exp(-0.3 * layer)
    qg, kg = np.tile(inp["q_norm_g"][o], 2), np.tile(inp["k_norm_g"][o], 2)
    gv = np.stack([qg, _perm64(qg), kg, _perm64(kg)], 1).astype(np.float32)
    lamv = np.stack([_bc(inp[k][o]) for k in ("lambda_q1", "lambda_k1", "lambda_q2", "lambda_k2")], 1)
    lconst = np.stack([np.full(128, lam_init, np.float32), np.full(128, 1.0 - lam_init, np.float32)], 1)
    maps = []
    for c in range(NCORES):
        b, hp = c // 2, c % 2
        yb = y[b * SEQ:(b + 1) * SEQ]
        hin, bg, cg, q, k, v = (yb[:, j * 512:(j + 1) * 512] for j in range(6))
        grp = [2 * hp, 2 * hp + 1]
        fm = lambda a: np.ascontiguousarray(np.stack([a[:, g * 128:(g + 1) * 128].T for g in grp]))
        fmp = lambda a: np.ascontiguousarray(np.stack([_perm64(a[:, g * 128:(g + 1) * 128].T) for g in grp]))
        cw = np.concatenate([inp["conv_w"][o][:, g * 128:(g + 1) * 128].T for g in grp], 1)
        maps.append(dict(hinT=fm(hin), bgT=fm(bg), cgT=fm(cg), cw=np.ascontiguousarray(cw),
                         qT=fm(q), qTp=fmp(q), kT=fm(k), kTp=fmp(k), v=np.ascontiguousarray(v[:, hp * 256:hp * 256 + 256]),
                         gv=gv, cosT=cosT, sinT=sinT, lamv=np.ascontiguousarray(lamv), lconst=lconst,
                         subg=_bc(inp["diff_norm_g"][o]), bones=_BONES))
    r = _run(nc, maps)
    cat = np.empty((4 * SEQ, D), np.float32)
    for c in range(NCORES):
        b, hp = c // 2, c % 2
        for gi in range(2):
            cat[b * SEQ:(b + 1) * SEQ, (2 * hp + gi) * 128:(2 * hp + gi + 1) * 128] = r[c]["oc"][gi].T
        cat[b * SEQ:(b + 1) * SEQ, 512 + hp * 256:512 + hp * 256 + 256] = r[c]["od"]
    return cat


def kernel(**inputs):
    inp = {k: np.asarray(v) for k, v in inputs.items()}
    x = np.ascontiguousarray(inp["x"].reshape(-1, D).astype(np.float32))
    for l in range(4):
        if l % 2 == 0:
            e = l // 2
            y = run_proj(x, inp["norm_mix_g"][l], inp["w_in_even"][e])
            cat = run_meven(y, e, inp)
            wo = inp["w_out_even"][e]
        else:
            o = l // 2
            y = run_proj(x, inp["norm_mix_g"][l], inp["w_in_odd"][o])
            cat = run_modd(y, o, l, inp)
            wo = inp["w_out_odd"][o]
        x = run_outmlp(x, cat, wo, inp["norm_mlp_g"][l], inp["mlp_w1"][l], inp["mlp_w2"][l])
    return x.reshape(4, SEQ, D).astype(np.float32)


GROUPS = [[0, 1], [2, 3], [4, 5], [6, 7]]
NTI = TOK // 128
v3 = lambda t: t[:].rearrange("p (t c) -> p t c", c=128)


def allgather(c, in_ap, out_ap, rkey, wkey):
    slot = c.P.slot()

    def fn(e):
        return e.collective_compute("AllGather", ALU.bypass, replica_groups=GROUPS, ins=[in_ap], outs=[out_ap]).then_inc(slot.sem)

    return c.P.add("gpsimd", fn, r=[rkey], w=[wkey], slot=slot, dval=1)


def f_proj(c, xres, idb, w_d, gT_d, Ntot, tm_blocks, fm_chunks, y_tm, y_fm):
    c.stage_begin()
    ntm = len(tm_blocks)
    wbf = c.sb("wbf", [128, 8, Ntot], BF16)
    gT = c.sb("gT", [128, 8]); c.load_const("gT", gT[:], gT_d)
    junk = [c.sb("junk%d" % i, [128, D]) for i in range(2)]
    xs = [c.sb("xs%d" % i, [128, D], BF16) for i in range(2)]
    ss = [c.sb("ss%d" % i, [128, 4]) for i in range(2)]
    hT = [c.sb("hT%d" % i, [128, 8, 512], BF16) for i in range(2)]
    yt = [c.sb("yt%d" % i, [128, max(ntm, 1) * 512]) for i in range(2)]
    yf = [c.sb("yf%d" % i, [128, 512]) for i in range(3)]
    pT = [c.ps("pT%d" % i, [128, 1024], BF16) for i in range(2)]
    pm = [c.ps("pm%d" % i, [128, 512]) for i in range(4)]
    c.const_done()
    wv = w_d.rearrange("(k p) n -> p k n", p=128)
    order = [cs // 512 for cs in tm_blocks]
    for cs in fm_chunks:
        if cs // 512 not in order:
            order.append(cs // 512)
    for sl in order:
        c.load("wslab%d" % sl, wbf[:, :, sl * 512:(sl + 1) * 512], wv[:, :, sl * 512:(sl + 1) * 512], eng="gpsimd")
    n = 0
    nf = 0
    def norm_tb(tb):
        hb = tb % 2
        for j in range(4):
            i = tb * 4 + j
            norm_transpose(c, i, xres[:, i, :], "xres%d" % i, junk, ss, xs, pT, hT[hb][:, :, j * 128:(j + 1) * 128],
                           "hT%d_%d" % (hb, j), gT, idb)

    norm_tb(0)
    for tb in range(TOK // 512):
        hb = tb % 2
        if tb + 1 < TOK // 512:
            norm_tb(tb + 1)
        hkeys = ["hT%d_%d" % (hb, j) for j in range(4)]
        for j in range(4):
            if ntm == 0:
                break
            i = tb * 4 + j
            yb = i % 2
            for bi, cs in enumerate(tm_blocks):
                pb = n % 4
                n += 1

                def mm(e, hb=hb, j=j, cs=cs, pb=pb):
                    for k in range(8):
                        ins = e.matmul(pm[pb][:], lhsT=hT[hb][:, k, j * 128:(j + 1) * 128], rhs=wbf[:, k, cs:cs + 512],
                                       start=(k == 0), stop=(k == 7))
                    return ins

                c.pe(mm, r=["hT%d_%d" % (hb, j), "wslab%d" % (cs // 512)], w=["pm%d" % pb])
                if bi % 2 == 0:
                    c.act(yt[yb][:, bi * 512:(bi + 1) * 512], pm[pb][:], AF.Copy, r=["pm%d" % pb], w=["yt%d" % yb])
                else:
                    c.vec("tensor_copy", r=["pm%d" % pb], w=["yt%d" % yb], out=yt[yb][:, bi * 512:(bi + 1) * 512], in_=pm[pb][:])
            c.P.dma("sync", c.slot("st_yt%d" % yb), y_tm[i * 128:(i + 1) * 128, 0:ntm * 512], yt[yb][:, 0:ntm * 512],
                    r=["yt%d" % yb], w=["ytm%d" % i])
        for fi, cs in enumerate(fm_chunks):
            pb = n % 4
            n += 1
            fb = nf % 3
            nf += 1

            def mmf(e, hb=hb, cs=cs, pb=pb):
                for k in range(8):
                    ins = e.matmul(pm[pb][:], lhsT=wbf[:, k, cs:cs + 128], rhs=hT[hb][:, k, :], start=(k == 0), stop=(k == 7))
                return ins

            c.pe(mmf, r=hkeys + ["wslab%d" % (cs // 512)], w=["pm%d" % pb])
            if fi % 2 == 0:
                c.act(yf[fb][:], pm[pb][:], AF.Copy, r=["pm%d" % pb], w=["yf%d" % fb])
            else:
                c.vec("tensor_copy", r=["pm%d" % pb], w=["yf%d" % fb], out=yf[fb][:], in_=pm[pb][:])
            c.P.dma("sync", c.slot("st_yf%d" % fb), y_fm[fi * 128:(fi + 1) * 128, tb * 512:(tb + 1) * 512], yf[fb][:],
                    r=["yf%d" % fb], w=["yfm%d_%d" % (fi, tb)])
    c.stage_end()


def f_outmlp(c, xres, idb, cat_tm, cat_fm, n_fm, wo, gT_d, w1, w2):
    c.stage_begin()
    hTa = c.sb("hTa", [128, 8, TOK], BF16)
    wob = c.sb("wob", [128, 8, D], BF16)
    gT = c.sb("gT", [128, 8]); c.load_const("gT", gT[:], gT_d)
    catb = [c.sb("catb%d" % i, [128, D], BF16) for i in range(2)]
    catT = [c.sb("catT%d" % i, [128, 8, 128], BF16) for i in range(2)]
    junk = [c.sb("junk%d" % i, [128, D]) for i in range(2)]
    xs = [c.sb("xs%d" % i, [128, D], BF16) for i in range(2)]
    ss = [c.sb("ss%d" % i, [128, 4]) for i in range(2)]
    w1g = [c.sb("w1g%d" % i, [128, 8, GS], BF16) for i in range(2)]
    w2g = [c.sb("w2g%d" % i, [128, GS // 128, D], BF16) for i in range(2)]
    rl = [c.sb("rl%d" % i, [128, 512]) for i in range(2)]
    actb = [c.sb("actb%d" % i, [128, GS // 128, 512], BF16) for i in range(2)]
    pT = [c.ps("pT%d" % i, [128, 1024], BF16) for i in range(2)]
    pN = [c.ps("pN%d" % i, [128, 1024], BF16) for i in range(2)]
    pm = [c.ps("pm%d" % i, [128, 512]) for i in range(4)]
    c.const_done()
    c.load("wob", wob[:], wo.rearrange("(k p) n -> p k n", p=128), eng="gpsimd")
    wokeys = ["wob"]
    n = 0
    c0 = n_fm * 128
    for i in range(NTI):
        b = i % 2
        c.P.dma("gpsimd", c.slot("catb%d" % b), catb[b][:, c0:D], cat_tm[i * 128:(i + 1) * 128, c0:D], r=["cat_tm"], w=["catb%d" % b])
        rk = ["pT%d" % b]
        if n_fm:
            c.P.dma("gpsimd", c.slot("catTf%d" % b), catT[b][:, 0:n_fm, :],
                    cat_fm.rearrange("(k p) t -> p k t", p=128)[:, 0:n_fm, i * 128:(i + 1) * 128], r=["cat_fm"], w=["catTf%d" % b])

        def tr(e, b=b):
            for k in range(n_fm, 8):
                ins = e.transpose(pT[b][:, k * 128:(k + 1) * 128], catb[b][:, k * 128:(k + 1) * 128], idb[:])
            return ins

        c.pe(tr, r=["catb%d" % b, "idb"], w=["pT%d" % b])
        c.vec("tensor_copy", r=["pT%d" % b], w=["catT%d" % b], out=catT[b][:, n_fm:8, :],
              in_=pT[b][:].rearrange("p (k t) -> p k t", t=128)[:, n_fm:8, :])
        for cb in range(2):
            pb = n % 4
            n += 1

            def mm(e, b=b, cb=cb, pb=pb):
                for k in range(8):
                    ins = e.matmul(pm[pb][:], lhsT=catT[b][:, k, :], rhs=wob[:, k, cb * 512:(cb + 1) * 512],
                                   start=(k == 0), stop=(k == 7))
                return ins

            c.pe(mm, r=["catT%d" % b, "catTf%d" % b] + wokeys, w=["pm%d" % pb])
            c.vec("tensor_tensor", r=["pm%d" % pb, "xres%d" % i], w=["xres%d" % i],
                  out=xres[:, i, cb * 512:(cb + 1) * 512], in0=pm[pb][:], in1=xres[:, i, cb * 512:(cb + 1) * 512], op=ALU.add)
        if i >= 1:
            norm_transpose(c, i - 1, xres[:, i - 1, :], "xres%d" % (i - 1), junk, ss, xs, pN, hTa[:, :, (i - 1) * 128:i * 128],
                           "hTa%d" % (i - 1), gT, idb, pkey="pN")
    norm_transpose(c, NTI - 1, xres[:, NTI - 1, :], "xres%d" % (NTI - 1), junk, ss, xs, pN, hTa[:, :, (NTI - 1) * 128:NTI * 128],
                   "hTa%d" % (NTI - 1), gT, idb, pkey="pN")
    NG = 4 * D // GS
    CPG = GS // 128
    NTB = TOK // 512

    def wload(g):
        gb = g % 2
        c.load("w1g%d" % gb, w1g[gb][:], w1[:, g * GS:(g + 1) * GS].rearrange("(k p) c -> p k c", p=128), eng="gpsimd")
        c.load("w2g%d" % gb, w2g[gb][:], w2[g * GS:(g + 1) * GS, :].rearrange("(k p) c -> p k c", p=128), eng="gpsimd")

    cnt = [n]

    def stage1(g, tb, ab):
        gb = g % 2
        hkeys = ["hTa%d" % (tb * 4 + j) for j in range(4)]
        for cc in range(CPG):
            pb = cnt[0] % 4
            cnt[0] += 1
            rb = cnt[0] % 2

            def mm1(e, gb=gb, cc=cc, tb=tb, pb=pb):
                for k in range(8):
                    ins = e.matmul(pm[pb][:], lhsT=w1g[gb][:, k, cc * 128:(cc + 1) * 128],
                                   rhs=hTa[:, k, tb * 512:(tb + 1) * 512], start=(k == 0), stop=(k == 7))
                return ins

            c.pe(mm1, r=hkeys + ["w1g%d" % gb], w=["pm%d" % pb])
            c.vec("tensor_scalar", r=["pm%d" % pb], w=["rl%d" % rb], out=rl[rb][:], in0=pm[pb][:], scalar1=0.0,
                  scalar2=None, op0=ALU.max)
            c.act(actb[ab][:, cc, :], rl[rb][:], AF.Square, r=["rl%d" % rb], w=["actb%d_%d" % (ab, cc)])

    def stage2(g, tb, ab):
        gb = g % 2
        akeys = ["actb%d_%d" % (ab, cc) for cc in range(CPG)]
        for tt in range(4):
            ti = tb * 4 + tt
            for cb in range(2):
                pb = cnt[0] % 4
                cnt[0] += 1

                def mm2(e, gb=gb, ab=ab, tt=tt, cb=cb, pb=pb):
                    for cc in range(CPG):
                        ins = e.matmul(pm[pb][:], lhsT=actb[ab][:, cc, tt * 128:(tt + 1) * 128],
                                       rhs=w2g[gb][:, cc, cb * 512:(cb + 1) * 512], start=(cc == 0), stop=(cc == CPG - 1))
                    return ins

                c.pe(mm2, r=akeys + ["w2g%d" % gb], w=["pm%d" % pb])
                c.vec("tensor_tensor", r=["pm%d" % pb, "xres%d" % ti], w=["xres%d" % ti],
                      out=xres[:, ti, cb * 512:(cb + 1) * 512], in0=pm[pb][:], in1=xres[:, ti, cb * 512:(cb + 1) * 512],
                      op=ALU.add)

    items = [(g, tb) for g in range(NG) for tb in range(NTB)]
    wload(0)
    wload(1)
    stage1(items[0][0], items[0][1], 0)
    for i_, (g, tb) in enumerate(items):
        if i_ + 1 < len(items):
            stage1(items[i_ + 1][0], items[i_ + 1][1], (i_ + 1) % 2)
        stage2(g, tb, i_ % 2)
        if tb == NTB - 1 and g + 2 < NG:
            wload(g + 2)
    c.stage_end()


def f_meven(c, idb, cst, y_tm, y_fm, cat_tm, dd, lname):
    mF, mB, segm, sel = cst["mF"], cst["mB"], cst["segm"], cst["sel"]
    ex_in = c.dint("exs_in" + lname, [512, 128]); ex_out = c.dint("exs_out" + lname, [1024, 128])
    c.stage_begin()
    sgug = c.sb("sgug", [128, 512]); c.load_const("sgug", sgug[:], dd["sgug"])
    wsb = c.sb("wsb", [128, 4, 128], BF16); c.load_const("wsb", wsb[:], dd["wsT"], eng="gpsimd")
    bs = c.sb("bs", [128, 4]); c.load_const("bs", bs[:], dd["bs"])
    lbl = c.sb("lbl", [128, 16]); c.load_const("lbl", lbl[:], dd["lbl"])
    lbm = c.sb("lbm", [128, 1]); c.load_const("lbm", lbm[:], dd["lbm"])
    hng = c.sb("hng", [128, 512]); c.load_const("hng", hng[:], dd["hng"])
    c.const_done()
    onesb = c.sb("onesb", [128, 1]); c.vec("memset", r=[], w=["onesb"], ap=onesb[:], constant=1.0)
    epsb = c.sb("epsb", [128, 1]); c.vec("memset", r=[], w=["epsb"], ap=epsb[:], constant=EPS)
    lb8 = c.sb("lb8", [128, 8]); oml8 = c.sb("oml8", [128, 8]); noml8 = c.sb("noml8", [128, 8])
    l3 = lbl[:].rearrange("p (a e) -> p a e", e=2)
    c.vec("tensor_tensor", r=["lbl"], w=["lb8"], out=lb8[:].rearrange("p (a o) -> p a o", o=1), in0=l3[:, :, 1:2],
          in1=l3[:, :, 0:1], op=ALU.subtract)
    c.act(lb8[:], lb8[:], AF.Sigmoid, r=["lb8"], w=["lb8"])
    c.vec("tensor_scalar", r=["lb8", "lbm"], w=["lb8"], out=lb8[:], in0=lb8[:], scalar1=lbm[:, 0:1], scalar2=None, op0=ALU.mult)
    c.vec("tensor_scalar", r=["lb8"], w=["oml8"], out=oml8[:], in0=lb8[:], scalar1=-1.0, scalar2=1.0, op0=ALU.mult, op1=ALU.add)
    c.vec("tensor_scalar", r=["lb8"], w=["noml8"], out=noml8[:], in0=lb8[:], scalar1=1.0, scalar2=-1.0, op0=ALU.mult, op1=ALU.add)
    c.stage_begin()
    pm = c.ps("pm", [128, 512])
    ut = [c.sb("ut%d" % i, [128, 512]) for i in range(2)]
    vt = [c.sb("vt%d" % i, [128, 512]) for i in range(2)]
    vn = [c.sb("vn%d" % i, [128, 512], BF16) for i in range(2)]
    oa = [c.sb("oa%d" % i, [128, 512]) for i in range(2)]
    sj = c.sb("sj", [128, 128]); ssg = c.sb("ssg", [128, 4])
    for n in range(NTI):
        b = n % 2
        uk, vk, nk, ok = "ut%d" % b, "vt%d" % b, "vn%d" % b, "oa%d" % b
        c.load(uk, ut[b][:], y_tm[n * 128:(n + 1) * 128, 0:512])
        c.load(vk, vt[b][:], y_tm[n * 128:(n + 1) * 128, 512:1024])
        c.act(vt[b][:], vt[b][:], AF.Gelu_apprx_tanh, r=[vk], w=[vk])
        c.act(ut[b][:], ut[b][:], AF.Gelu_apprx_tanh, r=[uk], w=[uk])
        for h in range(4):
            c.act(sj[:], vt[b][:, h * 128:(h + 1) * 128], AF.Square, r=[vk], w=["sj", "ssg"], accum_out=ssg[:, h:h + 1])
        rstd_ops(c, ssg, 4, 1.0 / 128, ["ssg"], "ssg")
        for h in range(4):
            hs = slice(h * 128, (h + 1) * 128)
            c.vec("scalar_tensor_tensor", r=[vk, "ssg", "sgug"], w=[nk], out=vn[b][:, hs], in0=vt[b][:, hs],
                  scalar=ssg[:, h:h + 1], in1=sgug[:, hs], op0=ALU.mult, op1=ALU.mult)

        def mm(e, b=b):
            for h in range(4):
                ins = e.matmul(pm[:, h * 128:(h + 1) * 128], lhsT=wsb[:, h, :], rhs=vn[b][:, h * 128:(h + 1) * 128],
                               start=True, stop=True)
            return ins

        c.pe(mm, r=[nk, "wsb"], w=["pm"])
        for h in range(4):
            hs = slice(h * 128, (h + 1) * 128)
            c.vec("scalar_tensor_tensor", r=["pm", "bs", uk], w=[ok], out=oa[b][:, hs], in0=pm[:, hs],
                  scalar=bs[:, h:h + 1], in1=ut[b][:, hs], op0=ALU.add, op1=ALU.mult)
        c.P.dma("sync", c.slot("st_" + ok), cat_tm[n * 128:(n + 1) * 128, 0:512], oa[b][:], r=[ok])
    c.stage_end()
    c.stage_begin()
    NCH = 2
    pkt = [c.ps("pkt%d" % s, [128, 1024], BF16) for s in range(NCH)]
    pa = [c.ps("pa%d" % s, [128, 512]) for s in range(NCH)]
    po = [c.ps("po%d" % s, [128, 512]) for s in range(NCH)]
    pu = [c.ps("pu%d" % s, [128, 512]) for s in range(NCH)]
    OF = c.sb("OF", [128, 4 * NTI, 128])
    B_ = []
    for s in range(NCH):
        d = {}
        d["A"] = [c.sb("A%d_%d" % (s, i), [128, 512]) for i in range(2)]
        d["Q"] = [c.sb("Q%d_%d" % (s, i), [128, 512]) for i in range(2)]
        d["IV"] = [c.sb("IV%d_%d" % (s, i), [128, 4, 128], BF16) for i in range(2)]
        d["GG"] = [c.sb("GG%d_%d" % (s, i), [128, 512]) for i in range(2)]
        for nm in ("L", "KK", "BFW", "BB", "BR", "EQ", "RC"):
            d[nm] = c.sb("%s%d" % (nm, s), [128, 512])
        for nm in ("QE", "KD", "QI", "KDEC", "kdT"):
            d[nm] = c.sb("%s%d" % (nm, s), [128, 512], BF16)
        d["KDZ"] = [c.sb("KDZ%d_%d" % (s, i), [128, 512], BF16) for i in range(2)]
        d["dS"] = c.sb("dS%d" % s, [128, 4]); d["nref"] = c.sb("nref%d" % s, [128, 4])
        d["attT"] = c.sb("attT%d" % s, [128, 128], BF16)
        d["S32"] = c.sb("S32_%d" % s, [128, 128]); d["Sbf"] = c.sb("Sbf%d" % s, [128, 128], BF16)
        d["SG"] = c.sb("SG%d" % s, [128, 2, 128])
        d["osum"] = c.sb("osum%d" % s, [128, 128]); d["obst"] = [c.sb("obst%d_%d" % (s, i), [128, 128]) for i in range(2)]
        d["sj2"] = c.sb("sj2_%d" % s, [128, 128]); d["ssh"] = c.sb("ssh%d" % s, [128, 4])
        for i in range(2):
            c.vec("memset", r=[], w=["KDZ%d_%d" % (s, i)], ap=d["KDZ"][i][:], constant=0.0)
        B_.append(d)

    def chain(s, hh, dr):
        d = B_[s]
        K = lambda n: "%s_c%d" % (n, s)
        L, KK, BFW, BB, BR, EQ, RC = d["L"], d["KK"], d["BFW"], d["BB"], d["BR"], d["EQ"], d["RC"]
        QE, KD, QI, KDEC, kdT, KDZ = d["QE"], d["KD"], d["QI"], d["KDEC"], d["kdT"], d["KDZ"]
        dS, attT, S32, Sbf, SG = d["dS"], d["attT"], d["S32"], d["Sbf"], d["SG"]
        osum, obst, sj2, ssh = d["osum"], d["obst"], d["sj2"], d["ssh"]
        col = dr * 4 + hh
        lbc, omlc, nomlc = lb8[:, col:col + 1], oml8[:, col:col + 1], noml8[:, col:col + 1]
        if dr == 0:
            c.vec("memset", r=[], w=[K("S32")], ap=S32[:], constant=0.0)
            c.vec("memset", r=[], w=[K("Sbf")], ap=Sbf[:], constant=0.0)
        else:
            c.P.dma("sync", c.slot("SG%d" % s), SG[:], ex_out.rearrange("(r h p) v -> p r h v", r=2, h=4)[:, :, hh, :],
                    r=["ex_out"], w=[K("SG")])
            c.vec("tensor_scalar", r=[K("SG"), "sel"], w=[K("S32")], out=S32[:], in0=SG[:, 0, :], scalar1=sel[:, 0:1], scalar2=None,
                  op0=ALU.mult)
            c.vec("scalar_tensor_tensor", r=[K("SG"), "sel", K("S32")], w=[K("S32")], out=S32[:], in0=SG[:, 1, :], scalar=sel[:, 1:2],
                  in1=S32[:], op0=ALU.mult, op1=ALU.add)
            c.act(Sbf[:], S32[:], AF.Copy, r=[K("S32")], w=[K("Sbf")])
        yield
        tbs = range(TOK // 512) if dr == 0 else range(TOK // 512 - 1, -1, -1)
        nob = 0
        for ntb, tb in enumerate(tbs):
            p2 = ntb % 2
            ak, qk, ik, gk = K("A%d" % p2), K("Q%d" % p2), K("IV%d" % p2), K("GG%d" % p2)
            a_, q_, IVt, GGt = d["A"][p2], d["Q"][p2], d["IV"][p2], d["GG"][p2]
            tbs_ = slice(tb * 512, (tb + 1) * 512)
            c.load(ak, a_[:], y_fm[512 + dr * 512 + hh * 128:512 + dr * 512 + (hh + 1) * 128, tbs_])
            c.load(qk, q_[:], y_fm[hh * 128:(hh + 1) * 128, tbs_])
            c.load(ik, IVt[:], y_tm[tbs_, 1024 + hh * 128:1024 + (hh + 1) * 128].rearrange("(t p) c -> p t c", p=128), eng="gpsimd")
            if dr == 1:
                c.load(gk, v3(GGt), y_tm[tbs_, 1536 + hh * 128:1536 + (hh + 1) * 128].rearrange("(t p) c -> p t c", p=128))
                c.act(RC[:], GGt[:], AF.Exp, r=[gk], w=[K("RC")], scale=-1.0)
                yield
                c.vec("tensor_scalar", r=[K("RC")], w=[K("RC")], out=RC[:], in0=RC[:], scalar1=1.0, scalar2=None, op0=ALU.add)
                c.vec("reciprocal", r=[K("RC")], w=[K("RC")], out=RC[:], in_=RC[:])
                c.vec("tensor_tensor", r=[K("RC"), gk], w=[gk], out=GGt[:], in0=GGt[:], in1=RC[:], op=ALU.mult)
                yield
            c.act(a_[:], a_[:], AF.Exp, r=[ak], w=[ak], scale=-1.0)
            yield
            c.vec("tensor_scalar", r=[ak], w=[K("RC")], out=RC[:], in0=a_[:], scalar1=1.0, scalar2=None, op0=ALU.add)
            c.act(L[:], a_[:], AF.Ln, r=[ak, "lb8", "onesb"], w=[K("L")], scale=lbc, bias=onesb[:, 0:1])
            yield
            c.act(KK[:], RC[:], AF.Ln, r=[K("RC")], w=[K("KK")])
            c.vec("reciprocal", r=[K("RC"), K("KK")], w=[K("RC")], out=RC[:], in_=RC[:])
            yield
            c.vec("tensor_tensor", r=[K("L"), K("KK")], w=[K("L")], out=L[:], in0=L[:], in1=KK[:], op=ALU.subtract)
            c.vec("scalar_tensor_tensor", r=[ak, "oml8", K("RC"), K("L")], w=[K("KK")], out=KK[:], in0=a_[:], scalar=omlc, in1=RC[:],
                  op0=ALU.mult, op1=ALU.mult)
            yield
            c.vec("tensor_tensor_scan", r=[K("L"), "segm"], w=[K("BFW")], out=BFW[:], data0=segm[:], data1=L[:], initial=0.0,
                  op0=ALU.mult, op1=ALU.add)
            yield
            if dr == 0:
                Bt, bkey, ri, li = BFW, K("BFW"), 63, 127
            else:
                c.vec("tensor_tensor", r=[K("L"), K("BFW")], w=[K("L")], out=L[:], in0=L[:], in1=BFW[:], op=ALU.subtract)
                c.vec("tensor_tensor", r=[K("L"), K("BFW")], w=[K("BB")], out=v3(BB), in0=v3(L),
                      in1=v3(BFW)[:, :, 127:128].to_broadcast([128, 4, 128]), op=ALU.add)
                Bt, bkey, ri, li = BB, K("BB"), 64, 0
            B3 = v3(Bt)
            c.vec("tensor_scalar", r=[bkey], w=[K("nref")], out=d["nref"][:].rearrange("p (t o) -> p t o", o=1),
                  in0=B3[:, :, ri:ri + 1], scalar1=-1.0, scalar2=None, op0=ALU.mult)
            yield
            for t in range(4):
                c.act(EQ[:, t * 128:(t + 1) * 128], Bt[:, t * 128:(t + 1) * 128], AF.Exp, r=[bkey, K("nref")], w=[K("EQ")],
                      bias=d["nref"][:, t:t + 1])
            yield
            c.vec("tensor_tensor", r=[qk, K("EQ")], w=[K("QE")], out=QE[:], in0=q_[:], in1=EQ[:], op=ALU.mult)
            for t in range(4):
                c.act(BR[:, t * 128:(t + 1) * 128], Bt[:, t * 128:(t + 1) * 128], AF.Exp, r=[bkey], w=[K("BR")], scale=-1.0,
                      bias=B3[:, t, ri:ri + 1])
            yield
            c.vec("tensor_tensor", r=[K("KK"), K("BR")], w=[K("KD")], out=KD[:], in0=KK[:], in1=BR[:], op=ALU.mult)
            hsl = slice(0, 64) if dr == 0 else slice(64, 128)
            c.vec("tensor_tensor", r=[K("KK"), K("BR")], w=[K("KDZ%d" % dr)], out=v3(KDZ[dr])[:, :, hsl], in0=v3(KK)[:, :, hsl],
                  in1=v3(BR)[:, :, hsl], op=ALU.mult)
            c.act(EQ[:], Bt[:], AF.Exp, r=[bkey, K("QE")], w=[K("EQ")])
            yield
            c.vec("tensor_tensor", r=[qk, K("EQ")], w=[K("QI")], out=QI[:], in0=q_[:], in1=EQ[:], op=ALU.mult)
            for t in range(4):
                c.act(BR[:, t * 128:(t + 1) * 128], Bt[:, t * 128:(t + 1) * 128], AF.Exp, r=[bkey, K("KD"), K("KDZ%d" % dr)],
                      w=[K("BR")], scale=-1.0, bias=B3[:, t, li:li + 1])
            c.act(dS[:].rearrange("p (t o) -> p t o", o=1), B3[:, :, li:li + 1], AF.Exp, r=[bkey], w=[K("dS")])
            yield
            c.vec("tensor_tensor", r=[K("KK"), K("BR")], w=[K("KDEC")], out=KDEC[:], in0=KK[:], in1=BR[:], op=ALU.mult)
            yield

            def trk(e):
                for t in range(4):
                    ins = e.transpose(pkt[s][:, t * 128:(t + 1) * 128], KDEC[:, t * 128:(t + 1) * 128], idb[:])
                return ins

            c.pe(trk, r=[K("KDEC"), "idb"], w=[K("pkt")])
            yield
            c.act(kdT[:], pkt[s][:, 0:512], AF.Copy, r=[K("pkt")], w=[K("kdT")])
            yield
            tts = range(4) if dr == 0 else range(3, -1, -1)
            for tt in tts:
                ti = tb * 4 + tt
                oi = hh * NTI + ti
                kA, kB = (KDZ[0], KD) if dr == 0 else (KD, KDZ[1])

                def mma(e, tt=tt, kA=kA, kB=kB):
                    e.matmul(pa[s][:, 0:64], lhsT=kA[:, tt * 128:(tt + 1) * 128], rhs=QE[:, tt * 128:tt * 128 + 64],
                             start=True, stop=True)
                    return e.matmul(pa[s][:, 64:128], lhsT=kB[:, tt * 128:(tt + 1) * 128],
                                    rhs=QE[:, tt * 128 + 64:(tt + 1) * 128], start=True, stop=True)

                c.pe(mma, r=[K("KD"), K("KDZ%d" % dr), K("QE")], w=[K("pa")])
                c.pe(lambda e, tt=tt, IVt=IVt: e.matmul(pu[s][:, 0:128], lhsT=kdT[:, tt * 128:(tt + 1) * 128], rhs=IVt[:, tt, :],
                                               start=True, stop=True), r=[K("kdT"), ik], w=[K("pu")])
                yield
                c.vec("tensor_tensor", r=[K("pa"), "mF", "mB"], w=[K("attT")], out=attT[:], in0=pa[s][:, 0:128],
                      in1=(mF if dr == 0 else mB)[:], op=ALU.mult)
                yield

                def mmo(e, tt=tt, IVt=IVt):
                    e.matmul(po[s][:, 0:128], lhsT=QI[:, tt * 128:(tt + 1) * 128], rhs=Sbf[:], start=True, stop=False)
                    return e.matmul(po[s][:, 0:128], lhsT=attT[:], rhs=IVt[:, tt, :], start=False, stop=True)

                c.pe(mmo, r=[K("QI"), K("Sbf"), K("attT"), ik], w=[K("po")])
                yield
                c.vec("scalar_tensor_tensor", r=[K("S32"), K("dS"), K("pu")], w=[K("S32")], out=S32[:], in0=S32[:],
                      scalar=dS[:, tt:tt + 1], in1=pu[s][:, 0:128], op0=ALU.mult, op1=ALU.add)
                yield
                c.act(Sbf[:], S32[:], AF.Copy, r=[K("S32")], w=[K("Sbf")])
                if dr == 0:
                    c.act(OF[:, oi, :], po[s][:, 0:128], AF.Copy, r=[K("po")], w=["OF%d" % oi])
                    yield
                else:
                    ob_ = nob % 2
                    nob += 1
                    c.vec("tensor_tensor", r=[K("po"), "OF%d" % oi], w=[K("osum")], out=osum[:],
                          in0=po[s][:, 0:128], in1=OF[:, oi, :], op=ALU.add)
                    yield
                    c.act(sj2[:], osum[:], AF.Square, r=[K("osum")], w=[K("sj2"), K("ssh")], accum_out=ssh[:, 0:1])
                    yield
                    c.act(ssh[:, 0:1], ssh[:, 0:1], AF.Ln, r=[K("ssh"), "epsb"], w=[K("ssh")], scale=1.0 / 128, bias=epsb[:, 0:1])
                    c.act(ssh[:, 0:1], ssh[:, 0:1], AF.Exp, r=[K("ssh")], w=[K("ssh")], scale=-0.5)
                    yield
                    c.vec("scalar_tensor_tensor", r=[K("osum"), K("ssh"), "hng"], w=[K("osum")], out=osum[:],
                          in0=osum[:], scalar=ssh[:, 0:1], in1=hng[:, hh * 128:(hh + 1) * 128], op0=ALU.mult, op1=ALU.mult)
                    c.vec("tensor_tensor", r=[K("osum"), gk], w=[K("obst%d" % ob_)], out=obst[ob_][:], in0=osum[:],
                          in1=GGt[:, tt * 128:(tt + 1) * 128], op=ALU.mult)
                    c.P.dma("sync", c.slot("st_obst%d_%d" % (s, ob_)), cat_tm[ti * 128:(ti + 1) * 128, 512 + hh * 128:512 + (hh + 1) * 128],
                            obst[ob_][:], r=[K("obst%d" % ob_)])
                    yield
        if dr == 0:
            c.P.dma("sync", c.slot("st_S32_%d" % s), ex_in[hh * 128:(hh + 1) * 128, :], S32[:], r=[K("S32")], w=["ex_in%d" % hh])
        yield

    for dr in range(2):
        if dr == 1:
            slot_ag = c.P.slot()

            def agfn(e, slot_ag=slot_ag):
                return e.collective_compute("AllGather", ALU.bypass, replica_groups=GROUPS, ins=[ex_in], outs=[ex_out]).then_inc(slot_ag.sem)

            c.P.add("gpsimd", agfn, r=["ex_in%d" % h for h in range(4)], w=["ex_out"], slot=slot_ag, dval=1)
        for h0 in range(0, 4, NCH):
            gens = [chain(s, h0 + s, dr) for s in range(NCH)]
            alive = [True] * NCH
            while any(alive):
                for s in range(NCH):
                    if alive[s]:
                        try:
                            next(gens[s])
                        except StopIteration:
                            alive[s] = False
    c.stage_end()
    c.stage_end()


def f_modd(c, idb, cst, y_tm, y_fm, cat_tm, cat_fm, dd, lname):
    sel = cst["sel"]
    zx_in = c.dint("zx_in" + lname, [128, 4]); zx_out = c.dint("zx_out" + lname, [256, 4])
    kx_in = c.dint("kx_in" + lname, [128, 4 * TOK], BF16); kx_out = c.dint("kx_out" + lname, [256, 4 * TOK], BF16)
    vx_in = c.dint("vx_in" + lname, [TOK, 512], BF16); vx_out = c.dint("vx_out" + lname, [2 * TOK, 512], BF16)
    NKT = 2 * NTI
    c.stage_begin()
    cw = c.sb("cw", [128, 12]); c.load_const("cw", cw[:], dd["cw"])
    gv = c.sb("gv", [128, 4]); c.load_const("gv", gv[:], dd["gv"])
    cosT = c.sb("cosT", [128, TOK]); c.load_const("cosT", cosT[:], cst["cosT"])
    sinT = c.sb("sinT", [128, TOK]); c.load_const("sinT", sinT[:], cst["sinT"])
    lamv = c.sb("lamv", [128, 4, 64]); c.load_const("lamv", lamv[:], dd["lamv"])
    lconst = c.sb("lconst", [128, 2]); c.load_const("lconst", lconst[:], dd["lconst"])
    subg = c.sb("subg", [128, 128]); c.load_const("subg", subg[:], dd["subg"])
    bones = c.sb("bones", [128, 128]); c.load_const("bones", bones[:], cst["bones"])
    c.const_done()
    epsb = c.sb("epsb", [128, 1]); c.vec("memset", r=[], w=["epsb"], ap=epsb[:], constant=EPS)
    lj = c.sb("lj", [128, 64]); ls = c.sb("ls", [128, 2]); nlam = c.sb("nlam", [128, 1])
    for j in range(2):
        c.vec("tensor_tensor", r=["lamv"], w=["lj"], out=lj[:], in0=lamv[:, 2 * j, :], in1=lamv[:, 2 * j + 1, :], op=ALU.mult)
        c.vec("tensor_reduce", r=["lj"], w=["ls%d" % j], out=ls[:, j:j + 1], in_=lj[:], axis=AX.X, op=ALU.add)
    c.act(ls[:], ls[:], AF.Exp, r=["ls0", "ls1"], w=["ls"])
    c.vec("tensor_tensor", r=["ls"], w=["nlam"], out=nlam[:], in0=ls[:, 1:2], in1=ls[:, 0:1], op=ALU.subtract)
    c.vec("tensor_tensor", r=["nlam", "lconst"], w=["nlam"], out=nlam[:], in0=nlam[:], in1=lconst[:, 0:1], op=ALU.subtract)
    c.vec("tensor_scalar", r=["subg", "lconst"], w=["subg"], out=subg[:], in0=subg[:], scalar1=lconst[:, 1:2], scalar2=None,
          op0=ALU.mult)
    z4 = c.sb("z4", [128, 4, TOK + 2])
    xt = [c.sb("axt%d" % i, [128, 512]) for i in range(2)]
    xp = [c.sb("axp%d" % i, [128, 512]) for i in range(2)]
    sq = c.sb("sq", [128, 512]); rr = c.sb("rr", [128, 512]); t1 = c.sb("t1", [128, 512]); t2 = c.sb("t2", [128, 512])
    pss = c.ps("pss", [128, 512])
    pS = [c.ps("pS%d" % i, [128, 1024]) for i in range(2)]
    pacc = c.ps("pacc", [128, 3, 512])
    nbc = [0]

    def qk_prep(rows0, rowsp0, hh, dst, dkey, gi, nblk):
        for blk in range(nblk):
            b = nbc[0] % 2
            nbc[0] += 1
            bs_ = slice(blk * 512, (blk + 1) * 512)
            xk, pk = "axt%d" % b, "axp%d" % b
            c.load(xk, xt[b][:], y_fm[rows0 + hh * 128:rows0 + (hh + 1) * 128, bs_])
            c.load(pk, xp[b][:], y_fm[rowsp0 + hh * 128:rowsp0 + (hh + 1) * 128, bs_])
            c.act(sq[:], xt[b][:], AF.Square, r=[xk], w=["sq"])
            c.pe(lambda e: e.matmul(pss[:], lhsT=bones[:], rhs=sq[:], start=True, stop=True), r=["sq", "bones"], w=["pss"])
            c.act(rr[:], pss[:], AF.Ln, r=["pss", "epsb"], w=["rr"], scale=1.0 / 64, bias=epsb[:, 0:1])
            c.act(rr[:], rr[:], AF.Exp, r=["rr"], w=["rr"], scale=-0.5)
            c.vec("scalar_tensor_tensor", r=[xk, "gv", "cosT"], w=["t1"], out=t1[:], in0=xt[b][:], scalar=gv[:, gi:gi + 1],
                  in1=cosT[:, bs_], op0=ALU.mult, op1=ALU.mult)
            c.vec("scalar_tensor_tensor", r=[pk, "gv", "sinT"], w=["t2"], out=t2[:], in0=xp[b][:], scalar=gv[:, gi + 1:gi + 2],
                  in1=sinT[:, bs_], op0=ALU.mult, op1=ALU.mult)
            c.vec("tensor_tensor", r=["t1", "t2"], w=["t1"], out=t1[:], in0=t1[:], in1=t2[:], op=ALU.add)
            c.vec("tensor_tensor", r=["t1", "rr"], w=[dkey], out=dst(blk), in0=t1[:], in1=rr[:], op=ALU.mult)

    c.stage_begin()
    hin = c.sb("hin", [128, TOK]); cg = c.sb("cg", [128, TOK])
    Kown = c.sb("Kown", [128, 4, TOK], BF16)
    vbf = c.sb("vbf", [128, NTI, 512], BF16)
    c.vec("memset", r=[], w=["z4"], ap=z4[:], constant=0.0)
    for g in range(4):
        c.load("hin", hin[:], y_fm[g * 128:(g + 1) * 128, :])
        c.load("cg", cg[:], y_fm[1024 + g * 128:1024 + (g + 1) * 128, :])
        c.vec("tensor_tensor", r=["hin", "cg", "z4"], w=["z4_%d" % g], out=z4[:, g, 1:TOK + 1], in0=cg[:], in1=hin[:], op=ALU.mult)
    zc = c.sb("zc", [128, 4])
    c.vec("tensor_copy", r=["z4_%d" % g for g in range(4)], w=["zc"], out=zc[:], in_=z4[:, :, TOK])
    c.P.dma("sync", c.slot("st_zx"), zx_in, zc[:], r=["zc"], w=["zx_in"])
    allgather(c, zx_in, zx_out, "zx_in", "zx_out")
    c.load("vbf", vbf[:], y_tm[:, 0:512].rearrange("(t p) c -> p t c", p=128), eng="gpsimd")
    c.P.dma("sync", c.slot("st_vx"), vx_in.rearrange("(t p) c -> p t c", p=128), vbf[:], r=["vbf"], w=["vx_in"])
    allgather(c, vx_in, vx_out, "vx_in", "vx_out")
    for hh in range(4):
        qk_prep(2048, 3072, hh, lambda blk, hh=hh: Kown[:, hh, blk * 512:(blk + 1) * 512], "Kown", 2, TOK // 512)
    c.P.dma("sync", c.slot("st_kx"), kx_in, Kown[:].rearrange("p h t -> p (h t)"), r=["Kown"], w=["kx_in"])
    allgather(c, kx_in, kx_out, "kx_in", "kx_out")
    c.stage_end()
    c.stage_begin()
    bg = c.sb("bg", [128, TOK]); yy = c.sb("yy", [128, TOK]); ZG = c.sb("ZG", [128, 2, 4]); zh = c.sb("zh", [128, 4])
    c.P.dma("sync", c.slot("ZG"), ZG[:], zx_out.rearrange("(r p) g -> p r g", r=2), r=["zx_out"], w=["ZG"])
    c.vec("tensor_scalar", r=["ZG", "sel"], w=["zh"], out=zh[:], in0=ZG[:, 0, :], scalar1=sel[:, 0:1], scalar2=None, op0=ALU.mult)
    c.vec("scalar_tensor_tensor", r=["ZG", "sel", "zh"], w=["zh"], out=zh[:], in0=ZG[:, 1, :], scalar=sel[:, 1:2], in1=zh[:],
          op0=ALU.mult, op1=ALU.add)
    c.vec("tensor_copy", r=["zh"], w=["z4h"], out=z4[:, :, TOK + 1], in_=zh[:])
    for g in range(4):
        c.load("bg", bg[:], y_fm[512 + g * 128:512 + (g + 1) * 128, :])
        c.vec("tensor_scalar", r=["z4h", "cw", "yy"], w=["yy"], out=yy[:], in0=z4[:, g, 1:TOK + 1], scalar1=cw[:, 3 * g + 1:3 * g + 2],
              scalar2=None, op0=ALU.mult)
        c.vec("scalar_tensor_tensor", r=["cw", "yy"], w=["yy"], out=yy[:], in0=z4[:, g, 0:TOK], scalar=cw[:, 3 * g:3 * g + 1],
              in1=yy[:], op0=ALU.mult, op1=ALU.add)
        c.vec("scalar_tensor_tensor", r=["cw", "yy"], w=["yy"], out=yy[:], in0=z4[:, g, 2:TOK + 2],
              scalar=cw[:, 3 * g + 2:3 * g + 3], in1=yy[:], op0=ALU.mult, op1=ALU.add)
        c.vec("tensor_tensor", r=["yy", "bg"], w=["yy"], out=yy[:], in0=yy[:], in1=bg[:], op=ALU.mult)
        c.P.dma("sync", c.slot("st_yy"), cat_fm[g * 128:(g + 1) * 128, :], yy[:], r=["yy"])
    c.stage_end()
    c.stage_begin()
    Kr = [c.sb("Kr%d" % i, [128, 2 * TOK], BF16) for i in range(2)]
    Qr = [c.sb("Qr%d" % i, [128, TOK], BF16) for i in range(2)]
    Vaug = [c.sb("Vaug%d" % i, [128, NKT, 132], BF16) for i in range(2)]
    pT = [c.sb("pTs%d" % i, [128, 1024], BF16) for i in range(2)]
    rc = c.sb("rc", [128, 8]); o1 = [c.sb("o1_%d" % i, [128, 128]) for i in range(4)]
    odt = [c.sb("odt%d" % i, [128, 128]) for i in range(2)]
    sj = c.sb("sj", [128, 128]); ssd = c.sb("ssd", [128, 4])

    def head_loads(hh):
        hp = hh % 2
        for r_ in range(2):
            c.P.dma("sync", c.slot("Kr%d_%d" % (hp, r_)), Kr[hp][:, r_ * TOK:(r_ + 1) * TOK],
                    kx_out[r_ * 128:(r_ + 1) * 128, hh * TOK:(hh + 1) * TOK], r=["kx_out"], w=["Kr%d_%d" % (hp, r_)])
        c.P.dma("sync", c.slot("Vaug%d" % hp), Vaug[hp][:, :, 0:128],
                vx_out[:, hh * 128:(hh + 1) * 128].rearrange("(t p) c -> p t c", p=128), r=["vx_out"], w=["Vaug%d" % hp])
        c.vec("memset", r=[], w=["Vones%d" % hp], ap=Vaug[hp][:, :, 128:129], constant=1.0)

    def q_block(hh, blk):
        hp = hh % 2
        b = nbc[0] % 2
        nbc[0] += 1
        bs_ = slice(blk * 512, (blk + 1) * 512)
        xk, pk = "axt%d" % b, "axp%d" % b
        c.load(xk, xt[b][:], y_fm[1536 + hh * 128:1536 + (hh + 1) * 128, bs_])
        c.load(pk, xp[b][:], y_fm[2560 + hh * 128:2560 + (hh + 1) * 128, bs_])
        c.vec("tensor_tensor", r=[xk], w=["sq"], out=sq[:], in0=xt[b][:], in1=xt[b][:], op=ALU.mult)
        c.pe(lambda e: e.matmul(pss[:], lhsT=bones[:], rhs=sq[:], start=True, stop=True), r=["sq", "bones"], w=["pss"])
        c.act(rr[:], pss[:], AF.Ln, r=["pss", "epsb"], w=["rr"], scale=1.0 / 64, bias=epsb[:, 0:1])
        c.act(rr[:], rr[:], AF.Exp, r=["rr"], w=["rr"], scale=-0.5)
        c.vec("scalar_tensor_tensor", r=[xk, "gv", "cosT"], w=["t1"], out=t1[:], in0=xt[b][:], scalar=gv[:, 0:1],
              in1=cosT[:, bs_], op0=ALU.mult, op1=ALU.mult)
        c.vec("scalar_tensor_tensor", r=[pk, "gv", "sinT"], w=["t2"], out=t2[:], in0=xp[b][:], scalar=gv[:, 1:2],
              in1=sinT[:, bs_], op0=ALU.mult, op1=ALU.mult)
        c.vec("tensor_tensor", r=["t1", "t2"], w=["t1"], out=t1[:], in0=t1[:], in1=t2[:], op=ALU.add)
        c.vec("tensor_tensor", r=["t1", "rr"], w=["Qr%d_%d" % (hp, blk)], out=Qr[hp][:, bs_], in0=t1[:], in1=rr[:], op=ALU.mult)

    def sc_exp(hh, qb, kt, sb_):
        hp = hh % 2

        def mms(e):
            e.matmul(pS[sb_][:, 0:512], lhsT=Kr[hp][0:64, kt * 128:(kt + 1) * 128], rhs=Qr[hp][0:64, qb * 512:(qb + 1) * 512],
                     start=True, stop=True)
            return e.matmul(pS[sb_][:, 512:1024], lhsT=Kr[hp][64:128, kt * 128:(kt + 1) * 128],
                            rhs=Qr[hp][64:128, qb * 512:(qb + 1) * 512], start=True, stop=True)

        c.pe(mms, r=["Kr%d_%d" % (hp, kt // NTI), "Qr%d_%d" % (hp, qb)], w=["pS%d" % sb_])
        c.act(pT[sb_][:], pS[sb_][:], AF.Exp, r=["pS%d" % sb_], w=["pTs%d" % sb_], scale=0.125)

    def epi1():
        for a in range(8):
            bank, off = a // 3, (a % 3) * 132
            c.vec("reciprocal", r=["pacc"], w=["rc%d" % a], out=rc[:, a:a + 1], in_=pacc[:, bank, off + 128:off + 129])
        c.vec("tensor_scalar", r=["rc%d" % a for a in range(4, 8)] + ["nlam"], w=["rc%d" % a for a in range(4, 8)],
              out=rc[:, 4:8], in0=rc[:, 4:8], scalar1=nlam[:, 0:1], scalar2=None, op0=ALU.mult)
        for qs in range(4):
            a1, a2 = qs, 4 + qs
            c.vec("tensor_scalar", r=["pacc", "rc%d" % a1], w=["o1_%d" % qs], out=o1[qs][:],
                  in0=pacc[:, a1 // 3, (a1 % 3) * 132:(a1 % 3) * 132 + 128], scalar1=rc[:, a1:a1 + 1], scalar2=None, op0=ALU.mult)
            c.vec("scalar_tensor_tensor", r=["pacc", "rc%d" % a2, "o1_%d" % qs], w=["o1_%d" % qs], out=o1[qs][:],
                  in0=pacc[:, a2 // 3, (a2 % 3) * 132:(a2 % 3) * 132 + 128], scalar=rc[:, a2:a2 + 1], in1=o1[qs][:],
                  op0=ALU.mult, op1=ALU.add)

    def epi2(hh, qb, qs):
        e_ = qs % 2
        c.vec("tensor_tensor", r=["o1_%d" % qs], w=["sj"], out=sj[:], in0=o1[qs][:], in1=o1[qs][:], op=ALU.mult)
        c.vec("tensor_reduce", r=["sj"], w=["ssd"], out=ssd[:, 0:1], in_=sj[:], axis=AX.X, op=ALU.add)
        c.act(ssd[:, 0:1], ssd[:, 0:1], AF.Ln, r=["ssd", "epsb"], w=["ssd"], scale=1.0 / 128, bias=epsb[:, 0:1])
        c.act(ssd[:, 0:1], ssd[:, 0:1], AF.Exp, r=["ssd"], w=["ssd"], scale=-0.5)
        c.vec("scalar_tensor_tensor", r=["o1_%d" % qs, "ssd", "subg"], w=["odt%d" % e_], out=odt[e_][:], in0=o1[qs][:],
              scalar=ssd[:, 0:1], in1=subg[:], op0=ALU.mult, op1=ALU.mult)
        ti = qb * 4 + qs
        c.P.dma("sync", c.slot("st_odt%d" % e_), cat_tm[ti * 128:(ti + 1) * 128, 512 + hh * 128:512 + (hh + 1) * 128],
                odt[e_][:], r=["odt%d" % e_])

    its = [(hh, qb, kt) for hh in range(4) for qb in range(TOK // 512) for kt in range(NKT)]
    pending = {}
    head_loads(0)
    for blk in range(TOK // 512):
        q_block(0, blk)
    sc_exp(its[0][0], its[0][1], its[0][2], 0)
    for ii, (hh, qb, kt) in enumerate(its):
        sb_ = ii % 2
        if qb == 0 and kt == 0 and hh + 1 < 4:
            head_loads(hh + 1)
            for blk in range(TOK // 512):
                pending.setdefault(ii + 16 + 24 * blk, []).append(lambda hh=hh, blk=blk: q_block(hh + 1, blk))
        if ii + 1 < len(its):
            sc_exp(its[ii + 1][0], its[ii + 1][1], its[ii + 1][2], (ii + 1) % 2)

        def mmv(e, sb_=sb_, kt=kt, hp=hh % 2):
            for a in range(8):
                bank, off = a // 3, (a % 3) * 132
                ins = e.matmul(pacc[:, bank, off:off + 129], lhsT=pT[sb_][:, a * 128:(a + 1) * 128], rhs=Vaug[hp][:, kt, 0:129],
                               start=(kt == 0 and a % 3 == 0), stop=(kt == NKT - 1), skip_group_check=True)
            return ins

        c.pe(mmv, r=["pTs%d" % sb_, "Vaug%d" % (hh % 2), "Vones%d" % (hh % 2)], w=["pacc"])
        if kt == NKT - 1:
            epi1()
            for qs in range(4):
                pending.setdefault(ii + 2 + qs, []).append(lambda hh=hh, qb=qb, qs=qs: epi2(hh, qb, qs))
        for f in pending.pop(ii, []):
            f()
    for k in sorted(pending):
        for f in pending[k]:
            f()
    c.stage_end()
    c.stage_end()


EVEN_TM = [0, 512, 1536, 2048]
EVEN_FM = list(range(1024, 1536, 128)) + list(range(2560, 3584, 128))
ODD_TM = [2560]
ODD_FM = list(range(0, 2560, 128)) + list(range(3072, 4096, 128))


def build_fused(nlayers=4):
    c = Ctx()
    x = c.din("x", [TOK, D]); xo = c.dout("xo", [TOK, D])
    cst_d = {k: c.din(k, shp) for k, shp in (("ident", [128, 128]), ("maskF", [128, 128]), ("maskB", [128, 128]),
                                             ("segm", [128, 512]), ("sel", [128, 2]), ("bones", [128, 128]),
                                             ("cosT", [128, TOK]), ("sinT", [128, TOK]))}
    L = []
    for l in range(nlayers):
        d = dict(gmix=c.din("gmix%d" % l, [128, 8]), gmlp=c.din("gmlp%d" % l, [128, 8]),
                 w_in=c.din("w_in%d" % l, [D, 3584 if l % 2 == 0 else 4096]), w_out=c.din("w_out%d" % l, [D, D]),
                 w1=c.din("w1_%d" % l, [D, 4 * D]), w2=c.din("w2_%d" % l, [4 * D, D]))
        if l % 2 == 0:
            d.update(sgug=c.din("sgug%d" % l, [128, 512]), wsT=c.din("wsT%d" % l, [128, 4, 128]), bs=c.din("bs%d" % l, [128, 4]),
                     lbl=c.din("lbl%d" % l, [128, 16]), lbm=c.din("lbm%d" % l, [128, 1]), hng=c.din("hng%d" % l, [128, 512]))
        else:
            d.update(cw=c.din("cw%d" % l, [128, 12]), gv=c.din("gv%d" % l, [128, 4]), lamv=c.din("lamv%d" % l, [128, 4, 64]),
                     lconst=c.din("lconst%d" % l, [128, 2]), subg=c.din("subg%d" % l, [128, 128]))
        L.append(d)
    y_tm = c.dint("y_tm", [TOK, 2048]); y_fm = c.dint("y_fm", [3584, TOK])
    cat_tm = c.dint("cat_tm", [TOK, D]); cat_fm = c.dint("cat_fm", [512, TOK])
    xres = c.sb("xres", [128, NTI, D])
    idb = c.sb("idb", [128, 128], BF16); c.load_const("idb", idb[:], cst_d["ident"], eng="gpsimd")
    mF = c.sb("mF", [128, 128]); c.load_const("mF", mF[:], cst_d["maskF"])
    mB = c.sb("mB", [128, 128]); c.load_const("mB", mB[:], cst_d["maskB"])
    segm = c.sb("segm", [128, 512]); c.load_const("segm", segm[:], cst_d["segm"])
    sel = c.sb("sel", [128, 2]); c.load_const("sel", sel[:], cst_d["sel"])
    cst = dict(mF=mF, mB=mB, segm=segm, sel=sel, bones=cst_d["bones"], cosT=cst_d["cosT"], sinT=cst_d["sinT"])
    c.const_done()
    xkeys = ["xres%d" % i for i in range(NTI)]
    c.P.dma("sync", c.slot("xres"), xres[:], x.rearrange("(t p) d -> p t d", p=128), w=xkeys)
    for l in range(nlayers):
        d = L[l]
        if l % 2 == 0:
            f_proj(c, xres, idb, d["w_in"], d["gmix"], 3584, EVEN_TM, EVEN_FM, y_tm, y_fm)
            f_meven(c, idb, cst, y_tm, y_fm, cat_tm, d, "_%d" % l)
            f_outmlp(c, xres, idb, cat_tm, cat_fm, 0, d["w_out"], d["gmlp"], d["w1"], d["w2"])
        else:
            f_proj(c, xres, idb, d["w_in"], d["gmix"], 4096, ODD_TM, ODD_FM, y_tm, y_fm)
            f_modd(c, idb, cst, y_tm, y_fm, cat_tm, cat_fm, d, "_%d" % l)
            f_outmlp(c, xres, idb, cat_tm, cat_fm, 4, d["w_out"], d["gmlp"], d["w1"], d["w2"])
    c.P.store("sync", c.slot("st_xres"), xo.rearrange("(t p) d -> p t d", p=128), xres[:], r=xkeys)
    return c.finish()


def _perm_cols64(w):
    n = w.shape[1]
    idx = np.arange(n).reshape(n // 64, 2, 32)[:, ::-1, :].reshape(n)
    return w[:, idx]


def fused_inputs(inp, nlayers=4):
    maps = []
    inv = 1.0 / (10000.0 ** (np.arange(0, 64, 2, dtype=np.float32) / 64.0))
    for c_ in range(NCORES):
        b, r = c_ // 2, c_ % 2
        xs = inp["x"][b]
        xl = xs[0:TOK] if r == 0 else xs[::-1][0:TOK]
        pos = np.arange(TOK, dtype=np.float32) if r == 0 else (SEQ - 1 - np.arange(TOK)).astype(np.float32)
        ang = pos[:, None] * inv[None, :]
        cos, sin = np.cos(ang).astype(np.float32).T, np.sin(ang).astype(np.float32).T
        m = dict(x=np.ascontiguousarray(xl), ident=_IDENT, maskF=_MASKF, maskB=_MASKB, segm=_SEGM,
                 sel=np.ascontiguousarray(np.broadcast_to(np.array([[0.0, 1.0]] if r == 0 else [[1.0, 0.0]], np.float32), (128, 2))),
                 bones=_BONES, cosT=np.ascontiguousarray(np.concatenate([cos] * 4, 0)),
                 sinT=np.ascontiguousarray(np.concatenate([-sin, sin, -sin, sin], 0)))
        for l in range(nlayers):
            m["gmix%d" % l] = _gT(inp["norm_mix_g"][l]); m["gmlp%d" % l] = _gT(inp["norm_mlp_g"][l])
            m["w1_%d" % l] = np.ascontiguousarray(inp["mlp_w1"][l]); m["w2_%d" % l] = np.ascontiguousarray(inp["mlp_w2"][l])
            if l % 2 == 0:
                e = l // 2
                w = inp["w_in_even"][e]
                if r == 1:
                    w = np.concatenate([w[:, :2560], w[:, 3072:3584], w[:, 2560:3072]], 1)
                m["w_in%d" % l] = np.ascontiguousarray(w)
                m["w_out%d" % l] = np.ascontiguousarray(inp["w_out_even"][e])
                m["sgug%d" % l] = _bc(inp["sgu_norm_g"][e])
                wsT = np.transpose(inp["sgu_w"][e], (2, 0, 1))
                bs = inp["sgu_b"][e].T
                if r == 1:
                    wsT = wsT[::-1, :, ::-1]
                    bs = bs[::-1]
                m["wsT%d" % l] = np.ascontiguousarray(wsT); m["bs%d" % l] = np.ascontiguousarray(bs)
                lbl = np.zeros((128, 16), np.float32)
                for dr in range(2):
                    src = dr if r == 0 else 1 - dr
                    for hh in range(4):
                        for le in range(2):
                            lbl[:, (dr * 4 + hh) * 2 + le] = inp["hgrn_lb_logits"][src, le, hh * 128:(hh + 1) * 128]
                m["lbl%d" % l] = lbl
                m["lbm%d" % l] = np.full((128, 1), float(e), np.float32)
                m["hng%d" % l] = _bc(inp["hgrn_norm_g"][e])
            else:
                o = l // 2
                w = inp["w_in_odd"][o]
                m["w_in%d" % l] = np.ascontiguousarray(np.concatenate([w, _perm_cols64(w[:, 1536:2048]), _perm_cols64(w[:, 2048:2560])], 1))
                m["w_out%d" % l] = np.ascontiguousarray(inp["w_out_odd"][o])
                cwm = inp["conv_w"][o] if r == 0 else inp["conv_w"][o][::-1]
                m["cw%d" % l] = np.ascontiguousarray(np.concatenate([cwm[:, g * 128:(g + 1) * 128].T for g in range(4)], 1))
                qg, kg = np.tile(inp["q_norm_g"][o], 2), np.tile(inp["k_norm_g"][o], 2)
                m["gv%d" % l] = np.ascontiguousarray(np.stack([qg, _perm64(qg), kg, _perm64(kg)], 1).astype(np.float32))
                m["lamv%d" % l] = np.ascontiguousarray(np.stack([_bc(inp[k][o]) for k in ("lambda_q1", "lambda_k1", "lambda_q2", "lambda_k2")], 1))
                lam_init = 0.8 - 0.6 * math.exp(-0.3 * l)
                m["lconst%d" % l] = np.ascontiguousarray(np.stack([np.full(128, lam_init, np.float32), np.full(128, 1.0 - lam_init, np.float32)], 1))
                m["subg%d" % l] = _bc(inp["diff_norm_g"][o])
        maps.append(m)
    return maps


def run_fused(inp, nlayers=4):
    nc = _prog(("F", nlayers), build_fused, nlayers)
    r = _run(nc, fused_inputs(inp, nlayers))
    out = np.empty((4, SEQ, D), np.float32)
    for c_ in range(NCORES):
        b, rr = c_ // 2, c_ % 2
        if rr == 0:
            out[b, 0:TOK] = r[c_]["xo"]
        else:
            out[b, TOK:SEQ] = r[c_]["xo"][::-1]
    return out


def kernel_unfused(**inputs):
    return kernel_12(**inputs)


kernel_12 = kernel


def kernel(**inputs):
    inp = {k: np.asarray(v) for k, v in inputs.items()}
    return run_fused(inp, 4)
```

```python
import math
import numpy as np
import concourse.bass as bass
import concourse.mybir as mybir
from concourse.bass_utils import run_bass_kernel_spmd

F32 = mybir.dt.float32
BF16 = mybir.dt.bfloat16
AF = mybir.ActivationFunctionType
ALU = mybir.AluOpType
AX = mybir.AxisListType

ENGS = ["sync", "gpsimd", "scalar", "vector", "tensor"]
NCORES = 8


class _Op:
    __slots__ = ("eng", "fn", "deps", "needs_inc", "ticket", "slot", "dval")

    def __init__(self, eng, fn):
        self.eng = eng
        self.fn = fn
        self.deps = []
        self.needs_inc = False
        self.ticket = 0
        self.slot = None
        self.dval = 0


class DmaSlot:
    def __init__(self, sem):
        self.sem = sem
        self.count = 0


class Prog:
    def __init__(self, nc, stack):
        self.nc = nc
        self.stack = stack
        self.ops = {e: [] for e in ENGS}
        self.last_w = {}
        self.readers = {}
        self.esem = {e: stack.enter_context(nc.semaphore("es_" + e)) for e in ENGS}
        self.nslots = 0
        self.store_ops = []
        self.stage_dma = {}

    def slot(self):
        self.nslots += 1
        return DmaSlot(self.stack.enter_context(self.nc.semaphore("ds%d" % self.nslots)))

    def barrier(self):
        deps = []
        for e in ENGS:
            for op in reversed(self.ops[e]):
                if op.slot is None and op.fn is not None:
                    op.needs_inc = True
                    deps.append(op)
                    break
        deps += list(self.stage_dma.values())
        for e in ENGS:
            b = _Op(e, None)
            b.deps = list(deps)
            self.ops[e].append(b)
        self.stage_dma = {}

    def add(self, eng, fn, r=(), w=(), slot=None, ndma=1, dval=None):
        op = _Op(eng, fn)
        deps = []
        seen = set()

        def push(d):
            if d is not None and id(d) not in seen:
                seen.add(id(d))
                deps.append(d)

        for k in r:
            push(self.last_w.get(k))
        for k in w:
            push(self.last_w.get(k))
            for rd in self.readers.get(k, {}).values():
                push(rd)
        for d in deps:
            if d.slot is None:
                if d.eng == eng and eng == "tensor":
                    continue
                d.needs_inc = True
            op.deps.append(d)
        if slot is not None:
            slot.count += (16 * ndma if dval is None else dval)
            op.slot = slot
            op.dval = slot.count
            self.stage_dma[id(slot)] = op
        for k in w:
            self.last_w[k] = op
            self.readers[k] = {}
        for k in r:
            rk = eng if slot is None else ("dma", id(op))
            self.readers.setdefault(k, {})[rk] = op
        self.ops[eng].append(op)
        return op

    def dma(self, eng, slot, out, in_, r=(), w=()):
        def fn(e, out=out, in_=in_, slot=slot):
            return e.dma_start(out=out, in_=in_).then_inc(slot.sem, 16)

        return self.add(eng, fn, r=r, w=w, slot=slot)

    def store(self, eng, slot, out, in_, r=()):
        op = self.dma(eng, slot, out, in_, r=r)
        self.store_ops.append(op)
        return op

    def emit(self):
        nc = self.nc
        fin = _Op("sync", None)
        fin.deps = list(self.store_ops)
        self.ops["sync"].append(fin)
        for e in ENGS:
            t = 0
            for op in self.ops[e]:
                if op.slot is None and op.needs_inc:
                    t += 1
                    op.ticket = t
        with nc.Block() as block:
            for e in ENGS:
                def body(eng, e=e):
                    waited = {}
                    for op in self.ops[e]:
                        for d in op.deps:
                            if d.slot is not None:
                                key, sem, val = ("d", id(d.slot)), d.slot.sem, d.dval
                            else:
                                key, sem, val = ("e", d.eng), self.esem[d.eng], d.ticket
                            if waited.get(key, 0) < val:
                                eng.wait_ge(sem, val)
                                waited[key] = val
                        if op.fn is None:
                            continue
                        inst = op.fn(eng)
                        if op.slot is None and op.needs_inc:
                            inst.then_inc(self.esem[e], 1)

                getattr(block, e)(body)


from contextlib import ExitStack

EPS = 1e-6
D = 1024
TOK = 2048
SEQ = 4096


class Ctx:
    def __init__(self):
        self.nc = bass.Bass("TRN2", target_bir_lowering=False)
        self.st = ExitStack()
        self.P = Prog(self.nc, self.st)
        self.slots = {}
        self.cur = self.st
        self.uid = 0

    def din(self, name, shape, dt=F32):
        return self.nc.dram_tensor(name, list(shape), dt, kind="ExternalInput").ap()

    def dout(self, name, shape, dt=F32):
        return self.nc.dram_tensor(name, list(shape), dt, kind="ExternalOutput").ap()

    def sb(self, name, shape, dt=F32):
        self.uid += 1
        return self.cur.enter_context(self.nc.sbuf_tensor("s%d_%s" % (self.uid, name), list(shape), dt))

    def ps(self, name, shape, dt=F32):
        self.uid += 1
        return self.cur.enter_context(self.nc.psum_tensor("p%d_%s" % (self.uid, name), list(shape), dt))

    def dint(self, name, shape, dt=F32):
        return self.nc.dram_tensor(name, list(shape), dt).ap()

    def stage_begin(self):
        self.stk = getattr(self, "stk", [])
        self.stk.append(self.cur)
        self.cur = ExitStack()

    def stage_end(self):
        self.P.barrier()
        self.cur.close()
        self.cur = self.stk.pop()

    def slot(self, key):
        if key not in self.slots:
            self.slots[key] = self.P.slot()
        return self.slots[key]

    def load(self, key, out, in_, eng="sync"):
        return self.P.dma(eng, self.slot(key), out, in_, w=[key])

    def load_const(self, key, out, in_, eng="sync"):
        op = self.P.dma(eng, self.slot("const_" + eng), out, in_, w=[key])
        self.cpend = getattr(self, "cpend", {})
        self.cpend.setdefault(eng, []).append(key)
        return op

    def const_done(self):
        for eng, keys in getattr(self, "cpend", {}).items():
            ops = [self.P.last_w[k] for k in keys]
            last = max(ops, key=lambda o: o.dval)
            for k in keys:
                self.P.last_w[k] = last
        self.cpend = {}

    def store(self, key, out, in_, eng="sync"):
        return self.P.store(eng, self.slot("st_" + key), out, in_, r=[key])

    def act(self, out, in_, func, r, w, **kw):
        return self.P.add("scalar", lambda e: e.activation(out=out, in_=in_, func=func, **kw), r=r, w=w)

    def vec(self, method, r, w, **kw):
        return self.P.add("vector", lambda e: getattr(e, method)(**kw), r=r, w=w)

    def pe(self, fn, r, w):
        return self.P.add("tensor", fn, r=r, w=w)

    def finish(self):
        self.P.emit()
        self.st.close()
        return self.nc


def rstd_ops(c, ss, n, inv_n, rkeys, key):
    c.vec("tensor_scalar", r=rkeys, w=[key], out=ss[:, 0:n], in0=ss[:, 0:n], scalar1=inv_n, scalar2=EPS,
          op0=ALU.mult, op1=ALU.add)
    c.act(ss[:, 0:n], ss[:, 0:n], AF.Sqrt, r=[key], w=[key])
    c.vec("reciprocal", r=[key], w=[key], out=ss[:, 0:n], in_=ss[:, 0:n])


def norm_transpose(c, i, xtile, xkey, junk, ss, xs, pT, hT_out, hkey, gT, idb, pkey="pT"):
    b = i % 2
    sk, jk, xk, pk = "ss%d" % b, "junk%d" % b, "xs%d" % b, pkey + "%d" % b
    c.act(junk[b][:], xtile, AF.Square, r=[xkey], w=[jk, sk], accum_out=ss[b][:, 0:1])
    rstd_ops(c, ss[b], 1, 1.0 / D, [sk], sk)
    c.act(xs[b][:], xtile, AF.Copy, r=[xkey, sk], w=[xk], scale=ss[b][:, 0:1])

    def tr(e, b=b):
        for k in range(8):
            ins = e.transpose(pT[b][:, k * 128:(k + 1) * 128], xs[b][:, k * 128:(k + 1) * 128], idb[:])
        return ins

    c.pe(tr, r=[xk, "idb"], w=[pk])
    c.vec("tensor_tensor", r=[pk, "gT"], w=[hkey], out=hT_out,
          in0=pT[b][:].rearrange("p (k t) -> p k t", t=128),
          in1=gT[:].rearrange("p (k o) -> p k o", o=1).to_broadcast([128, 8, 128]), op=ALU.mult)


def build_proj(N):
    c = Ctx()
    x = c.din("x", [TOK, D]); gTd = c.din("gT", [128, 8]); w = c.din("w", [D, N]); identd = c.din("ident", [128, 128])
    y = c.dout("y", [TOK, N])
    ncb = N // 512
    wbf = c.sb("wbf", [128, 8, N], BF16)
    idb = c.sb("idb", [128, 128], BF16); gT = c.sb("gT", [128, 8])
    xt = [c.sb("xt%d" % i, [128, D]) for i in range(2)]
    junk = [c.sb("junk%d" % i, [128, D]) for i in range(2)]
    xs = [c.sb("xs%d" % i, [128, D], BF16) for i in range(2)]
    ss = [c.sb("ss%d" % i, [128, 4]) for i in range(2)]
    hT = [c.sb("hT%d" % i, [128, 8, 128], BF16) for i in range(2)]
    yt = [c.sb("yt%d" % i, [128, N]) for i in range(2)]
    pT = [c.ps("pT%d" % i, [128, 1024], BF16) for i in range(2)]
    pm = [c.ps("pm%d" % i, [128, 512]) for i in range(4)]
    c.load("idb", idb[:], identd, eng="gpsimd")
    c.load("gT", gT[:], gTd)
    for k in range(8):
        c.load("wbf%d" % k, wbf[:, k, :], w[k * 128:(k + 1) * 128, :], eng="gpsimd")
    wkeys = ["wbf%d" % k for k in range(8)]
    n = 0
    for i in range(TOK // 128):
        b = i % 2
        c.load("xt%d" % b, xt[b][:], x[i * 128:(i + 1) * 128, :])
        norm_transpose(c, i, xt[b][:], "xt%d" % b, junk, ss, xs, pT, hT[b][:], "hT%d" % b, gT, idb)
        for cb in range(ncb):
            pb = n % 4
            n += 1

            def mm(e, b=b, cb=cb, pb=pb):
                for k in range(8):
                    ins = e.matmul(pm[pb][:], lhsT=hT[b][:, k, :], rhs=wbf[:, k, cb * 512:(cb + 1) * 512],
                                   start=(k == 0), stop=(k == 7))
                return ins

            c.pe(mm, r=["hT%d" % b] + wkeys, w=["pm%d" % pb])
            if cb % 2 == 0:
                c.act(yt[b][:, cb * 512:(cb + 1) * 512], pm[pb][:], AF.Copy, r=["pm%d" % pb], w=["yt%d" % b])
            else:
                c.vec("tensor_copy", r=["pm%d" % pb], w=["yt%d" % b], out=yt[b][:, cb * 512:(cb + 1) * 512], in_=pm[pb][:])
        c.store("yt%d" % b, y[i * 128:(i + 1) * 128, :], yt[b][:])
    return c.finish()


GS = 512


def build_outmlp():
    c = Ctx()
    x = c.din("x", [TOK, D]); cat = c.din("cat", [TOK, D]); wo = c.din("wo", [D, D]); gTd = c.din("gT", [128, 8])
    w1 = c.din("w1", [D, 4 * D]); w2 = c.din("w2", [4 * D, D]); identd = c.din("ident", [128, 128])
    xo = c.dout("xo", [TOK, D])
    NTI = TOK // 128
    xres = c.sb("xres", [128, NTI, D])
    hTa = c.sb("hTa", [128, 8, TOK], BF16)
    wob = c.sb("wob", [128, 8, D], BF16)
    idb = c.sb("idb", [128, 128], BF16); gT = c.sb("gT", [128, 8])
    catb = [c.sb("catb%d" % i, [128, D], BF16) for i in range(2)]
    catT = [c.sb("catT%d" % i, [128, 8, 128], BF16) for i in range(2)]
    junk = [c.sb("junk%d" % i, [128, D]) for i in range(2)]
    xs = [c.sb("xs%d" % i, [128, D], BF16) for i in range(2)]
    ss = [c.sb("ss%d" % i, [128, 4]) for i in range(2)]
    w1g = [c.sb("w1g%d" % i, [128, 8, GS], BF16) for i in range(2)]
    w2g = [c.sb("w2g%d" % i, [128, GS // 128, D], BF16) for i in range(2)]
    rl = [c.sb("rl%d" % i, [128, 512]) for i in range(2)]
    actb = [c.sb("actb%d" % i, [128, GS // 128, 512], BF16) for i in range(2)]
    pT = [c.ps("pT%d" % i, [128, 1024], BF16) for i in range(2)]
    pm = [c.ps("pm%d" % i, [128, 512]) for i in range(6)]
    c.load("idb", idb[:], identd, eng="gpsimd")
    c.load("gT", gT[:], gTd)
    for k in range(8):
        c.load("wob%d" % k, wob[:, k, :], wo[k * 128:(k + 1) * 128, :], eng="gpsimd")
    wokeys = ["wob%d" % k for k in range(8)]
    for i in range(NTI):
        c.load("xres%d" % i, xres[:, i, :], x[i * 128:(i + 1) * 128, :])
    n = 0
    for i in range(NTI):
        b = i % 2
        c.load("catb%d" % b, catb[b][:], cat[i * 128:(i + 1) * 128, :], eng="gpsimd")

        def tr(e, b=b):
            for k in range(8):
                ins = e.transpose(pT[b][:, k * 128:(k + 1) * 128], catb[b][:, k * 128:(k + 1) * 128], idb[:])
            return ins

        c.pe(tr, r=["catb%d" % b, "idb"], w=["pT%d" % b])
        c.vec("tensor_copy", r=["pT%d" % b], w=["catT%d" % b], out=catT[b][:],
              in_=pT[b][:].rearrange("p (k t) -> p k t", t=128))
        for cb in range(2):
            pb = n % 6
            n += 1

            def mm(e, b=b, cb=cb, pb=pb):
                for k in range(8):
                    ins = e.matmul(pm[pb][:], lhsT=catT[b][:, k, :], rhs=wob[:, k, cb * 512:(cb + 1) * 512],
                                   start=(k == 0), stop=(k == 7))
                return ins

            c.pe(mm, r=["catT%d" % b] + wokeys, w=["pm%d" % pb])
            c.vec("tensor_tensor", r=["pm%d" % pb, "xres%d" % i], w=["xres%d" % i],
                  out=xres[:, i, cb * 512:(cb + 1) * 512], in0=pm[pb][:], in1=xres[:, i, cb * 512:(cb + 1) * 512], op=ALU.add)
    for i in range(NTI):
        norm_transpose(c, i, xres[:, i, :], "xres%d" % i, junk, ss, xs, pT, hTa[:, :, i * 128:(i + 1) * 128],
                       "hTa%d" % i, gT, idb)
    NG = 4 * D // GS
    CPG = GS // 128
    m = 0
    for g in range(NG):
        gb = g % 2
        c.load("w1g%d" % gb, w1g[gb][:], w1[:, g * GS:(g + 1) * GS].rearrange("(k p) c -> p k c", p=128), eng="gpsimd")
        c.load("w2g%d" % gb, w2g[gb][:], w2[g * GS:(g + 1) * GS, :].rearrange("(k p) c -> p k c", p=128), eng="gpsimd")
        for tb in range(TOK // 512):
            ab = m % 2
            m += 1
            hkeys = ["hTa%d" % (tb * 4 + j) for j in range(4)]
            for cc in range(CPG):
                pb = n % 6
                n += 1
                rb = n % 2

                def mm1(e, gb=gb, cc=cc, tb=tb, pb=pb):
                    for k in range(8):
                        ins = e.matmul(pm[pb][:], lhsT=w1g[gb][:, k, cc * 128:(cc + 1) * 128],
                                       rhs=hTa[:, k, tb * 512:(tb + 1) * 512], start=(k == 0), stop=(k == 7))
                    return ins

                c.pe(mm1, r=hkeys + ["w1g%d" % gb], w=["pm%d" % pb])
                c.vec("tensor_scalar", r=["pm%d" % pb], w=["rl%d" % rb], out=rl[rb][:], in0=pm[pb][:], scalar1=0.0,
                      scalar2=None, op0=ALU.max)
                c.act(actb[ab][:, cc, :], rl[rb][:], AF.Square, r=["rl%d" % rb], w=["actb%d_%d" % (ab, cc)])
            akeys = ["actb%d_%d" % (ab, cc) for cc in range(CPG)]
            for tt in range(4):
                ti = tb * 4 + tt
                for cb in range(2):
                    pb = n % 6
                    n += 1

                    def mm2(e, gb=gb, ab=ab, tt=tt, cb=cb, pb=pb):
                        for cc in range(CPG):
                            ins = e.matmul(pm[pb][:], lhsT=actb[ab][:, cc, tt * 128:(tt + 1) * 128],
                                           rhs=w2g[gb][:, cc, cb * 512:(cb + 1) * 512], start=(cc == 0), stop=(cc == CPG - 1))
                        return ins

                    c.pe(mm2, r=akeys + ["w2g%d" % gb], w=["pm%d" % pb])
                    c.vec("tensor_tensor", r=["pm%d" % pb, "xres%d" % ti], w=["xres%d" % ti],
                          out=xres[:, ti, cb * 512:(cb + 1) * 512], in0=pm[pb][:], in1=xres[:, ti, cb * 512:(cb + 1) * 512],
                          op=ALU.add)
    for i in range(NTI):
        c.store("xres%d" % i, xo[i * 128:(i + 1) * 128, :], xres[:, i, :])
    return c.finish()


_CACHE = {}


def _prog(key, fn, *a):
    if key not in _CACHE:
        _CACHE[key] = fn(*a)
    return _CACHE[key]


def _run(nc, maps):
    res = run_bass_kernel_spmd(nc, maps, core_ids=list(range(NCORES)))
    return res.results


def _gT(g):
    return np.ascontiguousarray(g.reshape(8, 128).T)


_IDENT = np.eye(128, dtype=np.float32)


def run_proj(xf, g, W):
    N = W.shape[1]
    nc = _prog(("P", N), build_proj, N)
    W = np.ascontiguousarray(W)
    maps = [dict(x=np.ascontiguousarray(xf[c * TOK:(c + 1) * TOK]), gT=_gT(g), w=W, ident=_IDENT) for c in range(NCORES)]
    r = _run(nc, maps)
    return np.concatenate([r[c]["y"] for c in range(NCORES)], 0)


def run_outmlp(xf, catf, wo, g, w1, w2):
    nc = _prog("O", build_outmlp)
    wo, w1, w2 = (np.ascontiguousarray(a) for a in (wo, w1, w2))
    maps = [dict(x=np.ascontiguousarray(xf[c * TOK:(c + 1) * TOK]), cat=np.ascontiguousarray(catf[c * TOK:(c + 1) * TOK]),
                 wo=wo, gT=_gT(g), w1=w1, w2=w2, ident=_IDENT) for c in range(NCORES)]
    r = _run(nc, maps)
    return np.concatenate([r[c]["xo"] for c in range(NCORES)], 0)


def build_meven():
    c = Ctx()
    P = c.P
    u = c.din("u", [SEQ, 256]); v = c.din("v", [SEQ, 256]); sgugd = c.din("sgug", [128, 256])
    wsTd = c.din("wsT", [128, 2, 128]); bsd = c.din("bs", [128, 2])
    aT = c.din("aT", [2, 2, 128, SEQ]); qT = c.din("qT", [2, 128, SEQ]); iv = c.din("iv", [SEQ, 256]); gg = c.din("gg", [SEQ, 256])
    lbld = c.din("lbl", [128, 8]); lbmd = c.din("lbm", [128, 1]); hngd = c.din("hng", [128, 256])
    identd = c.din("ident", [128, 128]); mFd = c.din("maskF", [128, 128]); mBd = c.din("maskB", [128, 128])
    segd = c.din("segm", [128, 512])
    out = c.dout("out", [SEQ, 512])
    NT = SEQ // 128
    idb = c.sb("idb", [128, 128], BF16); c.load("idb", idb[:], identd, eng="gpsimd")
    sgug = c.sb("sgug", [128, 256]); c.load("sgug", sgug[:], sgugd)
    wsb = c.sb("wsb", [128, 2, 128], BF16); c.load("wsb", wsb[:], wsTd, eng="gpsimd")
    bs = c.sb("bs", [128, 2]); c.load("bs", bs[:], bsd)
    lbl = c.sb("lbl", [128, 8]); c.load("lbl", lbl[:], lbld)
    lbm = c.sb("lbm", [128, 1]); c.load("lbm", lbm[:], lbmd)
    hng = c.sb("hng", [128, 256]); c.load("hng", hng[:], hngd)
    mF = c.sb("mF", [128, 128]); c.load("mF", mF[:], mFd)
    mB = c.sb("mB", [128, 128]); c.load("mB", mB[:], mBd)
    segm = c.sb("segm", [128, 512]); c.load("segm", segm[:], segd)
    lb4 = c.sb("lb4", [128, 4]); oml4 = c.sb("oml4", [128, 4]); noml4 = c.sb("noml4", [128, 4])
    l3 = lbl[:].rearrange("p (a e) -> p a e", e=2)
    c.vec("tensor_tensor", r=["lbl"], w=["lb4"], out=lb4[:].rearrange("p (a o) -> p a o", o=1), in0=l3[:, :, 1:2],
          in1=l3[:, :, 0:1], op=ALU.subtract)
    c.act(lb4[:], lb4[:], AF.Sigmoid, r=["lb4"], w=["lb4"])
    c.vec("tensor_scalar", r=["lb4", "lbm"], w=["lb4"], out=lb4[:], in0=lb4[:], scalar1=lbm[:, 0:1], scalar2=None, op0=ALU.mult)
    c.vec("tensor_scalar", r=["lb4"], w=["oml4"], out=oml4[:], in0=lb4[:], scalar1=-1.0, scalar2=1.0, op0=ALU.mult, op1=ALU.add)
    c.vec("tensor_scalar", r=["lb4"], w=["noml4"], out=noml4[:], in0=lb4[:], scalar1=1.0, scalar2=-1.0, op0=ALU.mult, op1=ALU.add)

    pm = c.ps("pm", [128, 512])
    pkt = c.ps("pkt", [128, 1024], BF16)
    pa = [c.ps("pa%d" % i, [128, 512]) for i in range(2)]
    po = [c.ps("po%d" % i, [128, 512]) for i in range(2)]
    pu = [c.ps("pu%d" % i, [128, 512]) for i in range(2)]

    ut = [c.sb("ut%d" % i, [128, 256]) for i in range(2)]
    vt = [c.sb("vt%d" % i, [128, 256]) for i in range(2)]
    vn = [c.sb("vn%d" % i, [128, 256], BF16) for i in range(2)]
    oa = [c.sb("oa%d" % i, [128, 256]) for i in range(2)]
    sj = c.sb("sj", [128, 128]); ssg = c.sb("ssg", [128, 4])
    for n in range(NT):
        b = n % 2
        uk, vk, nk, ok = "ut%d" % b, "vt%d" % b, "vn%d" % b, "oa%d" % b
        c.load(uk, ut[b][:], u[n * 128:(n + 1) * 128, :])
        c.load(vk, vt[b][:], v[n * 128:(n + 1) * 128, :])
        c.act(vt[b][:], vt[b][:], AF.Gelu_apprx_tanh, r=[vk], w=[vk])
        c.act(ut[b][:], ut[b][:], AF.Gelu_apprx_tanh, r=[uk], w=[uk])
        for h in range(2):
            c.act(sj[:], vt[b][:, h * 128:(h + 1) * 128], AF.Square, r=[vk], w=["sj", "ssg"], accum_out=ssg[:, h:h + 1])
        rstd_ops(c, ssg, 2, 1.0 / 128, ["ssg"], "ssg")
        for h in range(2):
            hs = slice(h * 128, (h + 1) * 128)
            c.vec("scalar_tensor_tensor", r=[vk, "ssg", "sgug"], w=[nk], out=vn[b][:, hs], in0=vt[b][:, hs],
                  scalar=ssg[:, h:h + 1], in1=sgug[:, hs], op0=ALU.mult, op1=ALU.mult)

        def mm(e, b=b):
            for h in range(2):
                ins = e.matmul(pm[:, h * 128:(h + 1) * 128], lhsT=wsb[:, h, :], rhs=vn[b][:, h * 128:(h + 1) * 128],
                               start=True, stop=True)
            return ins

        c.pe(mm, r=[nk, "wsb"], w=["pm"])
        for h in range(2):
            hs = slice(h * 128, (h + 1) * 128)
            c.vec("scalar_tensor_tensor", r=["pm", "bs", uk], w=[ok], out=oa[b][:, hs], in0=pm[:, hs],
                  scalar=bs[:, h:h + 1], in1=ut[b][:, hs], op0=ALU.add, op1=ALU.mult)
        c.store(ok, out[n * 128:(n + 1) * 128, 0:256], oa[b][:])

    v3 = lambda t: t[:].rearrange("p (t c) -> p t c", c=128)
    A = [c.sb("A%d" % i, [128, 512]) for i in range(2)]
    Q = [c.sb("Q%d" % i, [128, 512]) for i in range(2)]
    IV = [c.sb("IV%d" % i, [128, 4, 128], BF16) for i in range(2)]
    GG = [c.sb("GG%d" % i, [128, 512]) for i in range(2)]
    L = c.sb("L", [128, 512]); KK = c.sb("KK", [128, 512]); BFW = c.sb("BFW", [128, 512]); BB = c.sb("BB", [128, 512])
    BR = c.sb("BR", [128, 512]); EQ = c.sb("EQ", [128, 512])
    QE = c.sb("QE", [128, 512], BF16); KD = c.sb("KD", [128, 512], BF16); QI = c.sb("QI", [128, 512], BF16)
    KDZ = [c.sb("KDZ%d" % i, [128, 512], BF16) for i in range(2)]
    KDEC = c.sb("KDEC", [128, 512], BF16); kdT = c.sb("kdT", [128, 512], BF16)
    dS = c.sb("dS", [128, 4])
    attT = [c.sb("attT%d" % i, [128, 128], BF16) for i in range(2)]
    S32 = c.sb("S32", [128, 128]); Sbf = c.sb("Sbf", [128, 128], BF16)
    OF = c.sb("OF", [128, NT, 128])
    osum = [c.sb("osum%d" % i, [128, 128]) for i in range(2)]
    obst = [c.sb("obst%d" % i, [128, 128]) for i in range(2)]
    sj2 = c.sb("sj2", [128, 128]); ssh = c.sb("ssh", [128, 4])
    for i in range(2):
        c.vec("memset", r=[], w=["KDZ%d" % i], ap=KDZ[i][:], constant=0.0)
    nt_ = 0
    ntb = 0
    for hh in range(2):
        for dr in range(2):
            col = dr * 2 + hh
            lbc, omlc, nomlc = lb4[:, col:col + 1], oml4[:, col:col + 1], noml4[:, col:col + 1]
            c.vec("memset", r=[], w=["S32"], ap=S32[:], constant=0.0)
            c.vec("memset", r=[], w=["Sbf"], ap=Sbf[:], constant=0.0)
            tbs = range(SEQ // 512) if dr == 0 else range(SEQ // 512 - 1, -1, -1)
            for tb in tbs:
                p2 = ntb % 2
                ntb += 1
                ak, qk, ik, gk = "A%d" % p2, "Q%d" % p2, "IV%d" % p2, "GG%d" % p2
                c.load(ak, A[p2][:], aT[dr, hh, :, tb * 512:(tb + 1) * 512])
                c.load(qk, Q[p2][:], qT[hh, :, tb * 512:(tb + 1) * 512])
                c.load(ik, IV[p2][:], iv[tb * 512:(tb + 1) * 512, hh * 128:(hh + 1) * 128].rearrange("(t p) c -> p t c", p=128),
                       eng="gpsimd")
                if dr == 1:
                    c.load(gk, v3(GG[p2]), gg[tb * 512:(tb + 1) * 512, hh * 128:(hh + 1) * 128].rearrange("(t p) c -> p t c", p=128))
                    c.act(GG[p2][:], GG[p2][:], AF.Silu, r=[gk], w=[gk])
                a_, q_ = A[p2], Q[p2]
                c.act(a_[:], a_[:], AF.Sigmoid, r=[ak], w=[ak])
                c.act(L[:], a_[:], AF.Ln, r=[ak, "oml4", "lb4"], w=["L"], scale=omlc, bias=lbc)
                c.vec("tensor_scalar", r=[ak, "oml4", "noml4"], w=["KK"], out=KK[:], in0=a_[:], scalar1=nomlc, scalar2=omlc,
                      op0=ALU.mult, op1=ALU.add)
                c.vec("tensor_tensor_scan", r=["L", "segm"], w=["BFW"], out=BFW[:], data0=segm[:], data1=L[:], initial=0.0,
                      op0=ALU.mult, op1=ALU.add)
                if dr == 0:
                    Bt, bkey, ri, li = BFW, "BFW", 63, 127
                else:
                    c.vec("tensor_tensor", r=["L", "BFW"], w=["L"], out=L[:], in0=L[:], in1=BFW[:], op=ALU.subtract)
                    c.vec("tensor_tensor", r=["L", "BFW"], w=["BB"], out=v3(BB), in0=v3(L),
                          in1=v3(BFW)[:, :, 127:128].to_broadcast([128, 4, 128]), op=ALU.add)
                    Bt, bkey, ri, li = BB, "BB", 64, 0
                B3 = v3(Bt)
                c.vec("tensor_tensor", r=[bkey], w=["BR"], out=v3(BR), in0=B3,
                      in1=B3[:, :, ri:ri + 1].to_broadcast([128, 4, 128]), op=ALU.subtract)
                c.act(EQ[:], BR[:], AF.Exp, r=["BR"], w=["EQ"])
                c.vec("tensor_tensor", r=[qk, "EQ"], w=["QE"], out=QE[:], in0=q_[:], in1=EQ[:], op=ALU.mult)
                c.act(BR[:], BR[:], AF.Exp, r=["BR"], w=["BR"], scale=-1.0)
                c.vec("tensor_tensor", r=["KK", "BR"], w=["KD"], out=KD[:], in0=KK[:], in1=BR[:], op=ALU.mult)
                hsl = slice(0, 64) if dr == 0 else slice(64, 128)
                c.vec("tensor_tensor", r=["KK", "BR"], w=["KDZ%d" % dr], out=v3(KDZ[dr])[:, :, hsl], in0=v3(KK)[:, :, hsl],
                      in1=v3(BR)[:, :, hsl], op=ALU.mult)
                c.act(EQ[:], Bt[:], AF.Exp, r=[bkey, "QE"], w=["EQ"])
                c.vec("tensor_tensor", r=[qk, "EQ"], w=["QI"], out=QI[:], in0=q_[:], in1=EQ[:], op=ALU.mult)
                c.vec("tensor_tensor", r=[bkey, "KD", "KDZ%d" % dr], w=["BR"], out=v3(BR),
                      in0=B3[:, :, li:li + 1].to_broadcast([128, 4, 128]), in1=B3, op=ALU.subtract)
                c.act(BR[:], BR[:], AF.Exp, r=["BR"], w=["BR"])
                c.vec("tensor_tensor", r=["KK", "BR"], w=["KDEC"], out=KDEC[:], in0=KK[:], in1=BR[:], op=ALU.mult)
                c.act(dS[:].rearrange("p (t o) -> p t o", o=1), B3[:, :, li:li + 1], AF.Exp, r=[bkey], w=["dS"])

                def trk(e):
                    for t in range(4):
                        ins = e.transpose(pkt[:, t * 128:(t + 1) * 128], KDEC[:, t * 128:(t + 1) * 128], idb[:])
                    return ins

                c.pe(trk, r=["KDEC", "idb"], w=["pkt"])
                c.act(kdT[:], pkt[:, 0:512], AF.Copy, r=["pkt"], w=["kdT"])
                tts = range(4) if dr == 0 else range(3, -1, -1)
                for tt in tts:
                    ti = tb * 4 + tt
                    pb = nt_ % 2
                    nt_ += 1
                    ts_ = slice(tt * 128, (tt + 1) * 128)
                    kA, kB = (KDZ[0], KD) if dr == 0 else (KD, KDZ[1])

                    def mma(e, pb=pb, tt=tt, kA=kA, kB=kB):
                        e.matmul(pa[pb][:, 0:64], lhsT=kA[:, tt * 128:(tt + 1) * 128], rhs=QE[:, tt * 128:tt * 128 + 64],
                                 start=True, stop=True)
                        return e.matmul(pa[pb][:, 64:128], lhsT=kB[:, tt * 128:(tt + 1) * 128],
                                        rhs=QE[:, tt * 128 + 64:(tt + 1) * 128], start=True, stop=True)

                    c.pe(mma, r=["KD", "KDZ%d" % dr, "QE"], w=["pa%d" % pb])
                    c.vec("tensor_tensor", r=["pa%d" % pb, "mF", "mB"], w=["attT%d" % pb], out=attT[pb][:], in0=pa[pb][:, 0:128],
                          in1=(mF if dr == 0 else mB)[:], op=ALU.mult)

                    def mmo(e, pb=pb, tt=tt, p2=p2):
                        e.matmul(po[pb][:, 0:128], lhsT=QI[:, tt * 128:(tt + 1) * 128], rhs=Sbf[:], start=True, stop=False)
                        return e.matmul(po[pb][:, 0:128], lhsT=attT[pb][:], rhs=IV[p2][:, tt, :], start=False, stop=True)

                    c.pe(mmo, r=["QI", "Sbf", "attT%d" % pb, ik], w=["po%d" % pb])
                    c.pe(lambda e, pb=pb, tt=tt, p2=p2: e.matmul(pu[pb][:, 0:128], lhsT=kdT[:, tt * 128:(tt + 1) * 128],
                                                                 rhs=IV[p2][:, tt, :], start=True, stop=True),
                         r=["kdT", ik], w=["pu%d" % pb])
                    c.vec("scalar_tensor_tensor", r=["S32", "dS", "pu%d" % pb], w=["S32"], out=S32[:], in0=S32[:],
                          scalar=dS[:, tt:tt + 1], in1=pu[pb][:, 0:128], op0=ALU.mult, op1=ALU.add)
                    c.act(Sbf[:], S32[:], AF.Copy, r=["S32"], w=["Sbf"])
                    if dr == 0:
                        c.act(OF[:, ti, :], po[pb][:, 0:128], AF.Copy, r=["po%d" % pb], w=["OF%d" % ti])
                    else:
                        ob_ = nt_ % 2
                        c.vec("tensor_tensor", r=["po%d" % pb, "OF%d" % ti], w=["osum%d" % ob_], out=osum[ob_][:],
                              in0=po[pb][:, 0:128], in1=OF[:, ti, :], op=ALU.add)
                        c.act(sj2[:], osum[ob_][:], AF.Square, r=["osum%d" % ob_], w=["sj2", "ssh"], accum_out=ssh[:, 0:1])
                        rstd_ops(c, ssh, 1, 1.0 / 128, ["ssh"], "ssh")
                        c.vec("scalar_tensor_tensor", r=["osum%d" % ob_, "ssh", "hng"], w=["osum%d" % ob_], out=osum[ob_][:],
                              in0=osum[ob_][:], scalar=ssh[:, 0:1], in1=hng[:, hh * 128:(hh + 1) * 128], op0=ALU.mult, op1=ALU.mult)
                        c.vec("tensor_tensor", r=["osum%d" % ob_, gk], w=["obst%d" % ob_], out=obst[ob_][:], in0=osum[ob_][:],
                              in1=GG[p2][:, tt * 128:(tt + 1) * 128], op=ALU.mult)
                        c.store("obst%d" % ob_, out[ti * 128:(ti + 1) * 128, 256 + hh * 128:256 + (hh + 1) * 128], obst[ob_][:])
    return c.finish()


_MASKF = np.triu(np.ones((128, 128), np.float32))
_MASKB = np.tril(np.ones((128, 128), np.float32))
_SEGM = np.ones((128, 512), np.float32)
_SEGM[:, ::128] = 0.0


def _bc(vec):
    return np.ascontiguousarray(np.broadcast_to(vec[None, :], (128, vec.shape[0])))


def run_meven(y, e, inp):
    nc = _prog("Me", build_meven)
    maps = []
    for c in range(NCORES):
        b, hp = c // 2, c % 2
        yb = y[b * SEQ:(b + 1) * SEQ]
        cs = slice(hp * 256, hp * 256 + 256)
        u, v, q, iv, g, ff, fb = (yb[:, k * 512:(k + 1) * 512] for k in range(7))
        heads = [2 * hp, 2 * hp + 1]
        aT = np.stack([np.stack([f[:, h * 128:(h + 1) * 128].T for h in heads]) for f in (ff, fb)])
        qT = np.stack([q[:, h * 128:(h + 1) * 128].T for h in heads])
        lbl = np.zeros((128, 8), np.float32)
        for dr in range(2):
            for hi, h in enumerate(heads):
                for le in range(2):
                    lbl[:, (dr * 2 + hi) * 2 + le] = inp["hgrn_lb_logits"][dr, le, h * 128:(h + 1) * 128]
        maps.append(dict(
            u=np.ascontiguousarray(u[:, cs]), v=np.ascontiguousarray(v[:, cs]), sgug=_bc(inp["sgu_norm_g"][e][cs]),
            wsT=np.ascontiguousarray(np.transpose(inp["sgu_w"][e][heads[0]:heads[1] + 1], (2, 0, 1))),
            bs=np.ascontiguousarray(inp["sgu_b"][e][heads[0]:heads[1] + 1].T),
            aT=np.ascontiguousarray(aT), qT=np.ascontiguousarray(qT), iv=np.ascontiguousarray(iv[:, cs]),
            gg=np.ascontiguousarray(g[:, cs]), lbl=lbl, lbm=np.full((128, 1), float(e), np.float32),
            hng=_bc(inp["hgrn_norm_g"][e][cs]), ident=_IDENT, maskF=_MASKF, maskB=_MASKB, segm=_SEGM))
    r = _run(nc, maps)
    cat = np.empty((4 * SEQ, D), np.float32)
    for c in range(NCORES):
        b, hp = c // 2, c % 2
        o = r[c]["out"]
        cat[b * SEQ:(b + 1) * SEQ, hp * 256:hp * 256 + 256] = o[:, 0:256]
        cat[b * SEQ:(b + 1) * SEQ, 512 + hp * 256:512 + hp * 256 + 256] = o[:, 256:512]
    return cat


def build_modd():
    c = Ctx()
    hinT = c.din("hinT", [2, 128, SEQ]); bgT = c.din("bgT", [2, 128, SEQ]); cgT = c.din("cgT", [2, 128, SEQ])
    cwd = c.din("cw", [128, 6])
    qT = c.din("qT", [2, 128, SEQ]); qTp = c.din("qTp", [2, 128, SEQ]); kT = c.din("kT", [2, 128, SEQ]); kTp = c.din("kTp", [2, 128, SEQ])
    vd = c.din("v", [SEQ, 256]); gvd = c.din("gv", [128, 4]); cosd = c.din("cosT", [128, SEQ]); sind = c.din("sinT", [128, SEQ])
    lamd = c.din("lamv", [128, 4, 64]); lcd = c.din("lconst", [128, 2]); subgd = c.din("subg", [128, 128])
    bonesd = c.din("bones", [128, 128])
    oc = c.dout("oc", [2, 128, SEQ]); od = c.dout("od", [SEQ, 256])
    NT = SEQ // 128
    cw = c.sb("cw", [128, 6]); c.load("cw", cw[:], cwd)
    gv = c.sb("gv", [128, 4]); c.load("gv", gv[:], gvd)
    cosT = c.sb("cosT", [128, SEQ]); c.load("cosT", cosT[:], cosd)
    sinT = c.sb("sinT", [128, SEQ]); c.load("sinT", sinT[:], sind)
    lamv = c.sb("lamv", [128, 4, 64]); c.load("lamv", lamv[:], lamd)
    lconst = c.sb("lconst", [128, 2]); c.load("lconst", lconst[:], lcd)
    subg = c.sb("subg", [128, 128]); c.load("subg", subg[:], subgd)
    bones = c.sb("bones", [128, 128]); c.load("bones", bones[:], bonesd)
    epsb = c.sb("epsb", [128, 1]); c.vec("memset", r=[], w=["epsb"], ap=epsb[:], constant=EPS)
    lj = c.sb("lj", [128, 64]); ls = c.sb("ls", [128, 2]); nlam = c.sb("nlam", [128, 1])
    for j in range(2):
        c.vec("tensor_tensor", r=["lamv"], w=["lj"], out=lj[:], in0=lamv[:, 2 * j, :], in1=lamv[:, 2 * j + 1, :], op=ALU.mult)
        c.vec("tensor_reduce", r=["lj"], w=["ls%d" % j], out=ls[:, j:j + 1], in_=lj[:], axis=AX.X, op=ALU.add)
    c.act(ls[:], ls[:], AF.Exp, r=["ls0", "ls1"], w=["ls"])
    c.vec("tensor_tensor", r=["ls"], w=["nlam"], out=nlam[:], in0=ls[:, 1:2], in1=ls[:, 0:1], op=ALU.subtract)
    c.vec("tensor_tensor", r=["nlam", "lconst"], w=["nlam"], out=nlam[:], in0=nlam[:], in1=lconst[:, 0:1], op=ALU.subtract)
    c.vec("tensor_scalar", r=["subg", "lconst"], w=["subg"], out=subg[:], in0=subg[:], scalar1=lconst[:, 1:2], scalar2=None,
          op0=ALU.mult)

    hin = c.sb("hin", [128, SEQ]); cg = c.sb("cg", [128, SEQ]); bg = c.sb("bg", [128, SEQ])
    z = c.sb("z", [128, SEQ + 2]); yy = c.sb("yy", [128, SEQ])
    c.vec("memset", r=[], w=["z"], ap=z[:], constant=0.0)
    for g in range(2):
        c.load("hin", hin[:], hinT[g]); c.load("cg", cg[:], cgT[g]); c.load("bg", bg[:], bgT[g])
        c.vec("tensor_tensor", r=["hin", "cg"], w=["z"], out=z[:, 1:SEQ + 1], in0=cg[:], in1=hin[:], op=ALU.mult)
        c.vec("tensor_scalar", r=["z", "cw"], w=["yy"], out=yy[:], in0=z[:, 1:SEQ + 1], scalar1=cw[:, 3 * g + 1:3 * g + 2],
              scalar2=None, op0=ALU.mult)
        c.vec("scalar_tensor_tensor", r=["z", "cw", "yy"], w=["yy"], out=yy[:], in0=z[:, 0:SEQ], scalar=cw[:, 3 * g:3 * g + 1],
              in1=yy[:], op0=ALU.mult, op1=ALU.add)
        c.vec("scalar_tensor_tensor", r=["z", "cw", "yy"], w=["yy"], out=yy[:], in0=z[:, 2:SEQ + 2],
              scalar=cw[:, 3 * g + 2:3 * g + 3], in1=yy[:], op0=ALU.mult, op1=ALU.add)
        c.vec("tensor_tensor", r=["yy", "bg"], w=["yy"], out=yy[:], in0=yy[:], in1=bg[:], op=ALU.mult)
        c.store("yy", oc[g], yy[:])

    pss = c.ps("pss", [128, 512])
    pS = [c.ps("pS%d" % i, [128, 1024]) for i in range(2)]
    pacc = c.ps("pacc", [128, 3, 512])
    Kr = c.sb("Kr", [128, SEQ], BF16); Qr = c.sb("Qr", [128, SEQ], BF16)
    Vaug = c.sb("Vaug", [128, NT, 132], BF16)
    xt = [c.sb("axt%d" % i, [128, 512]) for i in range(2)]
    xp = [c.sb("axp%d" % i, [128, 512]) for i in range(2)]
    sq = c.sb("sq", [128, 512]); rr = c.sb("rr", [128, 512]); t1 = c.sb("t1", [128, 512]); t2 = c.sb("t2", [128, 512])
    pT = [c.sb("pTs%d" % i, [128, 1024], BF16) for i in range(2)]
    rc = c.sb("rc", [128, 8]); o1 = [c.sb("o1_%d" % i, [128, 128]) for i in range(2)]
    odt = [c.sb("odt%d" % i, [128, 128]) for i in range(2)]
    sj = c.sb("sj", [128, 128]); ssd = c.sb("ssd", [128, 4])
    nb = 0
    it = 0
    ne = 0
    for hh in range(2):
        c.load("Vaug", Vaug[:, :, 0:128], vd[:, hh * 128:(hh + 1) * 128].rearrange("(t p) c -> p t c", p=128), eng="gpsimd")
        c.vec("memset", r=[], w=["Vones"], ap=Vaug[:, :, 128:129], constant=1.0)
        for (src, srcp, dst, dkey, gi) in ((kT, kTp, Kr, "Kr", 2), (qT, qTp, Qr, "Qr", 0)):
            for blk in range(SEQ // 512):
                b = nb % 2
                nb += 1
                bs_ = slice(blk * 512, (blk + 1) * 512)
                xk, pk = "axt%d" % b, "axp%d" % b
                c.load(xk, xt[b][:], src[hh, :, bs_])
                c.load(pk, xp[b][:], srcp[hh, :, bs_])
                c.act(sq[:], xt[b][:], AF.Square, r=[xk], w=["sq"])
                c.pe(lambda e: e.matmul(pss[:], lhsT=bones[:], rhs=sq[:], start=True, stop=True), r=["sq", "bones"], w=["pss"])
                c.act(rr[:], pss[:], AF.Ln, r=["pss", "epsb"], w=["rr"], scale=1.0 / 64, bias=epsb[:, 0:1])
                c.act(rr[:], rr[:], AF.Exp, r=["rr"], w=["rr"], scale=-0.5)
                c.vec("scalar_tensor_tensor", r=[xk, "gv", "cosT"], w=["t1"], out=t1[:], in0=xt[b][:], scalar=gv[:, gi:gi + 1],
                      in1=cosT[:, bs_], op0=ALU.mult, op1=ALU.mult)
                c.vec("scalar_tensor_tensor", r=[pk, "gv", "sinT"], w=["t2"], out=t2[:], in0=xp[b][:], scalar=gv[:, gi + 1:gi + 2],
                      in1=sinT[:, bs_], op0=ALU.mult, op1=ALU.mult)
                c.vec("tensor_tensor", r=["t1", "t2"], w=["t1"], out=t1[:], in0=t1[:], in1=t2[:], op=ALU.add)
                c.vec("tensor_tensor", r=["t1", "rr"], w=[dkey + "%d" % blk], out=dst[:, bs_], in0=t1[:], in1=rr[:], op=ALU.mult)
        kkeys = ["Kr%d" % j for j in range(8)]
        for qb in range(SEQ // 512):
            for kt in range(NT):
                sb_ = it % 2
                it += 1

                def mms(e, sb_=sb_, kt=kt, qb=qb):
                    e.matmul(pS[sb_][:, 0:512], lhsT=Kr[0:64, kt * 128:(kt + 1) * 128], rhs=Qr[0:64, qb * 512:(qb + 1) * 512],
                             start=True, stop=True)
                    return e.matmul(pS[sb_][:, 512:1024], lhsT=Kr[64:128, kt * 128:(kt + 1) * 128],
                                    rhs=Qr[64:128, qb * 512:(qb + 1) * 512], start=True, stop=True)

                c.pe(mms, r=["Kr%d" % (kt // 4), "Qr%d" % qb], w=["pS%d" % sb_])
                c.act(pT[sb_][:], pS[sb_][:], AF.Exp, r=["pS%d" % sb_], w=["pTs%d" % sb_], scale=0.125)

                def mmv(e, sb_=sb_, kt=kt):
                    for a in range(8):
                        bank, off = a // 3, (a % 3) * 132
                        ins = e.matmul(pacc[:, bank, off:off + 129], lhsT=pT[sb_][:, a * 128:(a + 1) * 128], rhs=Vaug[:, kt, 0:129],
                                       start=(kt == 0 and a % 3 == 0), stop=(kt == NT - 1), skip_group_check=True)
                    return ins

                c.pe(mmv, r=["pTs%d" % sb_, "Vaug", "Vones"], w=["pacc"])
            for a in range(8):
                bank, off = a // 3, (a % 3) * 132
                c.vec("reciprocal", r=["pacc"], w=["rc%d" % a], out=rc[:, a:a + 1], in_=pacc[:, bank, off + 128:off + 129])
            c.vec("tensor_scalar", r=["rc%d" % a for a in range(4, 8)] + ["nlam"], w=["rc%d" % a for a in range(4, 8)],
                  out=rc[:, 4:8], in0=rc[:, 4:8], scalar1=nlam[:, 0:1], scalar2=None, op0=ALU.mult)
            for qs in range(4):
                e_ = ne % 2
                ne += 1
                a1, a2 = qs, 4 + qs
                c.vec("tensor_scalar", r=["pacc", "rc%d" % a1], w=["o1_%d" % e_], out=o1[e_][:],
                      in0=pacc[:, a1 // 3, (a1 % 3) * 132:(a1 % 3) * 132 + 128], scalar1=rc[:, a1:a1 + 1], scalar2=None, op0=ALU.mult)
                c.vec("scalar_tensor_tensor", r=["pacc", "rc%d" % a2, "o1_%d" % e_], w=["o1_%d" % e_], out=o1[e_][:],
                      in0=pacc[:, a2 // 3, (a2 % 3) * 132:(a2 % 3) * 132 + 128], scalar=rc[:, a2:a2 + 1], in1=o1[e_][:],
                      op0=ALU.mult, op1=ALU.add)
                c.act(sj[:], o1[e_][:], AF.Square, r=["o1_%d" % e_], w=["sj", "ssd"], accum_out=ssd[:, 0:1])
                rstd_ops(c, ssd, 1, 1.0 / 128, ["ssd"], "ssd")
                c.vec("scalar_tensor_tensor", r=["o1_%d" % e_, "ssd", "subg"], w=["odt%d" % e_], out=odt[e_][:], in0=o1[e_][:],
                      scalar=ssd[:, 0:1], in1=subg[:], op0=ALU.mult, op1=ALU.mult)
                ti = qb * 4 + qs
                c.store("odt%d" % e_, od[ti * 128:(ti + 1) * 128, hh * 128:(hh + 1) * 128], odt[e_][:])
    return c.finish()


def _rope_tables():
    inv = 1.0 / (10000.0 ** (np.arange(0, 64, 2, dtype=np.float32) / 64.0))
    ang = np.arange(SEQ, dtype=np.float32)[:, None] * inv[None, :]
    cos, sin = np.cos(ang).astype(np.float32).T, np.sin(ang).astype(np.float32).T
    cosT = np.concatenate([cos, cos, cos, cos], 0)
    sinT = np.concatenate([-sin, sin, -sin, sin], 0)
    return np.ascontiguousarray(cosT), np.ascontiguousarray(sinT)


def _perm64(a):
    return np.concatenate([a[32:64], a[0:32], a[96:128], a[64:96]], 0)


_BONES = np.kron(np.eye(2, dtype=np.float32), np.ones((64, 64), np.float32))


def run_modd(y, o, layer, inp):
    nc = _prog("Mo", build_modd)
    cosT, sinT = _rope_tables()
    lam_init = 0.8 - 0.6 * math.exp(-0.3 * layer)
    qg, kg = np.tile(inp["q_norm_g"][o], 2), np.tile(inp["k_norm_g"][o], 2)
    gv = np.stack([qg, _perm64(qg), kg, _perm64(kg)], 1).astype(np.float32)
    lamv = np.stack([_bc(inp[k][o]) for k in ("lambda_q1", "lambda_k1", "lambda_q2", "lambda_k2")], 1)
    lconst = np.stack([np.full(128, lam_init, np.float32), np.full(128, 1.0 - lam_init, np.float32)], 1)
    maps = []
    for c in range(NCORES):
        b, hp = c // 2, c % 2
        yb = y[b * SEQ:(b + 1) * SEQ]
        hin, bg, cg, q, k, v = (yb[:, j * 512:(j + 1) * 512] for j in range(6))
        grp = [2 * hp, 2 * hp + 1]
        fm = lambda a: np.ascontiguousarray(np.stack([a[:, g * 128:(g + 1) * 128].T for g in grp]))
        fmp = lambda a: np.ascontiguousarray(np.stack([_perm64(a[:, g * 128:(g + 1) * 128].T) for g in grp]))
        cw = np.concatenate([inp["conv_w"][o][:, g * 128:(g + 1) * 128].T for g in grp], 1)
        maps.append(dict(hinT=fm(hin), bgT=fm(bg), cgT=fm(cg), cw=np.ascontiguousarray(cw),
                         qT=fm(q), qTp=fmp(q), kT=fm(k), kTp=fmp(k), v=np.ascontiguousarray(v[:, hp * 256:hp * 256 + 256]),
                         gv=gv, cosT=cosT, sinT=sinT, lamv=np.ascontiguousarray(lamv), lconst=lconst,
                         subg=_bc(inp["diff_norm_g"][o]), bones=_BONES))
    r = _run(nc, maps)
    cat = np.empty((4 * SEQ, D), np.float32)
    for c in range(NCORES):
        b, hp = c // 2, c % 2
        for gi in range(2):
            cat[b * SEQ:(b + 1) * SEQ, (2 * hp + gi) * 128:(2 * hp + gi + 1) * 128] = r[c]["oc"][gi].T
        cat[b * SEQ:(b + 1) * SEQ, 512 + hp * 256:512 + hp * 256 + 256] = r[c]["od"]
    return cat


def kernel(**inputs):
    inp = {k: np.asarray(v) for k, v in inputs.items()}
    x = np.ascontiguousarray(inp["x"].reshape(-1, D).astype(np.float32))
    for l in range(4):
        if l % 2 == 0:
            e = l // 2
            y = run_proj(x, inp["norm_mix_g"][l], inp["w_in_even"][e])
            cat = run_meven(y, e, inp)
            wo = inp["w_out_even"][e]
        else:
            o = l // 2
            y = run_proj(x, inp["norm_mix_g"][l], inp["w_in_odd"][o])
            cat = run_modd(y, o, l, inp)
            wo = inp["w_out_odd"][o]
        x = run_outmlp(x, cat, wo, inp["norm_mlp_g"][l], inp["mlp_w1"][l], inp["mlp_w2"][l])
    return x.reshape(4, SEQ, D).astype(np.float32)


GROUPS = [[0, 1], [2, 3], [4, 5], [6, 7]]
NTI = TOK // 128
v3 = lambda t: t[:].rearrange("p (t c) -> p t c", c=128)


def allgather(c, in_ap, out_ap, rkey, wkey):
    slot = c.P.slot()

    def fn(e):
        return e.collective_compute("AllGather", ALU.bypass, replica_groups=GROUPS, ins=[in_ap], outs=[out_ap]).then_inc(slot.sem)

    return c.P.add("gpsimd", fn, r=[rkey], w=[wkey], slot=slot, dval=1)


def f_proj(c, xres, idb, w_d, gT_d, Ntot, tm_blocks, fm_chunks, y_tm, y_fm):
    c.stage_begin()
    ntm = len(tm_blocks)
    wbf = c.sb("wbf", [128, 8, Ntot], BF16)
    gT = c.sb("gT", [128, 8]); c.load_const("gT", gT[:], gT_d)
    junk = [c.sb("junk%d" % i, [128, D]) for i in range(2)]
    xs = [c.sb("xs%d" % i, [128, D], BF16) for i in range(2)]
    ss = [c.sb("ss%d" % i, [128, 4]) for i in range(2)]
    hT = [c.sb("hT%d" % i, [128, 8, 512], BF16) for i in range(2)]
    yt = [c.sb("yt%d" % i, [128, max(ntm, 1) * 512]) for i in range(2)]
    yf = [c.sb("yf%d" % i, [128, 512]) for i in range(3)]
    pT = [c.ps("pT%d" % i, [128, 1024], BF16) for i in range(2)]
    pm = [c.ps("pm%d" % i, [128, 512]) for i in range(4)]
    c.const_done()
    wv = w_d.rearrange("(k p) n -> p k n", p=128)
    order = [cs // 512 for cs in tm_blocks]
    for cs in fm_chunks:
        if cs // 512 not in order:
            order.append(cs // 512)
    for sl in order:
        c.load("wslab%d" % sl, wbf[:, :, sl * 512:(sl + 1) * 512], wv[:, :, sl * 512:(sl + 1) * 512], eng="gpsimd")
    n = 0
    nf = 0
    def norm_tb(tb):
        hb = tb % 2
        for j in range(4):
            i = tb * 4 + j
            norm_transpose(c, i, xres[:, i, :], "xres%d" % i, junk, ss, xs, pT, hT[hb][:, :, j * 128:(j + 1) * 128],
                           "hT%d_%d" % (hb, j), gT, idb)

    norm_tb(0)
    for tb in range(TOK // 512):
        hb = tb % 2
        if tb + 1 < TOK // 512:
            norm_tb(tb + 1)
        hkeys = ["hT%d_%d" % (hb, j) for j in range(4)]
        for j in range(4):
            if ntm == 0:
                break
            i = tb * 4 + j
            yb = i % 2
            for bi, cs in enumerate(tm_blocks):
                pb = n % 4
                n += 1

                def mm(e, hb=hb, j=j, cs=cs, pb=pb):
                    for k in range(8):
                        ins = e.matmul(pm[pb][:], lhsT=hT[hb][:, k, j * 128:(j + 1) * 128], rhs=wbf[:, k, cs:cs + 512],
                                       start=(k == 0), stop=(k == 7))
                    return ins

                c.pe(mm, r=["hT%d_%d" % (hb, j), "wslab%d" % (cs // 512)], w=["pm%d" % pb])
                if bi % 2 == 0:
                    c.act(yt[yb][:, bi * 512:(bi + 1) * 512], pm[pb][:], AF.Copy, r=["pm%d" % pb], w=["yt%d" % yb])
                else:
                    c.vec("tensor_copy", r=["pm%d" % pb], w=["yt%d" % yb], out=yt[yb][:, bi * 512:(bi + 1) * 512], in_=pm[pb][:])
            c.P.dma("sync", c.slot("st_yt%d" % yb), y_tm[i * 128:(i + 1) * 128, 0:ntm * 512], yt[yb][:, 0:ntm * 512],
                    r=["yt%d" % yb], w=["ytm%d" % i])
        for fi, cs in enumerate(fm_chunks):
            pb = n % 4
            n += 1
            fb = nf % 3
            nf += 1

            def mmf(e, hb=hb, cs=cs, pb=pb):
                for k in range(8):
                    ins = e.matmul(pm[pb][:], lhsT=wbf[:, k, cs:cs + 128], rhs=hT[hb][:, k, :], start=(k == 0), stop=(k == 7))
                return ins

            c.pe(mmf, r=hkeys + ["wslab%d" % (cs // 512)], w=["pm%d" % pb])
            if fi % 2 == 0:
                c.act(yf[fb][:], pm[pb][:], AF.Copy, r=["pm%d" % pb], w=["yf%d" % fb])
            else:
                c.vec("tensor_copy", r=["pm%d" % pb], w=["yf%d" % fb], out=yf[fb][:], in_=pm[pb][:])
            c.P.dma("sync", c.slot("st_yf%d" % fb), y_fm[fi * 128:(fi + 1) * 128, tb * 512:(tb + 1) * 512], yf[fb][:],
                    r=["yf%d" % fb], w=["yfm%d_%d" % (fi, tb)])
    c.stage_end()


def f_outmlp(c, xres, idb, cat_tm, cat_fm, n_fm, wo, gT_d, w1, w2):
    c.stage_begin()
    hTa = c.sb("hTa", [128, 8, TOK], BF16)
    wob = c.sb("wob", [128, 8, D], BF16)
    gT = c.sb("gT", [128, 8]); c.load_const("gT", gT[:], gT_d)
    catb = [c.sb("catb%d" % i, [128, D], BF16) for i in range(2)]
    catT = [c.sb("catT%d" % i, [128, 8, 128], BF16) for i in range(2)]
    junk = [c.sb("junk%d" % i, [128, D]) for i in range(2)]
    xs = [c.sb("xs%d" % i, [128, D], BF16) for i in range(2)]
    ss = [c.sb("ss%d" % i, [128, 4]) for i in range(2)]
    w1g = [c.sb("w1g%d" % i, [128, 8, GS], BF16) for i in range(2)]
    w2g = [c.sb("w2g%d" % i, [128, GS // 128, D], BF16) for i in range(2)]
    rl = [c.sb("rl%d" % i, [128, 512]) for i in range(2)]
    actb = [c.sb("actb%d" % i, [128, GS // 128, 512], BF16) for i in range(2)]
    pT = [c.ps("pT%d" % i, [128, 1024], BF16) for i in range(2)]
    pN = [c.ps("pN%d" % i, [128, 1024], BF16) for i in range(2)]
    pm = [c.ps("pm%d" % i, [128, 512]) for i in range(4)]
    c.const_done()
    c.load("wob", wob[:], wo.rearrange("(k p) n -> p k n", p=128), eng="gpsimd")
    wokeys = ["wob"]
    n = 0
    c0 = n_fm * 128
    for i in range(NTI):
        b = i % 2
        c.P.dma("gpsimd", c.slot("catb%d" % b), catb[b][:, c0:D], cat_tm[i * 128:(i + 1) * 128, c0:D], r=["cat_tm"], w=["catb%d" % b])
        rk = ["pT%d" % b]
        if n_fm:
            c.P.dma("gpsimd", c.slot("catTf%d" % b), catT[b][:, 0:n_fm, :],
                    cat_fm.rearrange("(k p) t -> p k t", p=128)[:, 0:n_fm, i * 128:(i + 1) * 128], r=["cat_fm"], w=["catTf%d" % b])

        def tr(e, b=b):
            for k in range(n_fm, 8):
                ins = e.transpose(pT[b][:, k * 128:(k + 1) * 128], catb[b][:, k * 128:(k + 1) * 128], idb[:])
            return ins

        c.pe(tr, r=["catb%d" % b, "idb"], w=["pT%d" % b])
        c.vec("tensor_copy", r=["pT%d" % b], w=["catT%d" % b], out=catT[b][:, n_fm:8, :],
              in_=pT[b][:].rearrange("p (k t) -> p k t", t=128)[:, n_fm:8, :])
        for cb in range(2):
            pb = n % 4
            n += 1

            def mm(e, b=b, cb=cb, pb=pb):
                for k in range(8):
                    ins = e.matmul(pm[pb][:], lhsT=catT[b][:, k, :], rhs=wob[:, k, cb * 512:(cb + 1) * 512],
                                   start=(k == 0), stop=(k == 7))
                return ins

            c.pe(mm, r=["catT%d" % b, "catTf%d" % b] + wokeys, w=["pm%d" % pb])
            c.vec("tensor_tensor", r=["pm%d" % pb, "xres%d" % i], w=["xres%d" % i],
                  out=xres[:, i, cb * 512:(cb + 1) * 512], in0=pm[pb][:], in1=xres[:, i, cb * 512:(cb + 1) * 512], op=ALU.add)
        if i >= 1:
            norm_transpose(c, i - 1, xres[:, i - 1, :], "xres%d" % (i - 1), junk, ss, xs, pN, hTa[:, :, (i - 1) * 128:i * 128],
                           "hTa%d" % (i - 1), gT, idb, pkey="pN")
    norm_transpose(c, NTI - 1, xres[:, NTI - 1, :], "xres%d" % (NTI - 1), junk, ss, xs, pN, hTa[:, :, (NTI - 1) * 128:NTI * 128],
                   "hTa%d" % (NTI - 1), gT, idb, pkey="pN")
    NG = 4 * D // GS
    CPG = GS // 128
    NTB = TOK // 512

    def wload(g):
        gb = g % 2
        c.load("w1g%d" % gb, w1g[gb][:], w1[:, g * GS:(g + 1) * GS].rearrange("(k p) c -> p k c", p=128), eng="gpsimd")
        c.load("w2g%d" % gb, w2g[gb][:], w2[g * GS:(g + 1) * GS, :].rearrange("(k p) c -> p k c", p=128), eng="gpsimd")

    cnt = [n]

    def stage1(g, tb, ab):
        gb = g % 2
        hkeys = ["hTa%d" % (tb * 4 + j) for j in range(4)]
        for cc in range(CPG):
            pb = cnt[0] % 4
            cnt[0] += 1
            rb = cnt[0] % 2

            def mm1(e, gb=gb, cc=cc, tb=tb, pb=pb):
                for k in range(8):
                    ins = e.matmul(pm[pb][:], lhsT=w1g[gb][:, k, cc * 128:(cc + 1) * 128],
                                   rhs=hTa[:, k, tb * 512:(tb + 1) * 512], start=(k == 0), stop=(k == 7))
                return ins

            c.pe(mm1, r=hkeys + ["w1g%d" % gb], w=["pm%d" % pb])
            c.vec("tensor_scalar", r=["pm%d" % pb], w=["rl%d" % rb], out=rl[rb][:], in0=pm[pb][:], scalar1=0.0,
                  scalar2=None, op0=ALU.max)
            c.act(actb[ab][:, cc, :], rl[rb][:], AF.Square, r=["rl%d" % rb], w=["actb%d_%d" % (ab, cc)])

    def stage2(g, tb, ab):
        gb = g % 2
        akeys = ["actb%d_%d" % (ab, cc) for cc in range(CPG)]
        for tt in range(4):
            ti = tb * 4 + tt
            for cb in range(2):
                pb = cnt[0] % 4
                cnt[0] += 1

                def mm2(e, gb=gb, ab=ab, tt=tt, cb=cb, pb=pb):
                    for cc in range(CPG):
                        ins = e.matmul(pm[pb][:], lhsT=actb[ab][:, cc, tt * 128:(tt + 1) * 128],
                                       rhs=w2g[gb][:, cc, cb * 512:(cb + 1) * 512], start=(cc == 0), stop=(cc == CPG - 1))
                    return ins

                c.pe(mm2, r=akeys + ["w2g%d" % gb], w=["pm%d" % pb])
                c.vec("tensor_tensor", r=["pm%d" % pb, "xres%d" % ti], w=["xres%d" % ti],
                      out=xres[:, ti, cb * 512:(cb + 1) * 512], in0=pm[pb][:], in1=xres[:, ti, cb * 512:(cb + 1) * 512],
                      op=ALU.add)

    items = [(g, tb) for g in range(NG) for tb in range(NTB)]
    wload(0)
    wload(1)
    stage1(items[0][0], items[0][1], 0)
    for i_, (g, tb) in enumerate(items):
        if i_ + 1 < len(items):
            stage1(items[i_ + 1][0], items[i_ + 1][1], (i_ + 1) % 2)
        stage2(g, tb, i_ % 2)
        if tb == NTB - 1 and g + 2 < NG:
            wload(g + 2)
    c.stage_end()


def f_meven(c, idb, cst, y_tm, y_fm, cat_tm, dd, lname):
    mF, mB, segm, sel = cst["mF"], cst["mB"], cst["segm"], cst["sel"]
    ex_in = c.dint("exs_in" + lname, [512, 128]); ex_out = c.dint("exs_out" + lname, [1024, 128])
    c.stage_begin()
    sgug = c.sb("sgug", [128, 512]); c.load_const("sgug", sgug[:], dd["sgug"])
    wsb = c.sb("wsb", [128, 4, 128], BF16); c.load_const("wsb", wsb[:], dd["wsT"], eng="gpsimd")
    bs = c.sb("bs", [128, 4]); c.load_const("bs", bs[:], dd["bs"])
    lbl = c.sb("lbl", [128, 16]); c.load_const("lbl", lbl[:], dd["lbl"])
    lbm = c.sb("lbm", [128, 1]); c.load_const("lbm", lbm[:], dd["lbm"])
    hng = c.sb("hng", [128, 512]); c.load_const("hng", hng[:], dd["hng"])
    c.const_done()
    onesb = c.sb("onesb", [128, 1]); c.vec("memset", r=[], w=["onesb"], ap=onesb[:], constant=1.0)
    epsb = c.sb("epsb", [128, 1]); c.vec("memset", r=[], w=["epsb"], ap=epsb[:], constant=EPS)
    lb8 = c.sb("lb8", [128, 8]); oml8 = c.sb("oml8", [128, 8]); noml8 = c.sb("noml8", [128, 8])
    l3 = lbl[:].rearrange("p (a e) -> p a e", e=2)
    c.vec("tensor_tensor", r=["lbl"], w=["lb8"], out=lb8[:].rearrange("p (a o) -> p a o", o=1), in0=l3[:, :, 1:2],
          in1=l3[:, :, 0:1], op=ALU.subtract)
    c.act(lb8[:], lb8[:], AF.Sigmoid, r=["lb8"], w=["lb8"])
    c.vec("tensor_scalar", r=["lb8", "lbm"], w=["lb8"], out=lb8[:], in0=lb8[:], scalar1=lbm[:, 0:1], scalar2=None, op0=ALU.mult)
    c.vec("tensor_scalar", r=["lb8"], w=["oml8"], out=oml8[:], in0=lb8[:], scalar1=-1.0, scalar2=1.0, op0=ALU.mult, op1=ALU.add)
    c.vec("tensor_scalar", r=["lb8"], w=["noml8"], out=noml8[:], in0=lb8[:], scalar1=1.0, scalar2=-1.0, op0=ALU.mult, op1=ALU.add)
    c.stage_begin()
    pm = c.ps("pm", [128, 512])
    ut = [c.sb("ut%d" % i, [128, 512]) for i in range(2)]
    vt = [c.sb("vt%d" % i, [128, 512]) for i in range(2)]
    vn = [c.sb("vn%d" % i, [128, 512], BF16) for i in range(2)]
    oa = [c.sb("oa%d" % i, [128, 512]) for i in range(2)]
    sj = c.sb("sj", [128, 128]); ssg = c.sb("ssg", [128, 4])
    for n in range(NTI):
        b = n % 2
        uk, vk, nk, ok = "ut%d" % b, "vt%d" % b, "vn%d" % b, "oa%d" % b
        c.load(uk, ut[b][:], y_tm[n * 128:(n + 1) * 128, 0:512])
        c.load(vk, vt[b][:], y_tm[n * 128:(n + 1) * 128, 512:1024])
        c.act(vt[b][:], vt[b][:], AF.Gelu_apprx_tanh, r=[vk], w=[vk])
        c.act(ut[b][:], ut[b][:], AF.Gelu_apprx_tanh, r=[uk], w=[uk])
        for h in range(4):
            c.act(sj[:], vt[b][:, h * 128:(h + 1) * 128], AF.Square, r=[vk], w=["sj", "ssg"], accum_out=ssg[:, h:h + 1])
        rstd_ops(c, ssg, 4, 1.0 / 128, ["ssg"], "ssg")
        for h in range(4):
            hs = slice(h * 128, (h + 1) * 128)
            c.vec("scalar_tensor_tensor", r=[vk, "ssg", "sgug"], w=[nk], out=vn[b][:, hs], in0=vt[b][:, hs],
                  scalar=ssg[:, h:h + 1], in1=sgug[:, hs], op0=ALU.mult, op1=ALU.mult)

        def mm(e, b=b):
            for h in range(4):
                ins = e.matmul(pm[:, h * 128:(h + 1) * 128], lhsT=wsb[:, h, :], rhs=vn[b][:, h * 128:(h + 1) * 128],
                               start=True, stop=True)
            return ins

        c.pe(mm, r=[nk, "wsb"], w=["pm"])
        for h in range(4):
            hs = slice(h * 128, (h + 1) * 128)
            c.vec("scalar_tensor_tensor", r=["pm", "bs", uk], w=[ok], out=oa[b][:, hs], in0=pm[:, hs],
                  scalar=bs[:, h:h + 1], in1=ut[b][:, hs], op0=ALU.add, op1=ALU.mult)
        c.P.dma("sync", c.slot("st_" + ok), cat_tm[n * 128:(n + 1) * 128, 0:512], oa[b][:], r=[ok])
    c.stage_end()
    c.stage_begin()
    NCH = 2
    pkt = [c.ps("pkt%d" % s, [128, 1024], BF16) for s in range(NCH)]
    pa = [c.ps("pa%d" % s, [128, 512]) for s in range(NCH)]
    po = [c.ps("po%d" % s, [128, 512]) for s in range(NCH)]
    pu = [c.ps("pu%d" % s, [128, 512]) for s in range(NCH)]
    OF = c.sb("OF", [128, 4 * NTI, 128])
    B_ = []
    for s in range(NCH):
        d = {}
        d["A"] = [c.sb("A%d_%d" % (s, i), [128, 512]) for i in range(2)]
        d["Q"] = [c.sb("Q%d_%d" % (s, i), [128, 512]) for i in range(2)]
        d["IV"] = [c.sb("IV%d_%d" % (s, i), [128, 4, 128], BF16) for i in range(2)]
        d["GG"] = [c.sb("GG%d_%d" % (s, i), [128, 512]) for i in range(2)]
        for nm in ("L", "KK", "BFW", "BB", "BR", "EQ", "RC"):
            d[nm] = c.sb("%s%d" % (nm, s), [128, 512])
        for nm in ("QE", "KD", "QI", "kdT"):
            d[nm] = [c.sb("%s%d_%d" % (nm, s, i), [128, 512], BF16) for i in range(2)]
        d["KDEC"] = c.sb("KDEC%d" % s, [128, 512], BF16)
        d["KDZ"] = [[c.sb("KDZ%d_%d_%d" % (s, i, j), [128, 512], BF16) for j in range(2)] for i in range(2)]
        d["dS"] = [c.sb("dS%d_%d" % (s, i), [128, 4]) for i in range(2)]; d["nref"] = c.sb("nref%d" % s, [128, 4])
        d["attT"] = c.sb("attT%d" % s, [128, 128], BF16)
        d["S32"] = c.sb("S32_%d" % s, [128, 128]); d["Sbf"] = c.sb("Sbf%d" % s, [128, 128], BF16)
        d["SG"] = c.sb("SG%d" % s, [128, 2, 128])
        d["osum"] = c.sb("osum%d" % s, [128, 128]); d["obst"] = [c.sb("obst%d_%d" % (s, i), [128, 128]) for i in range(2)]
        d["sj2"] = c.sb("sj2_%d" % s, [128, 128]); d["ssh"] = c.sb("ssh%d" % s, [128, 4])
        for i in range(2):
            for j in range(2):
                c.vec("memset", r=[], w=["KDZ%d_c%d" % (j, s)], ap=d["KDZ"][i][j][:], constant=0.0)
        B_.append(d)

    def chain(s, hh, dr):
        d = B_[s]
        K = lambda n: "%s_c%d" % (n, s)
        L, KK, BFW, BB, BR, EQ, RC = d["L"], d["KK"], d["BFW"], d["BB"], d["BR"], d["EQ"], d["RC"]
        KDEC = d["KDEC"]
        attT, S32, Sbf, SG = d["attT"], d["S32"], d["Sbf"], d["SG"]
        osum, obst, sj2, ssh = d["osum"], d["obst"], d["sj2"], d["ssh"]
        col = dr * 4 + hh
        lbc, omlc = lb8[:, col:col + 1], oml8[:, col:col + 1]
        if dr == 0:
            c.vec("memset", r=[], w=[K("S32")], ap=S32[:], constant=0.0)
            c.vec("memset", r=[], w=[K("Sbf")], ap=Sbf[:], constant=0.0)
        else:
            c.P.dma("sync", c.slot("SG%d" % s), SG[:], ex_out.rearrange("(r h p) v -> p r h v", r=2, h=4)[:, :, hh, :],
                    r=["ex_out"], w=[K("SG")])
            c.vec("tensor_scalar", r=[K("SG"), "sel"], w=[K("S32")], out=S32[:], in0=SG[:, 0, :], scalar1=sel[:, 0:1], scalar2=None,
                  op0=ALU.mult)
            c.vec("scalar_tensor_tensor", r=[K("SG"), "sel", K("S32")], w=[K("S32")], out=S32[:], in0=SG[:, 1, :], scalar=sel[:, 1:2],
                  in1=S32[:], op0=ALU.mult, op1=ALU.add)
            c.act(Sbf[:], S32[:], AF.Copy, r=[K("S32")], w=[K("Sbf")])
        yield
        tbs = list(range(TOK // 512)) if dr == 0 else list(range(TOK // 512 - 1, -1, -1))
        nob = [0]

        def pre(ntb, tb):
            p2 = ntb % 2
            P2 = lambda n: K("%s%d" % (n, p2))
            ak, qk, ik, gk = P2("A"), P2("Q"), P2("IV"), P2("GG")
            a_, q_, IVt, GGt = d["A"][p2], d["Q"][p2], d["IV"][p2], d["GG"][p2]
            QE, KD, QI, kdT, dS, KDZ = d["QE"][p2], d["KD"][p2], d["QI"][p2], d["kdT"][p2], d["dS"][p2], d["KDZ"][dr][p2]
            tbs_ = slice(tb * 512, (tb + 1) * 512)
            c.load(ak, a_[:], y_fm[512 + dr * 512 + hh * 128:512 + dr * 512 + (hh + 1) * 128, tbs_])
            c.load(qk, q_[:], y_fm[hh * 128:(hh + 1) * 128, tbs_])
            c.load(ik, IVt[:], y_tm[tbs_, 1024 + hh * 128:1024 + (hh + 1) * 128].rearrange("(t p) c -> p t c", p=128), eng="gpsimd")
            if dr == 1:
                c.load(gk, v3(GGt), y_tm[tbs_, 1536 + hh * 128:1536 + (hh + 1) * 128].rearrange("(t p) c -> p t c", p=128))
                c.act(RC[:], GGt[:], AF.Exp, r=[gk], w=[K("RC")], scale=-1.0)
                yield
                c.vec("tensor_scalar", r=[K("RC")], w=[K("RC")], out=RC[:], in0=RC[:], scalar1=1.0, scalar2=None, op0=ALU.add)
                c.vec("reciprocal", r=[K("RC")], w=[K("RC")], out=RC[:], in_=RC[:])
                c.vec("tensor_tensor", r=[K("RC"), gk], w=[gk], out=GGt[:], in0=GGt[:], in1=RC[:], op=ALU.mult)
                yield
            c.act(a_[:], a_[:], AF.Exp, r=[ak], w=[ak], scale=-1.0)
            yield
            c.act(RC[:], a_[:], AF.Identity, r=[ak, "onesb"], w=[K("RC")], bias=onesb[:, 0:1])
            c.act(L[:], a_[:], AF.Ln, r=[ak, "lb8", "onesb"], w=[K("L")], scale=lbc, bias=onesb[:, 0:1])
            yield
            c.act(KK[:], RC[:], AF.Ln, r=[K("RC")], w=[K("KK")])
            c.vec("reciprocal", r=[K("RC"), K("KK")], w=[K("RC")], out=RC[:], in_=RC[:])
            yield
            c.vec("tensor_tensor", r=[K("L"), K("KK")], w=[K("L")], out=L[:], in0=L[:], in1=KK[:], op=ALU.subtract)
            c.vec("scalar_tensor_tensor", r=[ak, "oml8", K("RC"), K("L")], w=[K("KK")], out=KK[:], in0=a_[:], scalar=omlc, in1=RC[:],
                  op0=ALU.mult, op1=ALU.mult)
            yield
            c.vec("tensor_tensor_scan", r=[K("L"), "segm"], w=[K("BFW")], out=BFW[:], data0=segm[:], data1=L[:], initial=0.0,
                  op0=ALU.mult, op1=ALU.add)
            yield
            if dr == 0:
                Bt, bkey, ri, li = BFW, K("BFW"), 63, 127
            else:
                c.vec("tensor_tensor", r=[K("L"), K("BFW")], w=[K("L")], out=L[:], in0=L[:], in1=BFW[:], op=ALU.subtract)
                c.vec("tensor_tensor", r=[K("L"), K("BFW")], w=[K("BB")], out=v3(BB), in0=v3(L),
                      in1=v3(BFW)[:, :, 127:128].to_broadcast([128, 4, 128]), op=ALU.add)
                Bt, bkey, ri, li = BB, K("BB"), 64, 0
            B3 = v3(Bt)
            c.vec("tensor_scalar", r=[bkey], w=[K("nref")], out=d["nref"][:].rearrange("p (t o) -> p t o", o=1),
                  in0=B3[:, :, ri:ri + 1], scalar1=-1.0, scalar2=None, op0=ALU.mult)
            yield
            for t in range(4):
                c.act(EQ[:, t * 128:(t + 1) * 128], Bt[:, t * 128:(t + 1) * 128], AF.Exp, r=[bkey, K("nref")], w=[K("EQ")],
                      bias=d["nref"][:, t:t + 1])
            yield
            c.vec("tensor_tensor", r=[qk, K("EQ")], w=[P2("QE")], out=QE[:], in0=q_[:], in1=EQ[:], op=ALU.mult)
            for t in range(4):
                c.act(BR[:, t * 128:(t + 1) * 128], Bt[:, t * 128:(t + 1) * 128], AF.Exp, r=[bkey], w=[K("BR")], scale=-1.0,
                      bias=B3[:, t, ri:ri + 1])
            yield
            c.vec("tensor_tensor", r=[K("KK"), K("BR")], w=[P2("KD")], out=KD[:], in0=KK[:], in1=BR[:], op=ALU.mult)
            hsl = slice(0, 64) if dr == 0 else slice(64, 128)
            c.vec("tensor_tensor", r=[K("KK"), K("BR")], w=[P2("KDZ")], out=v3(KDZ)[:, :, hsl], in0=v3(KK)[:, :, hsl],
                  in1=v3(BR)[:, :, hsl], op=ALU.mult)
            c.act(EQ[:], Bt[:], AF.Exp, r=[bkey, P2("QE")], w=[K("EQ")])
            yield
            c.vec("tensor_tensor", r=[qk, K("EQ")], w=[P2("QI")], out=QI[:], in0=q_[:], in1=EQ[:], op=ALU.mult)
            for t in range(4):
                c.act(BR[:, t * 128:(t + 1) * 128], Bt[:, t * 128:(t + 1) * 128], AF.Exp, r=[bkey, P2("KD"), P2("KDZ")],
                      w=[K("BR")], scale=-1.0, bias=B3[:, t, li:li + 1])
            c.act(dS[:].rearrange("p (t o) -> p t o", o=1), B3[:, :, li:li + 1], AF.Exp, r=[bkey], w=[P2("dS")])
            yield
            c.vec("tensor_tensor", r=[K("KK"), K("BR")], w=[K("KDEC")], out=KDEC[:], in0=KK[:], in1=BR[:], op=ALU.mult)
            yield

            def trk(e):
                for t in range(4):
                    ins = e.transpose(pkt[s][:, t * 128:(t + 1) * 128], KDEC[:, t * 128:(t + 1) * 128], idb[:])
                return ins

            c.pe(trk, r=[K("KDEC"), "idb"], w=[K("pkt")])
            yield
            c.act(kdT[:], pkt[s][:, 0:512], AF.Copy, r=[K("pkt")], w=[P2("kdT")])
            yield

        def tiles(ntb, tb):
            p2 = ntb % 2
            P2 = lambda n: K("%s%d" % (n, p2))
            ik, gk = P2("IV"), P2("GG")
            IVt, GGt = d["IV"][p2], d["GG"][p2]
            QE, KD, QI, kdT, dS, KDZ = d["QE"][p2], d["KD"][p2], d["QI"][p2], d["kdT"][p2], d["dS"][p2], d["KDZ"][dr][p2]
            tts = range(4) if dr == 0 else range(3, -1, -1)
            for tt in tts:
                ti = tb * 4 + tt
                oi = hh * NTI + ti
                kA, kB = (KDZ, KD) if dr == 0 else (KD, KDZ)

                def mma(e, tt=tt, kA=kA, kB=kB, QE=QE):
                    e.matmul(pa[s][:, 0:64], lhsT=kA[:, tt * 128:(tt + 1) * 128], rhs=QE[:, tt * 128:tt * 128 + 64],
                             start=True, stop=True)
                    return e.matmul(pa[s][:, 64:128], lhsT=kB[:, tt * 128:(tt + 1) * 128],
                                    rhs=QE[:, tt * 128 + 64:(tt + 1) * 128], start=True, stop=True)

                c.pe(mma, r=[P2("KD"), P2("KDZ"), P2("QE")], w=[K("pa")])
                c.pe(lambda e, tt=tt, IVt=IVt, kdT=kdT: e.matmul(pu[s][:, 0:128], lhsT=kdT[:, tt * 128:(tt + 1) * 128],
                                                                 rhs=IVt[:, tt, :], start=True, stop=True),
                     r=[P2("kdT"), ik], w=[K("pu")])
                yield
                c.vec("tensor_tensor", r=[K("pa"), "mF", "mB"], w=[K("attT")], out=attT[:], in0=pa[s][:, 0:128],
                      in1=(mF if dr == 0 else mB)[:], op=ALU.mult)
                yield

                def mmo(e, tt=tt, IVt=IVt, QI=QI):
                    e.matmul(po[s][:, 0:128], lhsT=QI[:, tt * 128:(tt + 1) * 128], rhs=Sbf[:], start=True, stop=False)
                    return e.matmul(po[s][:, 0:128], lhsT=attT[:], rhs=IVt[:, tt, :], start=False, stop=True)

                c.pe(mmo, r=[P2("QI"), K("Sbf"), K("attT"), ik], w=[K("po")])
                yield
                c.vec("scalar_tensor_tensor", r=[K("S32"), P2("dS"), K("pu")], w=[K("S32")], out=S32[:], in0=S32[:],
                      scalar=dS[:, tt:tt + 1], in1=pu[s][:, 0:128], op0=ALU.mult, op1=ALU.add)
                yield
                c.act(Sbf[:], S32[:], AF.Copy, r=[K("S32")], w=[K("Sbf")])
                if dr == 0:
                    c.act(OF[:, oi, :], po[s][:, 0:128], AF.Copy, r=[K("po")], w=["OF%d" % oi])
                    yield
                else:
                    ob_ = nob[0] % 2
                    nob[0] += 1
                    c.vec("tensor_tensor", r=[K("po"), "OF%d" % oi], w=[K("osum")], out=osum[:],
                          in0=po[s][:, 0:128], in1=OF[:, oi, :], op=ALU.add)
                    yield
                    c.act(sj2[:], osum[:], AF.Square, r=[K("osum")], w=[K("sj2"), K("ssh")], accum_out=ssh[:, 0:1])
                    yield
                    c.act(ssh[:, 0:1], ssh[:, 0:1], AF.Ln, r=[K("ssh"), "epsb"], w=[K("ssh")], scale=1.0 / 128, bias=epsb[:, 0:1])
                    c.act(ssh[:, 0:1], ssh[:, 0:1], AF.Exp, r=[K("ssh")], w=[K("ssh")], scale=-0.5)
                    yield
                    c.vec("scalar_tensor_tensor", r=[K("osum"), K("ssh"), "hng"], w=[K("osum")], out=osum[:],
                          in0=osum[:], scalar=ssh[:, 0:1], in1=hng[:, hh * 128:(hh + 1) * 128], op0=ALU.mult, op1=ALU.mult)
                    c.vec("tensor_tensor", r=[K("osum"), gk], w=[K("obst%d" % ob_)], out=obst[ob_][:], in0=osum[:],
                          in1=GGt[:, tt * 128:(tt + 1) * 128], op=ALU.mult)
                    c.P.dma("sync", c.slot("st_obst%d_%d" % (s, ob_)), cat_tm[ti * 128:(ti + 1) * 128, 512 + hh * 128:512 + (hh + 1) * 128],
                            obst[ob_][:], r=[K("obst%d" % ob_)])
                    yield

        for _ in pre(0, tbs[0]):
            yield
        for n_ in range(len(tbs)):
            subs = [tiles(n_, tbs[n_])]
            if n_ + 1 < len(tbs):
                subs.append(pre(n_ + 1, tbs[n_ + 1]))
            live = [True] * len(subs)
            while any(live):
                for k_ in range(len(subs)):
                    if live[k_]:
                        try:
                            next(subs[k_])
                            yield
                        except StopIteration:
                            live[k_] = False
        if dr == 0:
            c.P.dma("sync", c.slot("st_S32_%d" % s), ex_in[hh * 128:(hh + 1) * 128, :], S32[:], r=[K("S32")], w=["ex_in%d" % hh])
        yield

    for dr in range(2):
        if dr == 1:
            slot_ag = c.P.slot()

            def agfn(e, slot_ag=slot_ag):
                return e.collective_compute("AllGather", ALU.bypass, replica_groups=GROUPS, ins=[ex_in], outs=[ex_out]).then_inc(slot_ag.sem)

            c.P.add("gpsimd", agfn, r=["ex_in%d" % h for h in range(4)], w=["ex_out"], slot=slot_ag, dval=1)
        for h0 in range(0, 4, NCH):
            gens = [chain(s, h0 + s, dr) for s in range(NCH)]
            alive = [True] * NCH
            while any(alive):
                for s in range(NCH):
                    if alive[s]:
                        try:
                            next(gens[s])
                        except StopIteration:
                            alive[s] = False
    c.stage_end()
    c.stage_end()


def f_modd(c, idb, cst, y_tm, y_fm, cat_tm, cat_fm, dd, lname):
    sel = cst["sel"]
    zx_in = c.dint("zx_in" + lname, [128, 4]); zx_out = c.dint("zx_out" + lname, [256, 4])
    kx_in = c.dint("kx_in" + lname, [128, 4 * TOK], BF16); kx_out = c.dint("kx_out" + lname, [256, 4 * TOK], BF16)
    vx_in = c.dint("vx_in" + lname, [TOK, 512], BF16); vx_out = c.dint("vx_out" + lname, [2 * TOK, 512], BF16)
    NKT = 2 * NTI
    c.stage_begin()
    cw = c.sb("cw", [128, 12]); c.load_const("cw", cw[:], dd["cw"])
    gv = c.sb("gv", [128, 4]); c.load_const("gv", gv[:], dd["gv"])
    cosT = c.sb("cosT", [128, TOK]); c.load_const("cosT", cosT[:], cst["cosT"])
    sinT = c.sb("sinT", [128, TOK]); c.load_const("sinT", sinT[:], cst["sinT"])
    lamv = c.sb("lamv", [128, 4, 64]); c.load_const("lamv", lamv[:], dd["lamv"])
    lconst = c.sb("lconst", [128, 2]); c.load_const("lconst", lconst[:], dd["lconst"])
    subg = c.sb("subg", [128, 128]); c.load_const("subg", subg[:], dd["subg"])
    bones = c.sb("bones", [128, 128]); c.load_const("bones", bones[:], cst["bones"])
    c.const_done()
    epsb = c.sb("epsb", [128, 1]); c.vec("memset", r=[], w=["epsb"], ap=epsb[:], constant=EPS)
    lj = c.sb("lj", [128, 64]); ls = c.sb("ls", [128, 2]); nlam = c.sb("nlam", [128, 1])
    for j in range(2):
        c.vec("tensor_tensor", r=["lamv"], w=["lj"], out=lj[:], in0=lamv[:, 2 * j, :], in1=lamv[:, 2 * j + 1, :], op=ALU.mult)
        c.vec("tensor_reduce", r=["lj"], w=["ls%d" % j], out=ls[:, j:j + 1], in_=lj[:], axis=AX.X, op=ALU.add)
    c.act(ls[:], ls[:], AF.Exp, r=["ls0", "ls1"], w=["ls"])
    c.vec("tensor_tensor", r=["ls"], w=["nlam"], out=nlam[:], in0=ls[:, 1:2], in1=ls[:, 0:1], op=ALU.subtract)
    c.vec("tensor_tensor", r=["nlam", "lconst"], w=["nlam"], out=nlam[:], in0=nlam[:], in1=lconst[:, 0:1], op=ALU.subtract)
    c.vec("tensor_scalar", r=["subg", "lconst"], w=["subg"], out=subg[:], in0=subg[:], scalar1=lconst[:, 1:2], scalar2=None,
          op0=ALU.mult)
    z4 = c.sb("z4", [128, 4, TOK + 2])
    xt = [c.sb("axt%d" % i, [128, 512]) for i in range(2)]
    xp = [c.sb("axp%d" % i, [128, 512]) for i in range(2)]
    sq = c.sb("sq", [128, 512]); rr = c.sb("rr", [128, 512]); t1 = c.sb("t1", [128, 512]); t2 = c.sb("t2", [128, 512])
    pss = c.ps("pss", [128, 512])
    pS = [c.ps("pS%d" % i, [128, 1024]) for i in range(2)]
    pacc = c.ps("pacc", [128, 3, 512])
    nbc = [0]

    def qk_prep(rows0, rowsp0, hh, dst, dkey, gi, nblk):
        for blk in range(nblk):
            b = nbc[0] % 2
            nbc[0] += 1
            bs_ = slice(blk * 512, (blk + 1) * 512)
            xk, pk = "axt%d" % b, "axp%d" % b
            c.load(xk, xt[b][:], y_fm[rows0 + hh * 128:rows0 + (hh + 1) * 128, bs_])
            c.load(pk, xp[b][:], y_fm[rowsp0 + hh * 128:rowsp0 + (hh + 1) * 128, bs_])
            c.act(sq[:], xt[b][:], AF.Square, r=[xk], w=["sq"])
            c.pe(lambda e: e.matmul(pss[:], lhsT=bones[:], rhs=sq[:], start=True, stop=True), r=["sq", "bones"], w=["pss"])
            c.act(rr[:], pss[:], AF.Ln, r=["pss", "epsb"], w=["rr"], scale=1.0 / 64, bias=epsb[:, 0:1])
            c.act(rr[:], rr[:], AF.Exp, r=["rr"], w=["rr"], scale=-0.5)
            c.vec("scalar_tensor_tensor", r=[xk, "gv", "cosT"], w=["t1"], out=t1[:], in0=xt[b][:], scalar=gv[:, gi:gi + 1],
                  in1=cosT[:, bs_], op0=ALU.mult, op1=ALU.mult)
            c.vec("scalar_tensor_tensor", r=[pk, "gv", "sinT"], w=["t2"], out=t2[:], in0=xp[b][:], scalar=gv[:, gi + 1:gi + 2],
                  in1=sinT[:, bs_], op0=ALU.mult, op1=ALU.mult)
            c.vec("tensor_tensor", r=["t1", "t2"], w=["t1"], out=t1[:], in0=t1[:], in1=t2[:], op=ALU.add)
            c.vec("tensor_tensor", r=["t1", "rr"], w=[dkey], out=dst(blk), in0=t1[:], in1=rr[:], op=ALU.mult)

    c.stage_begin()
    hin = c.sb("hin", [128, TOK]); cg = c.sb("cg", [128, TOK])
    Kown = c.sb("Kown", [128, 4, TOK], BF16)
    vbf = c.sb("vbf", [128, NTI, 512], BF16)
    c.vec("memset", r=[], w=["z4"], ap=z4[:], constant=0.0)
    for g in range(4):
        c.load("hin", hin[:], y_fm[g * 128:(g + 1) * 128, :])
        c.load("cg", cg[:], y_fm[1024 + g * 128:1024 + (g + 1) * 128, :])
        c.vec("tensor_tensor", r=["hin", "cg", "z4"], w=["z4_%d" % g], out=z4[:, g, 1:TOK + 1], in0=cg[:], in1=hin[:], op=ALU.mult)
    zc = c.sb("zc", [128, 4])
    c.vec("tensor_copy", r=["z4_%d" % g for g in range(4)], w=["zc"], out=zc[:], in_=z4[:, :, TOK])
    c.P.dma("sync", c.slot("st_zx"), zx_in, zc[:], r=["zc"], w=["zx_in"])
    allgather(c, zx_in, zx_out, "zx_in", "zx_out")
    c.load("vbf", vbf[:], y_tm[:, 0:512].rearrange("(t p) c -> p t c", p=128), eng="gpsimd")
    c.P.dma("sync", c.slot("st_vx"), vx_in.rearrange("(t p) c -> p t c", p=128), vbf[:], r=["vbf"], w=["vx_in"])
    allgather(c, vx_in, vx_out, "vx_in", "vx_out")
    for hh in range(4):
        qk_prep(2048, 3072, hh, lambda blk, hh=hh: Kown[:, hh, blk * 512:(blk + 1) * 512], "Kown", 2, TOK // 512)
    c.P.dma("sync", c.slot("st_kx"), kx_in, Kown[:].rearrange("p h t -> p (h t)"), r=["Kown"], w=["kx_in"])
    allgather(c, kx_in, kx_out, "kx_in", "kx_out")
    c.stage_end()
    c.stage_begin()
    bg = c.sb("bg", [128, TOK]); yy = c.sb("yy", [128, TOK]); ZG = c.sb("ZG", [128, 2, 4]); zh = c.sb("zh", [128, 4])
    c.P.dma("sync", c.slot("ZG"), ZG[:], zx_out.rearrange("(r p) g -> p r g", r=2), r=["zx_out"], w=["ZG"])
    c.vec("tensor_scalar", r=["ZG", "sel"], w=["zh"], out=zh[:], in0=ZG[:, 0, :], scalar1=sel[:, 0:1], scalar2=None, op0=ALU.mult)
    c.vec("scalar_tensor_tensor", r=["ZG", "sel", "zh"], w=["zh"], out=zh[:], in0=ZG[:, 1, :], scalar=sel[:, 1:2], in1=zh[:],
          op0=ALU.mult, op1=ALU.add)
    c.vec("tensor_copy", r=["zh"], w=["z4h"], out=z4[:, :, TOK + 1], in_=zh[:])
    for g in range(4):
        c.load("bg", bg[:], y_fm[512 + g * 128:512 + (g + 1) * 128, :])
        c.vec("tensor_scalar", r=["z4h", "cw", "yy"], w=["yy"], out=yy[:], in0=z4[:, g, 1:TOK + 1], scalar1=cw[:, 3 * g + 1:3 * g + 2],
              scalar2=None, op0=ALU.mult)
        c.vec("scalar_tensor_tensor", r=["cw", "yy"], w=["yy"], out=yy[:], in0=z4[:, g, 0:TOK], scalar=cw[:, 3 * g:3 * g + 1],
              in1=yy[:], op0=ALU.mult, op1=ALU.add)
        c.vec("scalar_tensor_tensor", r=["cw", "yy"], w=["yy"], out=yy[:], in0=z4[:, g, 2:TOK + 2],
              scalar=cw[:, 3 * g + 2:3 * g + 3], in1=yy[:], op0=ALU.mult, op1=ALU.add)
        c.vec("tensor_tensor", r=["yy", "bg"], w=["yy"], out=yy[:], in0=yy[:], in1=bg[:], op=ALU.mult)
        c.P.dma("sync", c.slot("st_yy"), cat_fm[g * 128:(g + 1) * 128, :], yy[:], r=["yy"])
    c.stage_end()
    c.stage_begin()
    Kr = [c.sb("Kr%d" % i, [128, 2 * TOK], BF16) for i in range(2)]
    Qr = [c.sb("Qr%d" % i, [128, TOK], BF16) for i in range(2)]
    Vaug = [c.sb("Vaug%d" % i, [128, NKT, 132], BF16) for i in range(2)]
    pT = [c.sb("pTs%d" % i, [128, 1024], BF16) for i in range(2)]
    rc = c.sb("rc", [128, 8]); o1 = [c.sb("o1_%d" % i, [128, 128]) for i in range(4)]
    odt = [c.sb("odt%d" % i, [128, 128]) for i in range(2)]
    sj = c.sb("sj", [128, 128]); ssd = c.sb("ssd", [128, 4])

    def head_loads(hh):
        hp = hh % 2
        for r_ in range(2):
            c.P.dma("sync", c.slot("Kr%d_%d" % (hp, r_)), Kr[hp][:, r_ * TOK:(r_ + 1) * TOK],
                    kx_out[r_ * 128:(r_ + 1) * 128, hh * TOK:(hh + 1) * TOK], r=["kx_out"], w=["Kr%d_%d" % (hp, r_)])
        c.P.dma("sync", c.slot("Vaug%d" % hp), Vaug[hp][:, :, 0:128],
                vx_out[:, hh * 128:(hh + 1) * 128].rearrange("(t p) c -> p t c", p=128), r=["vx_out"], w=["Vaug%d" % hp])
        c.vec("memset", r=[], w=["Vones%d" % hp], ap=Vaug[hp][:, :, 128:129], constant=1.0)

    def q_block(hh, blk):
        hp = hh % 2
        b = nbc[0] % 2
        nbc[0] += 1
        bs_ = slice(blk * 512, (blk + 1) * 512)
        xk, pk = "axt%d" % b, "axp%d" % b
        c.load(xk, xt[b][:], y_fm[1536 + hh * 128:1536 + (hh + 1) * 128, bs_])
        c.load(pk, xp[b][:], y_fm[2560 + hh * 128:2560 + (hh + 1) * 128, bs_])
        c.vec("tensor_tensor", r=[xk], w=["sq"], out=sq[:], in0=xt[b][:], in1=xt[b][:], op=ALU.mult)
        c.pe(lambda e: e.matmul(pss[:], lhsT=bones[:], rhs=sq[:], start=True, stop=True), r=["sq", "bones"], w=["pss"])
        c.act(rr[:], pss[:], AF.Ln, r=["pss", "epsb"], w=["rr"], scale=1.0 / 64, bias=epsb[:, 0:1])
        c.act(rr[:], rr[:], AF.Exp, r=["rr"], w=["rr"], scale=-0.5)
        c.vec("scalar_tensor_tensor", r=[xk, "gv", "cosT"], w=["t1"], out=t1[:], in0=xt[b][:], scalar=gv[:, 0:1],
              in1=cosT[:, bs_], op0=ALU.mult, op1=ALU.mult)
        c.vec("scalar_tensor_tensor", r=[pk, "gv", "sinT"], w=["t2"], out=t2[:], in0=xp[b][:], scalar=gv[:, 1:2],
              in1=sinT[:, bs_], op0=ALU.mult, op1=ALU.mult)
        c.vec("tensor_tensor", r=["t1", "t2"], w=["t1"], out=t1[:], in0=t1[:], in1=t2[:], op=ALU.add)
        c.vec("tensor_tensor", r=["t1", "rr"], w=["Qr%d_%d" % (hp, blk)], out=Qr[hp][:, bs_], in0=t1[:], in1=rr[:], op=ALU.mult)

    def sc_exp(hh, qb, kt, sb_):
        hp = hh % 2

        def mms(e):
            e.matmul(pS[sb_][:, 0:512], lhsT=Kr[hp][0:64, kt * 128:(kt + 1) * 128], rhs=Qr[hp][0:64, qb * 512:(qb + 1) * 512],
                     start=True, stop=True)
            return e.matmul(pS[sb_][:, 512:1024], lhsT=Kr[hp][64:128, kt * 128:(kt + 1) * 128],
                            rhs=Qr[hp][64:128, qb * 512:(qb + 1) * 512], start=True, stop=True)

        c.pe(mms, r=["Kr%d_%d" % (hp, kt // NTI), "Qr%d_%d" % (hp, qb)], w=["pS%d" % sb_])
        c.act(pT[sb_][:], pS[sb_][:], AF.Exp, r=["pS%d" % sb_], w=["pTs%d" % sb_], scale=0.125)

    def epi1():
        for a in range(8):
            bank, off = a // 3, (a % 3) * 132
            c.vec("reciprocal", r=["pacc"], w=["rc%d" % a], out=rc[:, a:a + 1], in_=pacc[:, bank, off + 128:off + 129])
        c.vec("tensor_scalar", r=["rc%d" % a for a in range(4, 8)] + ["nlam"], w=["rc%d" % a for a in range(4, 8)],
              out=rc[:, 4:8], in0=rc[:, 4:8], scalar1=nlam[:, 0:1], scalar2=None, op0=ALU.mult)
        for qs in range(4):
            a1, a2 = qs, 4 + qs
            c.vec("tensor_scalar", r=["pacc", "rc%d" % a1], w=["o1_%d" % qs], out=o1[qs][:],
                  in0=pacc[:, a1 // 3, (a1 % 3) * 132:(a1 % 3) * 132 + 128], scalar1=rc[:, a1:a1 + 1], scalar2=None, op0=ALU.mult)
            c.vec("scalar_tensor_tensor", r=["pacc", "rc%d" % a2, "o1_%d" % qs], w=["o1_%d" % qs], out=o1[qs][:],
                  in0=pacc[:, a2 // 3, (a2 % 3) * 132:(a2 % 3) * 132 + 128], scalar=rc[:, a2:a2 + 1], in1=o1[qs][:],
                  op0=ALU.mult, op1=ALU.add)

    def epi2(hh, qb, qs):
        e_ = qs % 2
        c.vec("tensor_tensor", r=["o1_%d" % qs], w=["sj"], out=sj[:], in0=o1[qs][:], in1=o1[qs][:], op=ALU.mult)
        c.vec("tensor_reduce", r=["sj"], w=["ssd"], out=ssd[:, 0:1], in_=sj[:], axis=AX.X, op=ALU.add)
        c.act(ssd[:, 0:1], ssd[:, 0:1], AF.Ln, r=["ssd", "epsb"], w=["ssd"], scale=1.0 / 128, bias=epsb[:, 0:1])
        c.act(ssd[:, 0:1], ssd[:, 0:1], AF.Exp, r=["ssd"], w=["ssd"], scale=-0.5)
        c.vec("scalar_tensor_tensor", r=["o1_%d" % qs, "ssd", "subg"], w=["odt%d" % e_], out=odt[e_][:], in0=o1[qs][:],
              scalar=ssd[:, 0:1], in1=subg[:], op0=ALU.mult, op1=ALU.mult)
        ti = qb * 4 + qs
        c.P.dma("sync", c.slot("st_odt%d" % e_), cat_tm[ti * 128:(ti + 1) * 128, 512 + hh * 128:512 + (hh + 1) * 128],
                odt[e_][:], r=["odt%d" % e_])

    its = [(hh, qb, kt) for hh in range(4) for qb in range(TOK // 512) for kt in range(NKT)]
    pending = {}
    head_loads(0)
    for blk in range(TOK // 512):
        q_block(0, blk)
    sc_exp(its[0][0], its[0][1], its[0][2], 0)
    for ii, (hh, qb, kt) in enumerate(its):
        sb_ = ii % 2
        if qb == 0 and kt == 0 and hh + 1 < 4:
            head_loads(hh + 1)
            for blk in range(TOK // 512):
                pending.setdefault(ii + 16 + 24 * blk, []).append(lambda hh=hh, blk=blk: q_block(hh + 1, blk))
        if ii + 1 < len(its):
            sc_exp(its[ii + 1][0], its[ii + 1][1], its[ii + 1][2], (ii + 1) % 2)

        def mmv(e, sb_=sb_, kt=kt, hp=hh % 2):
            for a in range(8):
                bank, off = a // 3, (a % 3) * 132
                ins = e.matmul(pacc[:, bank, off:off + 129], lhsT=pT[sb_][:, a * 128:(a + 1) * 128], rhs=Vaug[hp][:, kt, 0:129],
                               start=(kt == 0 and a % 3 == 0), stop=(kt == NKT - 1), skip_group_check=True)
            return ins

        c.pe(mmv, r=["pTs%d" % sb_, "Vaug%d" % (hh % 2), "Vones%d" % (hh % 2)], w=["pacc"])
        if kt == NKT - 1:
            epi1()
            for qs in range(4):
                pending.setdefault(ii + 2 + qs, []).append(lambda hh=hh, qb=qb, qs=qs: epi2(hh, qb, qs))
        for f in pending.pop(ii, []):
            f()
    for k in sorted(pending):
        for f in pending[k]:
            f()
    c.stage_end()
    c.stage_end()


EVEN_TM = [0, 512, 1536, 2048]
EVEN_FM = list(range(1024, 1536, 128)) + list(range(2560, 3584, 128))
ODD_TM = [2560]
ODD_FM = list(range(0, 2560, 128)) + list(range(3072, 4096, 128))


def build_fused(nlayers=4):
    c = Ctx()
    x = c.din("x", [TOK, D]); xo = c.dout("xo", [TOK, D])
    cst_d = {k: c.din(k, shp) for k, shp in (("ident", [128, 128]), ("maskF", [128, 128]), ("maskB", [128, 128]),
                                             ("segm", [128, 512]), ("sel", [128, 2]), ("bones", [128, 128]),
                                             ("cosT", [128, TOK]), ("sinT", [128, TOK]))}
    L = []
    for l in range(nlayers):
        d = dict(gmix=c.din("gmix%d" % l, [128, 8]), gmlp=c.din("gmlp%d" % l, [128, 8]),
                 w_in=c.din("w_in%d" % l, [D, 3584 if l % 2 == 0 else 4096]), w_out=c.din("w_out%d" % l, [D, D]),
                 w1=c.din("w1_%d" % l, [D, 4 * D]), w2=c.din("w2_%d" % l, [4 * D, D]))
        if l % 2 == 0:
            d.update(sgug=c.din("sgug%d" % l, [128, 512]), wsT=c.din("wsT%d" % l, [128, 4, 128]), bs=c.din("bs%d" % l, [128, 4]),
                     lbl=c.din("lbl%d" % l, [128, 16]), lbm=c.din("lbm%d" % l, [128, 1]), hng=c.din("hng%d" % l, [128, 512]))
        else:
            d.update(cw=c.din("cw%d" % l, [128, 12]), gv=c.din("gv%d" % l, [128, 4]), lamv=c.din("lamv%d" % l, [128, 4, 64]),
                     lconst=c.din("lconst%d" % l, [128, 2]), subg=c.din("subg%d" % l, [128, 128]))
        L.append(d)
    y_tm = c.dint("y_tm", [TOK, 2048]); y_fm = c.dint("y_fm", [3584, TOK])
    cat_tm = c.dint("cat_tm", [TOK, D]); cat_fm = c.dint("cat_fm", [512, TOK])
    xres = c.sb("xres", [128, NTI, D])
    idb = c.sb("idb", [128, 128], BF16); c.load_const("idb", idb[:], cst_d["ident"], eng="gpsimd")
    mF = c.sb("mF", [128, 128]); c.load_const("mF", mF[:], cst_d["maskF"])
    mB = c.sb("mB", [128, 128]); c.load_const("mB", mB[:], cst_d["maskB"])
    segm = c.sb("segm", [128, 512]); c.load_const("segm", segm[:], cst_d["segm"])
    sel = c.sb("sel", [128, 2]); c.load_const("sel", sel[:], cst_d["sel"])
    cst = dict(mF=mF, mB=mB, segm=segm, sel=sel, bones=cst_d["bones"], cosT=cst_d["cosT"], sinT=cst_d["sinT"])
    c.const_done()
    xkeys = ["xres%d" % i for i in range(NTI)]
    c.P.dma("sync", c.slot("xres"), xres[:], x.rearrange("(t p) d -> p t d", p=128), w=xkeys)
    for l in range(nlayers):
        d = L[l]
        if l % 2 == 0:
            f_proj(c, xres, idb, d["w_in"], d["gmix"], 3584, EVEN_TM, EVEN_FM, y_tm, y_fm)
            f_meven(c, idb, cst, y_tm, y_fm, cat_tm, d, "_%d" % l)
            f_outmlp(c, xres, idb, cat_tm, cat_fm, 0, d["w_out"], d["gmlp"], d["w1"], d["w2"])
        else:
            f_proj(c, xres, idb, d["w_in"], d["gmix"], 4096, ODD_TM, ODD_FM, y_tm, y_fm)
            f_modd(c, idb, cst, y_tm, y_fm, cat_tm, cat_fm, d, "_%d" % l)
            f_outmlp(c, xres, idb, cat_tm, cat_fm, 4, d["w_out"], d["gmlp"], d["w1"], d["w2"])
    c.P.store("sync", c.slot("st_xres"), xo.rearrange("(t p) d -> p t d", p=128), xres[:], r=xkeys)
    return c.finish()


def _perm_cols64(w):
    n = w.shape[1]
    idx = np.arange(n).reshape(n // 64, 2, 32)[:, ::-1, :].reshape(n)
    return w[:, idx]


def fused_inputs(inp, nlayers=4):
    maps = []
    inv = 1.0 / (10000.0 ** (np.arange(0, 64, 2, dtype=np.float32) / 64.0))
    for c_ in range(NCORES):
        b, r = c_ // 2, c_ % 2
        xs = inp["x"][b]
        xl = xs[0:TOK] if r == 0 else xs[::-1][0:TOK]
        pos = np.arange(TOK, dtype=np.float32) if r == 0 else (SEQ - 1 - np.arange(TOK)).astype(np.float32)
        ang = pos[:, None] * inv[None, :]
        cos, sin = np.cos(ang).astype(np.float32).T, np.sin(ang).astype(np.float32).T
        m = dict(x=np.ascontiguousarray(xl), ident=_IDENT, maskF=_MASKF, maskB=_MASKB, segm=_SEGM,
                 sel=np.ascontiguousarray(np.broadcast_to(np.array([[0.0, 1.0]] if r == 0 else [[1.0, 0.0]], np.float32), (128, 2))),
                 bones=_BONES, cosT=np.ascontiguousarray(np.concatenate([cos] * 4, 0)),
                 sinT=np.ascontiguousarray(np.concatenate([-sin, sin, -sin, sin], 0)))
        for l in range(nlayers):
            m["gmix%d" % l] = _gT(inp["norm_mix_g"][l]); m["gmlp%d" % l] = _gT(inp["norm_mlp_g"][l])
            m["w1_%d" % l] = np.ascontiguousarray(inp["mlp_w1"][l]); m["w2_%d" % l] = np.ascontiguousarray(inp["mlp_w2"][l])
            if l % 2 == 0:
                e = l // 2
                w = inp["w_in_even"][e]
                if r == 1:
                    w = np.concatenate([w[:, :2560], w[:, 3072:3584], w[:, 2560:3072]], 1)
                m["w_in%d" % l] = np.ascontiguousarray(w)
                m["w_out%d" % l] = np.ascontiguousarray(inp["w_out_even"][e])
                m["sgug%d" % l] = _bc(inp["sgu_norm_g"][e])
                wsT = np.transpose(inp["sgu_w"][e], (2, 0, 1))
                bs = inp["sgu_b"][e].T
                if r == 1:
                    wsT = wsT[::-1, :, ::-1]
                    bs = bs[::-1]
                m["wsT%d" % l] = np.ascontiguousarray(wsT); m["bs%d" % l] = np.ascontiguousarray(bs)
                lbl = np.zeros((128, 16), np.float32)
                for dr in range(2):
                    src = dr if r == 0 else 1 - dr
                    for hh in range(4):
                        for le in range(2):
                            lbl[:, (dr * 4 + hh) * 2 + le] = inp["hgrn_lb_logits"][src, le, hh * 128:(hh + 1) * 128]
                m["lbl%d" % l] = lbl
                m["lbm%d" % l] = np.full((128, 1), float(e), np.float32)
                m["hng%d" % l] = _bc(inp["hgrn_norm_g"][e])
            else:
                o = l // 2
                w = inp["w_in_odd"][o]
                m["w_in%d" % l] = np.ascontiguousarray(np.concatenate([w, _perm_cols64(w[:, 1536:2048]), _perm_cols64(w[:, 2048:2560])], 1))
                m["w_out%d" % l] = np.ascontiguousarray(inp["w_out_odd"][o])
                cwm = inp["conv_w"][o] if r == 0 else inp["conv_w"][o][::-1]
                m["cw%d" % l] = np.ascontiguousarray(np.concatenate([cwm[:, g * 128:(g + 1) * 128].T for g in range(4)], 1))
                qg, kg = np.tile(inp["q_norm_g"][o], 2), np.tile(inp["k_norm_g"][o], 2)
                m["gv%d" % l] = np.ascontiguousarray(np.stack([qg, _perm64(qg), kg, _perm64(kg)], 1).astype(np.float32))
                m["lamv%d" % l] = np.ascontiguousarray(np.stack([_bc(inp[k][o]) for k in ("lambda_q1", "lambda_k1", "lambda_q2", "lambda_k2")], 1))
                lam_init = 0.8 - 0.6 * math.exp(-0.3 * l)
                m["lconst%d" % l] = np.ascontiguousarray(np.stack([np.full(128, lam_init, np.float32), np.full(128, 1.0 - lam_init, np.float32)], 1))
                m["subg%d" % l] = _bc(inp["diff_norm_g"][o])
        maps.append(m)
    return maps


def run_fused(inp, nlayers=4):
    nc = _prog(("F", nlayers), build_fused, nlayers)
    r = _run(nc, fused_inputs(inp, nlayers))
    out = np.empty((4, SEQ, D), np.float32)
    for c_ in range(NCORES):
        b, rr = c_ // 2, c_ % 2
        if rr == 0:
            out[b, 0:TOK] = r[c_]["xo"]
        else:
            out[b, TOK:SEQ] = r[c_]["xo"][::-1]
    return out


def kernel_unfused(**inputs):
    return kernel_12(**inputs)


kernel_12 = kernel


def kernel(**inputs):
    inp = {k: np.asarray(v) for k, v in inputs.items()}
    return run_fused(inp, 4)
```

```python
import math
import numpy as np
import concourse.bass as bass
import concourse.mybir as mybir
from concourse.bass_utils import run_bass_kernel_spmd

F32 = mybir.dt.float32
BF16 = mybir.dt.bfloat16
AF = mybir.ActivationFunctionType
ALU = mybir.AluOpType
AX = mybir.AxisListType

ENGS = ["sync", "gpsimd", "scalar", "vector", "tensor"]
NCORES = 8


class _Op:
    __slots__ = ("eng", "fn", "deps", "needs_inc", "ticket", "slot", "dval")

    def __init__(self, eng, fn):
        self.eng = eng
        self.fn = fn
        self.deps = []
        self.needs_inc = False
        self.ticket = 0
        self.slot = None
        self.dval = 0


class DmaSlot:
    def __init__(self, sem):
        self.sem = sem
        self.count = 0


class Prog:
    def __init__(self, nc, stack):
        self.nc = nc
        self.stack = stack
        self.ops = {e: [] for e in ENGS}
        self.last_w = {}
        self.readers = {}
        self.esem = {e: stack.enter_context(nc.semaphore("es_" + e)) for e in ENGS}
        self.nslots = 0
        self.store_ops = []
        self.stage_dma = {}

    def slot(self):
        self.nslots += 1
        return DmaSlot(self.stack.enter_context(self.nc.semaphore("ds%d" % self.nslots)))

    def barrier(self):
        deps = []
        for e in ENGS:
            for op in reversed(self.ops[e]):
                if op.slot is None and op.fn is not None:
                    op.needs_inc = True
                    deps.append(op)
                    break
        deps += list(self.stage_dma.values())
        for e in ENGS:
            b = _Op(e, None)
            b.deps = list(deps)
            self.ops[e].append(b)
        self.stage_dma = {}

    def add(self, eng, fn, r=(), w=(), slot=None, ndma=1, dval=None):
        op = _Op(eng, fn)
        deps = []
        seen = set()

        def push(d):
            if d is not None and id(d) not in seen:
                seen.add(id(d))
                deps.append(d)

        for k in r:
            push(self.last_w.get(k))
        for k in w:
            push(self.last_w.get(k))
            for rd in self.readers.get(k, {}).values():
                push(rd)
        for d in deps:
            if d.slot is None:
                if d.eng == eng and eng == "tensor":
                    continue
                d.needs_inc = True
            op.deps.append(d)
        if slot is not None:
            slot.count += (16 * ndma if dval is None else dval)
            op.slot = slot
            op.dval = slot.count
            self.stage_dma[id(slot)] = op
        for k in w:
            self.last_w[k] = op
            self.readers[k] = {}
        for k in r:
            rk = eng if slot is None else ("dma", id(op))
            self.readers.setdefault(k, {})[rk] = op
        self.ops[eng].append(op)
        return op

    def dma(self, eng, slot, out, in_, r=(), w=()):
        def fn(e, out=out, in_=in_, slot=slot):
            return e.dma_start(out=out, in_=in_).then_inc(slot.sem, 16)

        return self.add(eng, fn, r=r, w=w, slot=slot)

    def store(self, eng, slot, out, in_, r=()):
        op = self.dma(eng, slot, out, in_, r=r)
        self.store_ops.append(op)
        return op

    def emit(self):
        nc = self.nc
        fin = _Op("sync", None)
        fin.deps = list(self.store_ops)
        self.ops["sync"].append(fin)
        for e in ENGS:
            t = 0
            for op in self.ops[e]:
                if op.slot is None and op.needs_inc:
                    t += 1
                    op.ticket = t
        with nc.Block() as block:
            for e in ENGS:
                def body(eng, e=e):
                    waited = {}
                    for op in self.ops[e]:
                        for d in op.deps:
                            if d.slot is not None:
                                key, sem, val = ("d", id(d.slot)), d.slot.sem, d.dval
                            else:
                                key, sem, val = ("e", d.eng), self.esem[d.eng], d.ticket
                            if waited.get(key, 0) < val:
                                eng.wait_ge(sem, val)
                                waited[key] = val
                        if op.fn is None:
                            continue
                        inst = op.fn(eng)
                        if op.slot is None and op.needs_inc:
                            inst.then_inc(self.esem[e], 1)

                getattr(block, e)(body)


from contextlib import ExitStack

EPS = 1e-6
D = 1024
TOK = 2048
SEQ = 4096


class Ctx:
    def __init__(self):
        self.nc = bass.Bass("TRN2", target_bir_lowering=False)
        self.st = ExitStack()
        self.P = Prog(self.nc, self.st)
        self.slots = {}
        self.cur = self.st
        self.uid = 0

    def din(self, name, shape, dt=F32):
        return self.nc.dram_tensor(name, list(shape), dt, kind="ExternalInput").ap()

    def dout(self, name, shape, dt=F32):
        return self.nc.dram_tensor(name, list(shape), dt, kind="ExternalOutput").ap()

    def sb(self, name, shape, dt=F32):
        self.uid += 1
        return self.cur.enter_context(self.nc.sbuf_tensor("s%d_%s" % (self.uid, name), list(shape), dt))

    def ps(self, name, shape, dt=F32):
        self.uid += 1
        return self.cur.enter_context(self.nc.psum_tensor("p%d_%s" % (self.uid, name), list(shape), dt))

    def dint(self, name, shape, dt=F32):
        return self.nc.dram_tensor(name, list(shape), dt).ap()

    def stage_begin(self):
        self.stk = getattr(self, "stk", [])
        self.stk.append(self.cur)
        self.cur = ExitStack()

    def stage_end(self):
        self.P.barrier()
        self.cur.close()
        self.cur = self.stk.pop()

    def slot(self, key):
        if key not in self.slots:
            self.slots[key] = self.P.slot()
        return self.slots[key]

    def load(self, key, out, in_, eng="sync"):
        return self.P.dma(eng, self.slot(key), out, in_, w=[key])

    def load_const(self, key, out, in_, eng="sync"):
        op = self.P.dma(eng, self.slot("const_" + eng), out, in_, w=[key])
        self.cpend = getattr(self, "cpend", {})
        self.cpend.setdefault(eng, []).append(key)
        return op

    def const_done(self):
        for eng, keys in getattr(self, "cpend", {}).items():
            ops = [self.P.last_w[k] for k in keys]
            last = max(ops, key=lambda o: o.dval)
            for k in keys:
                self.P.last_w[k] = last
        self.cpend = {}

    def store(self, key, out, in_, eng="sync"):
        return self.P.store(eng, self.slot("st_" + key), out, in_, r=[key])

    def act(self, out, in_, func, r, w, **kw):
        return self.P.add("scalar", lambda e: e.activation(out=out, in_=in_, func=func, **kw), r=r, w=w)

    def vec(self, method, r, w, **kw):
        return self.P.add("vector", lambda e: getattr(e, method)(**kw), r=r, w=w)

    def pe(self, fn, r, w):
        return self.P.add("tensor", fn, r=r, w=w)

    def finish(self):
        self.P.emit()
        self.st.close()
        return self.nc


def rstd_ops(c, ss, n, inv_n, rkeys, key):
    c.vec("tensor_scalar", r=rkeys, w=[key], out=ss[:, 0:n], in0=ss[:, 0:n], scalar1=inv_n, scalar2=EPS,
          op0=ALU.mult, op1=ALU.add)
    c.act(ss[:, 0:n], ss[:, 0:n], AF.Sqrt, r=[key], w=[key])
    c.vec("reciprocal", r=[key], w=[key], out=ss[:, 0:n], in_=ss[:, 0:n])


def norm_transpose(c, i, xtile, xkey, junk, ss, xs, pT, hT_out, hkey, gT, idb, pkey="pT"):
    b = i % 2
    sk, jk, xk, pk = "ss%d" % b, "junk%d" % b, "xs%d" % b, pkey + "%d" % b
    c.act(junk[b][:], xtile, AF.Square, r=[xkey], w=[jk, sk], accum_out=ss[b][:, 0:1])
    rstd_ops(c, ss[b], 1, 1.0 / D, [sk], sk)
    c.act(xs[b][:], xtile, AF.Copy, r=[xkey, sk], w=[xk], scale=ss[b][:, 0:1])

    def tr(e, b=b):
        for k in range(8):
            ins = e.transpose(pT[b][:, k * 128:(k + 1) * 128], xs[b][:, k * 128:(k + 1) * 128], idb[:])
        return ins

    c.pe(tr, r=[xk, "idb"], w=[pk])
    c.vec("tensor_tensor", r=[pk, "gT"], w=[hkey], out=hT_out,
          in0=pT[b][:].rearrange("p (k t) -> p k t", t=128),
          in1=gT[:].rearrange("p (k o) -> p k o", o=1).to_broadcast([128, 8, 128]), op=ALU.mult)


def build_proj(N):
    c = Ctx()
    x = c.din("x", [TOK, D]); gTd = c.din("gT", [128, 8]); w = c.din("w", [D, N]); identd = c.din("ident", [128, 128])
    y = c.dout("y", [TOK, N])
    ncb = N // 512
    wbf = c.sb("wbf", [128, 8, N], BF16)
    idb = c.sb("idb", [128, 128], BF16); gT = c.sb("gT", [128, 8])
    xt = [c.sb("xt%d" % i, [128, D]) for i in range(2)]
    junk = [c.sb("junk%d" % i, [128, D]) for i in range(2)]
    xs = [c.sb("xs%d" % i, [128, D], BF16) for i in range(2)]
    ss = [c.sb("ss%d" % i, [128, 4]) for i in range(2)]
    hT = [c.sb("hT%d" % i, [128, 8, 128], BF16) for i in range(2)]
    yt = [c.sb("yt%d" % i, [128, N]) for i in range(2)]
    pT = [c.ps("pT%d" % i, [128, 1024], BF16) for i in range(2)]
    pm = [c.ps("pm%d" % i, [128, 512]) for i in range(4)]
    c.load("idb", idb[:], identd, eng="gpsimd")
    c.load("gT", gT[:], gTd)
    for k in range(8):
        c.load("wbf%d" % k, wbf[:, k, :], w[k * 128:(k + 1) * 128, :], eng="gpsimd")
    wkeys = ["wbf%d" % k for k in range(8)]
    n = 0
    for i in range(TOK // 128):
        b = i % 2
        c.load("xt%d" % b, xt[b][:], x[i * 128:(i + 1) * 128, :])
        norm_transpose(c, i, xt[b][:], "xt%d" % b, junk, ss, xs, pT, hT[b][:], "hT%d" % b, gT, idb)
        for cb in range(ncb):
            pb = n % 4
            n += 1

            def mm(e, b=b, cb=cb, pb=pb):
                for k in range(8):
                    ins = e.matmul(pm[pb][:], lhsT=hT[b][:, k, :], rhs=wbf[:, k, cb * 512:(cb + 1) * 512],
                                   start=(k == 0), stop=(k == 7))
                return ins

            c.pe(mm, r=["hT%d" % b] + wkeys, w=["pm%d" % pb])
            if cb % 2 == 0:
                c.act(yt[b][:, cb * 512:(cb + 1) * 512], pm[pb][:], AF.Copy, r=["pm%d" % pb], w=["yt%d" % b])
            else:
                c.vec("tensor_copy", r=["pm%d" % pb], w=["yt%d" % b], out=yt[b][:, cb * 512:(cb + 1) * 512], in_=pm[pb][:])
        c.store("yt%d" % b, y[i * 128:(i + 1) * 128, :], yt[b][:])
    return c.finish()


GS = 512


def build_outmlp():
    c = Ctx()
    x = c.din("x", [TOK, D]); cat = c.din("cat", [TOK, D]); wo = c.din("wo", [D, D]); gTd = c.din("gT", [128, 8])
    w1 = c.din("w1", [D, 4 * D]); w2 = c.din("w2", [4 * D, D]); identd = c.din("ident", [128, 128])
    xo = c.dout("xo", [TOK, D])
    NTI = TOK // 128
    xres = c.sb("xres", [128, NTI, D])
    hTa = c.sb("hTa", [128, 8, TOK], BF16)
    wob = c.sb("wob", [128, 8, D], BF16)
    idb = c.sb("idb", [128, 128], BF16); gT = c.sb("gT", [128, 8])
    catb = [c.sb("catb%d" % i, [128, D], BF16) for i in range(2)]
    catT = [c.sb("catT%d" % i, [128, 8, 128], BF16) for i in range(2)]
    junk = [c.sb("junk%d" % i, [128, D]) for i in range(2)]
    xs = [c.sb("xs%d" % i, [128, D], BF16) for i in range(2)]
    ss = [c.sb("ss%d" % i, [128, 4]) for i in range(2)]
    w1g = [c.sb("w1g%d" % i, [128, 8, GS], BF16) for i in range(2)]
    w2g = [c.sb("w2g%d" % i, [128, GS // 128, D], BF16) for i in range(2)]
    rl = [c.sb("rl%d" % i, [128, 512]) for i in range(2)]
    actb = [c.sb("actb%d" % i, [128, GS // 128, 512], BF16) for i in range(2)]
    pT = [c.ps("pT%d" % i, [128, 1024], BF16) for i in range(2)]
    pm = [c.ps("pm%d" % i, [128, 512]) for i in range(6)]
    c.load("idb", idb[:], identd, eng="gpsimd")
    c.load("gT", gT[:], gTd)
    for k in range(8):
        c.load("wob%d" % k, wob[:, k, :], wo[k * 128:(k + 1) * 128, :], eng="gpsimd")
    wokeys = ["wob%d" % k for k in range(8)]
    for i in range(NTI):
        c.load("xres%d" % i, xres[:, i, :], x[i * 128:(i + 1) * 128, :])
    n = 0
    for i in range(NTI):
        b = i % 2
        c.load("catb%d" % b, catb[b][:], cat[i * 128:(i + 1) * 128, :], eng="gpsimd")

        def tr(e, b=b):
            for k in range(8):
                ins = e.transpose(pT[b][:, k * 128:(k + 1) * 128], catb[b][:, k * 128:(k + 1) * 128], idb[:])
            return ins

        c.pe(tr, r=["catb%d" % b, "idb"], w=["pT%d" % b])
        c.vec("tensor_copy", r=["pT%d" % b], w=["catT%d" % b], out=catT[b][:],
              in_=pT[b][:].rearrange("p (k t) -> p k t", t=128))
        for cb in range(2):
            pb = n % 6
            n += 1

            def mm(e, b=b, cb=cb, pb=pb):
                for k in range(8):
                    ins = e.matmul(pm[pb][:], lhsT=catT[b][:, k, :], rhs=wob[:, k, cb * 512:(cb + 1) * 512],
                                   start=(k == 0), stop=(k == 7))
                return ins

            c.pe(mm, r=["catT%d" % b] + wokeys, w=["pm%d" % pb])
            c.vec("tensor_tensor", r=["pm%d" % pb, "xres%d" % i], w=["xres%d" % i],
                  out=xres[:, i, cb * 512:(cb + 1) * 512], in0=pm[pb][:], in1=xres[:, i, cb * 512:(cb + 1) * 512], op=ALU.add)
    for i in range(NTI):
        norm_transpose(c, i, xres[:, i, :], "xres%d" % i, junk, ss, xs, pT, hTa[:, :, i * 128:(i + 1) * 128],
                       "hTa%d" % i, gT, idb)
    NG = 4 * D // GS
    CPG = GS // 128
    m = 0
    for g in range(NG):
        gb = g % 2
        c.load("w1g%d" % gb, w1g[gb][:], w1[:, g * GS:(g + 1) * GS].rearrange("(k p) c -> p k c", p=128), eng="gpsimd")
        c.load("w2g%d" % gb, w2g[gb][:], w2[g * GS:(g + 1) * GS, :].rearrange("(k p) c -> p k c", p=128), eng="gpsimd")
        for tb in range(TOK // 512):
            ab = m % 2
            m += 1
            hkeys = ["hTa%d" % (tb * 4 + j) for j in range(4)]
            for cc in range(CPG):
                pb = n % 6
                n += 1
                rb = n % 2

                def mm1(e, gb=gb, cc=cc, tb=tb, pb=pb):
                    for k in range(8):
                        ins = e.matmul(pm[pb][:], lhsT=w1g[gb][:, k, cc * 128:(cc + 1) * 128],
                                       rhs=hTa[:, k, tb * 512:(tb + 1) * 512], start=(k == 0), stop=(k == 7))
                    return ins

                c.pe(mm1, r=hkeys + ["w1g%d" % gb], w=["pm%d" % pb])
                c.vec("tensor_scalar", r=["pm%d" % pb], w=["rl%d" % rb], out=rl[rb][:], in0=pm[pb][:], scalar1=0.0,
                      scalar2=None, op0=ALU.max)
                c.act(actb[ab][:, cc, :], rl[rb][:], AF.Square, r=["rl%d" % rb], w=["actb%d_%d" % (ab, cc)])
            akeys = ["actb%d_%d" % (ab, cc) for cc in range(CPG)]
            for tt in range(4):
                ti = tb * 4 + tt
                for cb in range(2):
                    pb = n % 6
                    n += 1

                    def mm2(e, gb=gb, ab=ab, tt=tt, cb=cb, pb=pb):
                        for cc in range(CPG):
                            ins = e.matmul(pm[pb][:], lhsT=actb[ab][:, cc, tt * 128:(tt + 1) * 128],
                                           rhs=w2g[gb][:, cc, cb * 512:(cb + 1) * 512], start=(cc == 0), stop=(cc == CPG - 1))
                        return ins

                    c.pe(mm2, r=akeys + ["w2g%d" % gb], w=["pm%d" % pb])
                    c.vec("tensor_tensor", r=["pm%d" % pb, "xres%d" % ti], w=["xres%d" % ti],
                          out=xres[:, ti, cb * 512:(cb + 1) * 512], in0=pm[pb][:], in1=xres[:, ti, cb * 512:(cb + 1) * 512],
                          op=ALU.add)
    for i in range(NTI):
        c.store("xres%d" % i, xo[i * 128:(i + 1) * 128, :], xres[:, i, :])
    return c.finish()


_CACHE = {}


def _prog(key, fn, *a):
    if key not in _CACHE:
        _CACHE[key] = fn(*a)
    return _CACHE[key]


def _run(nc, maps):
    res = run_bass_kernel_spmd(nc, maps, core_ids=list(range(NCORES)))
    return res.results


def _gT(g):
    return np.ascontiguousarray(g.reshape(8, 128).T)


_IDENT = np.eye(128, dtype=np.float32)


def run_proj(xf, g, W):
    N = W.shape[1]
    nc = _prog(("P", N), build_proj, N)
    W = np.ascontiguousarray(W)
    maps = [dict(x=np.ascontiguousarray(xf[c * TOK:(c + 1) * TOK]), gT=_gT(g), w=W, ident=_IDENT) for c in range(NCORES)]
    r = _run(nc, maps)
    return np.concatenate([r[c]["y"] for c in range(NCORES)], 0)


def run_outmlp(xf, catf, wo, g, w1, w2):
    nc = _prog("O", build_outmlp)
    wo, w1, w2 = (np.ascontiguousarray(a) for a in (wo, w1, w2))
    maps = [dict(x=np.ascontiguousarray(xf[c * TOK:(c + 1) * TOK]), cat=np.ascontiguousarray(catf[c * TOK:(c + 1) * TOK]),
                 wo=wo, gT=_gT(g), w1=w1, w2=w2, ident=_IDENT) for c in range(NCORES)]
    r = _run(nc, maps)
    return np.concatenate([r[c]["xo"] for c in range(NCORES)], 0)


def build_meven():
    c = Ctx()
    P = c.P
    u = c.din("u", [SEQ, 256]); v = c.din("v", [SEQ, 256]); sgugd = c.din("sgug", [128, 256])
    wsTd = c.din("wsT", [128, 2, 128]); bsd = c.din("bs", [128, 2])
    aT = c.din("aT", [2, 2, 128, SEQ]); qT = c.din("qT", [2, 128, SEQ]); iv = c.din("iv", [SEQ, 256]); gg = c.din("gg", [SEQ, 256])
    lbld = c.din("lbl", [128, 8]); lbmd = c.din("lbm", [128, 1]); hngd = c.din("hng", [128, 256])
    identd = c.din("ident", [128, 128]); mFd = c.din("maskF", [128, 128]); mBd = c.din("maskB", [128, 128])
    segd = c.din("segm", [128, 512])
    out = c.dout("out", [SEQ, 512])
    NT = SEQ // 128
    idb = c.sb("idb", [128, 128], BF16); c.load("idb", idb[:], identd, eng="gpsimd")
    sgug = c.sb("sgug", [128, 256]); c.load("sgug", sgug[:], sgugd)
    wsb = c.sb("wsb", [128, 2, 128], BF16); c.load("wsb", wsb[:], wsTd, eng="gpsimd")
    bs = c.sb("bs", [128, 2]); c.load("bs", bs[:], bsd)
    lbl = c.sb("lbl", [128, 8]); c.load("lbl", lbl[:], lbld)
    lbm = c.sb("lbm", [128, 1]); c.load("lbm", lbm[:], lbmd)
    hng = c.sb("hng", [128, 256]); c.load("hng", hng[:], hngd)
    mF = c.sb("mF", [128, 128]); c.load("mF", mF[:], mFd)
    mB = c.sb("mB", [128, 128]); c.load("mB", mB[:], mBd)
    segm = c.sb("segm", [128, 512]); c.load("segm", segm[:], segd)
    lb4 = c.sb("lb4", [128, 4]); oml4 = c.sb("oml4", [128, 4]); noml4 = c.sb("noml4", [128, 4])
    l3 = lbl[:].rearrange("p (a e) -> p a e", e=2)
    c.vec("tensor_tensor", r=["lbl"], w=["lb4"], out=lb4[:].rearrange("p (a o) -> p a o", o=1), in0=l3[:, :, 1:2],
          in1=l3[:, :, 0:1], op=ALU.subtract)
    c.act(lb4[:], lb4[:], AF.Sigmoid, r=["lb4"], w=["lb4"])
    c.vec("tensor_scalar", r=["lb4", "lbm"], w=["lb4"], out=lb4[:], in0=lb4[:], scalar1=lbm[:, 0:1], scalar2=None, op0=ALU.mult)
    c.vec("tensor_scalar", r=["lb4"], w=["oml4"], out=oml4[:], in0=lb4[:], scalar1=-1.0, scalar2=1.0, op0=ALU.mult, op1=ALU.add)
    c.vec("tensor_scalar", r=["lb4"], w=["noml4"], out=noml4[:], in0=lb4[:], scalar1=1.0, scalar2=-1.0, op0=ALU.mult, op1=ALU.add)

    pm = c.ps("pm", [128, 512])
    pkt = c.ps("pkt", [128, 1024], BF16)
    pa = [c.ps("pa%d" % i, [128, 512]) for i in range(2)]
    po = [c.ps("po%d" % i, [128, 512]) for i in range(2)]
    pu = [c.ps("pu%d" % i, [128, 512]) for i in range(2)]

    ut = [c.sb("ut%d" % i, [128, 256]) for i in range(2)]
    vt = [c.sb("vt%d" % i, [128, 256]) for i in range(2)]
    vn = [c.sb("vn%d" % i, [128, 256], BF16) for i in range(2)]
    oa = [c.sb("oa%d" % i, [128, 256]) for i in range(2)]
    sj = c.sb("sj", [128, 128]); ssg = c.sb("ssg", [128, 4])
    for n in range(NT):
        b = n % 2
        uk, vk, nk, ok = "ut%d" % b, "vt%d" % b, "vn%d" % b, "oa%d" % b
        c.load(uk, ut[b][:], u[n * 128:(n + 1) * 128, :])
        c.load(vk, vt[b][:], v[n * 128:(n + 1) * 128, :])
        c.act(vt[b][:], vt[b][:], AF.Gelu_apprx_tanh, r=[vk], w=[vk])
        c.act(ut[b][:], ut[b][:], AF.Gelu_apprx_tanh, r=[uk], w=[uk])
        for h in range(2):
            c.act(sj[:], vt[b][:, h * 128:(h + 1) * 128], AF.Square, r=[vk], w=["sj", "ssg"], accum_out=ssg[:, h:h + 1])
        rstd_ops(c, ssg, 2, 1.0 / 128, ["ssg"], "ssg")
        for h in range(2):
            hs = slice(h * 128, (h + 1) * 128)
            c.vec("scalar_tensor_tensor", r=[vk, "ssg", "sgug"], w=[nk], out=vn[b][:, hs], in0=vt[b][:, hs],
                  scalar=ssg[:, h:h + 1], in1=sgug[:, hs], op0=ALU.mult, op1=ALU.mult)

        def mm(e, b=b):
            for h in range(2):
                ins = e.matmul(pm[:, h * 128:(h + 1) * 128], lhsT=wsb[:, h, :], rhs=vn[b][:, h * 128:(h + 1) * 128],
                               start=True, stop=True)
            return ins

        c.pe(mm, r=[nk, "wsb"], w=["pm"])
        for h in range(2):
            hs = slice(h * 128, (h + 1) * 128)
            c.vec("scalar_tensor_tensor", r=["pm", "bs", uk], w=[ok], out=oa[b][:, hs], in0=pm[:, hs],
                  scalar=bs[:, h:h + 1], in1=ut[b][:, hs], op0=ALU.add, op1=ALU.mult)
        c.store(ok, out[n * 128:(n + 1) * 128, 0:256], oa[b][:])

    v3 = lambda t: t[:].rearrange("p (t c) -> p t c", c=128)
    A = [c.sb("A%d" % i, [128, 512]) for i in range(2)]
    Q = [c.sb("Q%d" % i, [128, 512]) for i in range(2)]
    IV = [c.sb("IV%d" % i, [128, 4, 128], BF16) for i in range(2)]
    GG = [c.sb("GG%d" % i, [128, 512]) for i in range(2)]
    L = c.sb("L", [128, 512]); KK = c.sb("KK", [128, 512]); BFW = c.sb("BFW", [128, 512]); BB = c.sb("BB", [128, 512])
    BR = c.sb("BR", [128, 512]); EQ = c.sb("EQ", [128, 512])
    QE = c.sb("QE", [128, 512], BF16); KD = c.sb("KD", [128, 512], BF16); QI = c.sb("QI", [128, 512], BF16)
    KDZ = [c.sb("KDZ%d" % i, [128, 512], BF16) for i in range(2)]
    KDEC = c.sb("KDEC", [128, 512], BF16); kdT = c.sb("kdT", [128, 512], BF16)
    dS = c.sb("dS", [128, 4])
    attT = [c.sb("attT%d" % i, [128, 128], BF16) for i in range(2)]
    S32 = c.sb("S32", [128, 128]); Sbf = c.sb("Sbf", [128, 128], BF16)
    OF = c.sb("OF", [128, NT, 128])
    osum = [c.sb("osum%d" % i, [128, 128]) for i in range(2)]
    obst = [c.sb("obst%d" % i, [128, 128]) for i in range(2)]
    sj2 = c.sb("sj2", [128, 128]); ssh = c.sb("ssh", [128, 4])
    for i in range(2):
        c.vec("memset", r=[], w=["KDZ%d" % i], ap=KDZ[i][:], constant=0.0)
    nt_ = 0
    ntb = 0
    for hh in range(2):
        for dr in range(2):
            col = dr * 2 + hh
            lbc, omlc, nomlc = lb4[:, col:col + 1], oml4[:, col:col + 1], noml4[:, col:col + 1]
            c.vec("memset", r=[], w=["S32"], ap=S32[:], constant=0.0)
            c.vec("memset", r=[], w=["Sbf"], ap=Sbf[:], constant=0.0)
            tbs = range(SEQ // 512) if dr == 0 else range(SEQ // 512 - 1, -1, -1)
            for tb in tbs:
                p2 = ntb % 2
                ntb += 1
                ak, qk, ik, gk = "A%d" % p2, "Q%d" % p2, "IV%d" % p2, "GG%d" % p2
                c.load(ak, A[p2][:], aT[dr, hh, :, tb * 512:(tb + 1) * 512])
                c.load(qk, Q[p2][:], qT[hh, :, tb * 512:(tb + 1) * 512])
                c.load(ik, IV[p2][:], iv[tb * 512:(tb + 1) * 512, hh * 128:(hh + 1) * 128].rearrange("(t p) c -> p t c", p=128),
                       eng="gpsimd")
                if dr == 1:
                    c.load(gk, v3(GG[p2]), gg[tb * 512:(tb + 1) * 512, hh * 128:(hh + 1) * 128].rearrange("(t p) c -> p t c", p=128))
                    c.act(GG[p2][:], GG[p2][:], AF.Silu, r=[gk], w=[gk])
                a_, q_ = A[p2], Q[p2]
                c.act(a_[:], a_[:], AF.Sigmoid, r=[ak], w=[ak])
                c.act(L[:], a_[:], AF.Ln, r=[ak, "oml4", "lb4"], w=["L"], scale=omlc, bias=lbc)
                c.vec("tensor_scalar", r=[ak, "oml4", "noml4"], w=["KK"], out=KK[:], in0=a_[:], scalar1=nomlc, scalar2=omlc,
                      op0=ALU.mult, op1=ALU.add)
                c.vec("tensor_tensor_scan", r=["L", "segm"], w=["BFW"], out=BFW[:], data0=segm[:], data1=L[:], initial=0.0,
                      op0=ALU.mult, op1=ALU.add)
                if dr == 0:
                    Bt, bkey, ri, li = BFW, "BFW", 63, 127
                else:
                    c.vec("tensor_tensor", r=["L", "BFW"], w=["L"], out=L[:], in0=L[:], in1=BFW[:], op=ALU.subtract)
                    c.vec("tensor_tensor", r=["L", "BFW"], w=["BB"], out=v3(BB), in0=v3(L),
                          in1=v3(BFW)[:, :, 127:128].to_broadcast([128, 4, 128]), op=ALU.add)
                    Bt, bkey, ri, li = BB, "BB", 64, 0
                B3 = v3(Bt)
                c.vec("tensor_tensor", r=[bkey], w=["BR"], out=v3(BR), in0=B3,
                      in1=B3[:, :, ri:ri + 1].to_broadcast([128, 4, 128]), op=ALU.subtract)
                c.act(EQ[:], BR[:], AF.Exp, r=["BR"], w=["EQ"])
                c.vec("tensor_tensor", r=[qk, "EQ"], w=["QE"], out=QE[:], in0=q_[:], in1=EQ[:], op=ALU.mult)
                c.act(BR[:], BR[:], AF.Exp, r=["BR"], w=["BR"], scale=-1.0)
                c.vec("tensor_tensor", r=["KK", "BR"], w=["KD"], out=KD[:], in0=KK[:], in1=BR[:], op=ALU.mult)
                hsl = slice(0, 64) if dr == 0 else slice(64, 128)
                c.vec("tensor_tensor", r=["KK", "BR"], w=["KDZ%d" % dr], out=v3(KDZ[dr])[:, :, hsl], in0=v3(KK)[:, :, hsl],
                      in1=v3(BR)[:, :, hsl], op=ALU.mult)
                c.act(EQ[:], Bt[:], AF.Exp, r=[bkey, "QE"], w=["EQ"])
                c.vec("tensor_tensor", r=[qk, "EQ"], w=["QI"], out=QI[:], in0=q_[:], in1=EQ[:], op=ALU.mult)
                c.vec("tensor_tensor", r=[bkey, "KD", "KDZ%d" % dr], w=["BR"], out=v3(BR),
                      in0=B3[:, :, li:li + 1].to_broadcast([128, 4, 128]), in1=B3, op=ALU.subtract)
                c.act(BR[:], BR[:], AF.Exp, r=["BR"], w=["BR"])
                c.vec("tensor_tensor", r=["KK", "BR"], w=["KDEC"], out=KDEC[:], in0=KK[:], in1=BR[:], op=ALU.mult)
                c.act(dS[:].rearrange("p (t o) -> p t o", o=1), B3[:, :, li:li + 1], AF.Exp, r=[bkey], w=["dS"])

                def trk(e):
                    for t in range(4):
                        ins = e.transpose(pkt[:, t * 128:(t + 1) * 128], KDEC[:, t * 128:(t + 1) * 128], idb[:])
                    return ins

                c.pe(trk, r=["KDEC", "idb"], w=["pkt"])
                c.act(kdT[:], pkt[:, 0:512], AF.Copy, r=["pkt"], w=["kdT"])
                tts = range(4) if dr == 0 else range(3, -1, -1)
                for tt in tts:
                    ti = tb * 4 + tt
                    pb = nt_ % 2
                    nt_ += 1
                    ts_ = slice(tt * 128, (tt + 1) * 128)
                    kA, kB = (KDZ[0], KD) if dr == 0 else (KD, KDZ[1])

                    def mma(e, pb=pb, tt=tt, kA=kA, kB=kB):
                        e.matmul(pa[pb][:, 0:64], lhsT=kA[:, tt * 128:(tt + 1) * 128], rhs=QE[:, tt * 128:tt * 128 + 64],
                                 start=True, stop=True)
                        return e.matmul(pa[pb][:, 64:128], lhsT=kB[:, tt * 128:(tt + 1) * 128],
                                        rhs=QE[:, tt * 128 + 64:(tt + 1) * 128], start=True, stop=True)

                    c.pe(mma, r=["KD", "KDZ%d" % dr, "QE"], w=["pa%d" % pb])
                    c.vec("tensor_tensor", r=["pa%d" % pb, "mF", "mB"], w=["attT%d" % pb], out=attT[pb][:], in0=pa[pb][:, 0:128],
                          in1=(mF if dr == 0 else mB)[:], op=ALU.mult)

                    def mmo(e, pb=pb, tt=tt, p2=p2):
                        e.matmul(po[pb][:, 0:128], lhsT=QI[:, tt * 128:(tt + 1) * 128], rhs=Sbf[:], start=True, stop=False)
                        return e.matmul(po[pb][:, 0:128], lhsT=attT[pb][:], rhs=IV[p2][:, tt, :], start=False, stop=True)

                    c.pe(mmo, r=["QI", "Sbf", "attT%d" % pb, ik], w=["po%d" % pb])
                    c.pe(lambda e, pb=pb, tt=tt, p2=p2: e.matmul(pu[pb][:, 0:128], lhsT=kdT[:, tt * 128:(tt + 1) * 128],
                                                                 rhs=IV[p2][:, tt, :], start=True, stop=True),
                         r=["kdT", ik], w=["pu%d" % pb])
                    c.vec("scalar_tensor_tensor", r=["S32", "dS", "pu%d" % pb], w=["S32"], out=S32[:], in0=S32[:],
                          scalar=dS[:, tt:tt + 1], in1=pu[pb][:, 0:128], op0=ALU.mult, op1=ALU.add)
                    c.act(Sbf[:], S32[:], AF.Copy, r=["S32"], w=["Sbf"])
                    if dr == 0:
                        c.act(OF[:, ti, :], po[pb][:, 0:128], AF.Copy, r=["po%d" % pb], w=["OF%d" % ti])
                    else:
                        ob_ = nt_ % 2
                        c.vec("tensor_tensor", r=["po%d" % pb, "OF%d" % ti], w=["osum%d" % ob_], out=osum[ob_][:],
                              in0=po[pb][:, 0:128], in1=OF[:, ti, :], op=ALU.add)
                        c.act(sj2[:], osum[ob_][:], AF.Square, r=["osum%d" % ob_], w=["sj2", "ssh"], accum_out=ssh[:, 0:1])
                        rstd_ops(c, ssh, 1, 1.0 / 128, ["ssh"], "ssh")
                        c.vec("scalar_tensor_tensor", r=["osum%d" % ob_, "ssh", "hng"], w=["osum%d" % ob_], out=osum[ob_][:],
                              in0=osum[ob_][:], scalar=ssh[:, 0:1], in1=hng[:, hh * 128:(hh + 1) * 128], op0=ALU.mult, op1=ALU.mult)
                        c.vec("tensor_tensor", r=["osum%d" % ob_, gk], w=["obst%d" % ob_], out=obst[ob_][:], in0=osum[ob_][:],
                              in1=GG[p2][:, tt * 128:(tt + 1) * 128], op=ALU.mult)
                        c.store("obst%d" % ob_, out[ti * 128:(ti + 1) * 128, 256 + hh * 128:256 + (hh + 1) * 128], obst[ob_][:])
    return c.finish()


_MASKF = np.triu(np.ones((128, 128), np.float32))
_MASKB = np.tril(np.ones((128, 128), np.float32))
_SEGM = np.ones((128, 512), np.float32)
_SEGM[:, ::128] = 0.0


def _bc(vec):
    return np.ascontiguousarray(np.broadcast_to(vec[None, :], (128, vec.shape[0])))


def run_meven(y, e, inp):
    nc = _prog("Me", build_meven)
    maps = []
    for c in range(NCORES):
        b, hp = c // 2, c % 2
        yb = y[b * SEQ:(b + 1) * SEQ]
        cs = slice(hp * 256, hp * 256 + 256)
        u, v, q, iv, g, ff, fb = (yb[:, k * 512:(k + 1) * 512] for k in range(7))
        heads = [2 * hp, 2 * hp + 1]
        aT = np.stack([np.stack([f[:, h * 128:(h + 1) * 128].T for h in heads]) for f in (ff, fb)])
        qT = np.stack([q[:, h * 128:(h + 1) * 128].T for h in heads])
        lbl = np.zeros((128, 8), np.float32)
        for dr in range(2):
            for hi, h in enumerate(heads):
                for le in range(2):
                    lbl[:, (dr * 2 + hi) * 2 + le] = inp["hgrn_lb_logits"][dr, le, h * 128:(h + 1) * 128]
        maps.append(dict(
            u=np.ascontiguousarray(u[:, cs]), v=np.ascontiguousarray(v[:, cs]), sgug=_bc(inp["sgu_norm_g"][e][cs]),
            wsT=np.ascontiguousarray(np.transpose(inp["sgu_w"][e][heads[0]:heads[1] + 1], (2, 0, 1))),
            bs=np.ascontiguousarray(inp["sgu_b"][e][heads[0]:heads[1] + 1].T),
            aT=np.ascontiguousarray(aT), qT=np.ascontiguousarray(qT), iv=np.ascontiguousarray(iv[:, cs]),
            gg=np.ascontiguousarray(g[:, cs]), lbl=lbl, lbm=np.full((128, 1), float(e), np.float32),
            hng=_bc(inp["hgrn_norm_g"][e][cs]), ident=_IDENT, maskF=_MASKF, maskB=_MASKB, segm=_SEGM))
    r = _run(nc, maps)
    cat = np.empty((4 * SEQ, D), np.float32)
    for c in range(NCORES):
        b, hp = c // 2, c % 2
        o = r[c]["out"]
        cat[b * SEQ:(b + 1) * SEQ, hp * 256:hp * 256 + 256] = o[:, 0:256]
        cat[b * SEQ:(b + 1) * SEQ, 512 + hp * 256:512 + hp * 256 + 256] = o[:, 256:512]
    return cat


def build_modd():
    c = Ctx()
    hinT = c.din("hinT", [2, 128, SEQ]); bgT = c.din("bgT", [2, 128, SEQ]); cgT = c.din("cgT", [2, 128, SEQ])
    cwd = c.din("cw", [128, 6])
    qT = c.din("qT", [2, 128, SEQ]); qTp = c.din("qTp", [2, 128, SEQ]); kT = c.din("kT", [2, 128, SEQ]); kTp = c.din("kTp", [2, 128, SEQ])
    vd = c.din("v", [SEQ, 256]); gvd = c.din("gv", [128, 4]); cosd = c.din("cosT", [128, SEQ]); sind = c.din("sinT", [128, SEQ])
    lamd = c.din("lamv", [128, 4, 64]); lcd = c.din("lconst", [128, 2]); subgd = c.din("subg", [128, 128])
    bonesd = c.din("bones", [128, 128])
    oc = c.dout("oc", [2, 128, SEQ]); od = c.dout("od", [SEQ, 256])
    NT = SEQ // 128
    cw = c.sb("cw", [128, 6]); c.load("cw", cw[:], cwd)
    gv = c.sb("gv", [128, 4]); c.load("gv", gv[:], gvd)
    cosT = c.sb("cosT", [128, SEQ]); c.load("cosT", cosT[:], cosd)
    sinT = c.sb("sinT", [128, SEQ]); c.load("sinT", sinT[:], sind)
    lamv = c.sb("lamv", [128, 4, 64]); c.load("lamv", lamv[:], lamd)
    lconst = c.sb("lconst", [128, 2]); c.load("lconst", lconst[:], lcd)
    subg = c.sb("subg", [128, 128]); c.load("subg", subg[:], subgd)
    bones = c.sb("bones", [128, 128]); c.load("bones", bones[:], bonesd)
    epsb = c.sb("epsb", [128, 1]); c.vec("memset", r=[], w=["epsb"], ap=epsb[:], constant=EPS)
    lj = c.sb("lj", [128, 64]); ls = c.sb("ls", [128, 2]); nlam = c.sb("nlam", [128, 1])
    for j in range(2):
        c.vec("tensor_tensor", r=["lamv"], w=["lj"], out=lj[:], in0=lamv[:, 2 * j, :], in1=lamv[:, 2 * j + 1, :], op=ALU.mult)
        c.vec("tensor_reduce", r=["lj"], w=["ls%d" % j], out=ls[:, j:j + 1], in_=lj[:], axis=AX.X, op=ALU.add)
    c.act(ls[:], ls[:], AF.Exp, r=["ls0", "ls1"], w=["ls"])
    c.vec("tensor_tensor", r=["ls"], w=["nlam"], out=nlam[:], in0=ls[:, 1:2], in1=ls[:, 0:1], op=ALU.subtract)
    c.vec("tensor_tensor", r=["nlam", "lconst"], w=["nlam"], out=nlam[:], in0=nlam[:], in1=lconst[:, 0:1], op=ALU.subtract)
    c.vec("tensor_scalar", r=["subg", "lconst"], w=["subg"], out=subg[:], in0=subg[:], scalar1=lconst[:, 1:2], scalar2=None,
          op0=ALU.mult)

    hin = c.sb("hin", [128, SEQ]); cg = c.sb("cg", [128, SEQ]); bg = c.sb("bg", [128, SEQ])
    z = c.sb("z", [128, SEQ + 2]); yy = c.sb("yy", [128, SEQ])
    c.vec("memset", r=[], w=["z"], ap=z[:], constant=0.0)
    for g in range(2):
        c.load("hin", hin[:], hinT[g]); c.load("cg", cg[:], cgT[g]); c.load("bg", bg[:], bgT[g])
        c.vec("tensor_tensor", r=["hin", "cg"], w=["z"], out=z[:, 1:SEQ + 1], in0=cg[:], in1=hin[:], op=ALU.mult)
        c.vec("tensor_scalar", r=["z", "cw"], w=["yy"], out=yy[:], in0=z[:, 1:SEQ + 1], scalar1=cw[:, 3 * g + 1:3 * g + 2],
              scalar2=None, op0=ALU.mult)
        c.vec("scalar_tensor_tensor", r=["z", "cw", "yy"], w=["yy"], out=yy[:], in0=z[:, 0:SEQ], scalar=cw[:, 3 * g:3 * g + 1],
              in1=yy[:], op0=ALU.mult, op1=ALU.add)
        c.vec("scalar_tensor_tensor", r=["z", "cw", "yy"], w=["yy"], out=yy[:], in0=z[:, 2:SEQ + 2],
              scalar=cw[:, 3 * g + 2:3 * g + 3], in1=yy[:], op0=ALU.mult, op1=ALU.add)
        c.vec("tensor_tensor", r=["yy", "bg"], w=["yy"], out=yy[:], in0=yy[:], in1=bg[:], op=ALU.mult)
        c.store("yy", oc[g], yy[:])

    pss = c.ps("pss", [128, 512])
    pS = [c.ps("pS%d" % i, [128, 1024]) for i in range(2)]
    pacc = c.ps("pacc", [128, 3, 512])
    Kr = c.sb("Kr", [128, SEQ], BF16); Qr = c.sb("Qr", [128, SEQ], BF16)
    Vaug = c.sb("Vaug", [128, NT, 132], BF16)
    xt = [c.sb("axt%d" % i, [128, 512]) for i in range(2)]
    xp = [c.sb("axp%d" % i, [128, 512]) for i in range(2)]
    sq = c.sb("sq", [128, 512]); rr = c.sb("rr", [128, 512]); t1 = c.sb("t1", [128, 512]); t2 = c.sb("t2", [128, 512])
    pT = [c.sb("pTs%d" % i, [128, 1024], BF16) for i in range(2)]
    rc = c.sb("rc", [128, 8]); o1 = [c.sb("o1_%d" % i, [128, 128]) for i in range(2)]
    odt = [c.sb("odt%d" % i, [128, 128]) for i in range(2)]
    sj = c.sb("sj", [128, 128]); ssd = c.sb("ssd", [128, 4])
    nb = 0
    it = 0
    ne = 0
    for hh in range(2):
        c.load("Vaug", Vaug[:, :, 0:128], vd[:, hh * 128:(hh + 1) * 128].rearrange("(t p) c -> p t c", p=128), eng="gpsimd")
        c.vec("memset", r=[], w=["Vones"], ap=Vaug[:, :, 128:129], constant=1.0)
        for (src, srcp, dst, dkey, gi) in ((kT, kTp, Kr, "Kr", 2), (qT, qTp, Qr, "Qr", 0)):
            for blk in range(SEQ // 512):
                b = nb % 2
                nb += 1
                bs_ = slice(blk * 512, (blk + 1) * 512)
                xk, pk = "axt%d" % b, "axp%d" % b
                c.load(xk, xt[b][:], src[hh, :, bs_])
                c.load(pk, xp[b][:], srcp[hh, :, bs_])
                c.act(sq[:], xt[b][:], AF.Square, r=[xk], w=["sq"])
                c.pe(lambda e: e.matmul(pss[:], lhsT=bones[:], rhs=sq[:], start=True, stop=True), r=["sq", "bones"], w=["pss"])
                c.act(rr[:], pss[:], AF.Ln, r=["pss", "epsb"], w=["rr"], scale=1.0 / 64, bias=epsb[:, 0:1])
                c.act(rr[:], rr[:], AF.Exp, r=["rr"], w=["rr"], scale=-0.5)
                c.vec("scalar_tensor_tensor", r=[xk, "gv", "cosT"], w=["t1"], out=t1[:], in0=xt[b][:], scalar=gv[:, gi:gi + 1],
                      in1=cosT[:, bs_], op0=ALU.mult, op1=ALU.mult)
                c.vec("scalar_tensor_tensor", r=[pk, "gv", "sinT"], w=["t2"], out=t2[:], in0=xp[b][:], scalar=gv[:, gi + 1:gi + 2],
                      in1=sinT[:, bs_], op0=ALU.mult, op1=ALU.mult)
                c.vec("tensor_tensor", r=["t1", "t2"], w=["t1"], out=t1[:], in0=t1[:], in1=t2[:], op=ALU.add)
                c.vec("tensor_tensor", r=["t1", "rr"], w=[dkey + "%d" % blk], out=dst[:, bs_], in0=t1[:], in1=rr[:], op=ALU.mult)
        kkeys = ["Kr%d" % j for j in range(8)]
        for qb in range(SEQ // 512):
            for kt in range(NT):
                sb_ = it % 2
                it += 1

                def mms(e, sb_=sb_, kt=kt, qb=qb):
                    e.matmul(pS[sb_][:, 0:512], lhsT=Kr[0:64, kt * 128:(kt + 1) * 128], rhs=Qr[0:64, qb * 512:(qb + 1) * 512],
                             start=True, stop=True)
                    return e.matmul(pS[sb_][:, 512:1024], lhsT=Kr[64:128, kt * 128:(kt + 1) * 128],
                                    rhs=Qr[64:128, qb * 512:(qb + 1) * 512], start=True, stop=True)

                c.pe(mms, r=["Kr%d" % (kt // 4), "Qr%d" % qb], w=["pS%d" % sb_])
                c.act(pT[sb_][:], pS[sb_][:], AF.Exp, r=["pS%d" % sb_], w=["pTs%d" % sb_], scale=0.125)

                def mmv(e, sb_=sb_, kt=kt):
                    for a in range(8):
                        bank, off = a // 3, (a % 3) * 132
                        ins = e.matmul(pacc[:, bank, off:off + 129], lhsT=pT[sb_][:, a * 128:(a + 1) * 128], rhs=Vaug[:, kt, 0:129],
                                       start=(kt == 0 and a % 3 == 0), stop=(kt == NT - 1), skip_group_check=True)
                    return ins

                c.pe(mmv, r=["pTs%d" % sb_, "Vaug", "Vones"], w=["pacc"])
            for a in range(8):
                bank, off = a // 3, (a % 3) * 132
                c.vec("reciprocal", r=["pacc"], w=["rc%d" % a], out=rc[:, a:a + 1], in_=pacc[:, bank, off + 128:off + 129])
            c.vec("tensor_scalar", r=["rc%d" % a for a in range(4, 8)] + ["nlam"], w=["rc%d" % a for a in range(4, 8)],
                  out=rc[:, 4:8], in0=rc[:, 4:8], scalar1=nlam[:, 0:1], scalar2=None, op0=ALU.mult)
            for qs in range(4):
                e_ = ne % 2
                ne += 1
                a1, a2 = qs, 4 + qs
                c.vec("tensor_scalar", r=["pacc", "rc%d" % a1], w=["o1_%d" % e_], out=o1[e_][:],
                      in0=pacc[:, a1 // 3, (a1 % 3) * 132:(a1 % 3) * 132 + 128], scalar1=rc[:, a1:a1 + 1], scalar2=None, op0=ALU.mult)
                c.vec("scalar_tensor_tensor", r=["pacc", "rc%d" % a2, "o1_%d" % e_], w=["o1_%d" % e_], out=o1[e_][:],
                      in0=pacc[:, a2 // 3, (a2 % 3) * 132:(a2 % 3) * 132 + 128], scalar=rc[:, a2:a2 + 1], in1=o1[e_][:],
                      op0=ALU.mult, op1=ALU.add)
                c.act(sj[:], o1[e_][:], AF.Square, r=["o1_%d" % e_], w=["sj", "ssd"], accum_out=ssd[:, 0:1])
                rstd_ops(c, ssd, 1, 1.0 / 128, ["ssd"], "ssd")
                c.vec("scalar_tensor_tensor", r=["o1_%d" % e_, "ssd", "subg"], w=["odt%d" % e_], out=odt[e_][:], in0=o1[e_][:],
                      scalar=ssd[:, 0:1], in1=subg[:], op0=ALU.mult, op1=ALU.mult)
                ti = qb * 4 + qs
                c.store("odt%d" % e_, od[ti * 128:(ti + 1) * 128, hh * 128:(hh + 1) * 128], odt[e_][:])
    return c.finish()


def _rope_tables():
    inv = 1.0 / (10000.0 ** (np.arange(0, 64, 2, dtype=np.float32) / 64.0))
    ang = np.arange(SEQ, dtype=np.float32)[:, None] * inv[None, :]
    cos, sin = np.cos(ang).astype(np.float32).T, np.sin(ang).astype(np.float32).T
    cosT = np.concatenate([cos, cos, cos, cos], 0)
    sinT = np.concatenate([-sin, sin, -sin, sin], 0)
    return np.ascontiguousarray(cosT), np.ascontiguousarray(sinT)


def _perm64(a):
    return np.concatenate([a[32:64], a[0:32], a[96:128], a[64:96]], 0)


_BONES = np.kron(np.eye(2, dtype=np.float32), np.ones((64, 64), np.float32))


def run_modd(y, o, layer, inp):
    nc = _prog("Mo", build_modd)
    cosT, sinT = _rope_tables()
    lam_init = 0.8 - 0.6 * math.exp(-0.3 * layer)
    qg, kg = np.tile(inp["q_norm_g"][o], 2), np.tile(inp["k_norm_g"][o], 2)
    gv = np.stack([qg, _perm64(qg), kg, _perm64(kg)], 1).astype(np.float32)
    lamv = np.stack([_bc(inp[k][o]) for k in ("lambda_q1", "lambda_k1", "lambda_q2", "lambda_k2")], 1)
    lconst = np.stack([np.full(128, lam_init, np.float32), np.full(128, 1.0 - lam_init, np.float32)], 1)
    maps = []
    for c in range(NCORES):
        b, hp = c // 2, c % 2
        yb = y[b * SEQ:(b + 1) * SEQ]
        hin, bg, cg, q, k, v = (yb[:, j * 512:(j + 1) * 512] for j in range(6))
        grp = [2 * hp, 2 * hp + 1]
        fm = lambda a: np.ascontiguousarray(np.stack([a[:, g * 128:(g + 1) * 128].T for g in grp]))
        fmp = lambda a: np.ascontiguousarray(np.stack([_perm64(a[:, g * 128:(g + 1) * 128].T) for g in grp]))
        cw = np.concatenate([inp["conv_w"][o][:, g * 128:(g + 1) * 128].T for g in grp], 1)
        maps.append(dict(hinT=fm(hin), bgT=fm(bg), cgT=fm(cg), cw=np.ascontiguousarray(cw),
                         qT=fm(q), qTp=fmp(q), kT=fm(k), kTp=fmp(k), v=np.ascontiguousarray(v[:, hp * 256:hp * 256 + 256]),
                         gv=gv, cosT=cosT, sinT=sinT, lamv=np.ascontiguousarray(lamv), lconst=lconst,
                         subg=_bc(inp["diff_norm_g"][o]), bones=_BONES))
    r = _run(nc, maps)
    cat = np.empty((4 * SEQ, D), np.float32)
    for c in range(NCORES):
        b, hp = c // 2, c % 2
        for gi in range(2):
            cat[b * SEQ:(b + 1) * SEQ, (2 * hp + gi) * 128:(2 * hp + gi + 1) * 128] = r[c]["oc"][gi].T
        cat[b * SEQ:(b + 1) * SEQ, 512 + hp * 256:512 + hp * 256 + 256] = r[c]["od"]
    return cat


def kernel(**inputs):
    inp = {k: np.asarray(v) for k, v in inputs.items()}
    x = np.ascontiguousarray(inp["x"].reshape(-1, D).astype(np.float32))
    for l in range(4):
        if l % 2 == 0:
            e = l // 2
            y = run_proj(x, inp["norm_mix_g"][l], inp["w_in_even"][e])
            cat = run_meven(y, e, inp)
            wo = inp["w_out_even"][e]
        else:
            o = l // 2
            y = run_proj(x, inp["norm_mix_g"][l], inp["w_in_odd"][o])
            cat = run_modd(y, o, l, inp)
            wo = inp["w_out_odd"][o]
        x = run_outmlp(x, cat, wo, inp["norm_mlp_g"][l], inp["mlp_w1"][l], inp["mlp_w2"][l])
    return x.reshape(4, SEQ, D).astype(np.float32)


GROUPS = [[0, 1], [2, 3], [4, 5], [6, 7]]
NTI = TOK // 128
v3 = lambda t: t[:].rearrange("p (t c) -> p t c", c=128)


def allgather(c, in_ap, out_ap, rkey, wkey):
    slot = c.P.slot()

    def fn(e):
        return e.collective_compute("AllGather", ALU.bypass, replica_groups=GROUPS, ins=[in_ap], outs=[out_ap]).then_inc(slot.sem)

    return c.P.add("gpsimd", fn, r=[rkey], w=[wkey], slot=slot, dval=1)


def f_proj(c, xres, idb, w_d, gT_d, Ntot, tm_blocks, fm_chunks, y_tm, y_fm):
    c.stage_begin()
    ntm = len(tm_blocks)
    wbf = c.sb("wbf", [128, 8, Ntot], BF16)
    gT = c.sb("gT", [128, 8]); c.load_const("gT", gT[:], gT_d)
    junk = [c.sb("junk%d" % i, [128, D]) for i in range(2)]
    xs = [c.sb("xs%d" % i, [128, D], BF16) for i in range(2)]
    ss = [c.sb("ss%d" % i, [128, 4]) for i in range(2)]
    hT = [c.sb("hT%d" % i, [128, 8, 512], BF16) for i in range(2)]
    yt = [c.sb("yt%d" % i, [128, max(ntm, 1) * 512]) for i in range(2)]
    yf = [c.sb("yf%d" % i, [128, 512]) for i in range(3)]
    pT = [c.ps("pT%d" % i, [128, 1024], BF16) for i in range(2)]
    pm = [c.ps("pm%d" % i, [128, 512]) for i in range(4)]
    c.const_done()
    wv = w_d.rearrange("(k p) n -> p k n", p=128)
    order = [cs // 512 for cs in tm_blocks]
    for cs in fm_chunks:
        if cs // 512 not in order:
            order.append(cs // 512)
    for sl in order:
        c.load("wslab%d" % sl, wbf[:, :, sl * 512:(sl + 1) * 512], wv[:, :, sl * 512:(sl + 1) * 512], eng="gpsimd")
    n = 0
    nf = 0
    def norm_tb(tb):
        hb = tb % 2
        for j in range(4):
            i = tb * 4 + j
            norm_transpose(c, i, xres[:, i, :], "xres%d" % i, junk, ss, xs, pT, hT[hb][:, :, j * 128:(j + 1) * 128],
                           "hT%d_%d" % (hb, j), gT, idb)

    norm_tb(0)
    for tb in range(TOK // 512):
        hb = tb % 2
        if tb + 1 < TOK // 512:
            norm_tb(tb + 1)
        hkeys = ["hT%d_%d" % (hb, j) for j in range(4)]
        for j in range(4):
            if ntm == 0:
                break
            i = tb * 4 + j
            yb = i % 2
            for bi, cs in enumerate(tm_blocks):
                pb = n % 4
                n += 1

                def mm(e, hb=hb, j=j, cs=cs, pb=pb):
                    for k in range(8):
                        ins = e.matmul(pm[pb][:], lhsT=hT[hb][:, k, j * 128:(j + 1) * 128], rhs=wbf[:, k, cs:cs + 512],
                                       start=(k == 0), stop=(k == 7))
                    return ins

                c.pe(mm, r=["hT%d_%d" % (hb, j), "wslab%d" % (cs // 512)], w=["pm%d" % pb])
                if bi % 2 == 0:
                    c.act(yt[yb][:, bi * 512:(bi + 1) * 512], pm[pb][:], AF.Copy, r=["pm%d" % pb], w=["yt%d" % yb])
                else:
                    c.vec("tensor_copy", r=["pm%d" % pb], w=["yt%d" % yb], out=yt[yb][:, bi * 512:(bi + 1) * 512], in_=pm[pb][:])
            c.P.dma("sync", c.slot("st_yt%d" % yb), y_tm[i * 128:(i + 1) * 128, 0:ntm * 512], yt[yb][:, 0:ntm * 512],
                    r=["yt%d" % yb], w=["ytm%d" % i])
        for fi, cs in enumerate(fm_chunks):
            pb = n % 4
            n += 1
            fb = nf % 3
            nf += 1

            def mmf(e, hb=hb, cs=cs, pb=pb):
                for k in range(8):
                    ins = e.matmul(pm[pb][:], lhsT=wbf[:, k, cs:cs + 128], rhs=hT[hb][:, k, :], start=(k == 0), stop=(k == 7))
                return ins

            c.pe(mmf, r=hkeys + ["wslab%d" % (cs // 512)], w=["pm%d" % pb])
            if fi % 2 == 0:
                c.act(yf[fb][:], pm[pb][:], AF.Copy, r=["pm%d" % pb], w=["yf%d" % fb])
            else:
                c.vec("tensor_copy", r=["pm%d" % pb], w=["yf%d" % fb], out=yf[fb][:], in_=pm[pb][:])
            c.P.dma("sync", c.slot("st_yf%d" % fb), y_fm[fi * 128:(fi + 1) * 128, tb * 512:(tb + 1) * 512], yf[fb][:],
                    r=["yf%d" % fb], w=["yfm%d_%d" % (fi, tb)])
    c.stage_end()


def f_outmlp(c, xres, idb, cat_tm, cat_fm, n_fm, wo, gT_d, w1, w2):
    c.stage_begin()
    hTa = c.sb("hTa", [128, 8, TOK], BF16)
    wob = c.sb("wob", [128, 8, D], BF16)
    gT = c.sb("gT", [128, 8]); c.load_const("gT", gT[:], gT_d)
    catb = [c.sb("catb%d" % i, [128, D], BF16) for i in range(2)]
    catT = [c.sb("catT%d" % i, [128, 8, 128], BF16) for i in range(2)]
    junk = [c.sb("junk%d" % i, [128, D]) for i in range(2)]
    xs = [c.sb("xs%d" % i, [128, D], BF16) for i in range(2)]
    ss = [c.sb("ss%d" % i, [128, 4]) for i in range(2)]
    w1g = [c.sb("w1g%d" % i, [128, 8, GS], BF16) for i in range(2)]
    w2g = [c.sb("w2g%d" % i, [128, GS // 128, D], BF16) for i in range(2)]
    rl = [c.sb("rl%d" % i, [128, 512]) for i in range(2)]
    actb = [c.sb("actb%d" % i, [128, GS // 128, 512], BF16) for i in range(2)]
    pT = [c.ps("pT%d" % i, [128, 1024], BF16) for i in range(2)]
    pN = [c.ps("pN%d" % i, [128, 1024], BF16) for i in range(2)]
    pm = [c.ps("pm%d" % i, [128, 512]) for i in range(4)]
    c.const_done()
    c.load("wob", wob[:], wo.rearrange("(k p) n -> p k n", p=128), eng="gpsimd")
    wokeys = ["wob"]
    n = 0
    c0 = n_fm * 128
    for i in range(NTI):
        b = i % 2
        c.P.dma("gpsimd", c.slot("catb%d" % b), catb[b][:, c0:D], cat_tm[i * 128:(i + 1) * 128, c0:D], r=["cat_tm"], w=["catb%d" % b])
        rk = ["pT%d" % b]
        if n_fm:
            c.P.dma("gpsimd", c.slot("catTf%d" % b), catT[b][:, 0:n_fm, :],
                    cat_fm.rearrange("(k p) t -> p k t", p=128)[:, 0:n_fm, i * 128:(i + 1) * 128], r=["cat_fm"], w=["catTf%d" % b])

        def tr(e, b=b):
            for k in range(n_fm, 8):
                ins = e.transpose(pT[b][:, k * 128:(k + 1) * 128], catb[b][:, k * 128:(k + 1) * 128], idb[:])
            return ins

        c.pe(tr, r=["catb%d" % b, "idb"], w=["pT%d" % b])
        c.vec("tensor_copy", r=["pT%d" % b], w=["catT%d" % b], out=catT[b][:, n_fm:8, :],
              in_=pT[b][:].rearrange("p (k t) -> p k t", t=128)[:, n_fm:8, :])
        for cb in range(2):
            pb = n % 4
            n += 1

            def mm(e, b=b, cb=cb, pb=pb):
                for k in range(8):
                    ins = e.matmul(pm[pb][:], lhsT=catT[b][:, k, :], rhs=wob[:, k, cb * 512:(cb + 1) * 512],
                                   start=(k == 0), stop=(k == 7))
                return ins

            c.pe(mm, r=["catT%d" % b, "catTf%d" % b] + wokeys, w=["pm%d" % pb])
            c.vec("tensor_tensor", r=["pm%d" % pb, "xres%d" % i], w=["xres%d" % i],
                  out=xres[:, i, cb * 512:(cb + 1) * 512], in0=pm[pb][:], in1=xres[:, i, cb * 512:(cb + 1) * 512], op=ALU.add)
        if i >= 1:
            norm_transpose(c, i - 1, xres[:, i - 1, :], "xres%d" % (i - 1), junk, ss, xs, pN, hTa[:, :, (i - 1) * 128:i * 128],
                           "hTa%d" % (i - 1), gT, idb, pkey="pN")
    norm_transpose(c, NTI - 1, xres[:, NTI - 1, :], "xres%d" % (NTI - 1), junk, ss, xs, pN, hTa[:, :, (NTI - 1) * 128:NTI * 128],
                   "hTa%d" % (NTI - 1), gT, idb, pkey="pN")
    NG = 4 * D // GS
    CPG = GS // 128
    NTB = TOK // 512

    def wload(g):
        gb = g % 2
        c.load("w1g%d" % gb, w1g[gb][:], w1[:, g * GS:(g + 1) * GS].rearrange("(k p) c -> p k c", p=128), eng="gpsimd")
        c.load("w2g%d" % gb, w2g[gb][:], w2[g * GS:(g + 1) * GS, :].rearrange("(k p) c -> p k c", p=128), eng="gpsimd")

    cnt = [n]

    def stage1(g, tb, ab):
        gb = g % 2
        hkeys = ["hTa%d" % (tb * 4 + j) for j in range(4)]
        for cc in range(CPG):
            pb = cnt[0] % 4
            cnt[0] += 1
            rb = cnt[0] % 2

            def mm1(e, gb=gb, cc=cc, tb=tb, pb=pb):
                for k in range(8):
                    ins = e.matmul(pm[pb][:], lhsT=w1g[gb][:, k, cc * 128:(cc + 1) * 128],
                                   rhs=hTa[:, k, tb * 512:(tb + 1) * 512], start=(k == 0), stop=(k == 7))
                return ins

            c.pe(mm1, r=hkeys + ["w1g%d" % gb], w=["pm%d" % pb])
            c.vec("tensor_scalar", r=["pm%d" % pb], w=["rl%d" % rb], out=rl[rb][:], in0=pm[pb][:], scalar1=0.0,
                  scalar2=None, op0=ALU.max)
            c.act(actb[ab][:, cc, :], rl[rb][:], AF.Square, r=["rl%d" % rb], w=["actb%d_%d" % (ab, cc)])

    def stage2(g, tb, ab):
        gb = g % 2
        akeys = ["actb%d_%d" % (ab, cc) for cc in range(CPG)]
        for tt in range(4):
            ti = tb * 4 + tt
            for cb in range(2):
                pb = cnt[0] % 4
                cnt[0] += 1

                def mm2(e, gb=gb, ab=ab, tt=tt, cb=cb, pb=pb):
                    for cc in range(CPG):
                        ins = e.matmul(pm[pb][:], lhsT=actb[ab][:, cc, tt * 128:(tt + 1) * 128],
                                       rhs=w2g[gb][:, cc, cb * 512:(cb + 1) * 512], start=(cc == 0), stop=(cc == CPG - 1))
                    return ins

                c.pe(mm2, r=akeys + ["w2g%d" % gb], w=["pm%d" % pb])
                c.vec("tensor_tensor", r=["pm%d" % pb, "xres%d" % ti], w=["xres%d" % ti],
                      out=xres[:, ti, cb * 512:(cb + 1) * 512], in0=pm[pb][:], in1=xres[:, ti, cb * 512:(cb + 1) * 512],
                      op=ALU.add)

    items = [(g, tb) for g in range(NG) for tb in range(NTB)]
    wload(0)
    wload(1)
    stage1(items[0][0], items[0][1], 0)
    for i_, (g, tb) in enumerate(items):
        if i_ + 1 < len(items):
            stage1(items[i_ + 1][0], items[i_ + 1][1], (i_ + 1) % 2)
        stage2(g, tb, i_ % 2)
        if tb == NTB - 1 and g + 2 < NG:
            wload(g + 2)
    c.stage_end()


def f_meven(c, idb, cst, y_tm, y_fm, cat_tm, dd, lname):
    mF, mB, segm, sel = cst["mF"], cst["mB"], cst["segm"], cst["sel"]
    ex_in = c.dint("exs_in" + lname, [512, 128]); ex_out = c.dint("exs_out" + lname, [1024, 128])
    c.stage_begin()
    sgug = c.sb("sgug", [128, 512]); c.load_const("sgug", sgug[:], dd["sgug"])
    wsb = c.sb("wsb", [128, 4, 128], BF16); c.load_const("wsb", wsb[:], dd["wsT"], eng="gpsimd")
    bs = c.sb("bs", [128, 4]); c.load_const("bs", bs[:], dd["bs"])
    lbl = c.sb("lbl", [128, 16]); c.load_const("lbl", lbl[:], dd["lbl"])
    lbm = c.sb("lbm", [128, 1]); c.load_const("lbm", lbm[:], dd["lbm"])
    hng = c.sb("hng", [128, 512]); c.load_const("hng", hng[:], dd["hng"])
    c.const_done()
    onesb = c.sb("onesb", [128, 1]); c.vec("memset", r=[], w=["onesb"], ap=onesb[:], constant=1.0)
    epsb = c.sb("epsb", [128, 1]); c.vec("memset", r=[], w=["epsb"], ap=epsb[:], constant=EPS)
    lb8 = c.sb("lb8", [128, 8]); oml8 = c.sb("oml8", [128, 8]); noml8 = c.sb("noml8", [128, 8])
    l3 = lbl[:].rearrange("p (a e) -> p a e", e=2)
    c.vec("tensor_tensor", r=["lbl"], w=["lb8"], out=lb8[:].rearrange("p (a o) -> p a o", o=1), in0=l3[:, :, 1:2],
          in1=l3[:, :, 0:1], op=ALU.subtract)
    c.act(lb8[:], lb8[:], AF.Sigmoid, r=["lb8"], w=["lb8"])
    c.vec("tensor_scalar", r=["lb8", "lbm"], w=["lb8"], out=lb8[:], in0=lb8[:], scalar1=lbm[:, 0:1], scalar2=None, op0=ALU.mult)
    c.vec("tensor_scalar", r=["lb8"], w=["oml8"], out=oml8[:], in0=lb8[:], scalar1=-1.0, scalar2=1.0, op0=ALU.mult, op1=ALU.add)
    c.vec("tensor_scalar", r=["lb8"], w=["noml8"], out=noml8[:], in0=lb8[:], scalar1=1.0, scalar2=-1.0, op0=ALU.mult, op1=ALU.add)
    c.stage_begin()
    pms = [c.ps("pm%d" % i, [128, 512]) for i in range(4)]
    gu = c.sb("gu", [128, NTI, 512]); gvv = c.sb("gvv", [128, NTI, 512])
    vn = [c.sb("vn%d" % i, [128, 512], BF16) for i in range(4)]
    oa = [c.sb("oa%d" % i, [128, 512]) for i in range(4)]
    sj = c.sb("sj", [128, 128]); ssa = c.sb("ssa", [128, NTI * 4])
    HT_ = NTI // 2
    for q_ in range(2):
        rows = slice(q_ * HT_ * 128, (q_ + 1) * HT_ * 128)
        c.P.dma("sync", c.slot("gvv_h%d" % q_), gvv[:, q_ * HT_:(q_ + 1) * HT_, :],
                y_tm[rows, 512:1024].rearrange("(t p) c -> p t c", p=128), w=["gvv%d" % n for n in range(q_ * HT_, (q_ + 1) * HT_)])
        c.P.dma("sync", c.slot("gu_h%d" % q_), gu[:, q_ * HT_:(q_ + 1) * HT_, :],
                y_tm[rows, 0:512].rearrange("(t p) c -> p t c", p=128), w=["gu%d" % n for n in range(q_ * HT_, (q_ + 1) * HT_)])
    for n in range(NTI):
        c.act(gvv[:, n, :], gvv[:, n, :], AF.Gelu_apprx_tanh, r=["gvv%d" % n], w=["gvv%d" % n])
        c.act(gu[:, n, :], gu[:, n, :], AF.Gelu_apprx_tanh, r=["gu%d" % n], w=["gu%d" % n])
        for h in range(4):
            c.act(sj[:], gvv[:, n, h * 128:(h + 1) * 128], AF.Square, r=["gvv%d" % n], w=["sj", "ssa%d" % n],
                  accum_out=ssa[:, n * 4 + h:n * 4 + h + 1])
    c.vec("tensor_scalar", r=["ssa%d" % n for n in range(NTI)], w=["ssa"], out=ssa[:], in0=ssa[:], scalar1=1.0 / 128, scalar2=EPS,
          op0=ALU.mult, op1=ALU.add)
    c.act(ssa[:], ssa[:], AF.Sqrt, r=["ssa"], w=["ssa"])
    c.vec("reciprocal", r=["ssa"], w=["ssa"], out=ssa[:], in_=ssa[:])
    for n in range(NTI):
        b = n % 4
        nk, ok, pk = "vn%d" % b, "oa%d" % b, "pm%d" % b
        for h in range(4):
            hs = slice(h * 128, (h + 1) * 128)
            c.vec("scalar_tensor_tensor", r=["gvv%d" % n, "ssa", "sgug"], w=[nk], out=vn[b][:, hs], in0=gvv[:, n, hs],
                  scalar=ssa[:, n * 4 + h:n * 4 + h + 1], in1=sgug[:, hs], op0=ALU.mult, op1=ALU.mult)

        def mm(e, b=b):
            for h in range(4):
                ins = e.matmul(pms[b][:, h * 128:(h + 1) * 128], lhsT=wsb[:, h, :], rhs=vn[b][:, h * 128:(h + 1) * 128],
                               start=True, stop=True)
            return ins

        c.pe(mm, r=[nk, "wsb"], w=[pk])
        for h in range(4):
            hs = slice(h * 128, (h + 1) * 128)
            c.vec("scalar_tensor_tensor", r=[pk, "bs", "gu%d" % n], w=[ok], out=oa[b][:, hs], in0=pms[b][:, hs],
                  scalar=bs[:, h:h + 1], in1=gu[:, n, hs], op0=ALU.add, op1=ALU.mult)
        c.P.dma("sync", c.slot("st_" + ok), cat_tm[n * 128:(n + 1) * 128, 0:512], oa[b][:], r=[ok])
    c.stage_end()
    c.stage_begin()
    NCH = 2
    pkt = [c.ps("pkt%d" % s, [128, 1024], BF16) for s in range(NCH)]
    pa = [c.ps("pa%d" % s, [128, 512]) for s in range(NCH)]
    po = [c.ps("po%d" % s, [128, 512]) for s in range(NCH)]
    pu = [c.ps("pu%d" % s, [128, 512]) for s in range(NCH)]
    OF = c.sb("OF", [128, 4 * NTI, 128])
    B_ = []
    for s in range(NCH):
        d = {}
        d["A"] = [c.sb("A%d_%d" % (s, i), [128, 512]) for i in range(2)]
        d["Q"] = [c.sb("Q%d_%d" % (s, i), [128, 512]) for i in range(2)]
        d["IV"] = [c.sb("IV%d_%d" % (s, i), [128, 4, 128], BF16) for i in range(2)]
        d["GG"] = [c.sb("GG%d_%d" % (s, i), [128, 512]) for i in range(2)]
        for nm in ("L", "KK", "BFW", "BB", "BR", "EQ", "RC"):
            d[nm] = c.sb("%s%d" % (nm, s), [128, 512])
        for nm in ("QE", "KD", "QI", "kdT"):
            d[nm] = [c.sb("%s%d_%d" % (nm, s, i), [128, 512], BF16) for i in range(2)]
        d["KDEC"] = c.sb("KDEC%d" % s, [128, 512], BF16)
        d["KDZ"] = [[c.sb("KDZ%d_%d_%d" % (s, i, j), [128, 512], BF16) for j in range(2)] for i in range(2)]
        d["dS"] = [c.sb("dS%d_%d" % (s, i), [128, 4]) for i in range(2)]; d["nref"] = c.sb("nref%d" % s, [128, 4])
        d["attT"] = c.sb("attT%d" % s, [128, 128], BF16)
        d["S32"] = c.sb("S32_%d" % s, [128, 128]); d["Sbf"] = c.sb("Sbf%d" % s, [128, 128], BF16)
        d["SG"] = c.sb("SG%d" % s, [128, 2, 128])
        d["osum"] = c.sb("osum%d" % s, [128, 128]); d["obst"] = [c.sb("obst%d_%d" % (s, i), [128, 128]) for i in range(2)]
        d["sj2"] = c.sb("sj2_%d" % s, [128, 128]); d["ssh"] = c.sb("ssh%d" % s, [128, 4])
        for i in range(2):
            for j in range(2):
                c.vec("memset", r=[], w=["KDZ%d_c%d" % (j, s)], ap=d["KDZ"][i][j][:], constant=0.0)
        B_.append(d)

    def chain(s, hh, dr):
        d = B_[s]
        K = lambda n: "%s_c%d" % (n, s)
        L, KK, BFW, BB, BR, EQ, RC = d["L"], d["KK"], d["BFW"], d["BB"], d["BR"], d["EQ"], d["RC"]
        KDEC = d["KDEC"]
        attT, S32, Sbf, SG = d["attT"], d["S32"], d["Sbf"], d["SG"]
        osum, obst, sj2, ssh = d["osum"], d["obst"], d["sj2"], d["ssh"]
        col = dr * 4 + hh
        lbc, omlc = lb8[:, col:col + 1], oml8[:, col:col + 1]
        if dr == 0:
            c.vec("memset", r=[], w=[K("S32")], ap=S32[:], constant=0.0)
            c.vec("memset", r=[], w=[K("Sbf")], ap=Sbf[:], constant=0.0)
        else:
            c.P.dma("sync", c.slot("SG%d" % s), SG[:], ex_out.rearrange("(r h p) v -> p r h v", r=2, h=4)[:, :, hh, :],
                    r=["ex_out"], w=[K("SG")])
            c.vec("tensor_scalar", r=[K("SG"), "sel"], w=[K("S32")], out=S32[:], in0=SG[:, 0, :], scalar1=sel[:, 0:1], scalar2=None,
                  op0=ALU.mult)
            c.vec("scalar_tensor_tensor", r=[K("SG"), "sel", K("S32")], w=[K("S32")], out=S32[:], in0=SG[:, 1, :], scalar=sel[:, 1:2],
                  in1=S32[:], op0=ALU.mult, op1=ALU.add)
            c.act(Sbf[:], S32[:], AF.Copy, r=[K("S32")], w=[K("Sbf")])
        yield
        tbs = list(range(TOK // 512)) if dr == 0 else list(range(TOK // 512 - 1, -1, -1))
        nob = [0]

        def pre(ntb, tb):
            p2 = ntb % 2
            P2 = lambda n: K("%s%d" % (n, p2))
            ak, qk, ik, gk = P2("A"), P2("Q"), P2("IV"), P2("GG")
            a_, q_, IVt, GGt = d["A"][p2], d["Q"][p2], d["IV"][p2], d["GG"][p2]
            QE, KD, QI, kdT, dS, KDZ = d["QE"][p2], d["KD"][p2], d["QI"][p2], d["kdT"][p2], d["dS"][p2], d["KDZ"][dr][p2]
            tbs_ = slice(tb * 512, (tb + 1) * 512)
            c.load(ak, a_[:], y_fm[512 + dr * 512 + hh * 128:512 + dr * 512 + (hh + 1) * 128, tbs_])
            c.load(qk, q_[:], y_fm[hh * 128:(hh + 1) * 128, tbs_])
            c.load(ik, IVt[:], y_tm[tbs_, 1024 + hh * 128:1024 + (hh + 1) * 128].rearrange("(t p) c -> p t c", p=128), eng="gpsimd")
            if dr == 1:
                c.load(gk, v3(GGt), y_tm[tbs_, 1536 + hh * 128:1536 + (hh + 1) * 128].rearrange("(t p) c -> p t c", p=128))
                c.act(RC[:], GGt[:], AF.Exp, r=[gk], w=[K("RC")], scale=-1.0)
                c.act(RC[:], RC[:], AF.Ln, r=[K("RC"), "onesb"], w=[K("RC")], bias=onesb[:, 0:1])
                c.act(RC[:], RC[:], AF.Exp, r=[K("RC")], w=[K("RC")], scale=-1.0)
                yield
                c.vec("tensor_tensor", r=[K("RC"), gk], w=[gk], out=GGt[:], in0=GGt[:], in1=RC[:], op=ALU.mult)
                yield
            c.act(a_[:], a_[:], AF.Exp, r=[ak], w=[ak], scale=-1.0)
            yield
            c.act(L[:], a_[:], AF.Ln, r=[ak, "lb8", "onesb"], w=[K("L")], scale=lbc, bias=onesb[:, 0:1])
            c.act(RC[:], a_[:], AF.Ln, r=[ak, "onesb"], w=[K("RC")], bias=onesb[:, 0:1])
            yield
            c.vec("tensor_tensor", r=[K("L"), K("RC")], w=[K("L")], out=L[:], in0=L[:], in1=RC[:], op=ALU.subtract)
            c.act(RC[:], RC[:], AF.Exp, r=[K("RC"), K("L")], w=[K("RC")], scale=-1.0)
            yield
            c.vec("scalar_tensor_tensor", r=[ak, "oml8", K("RC")], w=[K("KK")], out=KK[:], in0=a_[:], scalar=omlc, in1=RC[:],
                  op0=ALU.mult, op1=ALU.mult)
            yield
            c.vec("tensor_tensor_scan", r=[K("L"), "segm"], w=[K("BFW")], out=BFW[:], data0=segm[:], data1=L[:], initial=0.0,
                  op0=ALU.mult, op1=ALU.add)
            yield
            if dr == 0:
                Bt, bkey, ri, li = BFW, K("BFW"), 63, 127
            else:
                c.vec("tensor_tensor", r=[K("L"), K("BFW")], w=[K("L")], out=L[:], in0=L[:], in1=BFW[:], op=ALU.subtract)
                c.vec("tensor_tensor", r=[K("L"), K("BFW")], w=[K("BB")], out=v3(BB), in0=v3(L),
                      in1=v3(BFW)[:, :, 127:128].to_broadcast([128, 4, 128]), op=ALU.add)
                Bt, bkey, ri, li = BB, K("BB"), 64, 0
            B3 = v3(Bt)
            c.vec("tensor_scalar", r=[bkey], w=[K("nref")], out=d["nref"][:].rearrange("p (t o) -> p t o", o=1),
                  in0=B3[:, :, ri:ri + 1], scalar1=-1.0, scalar2=None, op0=ALU.mult)
            yield
            for t in range(4):
                c.act(EQ[:, t * 128:(t + 1) * 128], Bt[:, t * 128:(t + 1) * 128], AF.Exp, r=[bkey, K("nref")], w=[K("EQ")],
                      bias=d["nref"][:, t:t + 1])
            yield
            c.vec("tensor_tensor", r=[qk, K("EQ")], w=[P2("QE")], out=QE[:], in0=q_[:], in1=EQ[:], op=ALU.mult)
            for t in range(4):
                c.act(BR[:, t * 128:(t + 1) * 128], Bt[:, t * 128:(t + 1) * 128], AF.Exp, r=[bkey], w=[K("BR")], scale=-1.0,
                      bias=B3[:, t, ri:ri + 1])
            yield
            c.vec("tensor_tensor", r=[K("KK"), K("BR")], w=[P2("KD")], out=KD[:], in0=KK[:], in1=BR[:], op=ALU.mult)
            hsl = slice(0, 64) if dr == 0 else slice(64, 128)
            c.vec("tensor_tensor", r=[K("KK"), K("BR")], w=[P2("KDZ")], out=v3(KDZ)[:, :, hsl], in0=v3(KK)[:, :, hsl],
                  in1=v3(BR)[:, :, hsl], op=ALU.mult)
            c.act(EQ[:], Bt[:], AF.Exp, r=[bkey, P2("QE")], w=[K("EQ")])
            yield
            c.vec("tensor_tensor", r=[qk, K("EQ")], w=[P2("QI")], out=QI[:], in0=q_[:], in1=EQ[:], op=ALU.mult)
            for t in range(4):
                c.act(BR[:, t * 128:(t + 1) * 128], Bt[:, t * 128:(t + 1) * 128], AF.Exp, r=[bkey, P2("KD"), P2("KDZ")],
                      w=[K("BR")], scale=-1.0, bias=B3[:, t, li:li + 1])
            c.act(dS[:].rearrange("p (t o) -> p t o", o=1), B3[:, :, li:li + 1], AF.Exp, r=[bkey], w=[P2("dS")])
            yield
            c.vec("tensor_tensor", r=[K("KK"), K("BR")], w=[K("KDEC")], out=KDEC[:], in0=KK[:], in1=BR[:], op=ALU.mult)
            yield

            def trk(e):
                for t in range(4):
                    ins = e.transpose(pkt[s][:, t * 128:(t + 1) * 128], KDEC[:, t * 128:(t + 1) * 128], idb[:])
                return ins

            c.pe(trk, r=[K("KDEC"), "idb"], w=[K("pkt")])
            yield
            c.act(kdT[:], pkt[s][:, 0:512], AF.Copy, r=[K("pkt")], w=[P2("kdT")])
            yield

        def tiles(ntb, tb):
            p2 = ntb % 2
            P2 = lambda n: K("%s%d" % (n, p2))
            ik, gk = P2("IV"), P2("GG")
            IVt, GGt = d["IV"][p2], d["GG"][p2]
            QE, KD, QI, kdT, dS, KDZ = d["QE"][p2], d["KD"][p2], d["QI"][p2], d["kdT"][p2], d["dS"][p2], d["KDZ"][dr][p2]
            tts = range(4) if dr == 0 else range(3, -1, -1)
            for tt in tts:
                ti = tb * 4 + tt
                oi = hh * NTI + ti
                kA, kB = (KDZ, KD) if dr == 0 else (KD, KDZ)

                def mma(e, tt=tt, kA=kA, kB=kB, QE=QE):
                    e.matmul(pa[s][:, 0:64], lhsT=kA[:, tt * 128:(tt + 1) * 128], rhs=QE[:, tt * 128:tt * 128 + 64],
                             start=True, stop=True)
                    return e.matmul(pa[s][:, 64:128], lhsT=kB[:, tt * 128:(tt + 1) * 128],
                                    rhs=QE[:, tt * 128 + 64:(tt + 1) * 128], start=True, stop=True)

                c.pe(mma, r=[P2("KD"), P2("KDZ"), P2("QE")], w=[K("pa")])
                c.pe(lambda e, tt=tt, IVt=IVt, kdT=kdT: e.matmul(pu[s][:, 0:128], lhsT=kdT[:, tt * 128:(tt + 1) * 128],
                                                                 rhs=IVt[:, tt, :], start=True, stop=True),
                     r=[P2("kdT"), ik], w=[K("pu")])
                yield
                c.vec("tensor_tensor", r=[K("pa"), "mF", "mB"], w=[K("attT")], out=attT[:], in0=pa[s][:, 0:128],
                      in1=(mF if dr == 0 else mB)[:], op=ALU.mult)
                yield

                def mmo(e, tt=tt, IVt=IVt, QI=QI):
                    e.matmul(po[s][:, 0:128], lhsT=QI[:, tt * 128:(tt + 1) * 128], rhs=Sbf[:], start=True, stop=False)
                    return e.matmul(po[s][:, 0:128], lhsT=attT[:], rhs=IVt[:, tt, :], start=False, stop=True)

                c.pe(mmo, r=[P2("QI"), K("Sbf"), K("attT"), ik], w=[K("po")])
                yield
                c.vec("scalar_tensor_tensor", r=[K("S32"), P2("dS"), K("pu")], w=[K("S32")], out=S32[:], in0=S32[:],
                      scalar=dS[:, tt:tt + 1], in1=pu[s][:, 0:128], op0=ALU.mult, op1=ALU.add)
                yield
                c.act(Sbf[:], S32[:], AF.Copy, r=[K("S32")], w=[K("Sbf")])
                if dr == 0:
                    c.act(OF[:, oi, :], po[s][:, 0:128], AF.Copy, r=[K("po")], w=["OF%d" % oi])
                    yield
                else:
                    ob_ = nob[0] % 2
                    nob[0] += 1
                    c.vec("tensor_tensor", r=[K("po"), "OF%d" % oi], w=[K("osum")], out=osum[:],
                          in0=po[s][:, 0:128], in1=OF[:, oi, :], op=ALU.add)
                    yield
                    c.act(sj2[:], osum[:], AF.Square, r=[K("osum")], w=[K("sj2"), K("ssh")], accum_out=ssh[:, 0:1])
                    yield
                    c.act(ssh[:, 0:1], ssh[:, 0:1], AF.Ln, r=[K("ssh"), "epsb"], w=[K("ssh")], scale=1.0 / 128, bias=epsb[:, 0:1])
                    c.act(ssh[:, 0:1], ssh[:, 0:1], AF.Exp, r=[K("ssh")], w=[K("ssh")], scale=-0.5)
                    yield
                    c.vec("scalar_tensor_tensor", r=[K("osum"), K("ssh"), "hng"], w=[K("osum")], out=osum[:],
                          in0=osum[:], scalar=ssh[:, 0:1], in1=hng[:, hh * 128:(hh + 1) * 128], op0=ALU.mult, op1=ALU.mult)
                    c.vec("tensor_tensor", r=[K("osum"), gk], w=[K("obst%d" % ob_)], out=obst[ob_][:], in0=osum[:],
                          in1=GGt[:, tt * 128:(tt + 1) * 128], op=ALU.mult)
                    c.P.dma("sync", c.slot("st_obst%d_%d" % (s, ob_)), cat_tm[ti * 128:(ti + 1) * 128, 512 + hh * 128:512 + (hh + 1) * 128],
                            obst[ob_][:], r=[K("obst%d" % ob_)])
                    yield

        for _ in pre(0, tbs[0]):
            yield
        for n_ in range(len(tbs)):
            subs = [tiles(n_, tbs[n_])]
            if n_ + 1 < len(tbs):
                subs.append(pre(n_ + 1, tbs[n_ + 1]))
            live = [True] * len(subs)
            while any(live):
                for k_ in range(len(subs)):
                    if live[k_]:
                        try:
                            next(subs[k_])
                            yield
                        except StopIteration:
                            live[k_] = False
        if dr == 0:
            c.P.dma("sync", c.slot("st_S32_%d" % s), ex_in[hh * 128:(hh + 1) * 128, :], S32[:], r=[K("S32")], w=["ex_in%d" % hh])
        yield

    for dr in range(2):
        if dr == 1:
            slot_ag = c.P.slot()

            def agfn(e, slot_ag=slot_ag):
                return e.collective_compute("AllGather", ALU.bypass, replica_groups=GROUPS, ins=[ex_in], outs=[ex_out]).then_inc(slot_ag.sem)

            c.P.add("gpsimd", agfn, r=["ex_in%d" % h for h in range(4)], w=["ex_out"], slot=slot_ag, dval=1)
        for h0 in range(0, 4, NCH):
            gens = [chain(s, h0 + s, dr) for s in range(NCH)]
            alive = [True] * NCH
            while any(alive):
                for s in range(NCH):
                    if alive[s]:
                        try:
                            next(gens[s])
                        except StopIteration:
                            alive[s] = False
    c.stage_end()
    c.stage_end()


def f_modd(c, idb, cst, y_tm, y_fm, cat_tm, cat_fm, dd, lname):
    sel = cst["sel"]
    zx_in = c.dint("zx_in" + lname, [128, 4]); zx_out = c.dint("zx_out" + lname, [256, 4])
    kx_in = c.dint("kx_in" + lname, [128, 4 * TOK], BF16); kx_out = c.dint("kx_out" + lname, [256, 4 * TOK], BF16)
    vx_in = c.dint("vx_in" + lname, [TOK, 512], BF16); vx_out = c.dint("vx_out" + lname, [2 * TOK, 512], BF16)
    NKT = 2 * NTI
    c.stage_begin()
    cw = c.sb("cw", [128, 12]); c.load_const("cw", cw[:], dd["cw"])
    gv = c.sb("gv", [128, 4]); c.load_const("gv", gv[:], dd["gv"])
    cosT = c.sb("cosT", [128, TOK]); c.load_const("cosT", cosT[:], cst["cosT"])
    sinT = c.sb("sinT", [128, TOK]); c.load_const("sinT", sinT[:], cst["sinT"])
    lamv = c.sb("lamv", [128, 4, 64]); c.load_const("lamv", lamv[:], dd["lamv"])
    lconst = c.sb("lconst", [128, 2]); c.load_const("lconst", lconst[:], dd["lconst"])
    subg = c.sb("subg", [128, 128]); c.load_const("subg", subg[:], dd["subg"])
    bones = c.sb("bones", [128, 128]); c.load_const("bones", bones[:], cst["bones"])
    c.const_done()
    epsb = c.sb("epsb", [128, 1]); c.vec("memset", r=[], w=["epsb"], ap=epsb[:], constant=EPS)
    lj = c.sb("lj", [128, 64]); ls = c.sb("ls", [128, 2]); nlam = c.sb("nlam", [128, 1])
    for j in range(2):
        c.vec("tensor_tensor", r=["lamv"], w=["lj"], out=lj[:], in0=lamv[:, 2 * j, :], in1=lamv[:, 2 * j + 1, :], op=ALU.mult)
        c.vec("tensor_reduce", r=["lj"], w=["ls%d" % j], out=ls[:, j:j + 1], in_=lj[:], axis=AX.X, op=ALU.add)
    c.act(ls[:], ls[:], AF.Exp, r=["ls0", "ls1"], w=["ls"])
    c.vec("tensor_tensor", r=["ls"], w=["nlam"], out=nlam[:], in0=ls[:, 1:2], in1=ls[:, 0:1], op=ALU.subtract)
    c.vec("tensor_tensor", r=["nlam", "lconst"], w=["nlam"], out=nlam[:], in0=nlam[:], in1=lconst[:, 0:1], op=ALU.subtract)
    c.vec("tensor_scalar", r=["subg", "lconst"], w=["subg"], out=subg[:], in0=subg[:], scalar1=lconst[:, 1:2], scalar2=None,
          op0=ALU.mult)
    z4 = c.sb("z4", [128, 4, TOK + 2])
    xt = [c.sb("axt%d" % i, [128, 512]) for i in range(2)]
    xp = [c.sb("axp%d" % i, [128, 512]) for i in range(2)]
    sq = c.sb("sq", [128, 512]); rr = c.sb("rr", [128, 512]); t1 = c.sb("t1", [128, 512]); t2 = c.sb("t2", [128, 512])
    pss = c.ps("pss", [128, 512])
    pS = [c.ps("pS%d" % i, [128, 1024]) for i in range(2)]
    pacc = c.ps("pacc", [128, 3, 512])
    nbc = [0]

    def qk_prep(rows0, rowsp0, hh, dst, dkey, gi, nblk):
        for blk in range(nblk):
            b = nbc[0] % 2
            nbc[0] += 1
            bs_ = slice(blk * 512, (blk + 1) * 512)
            xk, pk = "axt%d" % b, "axp%d" % b
            c.load(xk, xt[b][:], y_fm[rows0 + hh * 128:rows0 + (hh + 1) * 128, bs_])
            c.load(pk, xp[b][:], y_fm[rowsp0 + hh * 128:rowsp0 + (hh + 1) * 128, bs_])
            c.act(sq[:], xt[b][:], AF.Square, r=[xk], w=["sq"])
            c.pe(lambda e: e.matmul(pss[:], lhsT=bones[:], rhs=sq[:], start=True, stop=True), r=["sq", "bones"], w=["pss"])
            c.act(rr[:], pss[:], AF.Ln, r=["pss", "epsb"], w=["rr"], scale=1.0 / 64, bias=epsb[:, 0:1])
            c.act(rr[:], rr[:], AF.Exp, r=["rr"], w=["rr"], scale=-0.5)
            c.vec("scalar_tensor_tensor", r=[xk, "gv", "cosT"], w=["t1"], out=t1[:], in0=xt[b][:], scalar=gv[:, gi:gi + 1],
                  in1=cosT[:, bs_], op0=ALU.mult, op1=ALU.mult)
            c.vec("scalar_tensor_tensor", r=[pk, "gv", "sinT"], w=["t2"], out=t2[:], in0=xp[b][:], scalar=gv[:, gi + 1:gi + 2],
                  in1=sinT[:, bs_], op0=ALU.mult, op1=ALU.mult)
            c.vec("tensor_tensor", r=["t1", "t2"], w=["t1"], out=t1[:], in0=t1[:], in1=t2[:], op=ALU.add)
            c.vec("tensor_tensor", r=["t1", "rr"], w=[dkey], out=dst(blk), in0=t1[:], in1=rr[:], op=ALU.mult)

    c.stage_begin()
    hin = c.sb("hin", [128, TOK]); cg = c.sb("cg", [128, TOK])
    Kown = c.sb("Kown", [128, 4, TOK], BF16)
    vbf = c.sb("vbf", [128, NTI, 512], BF16)
    c.vec("memset", r=[], w=["z4"], ap=z4[:], constant=0.0)
    for g in range(4):
        c.load("hin", hin[:], y_fm[g * 128:(g + 1) * 128, :])
        c.load("cg", cg[:], y_fm[1024 + g * 128:1024 + (g + 1) * 128, :])
        c.vec("tensor_tensor", r=["hin", "cg", "z4"], w=["z4_%d" % g], out=z4[:, g, 1:TOK + 1], in0=cg[:], in1=hin[:], op=ALU.mult)
    zc = c.sb("zc", [128, 4])
    c.vec("tensor_copy", r=["z4_%d" % g for g in range(4)], w=["zc"], out=zc[:], in_=z4[:, :, TOK])
    c.P.dma("sync", c.slot("st_zx"), zx_in, zc[:], r=["zc"], w=["zx_in"])
    allgather(c, zx_in, zx_out, "zx_in", "zx_out")
    c.load("vbf", vbf[:], y_tm[:, 0:512].rearrange("(t p) c -> p t c", p=128), eng="gpsimd")
    c.P.dma("sync", c.slot("st_vx"), vx_in.rearrange("(t p) c -> p t c", p=128), vbf[:], r=["vbf"], w=["vx_in"])
    allgather(c, vx_in, vx_out, "vx_in", "vx_out")
    for hh in range(4):
        qk_prep(2048, 3072, hh, lambda blk, hh=hh: Kown[:, hh, blk * 512:(blk + 1) * 512], "Kown", 2, TOK // 512)
    c.P.dma("sync", c.slot("st_kx"), kx_in, Kown[:].rearrange("p h t -> p (h t)"), r=["Kown"], w=["kx_in"])
    allgather(c, kx_in, kx_out, "kx_in", "kx_out")
    c.stage_end()
    c.stage_begin()
    bg = c.sb("bg", [128, TOK]); yy = c.sb("yy", [128, TOK]); ZG = c.sb("ZG", [128, 2, 4]); zh = c.sb("zh", [128, 4])
    c.P.dma("sync", c.slot("ZG"), ZG[:], zx_out.rearrange("(r p) g -> p r g", r=2), r=["zx_out"], w=["ZG"])
    c.vec("tensor_scalar", r=["ZG", "sel"], w=["zh"], out=zh[:], in0=ZG[:, 0, :], scalar1=sel[:, 0:1], scalar2=None, op0=ALU.mult)
    c.vec("scalar_tensor_tensor", r=["ZG", "sel", "zh"], w=["zh"], out=zh[:], in0=ZG[:, 1, :], scalar=sel[:, 1:2], in1=zh[:],
          op0=ALU.mult, op1=ALU.add)
    c.vec("tensor_copy", r=["zh"], w=["z4h"], out=z4[:, :, TOK + 1], in_=zh[:])
    for g in range(4):
        c.load("bg", bg[:], y_fm[512 + g * 128:512 + (g + 1) * 128, :])
        c.vec("tensor_scalar", r=["z4h", "cw", "yy"], w=["yy"], out=yy[:], in0=z4[:, g, 1:TOK + 1], scalar1=cw[:, 3 * g + 1:3 * g + 2],
              scalar2=None, op0=ALU.mult)
        c.vec("scalar_tensor_tensor", r=["cw", "yy"], w=["yy"], out=yy[:], in0=z4[:, g, 0:TOK], scalar=cw[:, 3 * g:3 * g + 1],
              in1=yy[:], op0=ALU.mult, op1=ALU.add)
        c.vec("scalar_tensor_tensor", r=["cw", "yy"], w=["yy"], out=yy[:], in0=z4[:, g, 2:TOK + 2],
              scalar=cw[:, 3 * g + 2:3 * g + 3], in1=yy[:], op0=ALU.mult, op1=ALU.add)
        c.vec("tensor_tensor", r=["yy", "bg"], w=["yy"], out=yy[:], in0=yy[:], in1=bg[:], op=ALU.mult)
        c.P.dma("sync", c.slot("st_yy"), cat_fm[g * 128:(g + 1) * 128, :], yy[:], r=["yy"])
    c.stage_end()
    c.stage_begin()
    Kr = [c.sb("Kr%d" % i, [128, 2 * TOK], BF16) for i in range(2)]
    Qr = [c.sb("Qr%d" % i, [128, TOK], BF16) for i in range(2)]
    Vaug = [c.sb("Vaug%d" % i, [128, NKT, 132], BF16) for i in range(2)]
    pT = [c.sb("pTs%d" % i, [128, 1024], BF16) for i in range(2)]
    rc = c.sb("rc", [128, 8]); o1 = [c.sb("o1_%d" % i, [128, 128]) for i in range(4)]
    odt = [c.sb("odt%d" % i, [128, 128]) for i in range(2)]
    sj = c.sb("sj", [128, 128]); ssd = c.sb("ssd", [128, 4])

    def head_loads(hh):
        hp = hh % 2
        for r_ in range(2):
            c.P.dma("sync", c.slot("Kr%d_%d" % (hp, r_)), Kr[hp][:, r_ * TOK:(r_ + 1) * TOK],
                    kx_out[r_ * 128:(r_ + 1) * 128, hh * TOK:(hh + 1) * TOK], r=["kx_out"], w=["Kr%d_%d" % (hp, r_)])
        c.P.dma("sync", c.slot("Vaug%d" % hp), Vaug[hp][:, :, 0:128],
                vx_out[:, hh * 128:(hh + 1) * 128].rearrange("(t p) c -> p t c", p=128), r=["vx_out"], w=["Vaug%d" % hp])
        c.vec("memset", r=[], w=["Vones%d" % hp], ap=Vaug[hp][:, :, 128:129], constant=1.0)

    def q_block(hh, blk):
        hp = hh % 2
        b = nbc[0] % 2
        nbc[0] += 1
        bs_ = slice(blk * 512, (blk + 1) * 512)
        xk, pk = "axt%d" % b, "axp%d" % b
        c.load(xk, xt[b][:], y_fm[1536 + hh * 128:1536 + (hh + 1) * 128, bs_])
        c.load(pk, xp[b][:], y_fm[2560 + hh * 128:2560 + (hh + 1) * 128, bs_])
        c.vec("tensor_tensor", r=[xk], w=["sq"], out=sq[:], in0=xt[b][:], in1=xt[b][:], op=ALU.mult)
        c.pe(lambda e: e.matmul(pss[:], lhsT=bones[:], rhs=sq[:], start=True, stop=True), r=["sq", "bones"], w=["pss"])
        c.act(rr[:], pss[:], AF.Ln, r=["pss", "epsb"], w=["rr"], scale=1.0 / 64, bias=epsb[:, 0:1])
        c.act(rr[:], rr[:], AF.Exp, r=["rr"], w=["rr"], scale=-0.5)
        c.vec("scalar_tensor_tensor", r=[xk, "gv", "cosT"], w=["t1"], out=t1[:], in0=xt[b][:], scalar=gv[:, 0:1],
              in1=cosT[:, bs_], op0=ALU.mult, op1=ALU.mult)
        c.vec("scalar_tensor_tensor", r=[pk, "gv", "sinT"], w=["t2"], out=t2[:], in0=xp[b][:], scalar=gv[:, 1:2],
              in1=sinT[:, bs_], op0=ALU.mult, op1=ALU.mult)
        c.vec("tensor_tensor", r=["t1", "t2"], w=["t1"], out=t1[:], in0=t1[:], in1=t2[:], op=ALU.add)
        c.vec("tensor_tensor", r=["t1", "rr"], w=["Qr%d_%d" % (hp, blk)], out=Qr[hp][:, bs_], in0=t1[:], in1=rr[:], op=ALU.mult)

    def sc_exp(hh, qb, kt, sb_):
        hp = hh % 2

        def mms(e):
            e.matmul(pS[sb_][:, 0:512], lhsT=Kr[hp][0:64, kt * 128:(kt + 1) * 128], rhs=Qr[hp][0:64, qb * 512:(qb + 1) * 512],
                     start=True, stop=True)
            return e.matmul(pS[sb_][:, 512:1024], lhsT=Kr[hp][64:128, kt * 128:(kt + 1) * 128],
                            rhs=Qr[hp][64:128, qb * 512:(qb + 1) * 512], start=True, stop=True)

        c.pe(mms, r=["Kr%d_%d" % (hp, kt // NTI), "Qr%d_%d" % (hp, qb)], w=["pS%d" % sb_])
        c.act(pT[sb_][:], pS[sb_][:], AF.Exp, r=["pS%d" % sb_], w=["pTs%d" % sb_], scale=0.125)

    def epi1():
        for a in range(8):
            bank, off = a // 3, (a % 3) * 132
            c.vec("reciprocal", r=["pacc"], w=["rc%d" % a], out=rc[:, a:a + 1], in_=pacc[:, bank, off + 128:off + 129])
        c.vec("tensor_scalar", r=["rc%d" % a for a in range(4, 8)] + ["nlam"], w=["rc%d" % a for a in range(4, 8)],
              out=rc[:, 4:8], in0=rc[:, 4:8], scalar1=nlam[:, 0:1], scalar2=None, op0=ALU.mult)
        for qs in range(4):
            a1, a2 = qs, 4 + qs
            c.vec("tensor_scalar", r=["pacc", "rc%d" % a1], w=["o1_%d" % qs], out=o1[qs][:],
                  in0=pacc[:, a1 // 3, (a1 % 3) * 132:(a1 % 3) * 132 + 128], scalar1=rc[:, a1:a1 + 1], scalar2=None, op0=ALU.mult)
            c.vec("scalar_tensor_tensor", r=["pacc", "rc%d" % a2, "o1_%d" % qs], w=["o1_%d" % qs], out=o1[qs][:],
                  in0=pacc[:, a2 // 3, (a2 % 3) * 132:(a2 % 3) * 132 + 128], scalar=rc[:, a2:a2 + 1], in1=o1[qs][:],
                  op0=ALU.mult, op1=ALU.add)

    def epi2(hh, qb, qs):
        e_ = qs % 2
        c.vec("tensor_tensor", r=["o1_%d" % qs], w=["sj"], out=sj[:], in0=o1[qs][:], in1=o1[qs][:], op=ALU.mult)
        c.vec("tensor_reduce", r=["sj"], w=["ssd"], out=ssd[:, 0:1], in_=sj[:], axis=AX.X, op=ALU.add)
        c.act(ssd[:, 0:1], ssd[:, 0:1], AF.Ln, r=["ssd", "epsb"], w=["ssd"], scale=1.0 / 128, bias=epsb[:, 0:1])
        c.act(ssd[:, 0:1], ssd[:, 0:1], AF.Exp, r=["ssd"], w=["ssd"], scale=-0.5)
        c.vec("scalar_tensor_tensor", r=["o1_%d" % qs, "ssd", "subg"], w=["odt%d" % e_], out=odt[e_][:], in0=o1[qs][:],
              scalar=ssd[:, 0:1], in1=subg[:], op0=ALU.mult, op1=ALU.mult)
        ti = qb * 4 + qs
        c.P.dma("sync", c.slot("st_odt%d" % e_), cat_tm[ti * 128:(ti + 1) * 128, 512 + hh * 128:512 + (hh + 1) * 128],
                odt[e_][:], r=["odt%d" % e_])

    its = [(hh, qb, kt) for hh in range(4) for qb in range(TOK // 512) for kt in range(NKT)]
    pending = {}
    head_loads(0)
    for blk in range(TOK // 512):
        q_block(0, blk)
    sc_exp(its[0][0], its[0][1], its[0][2], 0)
    for ii, (hh, qb, kt) in enumerate(its):
        sb_ = ii % 2
        if qb == 0 and kt == 0 and hh + 1 < 4:
            head_loads(hh + 1)
            for blk in range(TOK // 512):
                pending.setdefault(ii + 16 + 24 * blk, []).append(lambda hh=hh, blk=blk: q_block(hh + 1, blk))
        if ii + 1 < len(its):
            sc_exp(its[ii + 1][0], its[ii + 1][1], its[ii + 1][2], (ii + 1) % 2)

        def mmv(e, sb_=sb_, kt=kt, hp=hh % 2):
            for a in range(8):
                bank, off = a // 3, (a % 3) * 132
                ins = e.matmul(pacc[:, bank, off:off + 129], lhsT=pT[sb_][:, a * 128:(a + 1) * 128], rhs=Vaug[hp][:, kt, 0:129],
                               start=(kt == 0 and a % 3 == 0), stop=(kt == NKT - 1), skip_group_check=True)
            return ins

        c.pe(mmv, r=["pTs%d" % sb_, "Vaug%d" % (hh % 2), "Vones%d" % (hh % 2)], w=["pacc"])
        if kt == NKT - 1:
            epi1()
            for qs in range(4):
                pending.setdefault(ii + 2 + qs, []).append(lambda hh=hh, qb=qb, qs=qs: epi2(hh, qb, qs))
        for f in pending.pop(ii, []):
            f()
    for k in sorted(pending):
        for f in pending[k]:
            f()
    c.stage_end()
    c.stage_end()


EVEN_TM = [0, 512, 1536, 2048]
EVEN_FM = list(range(1024, 1536, 128)) + list(range(2560, 3584, 128))
ODD_TM = [2560]
ODD_FM = list(range(0, 2560, 128)) + list(range(3072, 4096, 128))


def build_fused(nlayers=4):
    c = Ctx()
    x = c.din("x", [TOK, D]); xo = c.dout("xo", [TOK, D])
    cst_d = {k: c.din(k, shp) for k, shp in (("ident", [128, 128]), ("maskF", [128, 128]), ("maskB", [128, 128]),
                                             ("segm", [128, 512]), ("sel", [128, 2]), ("bones", [128, 128]),
                                             ("cosT", [128, TOK]), ("sinT", [128, TOK]))}
    L = []
    for l in range(nlayers):
        d = dict(gmix=c.din("gmix%d" % l, [128, 8]), gmlp=c.din("gmlp%d" % l, [128, 8]),
                 w_in=c.din("w_in%d" % l, [D, 3584 if l % 2 == 0 else 4096]), w_out=c.din("w_out%d" % l, [D, D]),
                 w1=c.din("w1_%d" % l, [D, 4 * D]), w2=c.din("w2_%d" % l, [4 * D, D]))
        if l % 2 == 0:
            d.update(sgug=c.din("sgug%d" % l, [128, 512]), wsT=c.din("wsT%d" % l, [128, 4, 128]), bs=c.din("bs%d" % l, [128, 4]),
                     lbl=c.din("lbl%d" % l, [128, 16]), lbm=c.din("lbm%d" % l, [128, 1]), hng=c.din("hng%d" % l, [128, 512]))
        else:
            d.update(cw=c.din("cw%d" % l, [128, 12]), gv=c.din("gv%d" % l, [128, 4]), lamv=c.din("lamv%d" % l, [128, 4, 64]),
                     lconst=c.din("lconst%d" % l, [128, 2]), subg=c.din("subg%d" % l, [128, 128]))
        L.append(d)
    y_tm = c.dint("y_tm", [TOK, 2048]); y_fm = c.dint("y_fm", [3584, TOK])
    cat_tm = c.dint("cat_tm", [TOK, D]); cat_fm = c.dint("cat_fm", [512, TOK])
    xres = c.sb("xres", [128, NTI, D])
    idb = c.sb("idb", [128, 128], BF16); c.load_const("idb", idb[:], cst_d["ident"], eng="gpsimd")
    mF = c.sb("mF", [128, 128]); c.load_const("mF", mF[:], cst_d["maskF"])
    mB = c.sb("mB", [128, 128]); c.load_const("mB", mB[:], cst_d["maskB"])
    segm = c.sb("segm", [128, 512]); c.load_const("segm", segm[:], cst_d["segm"])
    sel = c.sb("sel", [128, 2]); c.load_const("sel", sel[:], cst_d["sel"])
    cst = dict(mF=mF, mB=mB, segm=segm, sel=sel, bones=cst_d["bones"], cosT=cst_d["cosT"], sinT=cst_d["sinT"])
    c.const_done()
    xkeys = ["xres%d" % i for i in range(NTI)]
    c.P.dma("sync", c.slot("xres"), xres[:], x.rearrange("(t p) d -> p t d", p=128), w=xkeys)
    for l in range(nlayers):
        d = L[l]
        if l % 2 == 0:
            f_proj(c, xres, idb, d["w_in"], d["gmix"], 3584, EVEN_TM, EVEN_FM, y_tm, y_fm)
            f_meven(c, idb, cst, y_tm, y_fm, cat_tm, d, "_%d" % l)
            f_outmlp(c, xres, idb, cat_tm, cat_fm, 0, d["w_out"], d["gmlp"], d["w1"], d["w2"])
        else:
            f_proj(c, xres, idb, d["w_in"], d["gmix"], 4096, ODD_TM, ODD_FM, y_tm, y_fm)
            f_modd(c, idb, cst, y_tm, y_fm, cat_tm, cat_fm, d, "_%d" % l)
            f_outmlp(c, xres, idb, cat_tm, cat_fm, 4, d["w_out"], d["gmlp"], d["w1"], d["w2"])
    c.P.store("sync", c.slot("st_xres"), xo.rearrange("(t p) d -> p t d", p=128), xres[:], r=xkeys)
    return c.finish()


def _perm_cols64(w):
    n = w.shape[1]
    idx = np.arange(n).reshape(n // 64, 2, 32)[:, ::-1, :].reshape(n)
    return w[:, idx]


def fused_inputs(inp, nlayers=4):
    maps = []
    inv = 1.0 / (10000.0 ** (np.arange(0, 64, 2, dtype=np.float32) / 64.0))
    for c_ in range(NCORES):
        b, r = c_ // 2, c_ % 2
        xs = inp["x"][b]
        xl = xs[0:TOK] if r == 0 else xs[::-1][0:TOK]
        pos = np.arange(TOK, dtype=np.float32) if r == 0 else (SEQ - 1 - np.arange(TOK)).astype(np.float32)
        ang = pos[:, None] * inv[None, :]
        cos, sin = np.cos(ang).astype(np.float32).T, np.sin(ang).astype(np.float32).T
        m = dict(x=np.ascontiguousarray(xl), ident=_IDENT, maskF=_MASKF, maskB=_MASKB, segm=_SEGM,
                 sel=np.ascontiguousarray(np.broadcast_to(np.array([[0.0, 1.0]] if r == 0 else [[1.0, 0.0]], np.float32), (128, 2))),
                 bones=_BONES, cosT=np.ascontiguousarray(np.concatenate([cos] * 4, 0)),
                 sinT=np.ascontiguousarray(np.concatenate([-sin, sin, -sin, sin], 0)))
        for l in range(nlayers):
            m["gmix%d" % l] = _gT(inp["norm_mix_g"][l]); m["gmlp%d" % l] = _gT(inp["norm_mlp_g"][l])
            m["w1_%d" % l] = np.ascontiguousarray(inp["mlp_w1"][l]); m["w2_%d" % l] = np.ascontiguousarray(inp["mlp_w2"][l])
            if l % 2 == 0:
                e = l // 2
                w = inp["w_in_even"][e]
                if r == 1:
                    w = np.concatenate([w[:, :2560], w[:, 3072:3584], w[:, 2560:3072]], 1)
                m["w_in%d" % l] = np.ascontiguousarray(w)
                m["w_out%d" % l] = np.ascontiguousarray(inp["w_out_even"][e])
                m["sgug%d" % l] = _bc(inp["sgu_norm_g"][e])
                wsT = np.transpose(inp["sgu_w"][e], (2, 0, 1))
                bs = inp["sgu_b"][e].T
                if r == 1:
                    wsT = wsT[::-1, :, ::-1]
                    bs = bs[::-1]
                m["wsT%d" % l] = np.ascontiguousarray(wsT); m["bs%d" % l] = np.ascontiguousarray(bs)
                lbl = np.zeros((128, 16), np.float32)
                for dr in range(2):
                    src = dr if r == 0 else 1 - dr
                    for hh in range(4):
                        for le in range(2):
                            lbl[:, (dr * 4 + hh) * 2 + le] = inp["hgrn_lb_logits"][src, le, hh * 128:(hh + 1) * 128]
                m["lbl%d" % l] = lbl
                m["lbm%d" % l] = np.full((128, 1), float(e), np.float32)
                m["hng%d" % l] = _bc(inp["hgrn_norm_g"][e])
            else:
                o = l // 2
                w = inp["w_in_odd"][o]
                m["w_in%d" % l] = np.ascontiguousarray(np.concatenate([w, _perm_cols64(w[:, 1536:2048]), _perm_cols64(w[:, 2048:2560])], 1))
                m["w_out%d" % l] = np.ascontiguousarray(inp["w_out_odd"][o])
                cwm = inp["conv_w"][o] if r == 0 else inp["conv_w"][o][::-1]
                m["cw%d" % l] = np.ascontiguousarray(np.concatenate([cwm[:, g * 128:(g + 1) * 128].T for g in range(4)], 1))
                qg, kg = np.tile(inp["q_norm_g"][o], 2), np.tile(inp["k_norm_g"][o], 2)
                m["gv%d" % l] = np.ascontiguousarray(np.stack([qg, _perm64(qg), kg, _perm64(kg)], 1).astype(np.float32))
                m["lamv%d" % l] = np.ascontiguousarray(np.stack([_bc(inp[k][o]) for k in ("lambda_q1", "lambda_k1", "lambda_q2", "lambda_k2")], 1))
                lam_init = 0.8 - 0.6 * math.exp(-0.3 * l)
                m["lconst%d" % l] = np.ascontiguousarray(np.stack([np.full(128, lam_init, np.float32), np.full(128, 1.0 - lam_init, np.float32)], 1))
                m["subg%d" % l] = _bc(inp["diff_norm_g"][o])
        maps.append(m)
    return maps


def run_fused(inp, nlayers=4):
    nc = _prog(("F", nlayers), build_fused, nlayers)
    r = _run(nc, fused_inputs(inp, nlayers))
    out = np.empty((4, SEQ, D), np.float32)
    for c_ in range(NCORES):
        b, rr = c_ // 2, c_ % 2
        if rr == 0:
            out[b, 0:TOK] = r[c_]["xo"]
        else:
            out[b, TOK:SEQ] = r[c_]["xo"][::-1]
    return out


def kernel_unfused(**inputs):
    return kernel_12(**inputs)


kernel_12 = kernel


def kernel(**inputs):
    inp = {k: np.asarray(v) for k, v in inputs.items()}
    return run_fused(inp, 4)
```

```python
import math
import numpy as np
import concourse.bass as bass
import concourse.mybir as mybir
from concourse.bass_utils import run_bass_kernel_spmd

F32 = mybir.dt.float32
BF16 = mybir.dt.bfloat16
AF = mybir.ActivationFunctionType
ALU = mybir.AluOpType
AX = mybir.AxisListType

ENGS = ["sync", "gpsimd", "scalar", "vector", "tensor"]
NCORES = 8


class _Op:
    __slots__ = ("eng", "fn", "deps", "needs_inc", "ticket", "slot", "dval")

    def __init__(self, eng, fn):
        self.eng = eng
        self.fn = fn
        self.deps = []
        self.needs_inc = False
        self.ticket = 0
        self.slot = None
        self.dval = 0


class DmaSlot:
    def __init__(self, sem):
        self.sem = sem
        self.count = 0


class Prog:
    def __init__(self, nc, stack):
        self.nc = nc
        self.stack = stack
        self.ops = {e: [] for e in ENGS}
        self.last_w = {}
        self.readers = {}
        self.esem = {e: stack.enter_context(nc.semaphore("es_" + e)) for e in ENGS}
        self.nslots = 0
        self.store_ops = []
        self.stage_dma = {}

    def slot(self):
        self.nslots += 1
        return DmaSlot(self.stack.enter_context(self.nc.semaphore("ds%d" % self.nslots)))

    def barrier(self):
        deps = []
        for e in ENGS:
            for op in reversed(self.ops[e]):
                if op.slot is None and op.fn is not None:
                    op.needs_inc = True
                    deps.append(op)
                    break
        deps += list(self.stage_dma.values())
        for e in ENGS:
            b = _Op(e, None)
            b.deps = list(deps)
            self.ops[e].append(b)
        self.stage_dma = {}

    def add(self, eng, fn, r=(), w=(), slot=None, ndma=1, dval=None):
        op = _Op(eng, fn)
        deps = []
        seen = set()

        def push(d):
            if d is not None and id(d) not in seen:
                seen.add(id(d))
                deps.append(d)

        for k in r:
            push(self.last_w.get(k))
        for k in w:
            push(self.last_w.get(k))
            for rd in self.readers.get(k, {}).values():
                push(rd)
        for d in deps:
            if d.slot is None:
                if d.eng == eng and eng == "tensor":
                    continue
                d.needs_inc = True
            op.deps.append(d)
        if slot is not None:
            slot.count += (16 * ndma if dval is None else dval)
            op.slot = slot
            op.dval = slot.count
            self.stage_dma[id(slot)] = op
        for k in w:
            self.last_w[k] = op
            self.readers[k] = {}
        for k in r:
            rk = eng if slot is None else ("dma", id(op))
            self.readers.setdefault(k, {})[rk] = op
        self.ops[eng].append(op)
        return op

    def dma(self, eng, slot, out, in_, r=(), w=()):
        def fn(e, out=out, in_=in_, slot=slot):
            return e.dma_start(out=out, in_=in_).then_inc(slot.sem, 16)

        return self.add(eng, fn, r=r, w=w, slot=slot)

    def store(self, eng, slot, out, in_, r=()):
        op = self.dma(eng, slot, out, in_, r=r)
        self.store_ops.append(op)
        return op

    def emit(self):
        nc = self.nc
        fin = _Op("sync", None)
        fin.deps = list(self.store_ops)
        self.ops["sync"].append(fin)
        for e in ENGS:
            t = 0
            for op in self.ops[e]:
                if op.slot is None and op.needs_inc:
                    t += 1
                    op.ticket = t
        with nc.Block() as block:
            for e in ENGS:
                def body(eng, e=e):
                    waited = {}
                    for op in self.ops[e]:
                        for d in op.deps:
                            if d.slot is not None:
                                key, sem, val = ("d", id(d.slot)), d.slot.sem, d.dval
                            else:
                                key, sem, val = ("e", d.eng), self.esem[d.eng], d.ticket
                            if waited.get(key, 0) < val:
                                eng.wait_ge(sem, val)
                                waited[key] = val
                        if op.fn is None:
                            continue
                        inst = op.fn(eng)
                        if op.slot is None and op.needs_inc:
                            inst.then_inc(self.esem[e], 1)

                getattr(block, e)(body)


from contextlib import ExitStack

EPS = 1e-6
D = 1024
TOK = 2048
SEQ = 4096


class Ctx:
    def __init__(self):
        self.nc = bass.Bass("TRN2", target_bir_lowering=False)
        self.st = ExitStack()
        self.P = Prog(self.nc, self.st)
        self.slots = {}
        self.cur = self.st
        self.uid = 0

    def din(self, name, shape, dt=F32):
        return self.nc.dram_tensor(name, list(shape), dt, kind="ExternalInput").ap()

    def dout(self, name, shape, dt=F32):
        return self.nc.dram_tensor(name, list(shape), dt, kind="ExternalOutput").ap()

    def sb(self, name, shape, dt=F32):
        self.uid += 1
        return self.cur.enter_context(self.nc.sbuf_tensor("s%d_%s" % (self.uid, name), list(shape), dt))

    def ps(self, name, shape, dt=F32):
        self.uid += 1
        return self.cur.enter_context(self.nc.psum_tensor("p%d_%s" % (self.uid, name), list(shape), dt))

    def dint(self, name, shape, dt=F32):
        return self.nc.dram_tensor(name, list(shape), dt).ap()

    def stage_begin(self):
        self.stk = getattr(self, "stk", [])
        self.stk.append(self.cur)
        self.cur = ExitStack()

    def stage_end(self):
        self.P.barrier()
        self.cur.close()
        self.cur = self.stk.pop()

    def slot(self, key):
        if key not in self.slots:
            self.slots[key] = self.P.slot()
        return self.slots[key]

    def load(self, key, out, in_, eng="sync"):
        return self.P.dma(eng, self.slot(key), out, in_, w=[key])

    def load_const(self, key, out, in_, eng="sync"):
        op = self.P.dma(eng, self.slot("const_" + eng), out, in_, w=[key])
        self.cpend = getattr(self, "cpend", {})
        self.cpend.setdefault(eng, []).append(key)
        return op

    def const_done(self):
        for eng, keys in getattr(self, "cpend", {}).items():
            ops = [self.P.last_w[k] for k in keys]
            last = max(ops, key=lambda o: o.dval)
            for k in keys:
                self.P.last_w[k] = last
        self.cpend = {}

    def store(self, key, out, in_, eng="sync"):
        return self.P.store(eng, self.slot("st_" + key), out, in_, r=[key])

    def act(self, out, in_, func, r, w, **kw):
        return self.P.add("scalar", lambda e: e.activation(out=out, in_=in_, func=func, **kw), r=r, w=w)

    def vec(self, method, r, w, **kw):
        return self.P.add("vector", lambda e: getattr(e, method)(**kw), r=r, w=w)

    def pe(self, fn, r, w):
        return self.P.add("tensor", fn, r=r, w=w)

    def finish(self):
        self.P.emit()
        self.st.close()
        return self.nc


def rstd_ops(c, ss, n, inv_n, rkeys, key):
    c.vec("tensor_scalar", r=rkeys, w=[key], out=ss[:, 0:n], in0=ss[:, 0:n], scalar1=inv_n, scalar2=EPS,
          op0=ALU.mult, op1=ALU.add)
    c.act(ss[:, 0:n], ss[:, 0:n], AF.Sqrt, r=[key], w=[key])
    c.vec("reciprocal", r=[key], w=[key], out=ss[:, 0:n], in_=ss[:, 0:n])


def norm_transpose(c, i, xtile, xkey, junk, ss, xs, pT, hT_out, hkey, gT, idb, pkey="pT"):
    b = i % 2
    sk, jk, xk, pk = "ss%d" % b, "junk%d" % b, "xs%d" % b, pkey + "%d" % b
    c.act(junk[b][:], xtile, AF.Square, r=[xkey], w=[jk, sk], accum_out=ss[b][:, 0:1])
    rstd_ops(c, ss[b], 1, 1.0 / D, [sk], sk)
    c.act(xs[b][:], xtile, AF.Copy, r=[xkey, sk], w=[xk], scale=ss[b][:, 0:1])

    def tr(e, b=b):
        for k in range(8):
            ins = e.transpose(pT[b][:, k * 128:(k + 1) * 128], xs[b][:, k * 128:(k + 1) * 128], idb[:])
        return ins

    c.pe(tr, r=[xk, "idb"], w=[pk])
    c.vec("tensor_tensor", r=[pk, "gT"], w=[hkey], out=hT_out,
          in0=pT[b][:].rearrange("p (k t) -> p k t", t=128),
          in1=gT[:].rearrange("p (k o) -> p k o", o=1).to_broadcast([128, 8, 128]), op=ALU.mult)


def build_proj(N):
    c = Ctx()
    x = c.din("x", [TOK, D]); gTd = c.din("gT", [128, 8]); w = c.din("w", [D, N]); identd = c.din("ident", [128, 128])
    y = c.dout("y", [TOK, N])
    ncb = N // 512
    wbf = c.sb("wbf", [128, 8, N], BF16)
    idb = c.sb("idb", [128, 128], BF16); gT = c.sb("gT", [128, 8])
    xt = [c.sb("xt%d" % i, [128, D]) for i in range(2)]
    junk = [c.sb("junk%d" % i, [128, D]) for i in range(2)]
    xs = [c.sb("xs%d" % i, [128, D], BF16) for i in range(2)]
    ss = [c.sb("ss%d" % i, [128, 4]) for i in range(2)]
    hT = [c.sb("hT%d" % i, [128, 8, 128], BF16) for i in range(2)]
    yt = [c.sb("yt%d" % i, [128, N]) for i in range(2)]
    pT = [c.ps("pT%d" % i, [128, 1024], BF16) for i in range(2)]
    pm = [c.ps("pm%d" % i, [128, 512]) for i in range(4)]
    c.load("idb", idb[:], identd, eng="gpsimd")
    c.load("gT", gT[:], gTd)
    for k in range(8):
        c.load("wbf%d" % k, wbf[:, k, :], w[k * 128:(k + 1) * 128, :], eng="gpsimd")
    wkeys = ["wbf%d" % k for k in range(8)]
    n = 0
    for i in range(TOK // 128):
        b = i % 2
        c.load("xt%d" % b, xt[b][:], x[i * 128:(i + 1) * 128, :])
        norm_transpose(c, i, xt[b][:], "xt%d" % b, junk, ss, xs, pT, hT[b][:], "hT%d" % b, gT, idb)
        for cb in range(ncb):
            pb = n % 4
            n += 1

            def mm(e, b=b, cb=cb, pb=pb):
                for k in range(8):
                    ins = e.matmul(pm[pb][:], lhsT=hT[b][:, k, :], rhs=wbf[:, k, cb * 512:(cb + 1) * 512],
                                   start=(k == 0), stop=(k == 7))
                return ins

            c.pe(mm, r=["hT%d" % b] + wkeys, w=["pm%d" % pb])
            if cb % 2 == 0:
                c.act(yt[b][:, cb * 512:(cb + 1) * 512], pm[pb][:], AF.Copy, r=["pm%d" % pb], w=["yt%d" % b])
            else:
                c.vec("tensor_copy", r=["pm%d" % pb], w=["yt%d" % b], out=yt[b][:, cb * 512:(cb + 1) * 512], in_=pm[pb][:])
        c.store("yt%d" % b, y[i * 128:(i + 1) * 128, :], yt[b][:])
    return c.finish()


GS = 512


def build_outmlp():
    c = Ctx()
    x = c.din("x", [TOK, D]); cat = c.din("cat", [TOK, D]); wo = c.din("wo", [D, D]); gTd = c.din("gT", [128, 8])
    w1 = c.din("w1", [D, 4 * D]); w2 = c.din("w2", [4 * D, D]); identd = c.din("ident", [128, 128])
    xo = c.dout("xo", [TOK, D])
    NTI = TOK // 128
    xres = c.sb("xres", [128, NTI, D])
    hTa = c.sb("hTa", [128, 8, TOK], BF16)
    wob = c.sb("wob", [128, 8, D], BF16)
    idb = c.sb("idb", [128, 128], BF16); gT = c.sb("gT", [128, 8])
    catb = [c.sb("catb%d" % i, [128, D], BF16) for i in range(2)]
    catT = [c.sb("catT%d" % i, [128, 8, 128], BF16) for i in range(2)]
    junk = [c.sb("junk%d" % i, [128, D]) for i in range(2)]
    xs = [c.sb("xs%d" % i, [128, D], BF16) for i in range(2)]
    ss = [c.sb("ss%d" % i, [128, 4]) for i in range(2)]
    w1g = [c.sb("w1g%d" % i, [128, 8, GS], BF16) for i in range(2)]
    w2g = [c.sb("w2g%d" % i, [128, GS // 128, D], BF16) for i in range(2)]
    rl = [c.sb("rl%d" % i, [128, 512]) for i in range(2)]
    actb = [c.sb("actb%d" % i, [128, GS // 128, 512], BF16) for i in range(2)]
    pT = [c.ps("pT%d" % i, [128, 1024], BF16) for i in range(2)]
    pm = [c.ps("pm%d" % i, [128, 512]) for i in range(6)]
    c.load("idb", idb[:], identd, eng="gpsimd")
    c.load("gT", gT[:], gTd)
    for k in range(8):
        c.load("wob%d" % k, wob[:, k, :], wo[k * 128:(k + 1) * 128, :], eng="gpsimd")
    wokeys = ["wob%d" % k for k in range(8)]
    for i in range(NTI):
        c.load("xres%d" % i, xres[:, i, :], x[i * 128:(i + 1) * 128, :])
    n = 0
    for i in range(NTI):
        b = i % 2
        c.load("catb%d" % b, catb[b][:], cat[i * 128:(i + 1) * 128, :], eng="gpsimd")

        def tr(e, b=b):
            for k in range(8):
                ins = e.transpose(pT[b][:, k * 128:(k + 1) * 128], catb[b][:, k * 128:(k + 1) * 128], idb[:])
            return ins

        c.pe(tr, r=["catb%d" % b, "idb"], w=["pT%d" % b])
        c.vec("tensor_copy", r=["pT%d" % b], w=["catT%d" % b], out=catT[b][:],
              in_=pT[b][:].rearrange("p (k t) -> p k t", t=128))
        for cb in range(2):
            pb = n % 6
            n += 1

            def mm(e, b=b, cb=cb, pb=pb):
                for k in range(8):
                    ins = e.matmul(pm[pb][:], lhsT=catT[b][:, k, :], rhs=wob[:, k, cb * 512:(cb + 1) * 512],
                                   start=(k == 0), stop=(k == 7))
                return ins

            c.pe(mm, r=["catT%d" % b] + wokeys, w=["pm%d" % pb])
            c.vec("tensor_tensor", r=["pm%d" % pb, "xres%d" % i], w=["xres%d" % i],
                  out=xres[:, i, cb * 512:(cb + 1) * 512], in0=pm[pb][:], in1=xres[:, i, cb * 512:(cb + 1) * 512], op=ALU.add)
    for i in range(NTI):
        norm_transpose(c, i, xres[:, i, :], "xres%d" % i, junk, ss, xs, pT, hTa[:, :, i * 128:(i + 1) * 128],
                       "hTa%d" % i, gT, idb)
    NG = 4 * D // GS
    CPG = GS // 128
    m = 0
    for g in range(NG):
        gb = g % 2
        c.load("w1g%d" % gb, w1g[gb][:], w1[:, g * GS:(g + 1) * GS].rearrange("(k p) c -> p k c", p=128), eng="gpsimd")
        c.load("w2g%d" % gb, w2g[gb][:], w2[g * GS:(g + 1) * GS, :].rearrange("(k p) c -> p k c", p=128), eng="gpsimd")
        for tb in range(TOK // 512):
            ab = m % 2
            m += 1
            hkeys = ["hTa%d" % (tb * 4 + j) for j in range(4)]
            for cc in range(CPG):
                pb = n % 6
                n += 1
                rb = n % 2

                def mm1(e, gb=gb, cc=cc, tb=tb, pb=pb):
                    for k in range(8):
                        ins = e.matmul(pm[pb][:], lhsT=w1g[gb][:, k, cc * 128:(cc + 1) * 128],
                                       rhs=hTa[:, k, tb * 512:(tb + 1) * 512], start=(k == 0), stop=(k == 7))
                    return ins

                c.pe(mm1, r=hkeys + ["w1g%d" % gb], w=["pm%d" % pb])
                c.vec("tensor_scalar", r=["pm%d" % pb], w=["rl%d" % rb], out=rl[rb][:], in0=pm[pb][:], scalar1=0.0,
                      scalar2=None, op0=ALU.max)
                c.act(actb[ab][:, cc, :], rl[rb][:], AF.Square, r=["rl%d" % rb], w=["actb%d_%d" % (ab, cc)])
            akeys = ["actb%d_%d" % (ab, cc) for cc in range(CPG)]
            for tt in range(4):
                ti = tb * 4 + tt
                for cb in range(2):
                    pb = n % 6
                    n += 1

                    def mm2(e, gb=gb, ab=ab, tt=tt, cb=cb, pb=pb):
                        for cc in range(CPG):
                            ins = e.matmul(pm[pb][:], lhsT=actb[ab][:, cc, tt * 128:(tt + 1) * 128],
                                           rhs=w2g[gb][:, cc, cb * 512:(cb + 1) * 512], start=(cc == 0), stop=(cc == CPG - 1))
                        return ins

                    c.pe(mm2, r=akeys + ["w2g%d" % gb], w=["pm%d" % pb])
                    c.vec("tensor_tensor", r=["pm%d" % pb, "xres%d" % ti], w=["xres%d" % ti],
                          out=xres[:, ti, cb * 512:(cb + 1) * 512], in0=pm[pb][:], in1=xres[:, ti, cb * 512:(cb + 1) * 512],
                          op=ALU.add)
    for i in range(NTI):
        c.store("xres%d" % i, xo[i * 128:(i + 1) * 128, :], xres[:, i, :])
    return c.finish()


_CACHE = {}


def _prog(key, fn, *a):
    if key not in _CACHE:
        _CACHE[key] = fn(*a)
    return _CACHE[key]


def _run(nc, maps):
    res = run_bass_kernel_spmd(nc, maps, core_ids=list(range(NCORES)))
    return res.results


def _gT(g):
    return np.ascontiguousarray(g.reshape(8, 128).T)


_IDENT = np.eye(128, dtype=np.float32)


def run_proj(xf, g, W):
    N = W.shape[1]
    nc = _prog(("P", N), build_proj, N)
    W = np.ascontiguousarray(W)
    maps = [dict(x=np.ascontiguousarray(xf[c * TOK:(c + 1) * TOK]), gT=_gT(g), w=W, ident=_IDENT) for c in range(NCORES)]
    r = _run(nc, maps)
    return np.concatenate([r[c]["y"] for c in range(NCORES)], 0)


def run_outmlp(xf, catf, wo, g, w1, w2):
    nc = _prog("O", build_outmlp)
    wo, w1, w2 = (np.ascontiguousarray(a) for a in (wo, w1, w2))
    maps = [dict(x=np.ascontiguousarray(xf[c * TOK:(c + 1) * TOK]), cat=np.ascontiguousarray(catf[c * TOK:(c + 1) * TOK]),
                 wo=wo, gT=_gT(g), w1=w1, w2=w2, ident=_IDENT) for c in range(NCORES)]
    r = _run(nc, maps)
    return np.concatenate([r[c]["xo"] for c in range(NCORES)], 0)


def build_meven():
    c = Ctx()
    P = c.P
    u = c.din("u", [SEQ, 256]); v = c.din("v", [SEQ, 256]); sgugd = c.din("sgug", [128, 256])
    wsTd = c.din("wsT", [128, 2, 128]); bsd = c.din("bs", [128, 2])
    aT = c.din("aT", [2, 2, 128, SEQ]); qT = c.din("qT", [2, 128, SEQ]); iv = c.din("iv", [SEQ, 256]); gg = c.din("gg", [SEQ, 256])
    lbld = c.din("lbl", [128, 8]); lbmd = c.din("lbm", [128, 1]); hngd = c.din("hng", [128, 256])
    identd = c.din("ident", [128, 128]); mFd = c.din("maskF", [128, 128]); mBd = c.din("maskB", [128, 128])
    segd = c.din("segm", [128, 512])
    out = c.dout("out", [SEQ, 512])
    NT = SEQ // 128
    idb = c.sb("idb", [128, 128], BF16); c.load("idb", idb[:], identd, eng="gpsimd")
    sgug = c.sb("sgug", [128, 256]); c.load("sgug", sgug[:], sgugd)
    wsb = c.sb("wsb", [128, 2, 128], BF16); c.load("wsb", wsb[:], wsTd, eng="gpsimd")
    bs = c.sb("bs", [128, 2]); c.load("bs", bs[:], bsd)
    lbl = c.sb("lbl", [128, 8]); c.load("lbl", lbl[:], lbld)
    lbm = c.sb("lbm", [128, 1]); c.load("lbm", lbm[:], lbmd)
    hng = c.sb("hng", [128, 256]); c.load("hng", hng[:], hngd)
    mF = c.sb("mF", [128, 128]); c.load("mF", mF[:], mFd)
    mB = c.sb("mB", [128, 128]); c.load("mB", mB[:], mBd)
    segm = c.sb("segm", [128, 512]); c.load("segm", segm[:], segd)
    lb4 = c.sb("lb4", [128, 4]); oml4 = c.sb("oml4", [128, 4]); noml4 = c.sb("noml4", [128, 4])
    l3 = lbl[:].rearrange("p (a e) -> p a e", e=2)
    c.vec("tensor_tensor", r=["lbl"], w=["lb4"], out=lb4[:].rearrange("p (a o) -> p a o", o=1), in0=l3[:, :, 1:2],
          in1=l3[:, :, 0:1], op=ALU.subtract)
    c.act(lb4[:], lb4[:], AF.Sigmoid, r=["lb4"], w=["lb4"])
    c.vec("tensor_scalar", r=["lb4", "lbm"], w=["lb4"], out=lb4[:], in0=lb4[:], scalar1=lbm[:, 0:1], scalar2=None, op0=ALU.mult)
    c.vec("tensor_scalar", r=["lb4"], w=["oml4"], out=oml4[:], in0=lb4[:], scalar1=-1.0, scalar2=1.0, op0=ALU.mult, op1=ALU.add)
    c.vec("tensor_scalar", r=["lb4"], w=["noml4"], out=noml4[:], in0=lb4[:], scalar1=1.0, scalar2=-1.0, op0=ALU.mult, op1=ALU.add)

    pm = c.ps("pm", [128, 512])
    pkt = c.ps("pkt", [128, 1024], BF16)
    pa = [c.ps("pa%d" % i, [128, 512]) for i in range(2)]
    po = [c.ps("po%d" % i, [128, 512]) for i in range(2)]
    pu = [c.ps("pu%d" % i, [128, 512]) for i in range(2)]

    ut = [c.sb("ut%d" % i, [128, 256]) for i in range(2)]
    vt = [c.sb("vt%d" % i, [128, 256]) for i in range(2)]
    vn = [c.sb("vn%d" % i, [128, 256], BF16) for i in range(2)]
    oa = [c.sb("oa%d" % i, [128, 256]) for i in range(2)]
    sj = c.sb("sj", [128, 128]); ssg = c.sb("ssg", [128, 4])
    for n in range(NT):
        b = n % 2
        uk, vk, nk, ok = "ut%d" % b, "vt%d" % b, "vn%d" % b, "oa%d" % b
        c.load(uk, ut[b][:], u[n * 128:(n + 1) * 128, :])
        c.load(vk, vt[b][:], v[n * 128:(n + 1) * 128, :])
        c.act(vt[b][:], vt[b][:], AF.Gelu_apprx_tanh, r=[vk], w=[vk])
        c.act(ut[b][:], ut[b][:], AF.Gelu_apprx_tanh, r=[uk], w=[uk])
        for h in range(2):
            c.act(sj[:], vt[b][:, h * 128:(h + 1) * 128], AF.Square, r=[vk], w=["sj", "ssg"], accum_out=ssg[:, h:h + 1])
        rstd_ops(c, ssg, 2, 1.0 / 128, ["ssg"], "ssg")
        for h in range(2):
            hs = slice(h * 128, (h + 1) * 128)
            c.vec("scalar_tensor_tensor", r=[vk, "ssg", "sgug"], w=[nk], out=vn[b][:, hs], in0=vt[b][:, hs],
                  scalar=ssg[:, h:h + 1], in1=sgug[:, hs], op0=ALU.mult, op1=ALU.mult)

        def mm(e, b=b):
            for h in range(2):
                ins = e.matmul(pm[:, h * 128:(h + 1) * 128], lhsT=wsb[:, h, :], rhs=vn[b][:, h * 128:(h + 1) * 128],
                               start=True, stop=True)
            return ins

        c.pe(mm, r=[nk, "wsb"], w=["pm"])
        for h in range(2):
            hs = slice(h * 128, (h + 1) * 128)
            c.vec("scalar_tensor_tensor", r=["pm", "bs", uk], w=[ok], out=oa[b][:, hs], in0=pm[:, hs],
                  scalar=bs[:, h:h + 1], in1=ut[b][:, hs], op0=ALU.add, op1=ALU.mult)
        c.store(ok, out[n * 128:(n + 1) * 128, 0:256], oa[b][:])

    v3 = lambda t: t[:].rearrange("p (t c) -> p t c", c=128)
    A = [c.sb("A%d" % i, [128, 512]) for i in range(2)]
    Q = [c.sb("Q%d" % i, [128, 512]) for i in range(2)]
    IV = [c.sb("IV%d" % i, [128, 4, 128], BF16) for i in range(2)]
    GG = [c.sb("GG%d" % i, [128, 512]) for i in range(2)]
    L = c.sb("L", [128, 512]); KK = c.sb("KK", [128, 512]); BFW = c.sb("BFW", [128, 512]); BB = c.sb("BB", [128, 512])
    BR = c.sb("BR", [128, 512]); EQ = c.sb("EQ", [128, 512])
    QE = c.sb("QE", [128, 512], BF16); KD = c.sb("KD", [128, 512], BF16); QI = c.sb("QI", [128, 512], BF16)
    KDZ = [c.sb("KDZ%d" % i, [128, 512], BF16) for i in range(2)]
    KDEC = c.sb("KDEC", [128, 512], BF16); kdT = c.sb("kdT", [128, 512], BF16)
    dS = c.sb("dS", [128, 4])
    attT = [c.sb("attT%d" % i, [128, 128], BF16) for i in range(2)]
    S32 = c.sb("S32", [128, 128]); Sbf = c.sb("Sbf", [128, 128], BF16)
    OF = c.sb("OF", [128, NT, 128])
    osum = [c.sb("osum%d" % i, [128, 128]) for i in range(2)]
    obst = [c.sb("obst%d" % i, [128, 128]) for i in range(2)]
    sj2 = c.sb("sj2", [128, 128]); ssh = c.sb("ssh", [128, 4])
    for i in range(2):
        c.vec("memset", r=[], w=["KDZ%d" % i], ap=KDZ[i][:], constant=0.0)
    nt_ = 0
    ntb = 0
    for hh in range(2):
        for dr in range(2):
            col = dr * 2 + hh
            lbc, omlc, nomlc = lb4[:, col:col + 1], oml4[:, col:col + 1], noml4[:, col:col + 1]
            c.vec("memset", r=[], w=["S32"], ap=S32[:], constant=0.0)
            c.vec("memset", r=[], w=["Sbf"], ap=Sbf[:], constant=0.0)
            tbs = range(SEQ // 512) if dr == 0 else range(SEQ // 512 - 1, -1, -1)
            for tb in tbs:
                p2 = ntb % 2
                ntb += 1
                ak, qk, ik, gk = "A%d" % p2, "Q%d" % p2, "IV%d" % p2, "GG%d" % p2
                c.load(ak, A[p2][:], aT[dr, hh, :, tb * 512:(tb + 1) * 512])
                c.load(qk, Q[p2][:], qT[hh, :, tb * 512:(tb + 1) * 512])
                c.load(ik, IV[p2][:], iv[tb * 512:(tb + 1) * 512, hh * 128:(hh + 1) * 128].rearrange("(t p) c -> p t c", p=128),
                       eng="gpsimd")
                if dr == 1:
                    c.load(gk, v3(GG[p2]), gg[tb * 512:(tb + 1) * 512, hh * 128:(hh + 1) * 128].rearrange("(t p) c -> p t c", p=128))
                    c.act(GG[p2][:], GG[p2][:], AF.Silu, r=[gk], w=[gk])
                a_, q_ = A[p2], Q[p2]
                c.act(a_[:], a_[:], AF.Sigmoid, r=[ak], w=[ak])
                c.act(L[:], a_[:], AF.Ln, r=[ak, "oml4", "lb4"], w=["L"], scale=omlc, bias=lbc)
                c.vec("tensor_scalar", r=[ak, "oml4", "noml4"], w=["KK"], out=KK[:], in0=a_[:], scalar1=nomlc, scalar2=omlc,
                      op0=ALU.mult, op1=ALU.add)
                c.vec("tensor_tensor_scan", r=["L", "segm"], w=["BFW"], out=BFW[:], data0=segm[:], data1=L[:], initial=0.0,
                      op0=ALU.mult, op1=ALU.add)
                if dr == 0:
                    Bt, bkey, ri, li = BFW, "BFW", 63, 127
                else:
                    c.vec("tensor_tensor", r=["L", "BFW"], w=["L"], out=L[:], in0=L[:], in1=BFW[:], op=ALU.subtract)
                    c.vec("tensor_tensor", r=["L", "BFW"], w=["BB"], out=v3(BB), in0=v3(L),
                          in1=v3(BFW)[:, :, 127:128].to_broadcast([128, 4, 128]), op=ALU.add)
                    Bt, bkey, ri, li = BB, "BB", 64, 0
                B3 = v3(Bt)
                c.vec("tensor_tensor", r=[bkey], w=["BR"], out=v3(BR), in0=B3,
                      in1=B3[:, :, ri:ri + 1].to_broadcast([128, 4, 128]), op=ALU.subtract)
                c.act(EQ[:], BR[:], AF.Exp, r=["BR"], w=["EQ"])
                c.vec("tensor_tensor", r=[qk, "EQ"], w=["QE"], out=QE[:], in0=q_[:], in1=EQ[:], op=ALU.mult)
                c.act(BR[:], BR[:], AF.Exp, r=["BR"], w=["BR"], scale=-1.0)
                c.vec("tensor_tensor", r=["KK", "BR"], w=["KD"], out=KD[:], in0=KK[:], in1=BR[:], op=ALU.mult)
                hsl = slice(0, 64) if dr == 0 else slice(64, 128)
                c.vec("tensor_tensor", r=["KK", "BR"], w=["KDZ%d" % dr], out=v3(KDZ[dr])[:, :, hsl], in0=v3(KK)[:, :, hsl],
                      in1=v3(BR)[:, :, hsl], op=ALU.mult)
                c.act(EQ[:], Bt[:], AF.Exp, r=[bkey, "QE"], w=["EQ"])
                c.vec("tensor_tensor", r=[qk, "EQ"], w=["QI"], out=QI[:], in0=q_[:], in1=EQ[:], op=ALU.mult)
                c.vec("tensor_tensor", r=[bkey, "KD", "KDZ%d" % dr], w=["BR"], out=v3(BR),
                      in0=B3[:, :, li:li + 1].to_broadcast([128, 4, 128]), in1=B3, op=ALU.subtract)
                c.act(BR[:], BR[:], AF.Exp, r=["BR"], w=["BR"])
                c.vec("tensor_tensor", r=["KK", "BR"], w=["KDEC"], out=KDEC[:], in0=KK[:], in1=BR[:], op=ALU.mult)
                c.act(dS[:].rearrange("p (t o) -> p t o", o=1), B3[:, :, li:li + 1], AF.Exp, r=[bkey], w=["dS"])

                def trk(e):
                    for t in range(4):
                        ins = e.transpose(pkt[:, t * 128:(t + 1) * 128], KDEC[:, t * 128:(t + 1) * 128], idb[:])
                    return ins

                c.pe(trk, r=["KDEC", "idb"], w=["pkt"])
                c.act(kdT[:], pkt[:, 0:512], AF.Copy, r=["pkt"], w=["kdT"])
                tts = range(4) if dr == 0 else range(3, -1, -1)
                for tt in tts:
                    ti = tb * 4 + tt
                    pb = nt_ % 2
                    nt_ += 1
                    ts_ = slice(tt * 128, (tt + 1) * 128)
                    kA, kB = (KDZ[0], KD) if dr == 0 else (KD, KDZ[1])

                    def mma(e, pb=pb, tt=tt, kA=kA, kB=kB):
                        e.matmul(pa[pb][:, 0:64], lhsT=kA[:, tt * 128:(tt + 1) * 128], rhs=QE[:, tt * 128:tt * 128 + 64],
                                 start=True, stop=True)
                        return e.matmul(pa[pb][:, 64:128], lhsT=kB[:, tt * 128:(tt + 1) * 128],
                                        rhs=QE[:, tt * 128 + 64:(tt + 1) * 128], start=True, stop=True)

                    c.pe(mma, r=["KD", "KDZ%d" % dr, "QE"], w=["pa%d" % pb])
                    c.vec("tensor_tensor", r=["pa%d" % pb, "mF", "mB"], w=["attT%d" % pb], out=attT[pb][:], in0=pa[pb][:, 0:128],
                          in1=(mF if dr == 0 else mB)[:], op=ALU.mult)

                    def mmo(e, pb=pb, tt=tt, p2=p2):
                        e.matmul(po[pb][:, 0:128], lhsT=QI[:, tt * 128:(tt + 1) * 128], rhs=Sbf[:], start=True, stop=False)
                        return e.matmul(po[pb][:, 0:128], lhsT=attT[pb][:], rhs=IV[p2][:, tt, :], start=False, stop=True)

                    c.pe(mmo, r=["QI", "Sbf", "attT%d" % pb, ik], w=["po%d" % pb])
                    c.pe(lambda e, pb=pb, tt=tt, p2=p2: e.matmul(pu[pb][:, 0:128], lhsT=kdT[:, tt * 128:(tt + 1) * 128],
                                                                 rhs=IV[p2][:, tt, :], start=True, stop=True),
                         r=["kdT", ik], w=["pu%d" % pb])
                    c.vec("scalar_tensor_tensor", r=["S32", "dS", "pu%d" % pb], w=["S32"], out=S32[:], in0=S32[:],
                          scalar=dS[:, tt:tt + 1], in1=pu[pb][:, 0:128], op0=ALU.mult, op1=ALU.add)
                    c.act(Sbf[:], S32[:], AF.Copy, r=["S32"], w=["Sbf"])
                    if dr == 0:
                        c.act(OF[:, ti, :], po[pb][:, 0:128], AF.Copy, r=["po%d" % pb], w=["OF%d" % ti])
                    else:
                        ob_ = nt_ % 2
                        c.vec("tensor_tensor", r=["po%d" % pb, "OF%d" % ti], w=["osum%d" % ob_], out=osum[ob_][:],
                              in0=po[pb][:, 0:128], in1=OF[:, ti, :], op=ALU.add)
                        c.act(sj2[:], osum[ob_][:], AF.Square, r=["osum%d" % ob_], w=["sj2", "ssh"], accum_out=ssh[:, 0:1])
                        rstd_ops(c, ssh, 1, 1.0 / 128, ["ssh"], "ssh")
                        c.vec("scalar_tensor_tensor", r=["osum%d" % ob_, "ssh", "hng"], w=["osum%d" % ob_], out=osum[ob_][:],
                              in0=osum[ob_][:], scalar=ssh[:, 0:1], in1=hng[:, hh * 128:(hh + 1) * 128], op0=ALU.mult, op1=ALU.mult)
                        c.vec("tensor_tensor", r=["osum%d" % ob_, gk], w=["obst%d" % ob_], out=obst[ob_][:], in0=osum[ob_][:],
                              in1=GG[p2][:, tt * 128:(tt + 1) * 128], op=ALU.mult)
                        c.store("obst%d" % ob_, out[ti * 128:(ti + 1) * 128, 256 + hh * 128:256 + (hh + 1) * 128], obst[ob_][:])
    return c.finish()


_MASKF = np.triu(np.ones((128, 128), np.float32))
_MASKB = np.tril(np.ones((128, 128), np.float32))
_SEGM = np.ones((128, 512), np.float32)
_SEGM[:, ::128] = 0.0


def _bc(vec):
    return np.ascontiguousarray(np.broadcast_to(vec[None, :], (128, vec.shape[0])))


def run_meven(y, e, inp):
    nc = _prog("Me", build_meven)
    maps = []
    for c in range(NCORES):
        b, hp = c // 2, c % 2
        yb = y[b * SEQ:(b + 1) * SEQ]
        cs = slice(hp * 256, hp * 256 + 256)
        u, v, q, iv, g, ff, fb = (yb[:, k * 512:(k + 1) * 512] for k in range(7))
        heads = [2 * hp, 2 * hp + 1]
        aT = np.stack([np.stack([f[:, h * 128:(h + 1) * 128].T for h in heads]) for f in (ff, fb)])
        qT = np.stack([q[:, h * 128:(h + 1) * 128].T for h in heads])
        lbl = np.zeros((128, 8), np.float32)
        for dr in range(2):
            for hi, h in enumerate(heads):
                for le in range(2):
                    lbl[:, (dr * 2 + hi) * 2 + le] = inp["hgrn_lb_logits"][dr, le, h * 128:(h + 1) * 128]
        maps.append(dict(
            u=np.ascontiguousarray(u[:, cs]), v=np.ascontiguousarray(v[:, cs]), sgug=_bc(inp["sgu_norm_g"][e][cs]),
            wsT=np.ascontiguousarray(np.transpose(inp["sgu_w"][e][heads[0]:heads[1] + 1], (2, 0, 1))),
            bs=np.ascontiguousarray(inp["sgu_b"][e][heads[0]:heads[1] + 1].T),
            aT=np.ascontiguousarray(aT), qT=np.ascontiguousarray(qT), iv=np.ascontiguousarray(iv[:, cs]),
            gg=np.ascontiguousarray(g[:, cs]), lbl=lbl, lbm=np.full((128, 1), float(e), np.float32),
            hng=_bc(inp["hgrn_norm_g"][e][cs]), ident=_IDENT, maskF=_MASKF, maskB=_MASKB, segm=_SEGM))
    r = _run(nc, maps)
    cat = np.empty((4 * SEQ, D), np.float32)
    for c in range(NCORES):
        b, hp = c // 2, c % 2
        o = r[c]["out"]
        cat[b * SEQ:(b + 1) * SEQ, hp * 256:hp * 256 + 256] = o[:, 0:256]
        cat[b * SEQ:(b + 1) * SEQ, 512 + hp * 256:512 + hp * 256 + 256] = o[:, 256:512]
    return cat


def build_modd():
    c = Ctx()
    hinT = c.din("hinT", [2, 128, SEQ]); bgT = c.din("bgT", [2, 128, SEQ]); cgT = c.din("cgT", [2, 128, SEQ])
    cwd = c.din("cw", [128, 6])
    qT = c.din("qT", [2, 128, SEQ]); qTp = c.din("qTp", [2, 128, SEQ]); kT = c.din("kT", [2, 128, SEQ]); kTp = c.din("kTp", [2, 128, SEQ])
    vd = c.din("v", [SEQ, 256]); gvd = c.din("gv", [128, 4]); cosd = c.din("cosT", [128, SEQ]); sind = c.din("sinT", [128, SEQ])
    lamd = c.din("lamv", [128, 4, 64]); lcd = c.din("lconst", [128, 2]); subgd = c.din("subg", [128, 128])
    bonesd = c.din("bones", [128, 128])
    oc = c.dout("oc", [2, 128, SEQ]); od = c.dout("od", [SEQ, 256])
    NT = SEQ // 128
    cw = c.sb("cw", [128, 6]); c.load("cw", cw[:], cwd)
    gv = c.sb("gv", [128, 4]); c.load("gv", gv[:], gvd)
    cosT = c.sb("cosT", [128, SEQ]); c.load("cosT", cosT[:], cosd)
    sinT = c.sb("sinT", [128, SEQ]); c.load("sinT", sinT[:], sind)
    lamv = c.sb("lamv", [128, 4, 64]); c.load("lamv", lamv[:], lamd)
    lconst = c.sb("lconst", [128, 2]); c.load("lconst", lconst[:], lcd)
    subg = c.sb("subg", [128, 128]); c.load("subg", subg[:], subgd)
    bones = c.sb("bones", [128, 128]); c.load("bones", bones[:], bonesd)
    epsb = c.sb("epsb", [128, 1]); c.vec("memset", r=[], w=["epsb"], ap=epsb[:], constant=EPS)
    lj = c.sb("lj", [128, 64]); ls = c.sb("ls", [128, 2]); nlam = c.sb("nlam", [128, 1])
    for j in range(2):
        c.vec("tensor_tensor", r=["lamv"], w=["lj"], out=lj[:], in0=lamv[:, 2 * j, :], in1=lamv[:, 2 * j + 1, :], op=ALU.mult)
        c.vec("tensor_reduce", r=["lj"], w=["ls%d" % j], out=ls[:, j:j + 1], in_=lj[:], axis=AX.X, op=ALU.add)
    c.act(ls[:], ls[:], AF.Exp, r=["ls0", "ls1"], w=["ls"])
    c.vec("tensor_tensor", r=["ls"], w=["nlam"], out=nlam[:], in0=ls[:, 1:2], in1=ls[:, 0:1], op=ALU.subtract)
    c.vec("tensor_tensor", r=["nlam", "lconst"], w=["nlam"], out=nlam[:], in0=nlam[:], in1=lconst[:, 0:1], op=ALU.subtract)
    c.vec("tensor_scalar", r=["subg", "lconst"], w=["subg"], out=subg[:], in0=subg[:], scalar1=lconst[:, 1:2], scalar2=None,
          op0=ALU.mult)

    hin = c.sb("hin", [128, SEQ]); cg = c.sb("cg", [128, SEQ]); bg = c.sb("bg", [128, SEQ])
    z = c.sb("z", [128, SEQ + 2]); yy = c.sb("yy", [128, SEQ])
    c.vec("memset", r=[], w=["z"], ap=z[:], constant=0.0)
    for g in range(2):
        c.load("hin", hin[:], hinT[g]); c.load("cg", cg[:], cgT[g]); c.load("bg", bg[:], bgT[g])
        c.vec("tensor_tensor", r=["hin", "cg"], w=["z"], out=z[:, 1:SEQ + 1], in0=cg[:], in1=hin[:], op=ALU.mult)
        c.vec("tensor_scalar", r=["z", "cw"], w=["yy"], out=yy[:], in0=z[:, 1:SEQ + 1], scalar1=cw[:, 3 * g + 1:3 * g + 2],
              scalar2=None, op0=ALU.mult)
        c.vec("scalar_tensor_tensor", r=["z", "cw", "yy"], w=["yy"], out=yy[:], in0=z[:, 0:SEQ], scalar=cw[:, 3 * g:3 * g + 1],
              in1=yy[:], op0=ALU.mult, op1=ALU.add)
        c.vec("scalar_tensor_tensor", r=["z", "cw", "yy"], w=["yy"], out=yy[:], in0=z[:, 2:SEQ + 2],
              scalar=cw[:, 3 * g + 2:3 * g + 3], in1=yy[:], op0=ALU.mult, op1=ALU.add)
        c.vec("tensor_tensor", r=["yy", "bg"], w=["yy"], out=yy[:], in0=yy[:], in1=bg[:], op=ALU.mult)
        c.store("yy", oc[g], yy[:])

    pss = c.ps("pss", [128, 512])
    pS = [c.ps("pS%d" % i, [128, 1024]) for i in range(2)]
    pacc = c.ps("pacc", [128, 3, 512])
    Kr = c.sb("Kr", [128, SEQ], BF16); Qr = c.sb("Qr", [128, SEQ], BF16)
    Vaug = c.sb("Vaug", [128, NT, 132], BF16)
    xt = [c.sb("axt%d" % i, [128, 512]) for i in range(2)]
    xp = [c.sb("axp%d" % i, [128, 512]) for i in range(2)]
    sq = c.sb("sq", [128, 512]); rr = c.sb("rr", [128, 512]); t1 = c.sb("t1", [128, 512]); t2 = c.sb("t2", [128, 512])
    pT = [c.sb("pTs%d" % i, [128, 1024], BF16) for i in range(2)]
    rc = c.sb("rc", [128, 8]); o1 = [c.sb("o1_%d" % i, [128, 128]) for i in range(2)]
    odt = [c.sb("odt%d" % i, [128, 128]) for i in range(2)]
    sj = c.sb("sj", [128, 128]); ssd = c.sb("ssd", [128, 4])
    nb = 0
    it = 0
    ne = 0
    for hh in range(2):
        c.load("Vaug", Vaug[:, :, 0:128], vd[:, hh * 128:(hh + 1) * 128].rearrange("(t p) c -> p t c", p=128), eng="gpsimd")
        c.vec("memset", r=[], w=["Vones"], ap=Vaug[:, :, 128:129], constant=1.0)
        for (src, srcp, dst, dkey, gi) in ((kT, kTp, Kr, "Kr", 2), (qT, qTp, Qr, "Qr", 0)):
            for blk in range(SEQ // 512):
                b = nb % 2
                nb += 1
                bs_ = slice(blk * 512, (blk + 1) * 512)
                xk, pk = "axt%d" % b, "axp%d" % b
                c.load(xk, xt[b][:], src[hh, :, bs_])
                c.load(pk, xp[b][:], srcp[hh, :, bs_])
                c.act(sq[:], xt[b][:], AF.Square, r=[xk], w=["sq"])
                c.pe(lambda e: e.matmul(pss[:], lhsT=bones[:], rhs=sq[:], start=True, stop=True), r=["sq", "bones"], w=["pss"])
                c.act(rr[:], pss[:], AF.Ln, r=["pss", "epsb"], w=["rr"], scale=1.0 / 64, bias=epsb[:, 0:1])
                c.act(rr[:], rr[:], AF.Exp, r=["rr"], w=["rr"], scale=-0.5)
                c.vec("scalar_tensor_tensor", r=[xk, "gv", "cosT"], w=["t1"], out=t1[:], in0=xt[b][:], scalar=gv[:, gi:gi + 1],
                      in1=cosT[:, bs_], op0=ALU.mult, op1=ALU.mult)
                c.vec("scalar_tensor_tensor", r=[pk, "gv", "sinT"], w=["t2"], out=t2[:], in0=xp[b][:], scalar=gv[:, gi + 1:gi + 2],
                      in1=sinT[:, bs_], op0=ALU.mult, op1=ALU.mult)
                c.vec("tensor_tensor", r=["t1", "t2"], w=["t1"], out=t1[:], in0=t1[:], in1=t2[:], op=ALU.add)
                c.vec("tensor_tensor", r=["t1", "rr"], w=[dkey + "%d" % blk], out=dst[:, bs_], in0=t1[:], in1=rr[:], op=ALU.mult)
        kkeys = ["Kr%d" % j for j in range(8)]
        for qb in range(SEQ // 512):
            for kt in range(NT):
                sb_ = it % 2
                it += 1

                def mms(e, sb_=sb_, kt=kt, qb=qb):
                    e.matmul(pS[sb_][:, 0:512], lhsT=Kr[0:64, kt * 128:(kt + 1) * 128], rhs=Qr[0:64, qb * 512:(qb + 1) * 512],
                             start=True, stop=True)
                    return e.matmul(pS[sb_][:, 512:1024], lhsT=Kr[64:128, kt * 128:(kt + 1) * 128],
                                    rhs=Qr[64:128, qb * 512:(qb + 1) * 512], start=True, stop=True)

                c.pe(mms, r=["Kr%d" % (kt // 4), "Qr%d" % qb], w=["pS%d" % sb_])
                c.act(pT[sb_][:], pS[sb_][:], AF.Exp, r=["pS%d" % sb_], w=["pTs%d" % sb_], scale=0.125)

                def mmv(e, sb_=sb_, kt=kt):
                    for a in range(8):
                        bank, off = a // 3, (a % 3) * 132
                        ins = e.matmul(pacc[:, bank, off:off + 129], lhsT=pT[sb_][:, a * 128:(a + 1) * 128], rhs=Vaug[:, kt, 0:129],
                                       start=(kt == 0 and a % 3 == 0), stop=(kt == NT - 1), skip_group_check=True)
                    return ins

                c.pe(mmv, r=["pTs%d" % sb_, "Vaug", "Vones"], w=["pacc"])
            for a in range(8):
                bank, off = a // 3, (a % 3) * 132
                c.vec("reciprocal", r=["pacc"], w=["rc%d" % a], out=rc[:, a:a + 1], in_=pacc[:, bank, off + 128:off + 129])
            c.vec("tensor_scalar", r=["rc%d" % a for a in range(4, 8)] + ["nlam"], w=["rc%d" % a for a in range(4, 8)],
                  out=rc[:, 4:8], in0=rc[:, 4:8], scalar1=nlam[:, 0:1], scalar2=None, op0=ALU.mult)
            for qs in range(4):
                e_ = ne % 2
                ne += 1
                a1, a2 = qs, 4 + qs
                c.vec("tensor_scalar", r=["pacc", "rc%d" % a1], w=["o1_%d" % e_], out=o1[e_][:],
                      in0=pacc[:, a1 // 3, (a1 % 3) * 132:(a1 % 3) * 132 + 128], scalar1=rc[:, a1:a1 + 1], scalar2=None, op0=ALU.mult)
                c.vec("scalar_tensor_tensor", r=["pacc", "rc%d" % a2, "o1_%d" % e_], w=["o1_%d" % e_], out=o1[e_][:],
                      in0=pacc[:, a2 // 3, (a2 % 3) * 132:(a2 % 3) * 132 + 128], scalar=rc[:, a2:a2 + 1], in1=o1[e_][:],
                      op0=ALU.mult, op1=ALU.add)
                c.act(sj[:], o1[e_][:], AF.Square, r=["o1_%d" % e_], w=["sj", "ssd"], accum_out=ssd[:, 0:1])
                rstd_ops(c, ssd, 1, 1.0 / 128, ["ssd"], "ssd")
                c.vec("scalar_tensor_tensor", r=["o1_%d" % e_, "ssd", "subg"], w=["odt%d" % e_], out=odt[e_][:], in0=o1[e_][:],
                      scalar=ssd[:, 0:1], in1=subg[:], op0=ALU.mult, op1=ALU.mult)
                ti = qb * 4 + qs
                c.store("odt%d" % e_, od[ti * 128:(ti + 1) * 128, hh * 128:(hh + 1) * 128], odt[e_][:])
    return c.finish()


def _rope_tables():
    inv = 1.0 / (10000.0 ** (np.arange(0, 64, 2, dtype=np.float32) / 64.0))
    ang = np.arange(SEQ, dtype=np.float32)[:, None] * inv[None, :]
    cos, sin = np.cos(ang).astype(np.float32).T, np.sin(ang).astype(np.float32).T
    cosT = np.concatenate([cos, cos, cos, cos], 0)
    sinT = np.concatenate([-sin, sin, -sin, sin], 0)
    return np.ascontiguousarray(cosT), np.ascontiguousarray(sinT)


def _perm64(a):
    return np.concatenate([a[32:64], a[0:32], a[96:128], a[64:96]], 0)


_BONES = np.kron(np.eye(2, dtype=np.float32), np.ones((64, 64), np.float32))


def run_modd(y, o, layer, inp):
    nc = _prog("Mo", build_modd)
    cosT, sinT = _rope_tables()
    lam_init = 0.8 - 0.6 * math.exp(-0.3 * layer)
    qg, kg = np.tile(inp["q_norm_g"][o], 2), np.tile(inp["k_norm_g"][o], 2)
    gv = np.stack([qg, _perm64(qg), kg, _perm64(kg)], 1).astype(np.float32)
    lamv = np.stack([_bc(inp[k][o]) for k in ("lambda_q1", "lambda_k1", "lambda_q2", "lambda_k2")], 1)
    lconst = np.stack([np.full(128, lam_init, np.float32), np.full(128, 1.0 - lam_init, np.float32)], 1)
    maps = []
    for c in range(NCORES):
        b, hp = c // 2, c % 2
        yb = y[b * SEQ:(b + 1) * SEQ]
        hin, bg, cg, q, k, v = (yb[:, j * 512:(j + 1) * 512] for j in range(6))
        grp = [2 * hp, 2 * hp + 1]
        fm = lambda a: np.ascontiguousarray(np.stack([a[:, g * 128:(g + 1) * 128].T for g in grp]))
        fmp = lambda a: np.ascontiguousarray(np.stack([_perm64(a[:, g * 128:(g + 1) * 128].T) for g in grp]))
        cw = np.concatenate([inp["conv_w"][o][:, g * 128:(g + 1) * 128].T for g in grp], 1)
        maps.append(dict(hinT=fm(hin), bgT=fm(bg), cgT=fm(cg), cw=np.ascontiguousarray(cw),
                         qT=fm(q), qTp=fmp(q), kT=fm(k), kTp=fmp(k), v=np.ascontiguousarray(v[:, hp * 256:hp * 256 + 256]),
                         gv=gv, cosT=cosT, sinT=sinT, lamv=np.ascontiguousarray(lamv), lconst=lconst,
                         subg=_bc(inp["diff_norm_g"][o]), bones=_BONES))
    r = _run(nc, maps)
    cat = np.empty((4 * SEQ, D), np.float32)
    for c in range(NCORES):
        b, hp = c // 2, c % 2
        for gi in range(2):
            cat[b * SEQ:(b + 1) * SEQ, (2 * hp + gi) * 128:(2 * hp + gi + 1) * 128] = r[c]["oc"][gi].T
        cat[b * SEQ:(b + 1) * SEQ, 512 + hp * 256:512 + hp * 256 + 256] = r[c]["od"]
    return cat


def kernel(**inputs):
    inp = {k: np.asarray(v) for k, v in inputs.items()}
    x = np.ascontiguousarray(inp["x"].reshape(-1, D).astype(np.float32))
    for l in range(4):
        if l % 2 == 0:
            e = l // 2
            y = run_proj(x, inp["norm_mix_g"][l], inp["w_in_even"][e])
            cat = run_meven(y, e, inp)
            wo = inp["w_out_even"][e]
        else:
            o = l // 2
            y = run_proj(x, inp["norm_mix_g"][l], inp["w_in_odd"][o])
            cat = run_modd(y, o, l, inp)
            wo = inp["w_out_odd"][o]
        x = run_outmlp(x, cat, wo, inp["norm_mlp_g"][l], inp["mlp_w1"][l], inp["mlp_w2"][l])
    return x.reshape(4, SEQ, D).astype(np.float32)


GROUPS = [[0, 1], [2, 3], [4, 5], [6, 7]]
NTI = TOK // 128
v3 = lambda t: t[:].rearrange("p (t c) -> p t c", c=128)


def allgather(c, in_ap, out_ap, rkey, wkey):
    slot = c.P.slot()

    def fn(e):
        return e.collective_compute("AllGather", ALU.bypass, replica_groups=GROUPS, ins=[in_ap], outs=[out_ap]).then_inc(slot.sem)

    return c.P.add("gpsimd", fn, r=[rkey], w=[wkey], slot=slot, dval=1)


def f_proj(c, xres, idb, w_d, gT_d, Ntot, tm_blocks, fm_chunks, y_tm, y_fm):
    c.stage_begin()
    ntm = len(tm_blocks)
    wbf = c.sb("wbf", [128, 8, Ntot], BF16)
    gT = c.sb("gT", [128, 8]); c.load_const("gT", gT[:], gT_d)
    junk = [c.sb("junk%d" % i, [128, D]) for i in range(2)]
    xs = [c.sb("xs%d" % i, [128, D], BF16) for i in range(2)]
    ss = [c.sb("ss%d" % i, [128, 4]) for i in range(2)]
    hT = [c.sb("hT%d" % i, [128, 8, 512], BF16) for i in range(2)]
    yt = [c.sb("yt%d" % i, [128, max(ntm, 1) * 512]) for i in range(2)]
    yf = [c.sb("yf%d" % i, [128, 512]) for i in range(3)]
    pT = [c.ps("pT%d" % i, [128, 1024], BF16) for i in range(2)]
    pm = [c.ps("pm%d" % i, [128, 512]) for i in range(4)]
    c.const_done()
    wv = w_d.rearrange("(k p) n -> p k n", p=128)
    order = [cs // 512 for cs in tm_blocks]
    for cs in fm_chunks:
        if cs // 512 not in order:
            order.append(cs // 512)
    for sl in order:
        c.load("wslab%d" % sl, wbf[:, :, sl * 512:(sl + 1) * 512], wv[:, :, sl * 512:(sl + 1) * 512], eng="gpsimd")
    n = 0
    nf = 0
    def norm_tb(tb):
        hb = tb % 2
        for j in range(4):
            i = tb * 4 + j
            norm_transpose(c, i, xres[:, i, :], "xres%d" % i, junk, ss, xs, pT, hT[hb][:, :, j * 128:(j + 1) * 128],
                           "hT%d_%d" % (hb, j), gT, idb)

    norm_tb(0)
    for tb in range(TOK // 512):
        hb = tb % 2
        if tb + 1 < TOK // 512:
            norm_tb(tb + 1)
        hkeys = ["hT%d_%d" % (hb, j) for j in range(4)]
        for j in range(4):
            if ntm == 0:
                break
            i = tb * 4 + j
            yb = i % 2
            for bi, cs in enumerate(tm_blocks):
                pb = n % 4
                n += 1

                def mm(e, hb=hb, j=j, cs=cs, pb=pb):
                    for k in range(8):
                        ins = e.matmul(pm[pb][:], lhsT=hT[hb][:, k, j * 128:(j + 1) * 128], rhs=wbf[:, k, cs:cs + 512],
                                       start=(k == 0), stop=(k == 7))
                    return ins

                c.pe(mm, r=["hT%d_%d" % (hb, j), "wslab%d" % (cs // 512)], w=["pm%d" % pb])
                if bi % 2 == 0:
                    c.act(yt[yb][:, bi * 512:(bi + 1) * 512], pm[pb][:], AF.Copy, r=["pm%d" % pb], w=["yt%d" % yb])
                else:
                    c.vec("tensor_copy", r=["pm%d" % pb], w=["yt%d" % yb], out=yt[yb][:, bi * 512:(bi + 1) * 512], in_=pm[pb][:])
            c.P.dma("sync", c.slot("st_yt%d" % yb), y_tm[i * 128:(i + 1) * 128, 0:ntm * 512], yt[yb][:, 0:ntm * 512],
                    r=["yt%d" % yb], w=["ytm%d" % i])
        for fi, cs in enumerate(fm_chunks):
            pb = n % 4
            n += 1
            fb = nf % 3
            nf += 1

            def mmf(e, hb=hb, cs=cs, pb=pb):
                for k in range(8):
                    ins = e.matmul(pm[pb][:], lhsT=wbf[:, k, cs:cs + 128], rhs=hT[hb][:, k, :], start=(k == 0), stop=(k == 7))
                return ins

            c.pe(mmf, r=hkeys + ["wslab%d" % (cs // 512)], w=["pm%d" % pb])
            if fi % 2 == 0:
                c.act(yf[fb][:], pm[pb][:], AF.Copy, r=["pm%d" % pb], w=["yf%d" % fb])
            else:
                c.vec("tensor_copy", r=["pm%d" % pb], w=["yf%d" % fb], out=yf[fb][:], in_=pm[pb][:])
            c.P.dma("sync", c.slot("st_yf%d" % fb), y_fm[fi * 128:(fi + 1) * 128, tb * 512:(tb + 1) * 512], yf[fb][:],
                    r=["yf%d" % fb], w=["yfm%d_%d" % (fi, tb)])
    c.stage_end()


def f_outmlp(c, xres, idb, cat_tm, cat_fm, n_fm, wo, gT_d, w1, w2):
    c.stage_begin()
    hTa = c.sb("hTa", [128, 8, TOK], BF16)
    wob = c.sb("wob", [128, 8, D], BF16)
    gT = c.sb("gT", [128, 8]); c.load_const("gT", gT[:], gT_d)
    catb = [c.sb("catb%d" % i, [128, D], BF16) for i in range(2)]
    catT = [c.sb("catT%d" % i, [128, 8, 128], BF16) for i in range(2)]
    junk = [c.sb("junk%d" % i, [128, D]) for i in range(2)]
    xs = [c.sb("xs%d" % i, [128, D], BF16) for i in range(2)]
    ss = [c.sb("ss%d" % i, [128, 4]) for i in range(2)]
    w1g = [c.sb("w1g%d" % i, [128, 8, GS], BF16) for i in range(2)]
    w2g = [c.sb("w2g%d" % i, [128, GS // 128, D], BF16) for i in range(2)]
    rl = [c.sb("rl%d" % i, [128, 512]) for i in range(2)]
    actb = [c.sb("actb%d" % i, [128, GS // 128, 512], BF16) for i in range(2)]
    pT = [c.ps("pT%d" % i, [128, 1024], BF16) for i in range(2)]
    pN = [c.ps("pN%d" % i, [128, 1024], BF16) for i in range(2)]
    pm = [c.ps("pm%d" % i, [128, 512]) for i in range(4)]
    c.const_done()
    c.load("wob", wob[:], wo.rearrange("(k p) n -> p k n", p=128), eng="gpsimd")
    wokeys = ["wob"]
    n = 0
    c0 = n_fm * 128
    for i in range(NTI):
        b = i % 2
        c.P.dma("gpsimd", c.slot("catb%d" % b), catb[b][:, c0:D], cat_tm[i * 128:(i + 1) * 128, c0:D], r=["cat_tm"], w=["catb%d" % b])
        rk = ["pT%d" % b]
        if n_fm:
            c.P.dma("gpsimd", c.slot("catTf%d" % b), catT[b][:, 0:n_fm, :],
                    cat_fm.rearrange("(k p) t -> p k t", p=128)[:, 0:n_fm, i * 128:(i + 1) * 128], r=["cat_fm"], w=["catTf%d" % b])

        def tr(e, b=b):
            for k in range(n_fm, 8):
                ins = e.transpose(pT[b][:, k * 128:(k + 1) * 128], catb[b][:, k * 128:(k + 1) * 128], idb[:])
            return ins

        c.pe(tr, r=["catb%d" % b, "idb"], w=["pT%d" % b])
        c.vec("tensor_copy", r=["pT%d" % b], w=["catT%d" % b], out=catT[b][:, n_fm:8, :],
              in_=pT[b][:].rearrange("p (k t) -> p k t", t=128)[:, n_fm:8, :])
        for cb in range(2):
            pb = n % 4
            n += 1

            def mm(e, b=b, cb=cb, pb=pb):
                for k in range(8):
                    ins = e.matmul(pm[pb][:], lhsT=catT[b][:, k, :], rhs=wob[:, k, cb * 512:(cb + 1) * 512],
                                   start=(k == 0), stop=(k == 7))
                return ins

            c.pe(mm, r=["catT%d" % b, "catTf%d" % b] + wokeys, w=["pm%d" % pb])
            c.vec("tensor_tensor", r=["pm%d" % pb, "xres%d" % i], w=["xres%d" % i],
                  out=xres[:, i, cb * 512:(cb + 1) * 512], in0=pm[pb][:], in1=xres[:, i, cb * 512:(cb + 1) * 512], op=ALU.add)
        if i >= 2:
            norm_transpose(c, i - 2, xres[:, i - 2, :], "xres%d" % (i - 2), junk, ss, xs, pN, hTa[:, :, (i - 2) * 128:(i - 1) * 128],
                           "hTa%d" % (i - 2), gT, idb, pkey="pN")
    for i in (NTI - 2, NTI - 1):
        norm_transpose(c, i, xres[:, i, :], "xres%d" % i, junk, ss, xs, pN, hTa[:, :, i * 128:(i + 1) * 128],
                       "hTa%d" % i, gT, idb, pkey="pN")
    NG = 4 * D // GS
    CPG = GS // 128
    NTB = TOK // 512

    def wload(g):
        gb = g % 2
        c.load("w1g%d" % gb, w1g[gb][:], w1[:, g * GS:(g + 1) * GS].rearrange("(k p) c -> p k c", p=128), eng="gpsimd")
        c.load("w2g%d" % gb, w2g[gb][:], w2[g * GS:(g + 1) * GS, :].rearrange("(k p) c -> p k c", p=128), eng="gpsimd")

    cnt = [n]

    def stage1(g, tb, ab):
        gb = g % 2
        hkeys = ["hTa%d" % (tb * 4 + j) for j in range(4)]
        for cc in range(CPG):
            pb = cnt[0] % 4
            cnt[0] += 1
            rb = cnt[0] % 2

            def mm1(e, gb=gb, cc=cc, tb=tb, pb=pb):
                for k in range(8):
                    ins = e.matmul(pm[pb][:], lhsT=w1g[gb][:, k, cc * 128:(cc + 1) * 128],
                                   rhs=hTa[:, k, tb * 512:(tb + 1) * 512], start=(k == 0), stop=(k == 7))
                return ins

            c.pe(mm1, r=hkeys + ["w1g%d" % gb], w=["pm%d" % pb])
            c.vec("tensor_scalar", r=["pm%d" % pb], w=["rl%d" % rb], out=rl[rb][:], in0=pm[pb][:], scalar1=0.0,
                  scalar2=None, op0=ALU.max)
            c.act(actb[ab][:, cc, :], rl[rb][:], AF.Square, r=["rl%d" % rb], w=["actb%d_%d" % (ab, cc)])

    def stage2(g, tb, ab):
        gb = g % 2
        akeys = ["actb%d_%d" % (ab, cc) for cc in range(CPG)]
        for tt in range(4):
            ti = tb * 4 + tt
            for cb in range(2):
                pb = cnt[0] % 4
                cnt[0] += 1

                def mm2(e, gb=gb, ab=ab, tt=tt, cb=cb, pb=pb):
                    for cc in range(CPG):
                        ins = e.matmul(pm[pb][:], lhsT=actb[ab][:, cc, tt * 128:(tt + 1) * 128],
                                       rhs=w2g[gb][:, cc, cb * 512:(cb + 1) * 512], start=(cc == 0), stop=(cc == CPG - 1))
                    return ins

                c.pe(mm2, r=akeys + ["w2g%d" % gb], w=["pm%d" % pb])
                c.vec("tensor_tensor", r=["pm%d" % pb, "xres%d" % ti], w=["xres%d" % ti],
                      out=xres[:, ti, cb * 512:(cb + 1) * 512], in0=pm[pb][:], in1=xres[:, ti, cb * 512:(cb + 1) * 512],
                      op=ALU.add)

    items = [(g, tb) for g in range(NG) for tb in range(NTB)]
    wload(0)
    wload(1)
    stage1(items[0][0], items[0][1], 0)
    for i_, (g, tb) in enumerate(items):
        if i_ + 1 < len(items):
            stage1(items[i_ + 1][0], items[i_ + 1][1], (i_ + 1) % 2)
        stage2(g, tb, i_ % 2)
        if tb == NTB - 1 and g + 2 < NG:
            wload(g + 2)
    c.stage_end()


def f_meven(c, idb, cst, y_tm, y_fm, cat_tm, dd, lname):
    mF, mB, segm, sel = cst["mF"], cst["mB"], cst["segm"], cst["sel"]
    ex_in = c.dint("exs_in" + lname, [512, 128]); ex_out = c.dint("exs_out" + lname, [1024, 128])
    c.stage_begin()
    sgug = c.sb("sgug", [128, 512]); c.load_const("sgug", sgug[:], dd["sgug"])
    wsb = c.sb("wsb", [128, 4, 128], BF16); c.load_const("wsb", wsb[:], dd["wsT"], eng="gpsimd")
    bs = c.sb("bs", [128, 4]); c.load_const("bs", bs[:], dd["bs"])
    lbl = c.sb("lbl", [128, 16]); c.load_const("lbl", lbl[:], dd["lbl"])
    lbm = c.sb("lbm", [128, 1]); c.load_const("lbm", lbm[:], dd["lbm"])
    hng = c.sb("hng", [128, 512]); c.load_const("hng", hng[:], dd["hng"])
    c.const_done()
    onesb = c.sb("onesb", [128, 1]); c.vec("memset", r=[], w=["onesb"], ap=onesb[:], constant=1.0)
    epsb = c.sb("epsb", [128, 1]); c.vec("memset", r=[], w=["epsb"], ap=epsb[:], constant=EPS)
    lb8 = c.sb("lb8", [128, 8]); oml8 = c.sb("oml8", [128, 8]); noml8 = c.sb("noml8", [128, 8])
    l3 = lbl[:].rearrange("p (a e) -> p a e", e=2)
    c.vec("tensor_tensor", r=["lbl"], w=["lb8"], out=lb8[:].rearrange("p (a o) -> p a o", o=1), in0=l3[:, :, 1:2],
          in1=l3[:, :, 0:1], op=ALU.subtract)
    c.act(lb8[:], lb8[:], AF.Sigmoid, r=["lb8"], w=["lb8"])
    c.vec("tensor_scalar", r=["lb8", "lbm"], w=["lb8"], out=lb8[:], in0=lb8[:], scalar1=lbm[:, 0:1], scalar2=None, op0=ALU.mult)
    c.vec("tensor_scalar", r=["lb8"], w=["oml8"], out=oml8[:], in0=lb8[:], scalar1=-1.0, scalar2=1.0, op0=ALU.mult, op1=ALU.add)
    c.vec("tensor_scalar", r=["lb8"], w=["noml8"], out=noml8[:], in0=lb8[:], scalar1=1.0, scalar2=-1.0, op0=ALU.mult, op1=ALU.add)
    c.stage_begin()
    pms = [c.ps("pm%d" % i, [128, 512]) for i in range(4)]
    gu = c.sb("gu", [128, NTI, 512]); gvv = c.sb("gvv", [128, NTI, 512])
    vn = [c.sb("vn%d" % i, [128, 512], BF16) for i in range(4)]
    oa = [c.sb("oa%d" % i, [128, 512]) for i in range(4)]
    sj = c.sb("sj", [128, 128]); ssa = c.sb("ssa", [128, NTI * 4])
    HT_ = NTI // 2
    for q_ in range(2):
        rows = slice(q_ * HT_ * 128, (q_ + 1) * HT_ * 128)
        c.P.dma("sync", c.slot("gvv_h%d" % q_), gvv[:, q_ * HT_:(q_ + 1) * HT_, :],
                y_tm[rows, 512:1024].rearrange("(t p) c -> p t c", p=128), w=["gvv%d" % n for n in range(q_ * HT_, (q_ + 1) * HT_)])
        c.P.dma("sync", c.slot("gu_h%d" % q_), gu[:, q_ * HT_:(q_ + 1) * HT_, :],
                y_tm[rows, 0:512].rearrange("(t p) c -> p t c", p=128), w=["gu%d" % n for n in range(q_ * HT_, (q_ + 1) * HT_)])
    for n in range(NTI):
        c.act(gvv[:, n, :], gvv[:, n, :], AF.Gelu_apprx_tanh, r=["gvv%d" % n], w=["gvv%d" % n])
        c.act(gu[:, n, :], gu[:, n, :], AF.Gelu_apprx_tanh, r=["gu%d" % n], w=["gu%d" % n])
        for h in range(4):
            c.act(sj[:], gvv[:, n, h * 128:(h + 1) * 128], AF.Square, r=["gvv%d" % n], w=["sj", "ssa%d" % n],
                  accum_out=ssa[:, n * 4 + h:n * 4 + h + 1])
    c.vec("tensor_scalar", r=["ssa%d" % n for n in range(NTI)], w=["ssa"], out=ssa[:], in0=ssa[:], scalar1=1.0 / 128, scalar2=EPS,
          op0=ALU.mult, op1=ALU.add)
    c.act(ssa[:], ssa[:], AF.Sqrt, r=["ssa"], w=["ssa"])
    c.vec("reciprocal", r=["ssa"], w=["ssa"], out=ssa[:], in_=ssa[:])
    for n in range(NTI):
        b = n % 4
        nk, ok, pk = "vn%d" % b, "oa%d" % b, "pm%d" % b
        for h in range(4):
            hs = slice(h * 128, (h + 1) * 128)
            c.vec("scalar_tensor_tensor", r=["gvv%d" % n, "ssa", "sgug"], w=[nk], out=vn[b][:, hs], in0=gvv[:, n, hs],
                  scalar=ssa[:, n * 4 + h:n * 4 + h + 1], in1=sgug[:, hs], op0=ALU.mult, op1=ALU.mult)

        def mm(e, b=b):
            for h in range(4):
                ins = e.matmul(pms[b][:, h * 128:(h + 1) * 128], lhsT=wsb[:, h, :], rhs=vn[b][:, h * 128:(h + 1) * 128],
                               start=True, stop=True)
            return ins

        c.pe(mm, r=[nk, "wsb"], w=[pk])
        for h in range(4):
            hs = slice(h * 128, (h + 1) * 128)
            c.vec("scalar_tensor_tensor", r=[pk, "bs", "gu%d" % n], w=[ok], out=oa[b][:, hs], in0=pms[b][:, hs],
                  scalar=bs[:, h:h + 1], in1=gu[:, n, hs], op0=ALU.add, op1=ALU.mult)
        c.P.dma("sync", c.slot("st_" + ok), cat_tm[n * 128:(n + 1) * 128, 0:512], oa[b][:], r=[ok])
    c.stage_end()
    c.stage_begin()
    NCH = 2
    pkt = [c.ps("pkt%d" % s, [128, 1024], BF16) for s in range(NCH)]
    pa = [c.ps("pa%d" % s, [128, 512]) for s in range(NCH)]
    po = [c.ps("po%d" % s, [128, 512]) for s in range(NCH)]
    pu = [c.ps("pu%d" % s, [128, 512]) for s in range(NCH)]
    OF = c.sb("OF", [128, 4 * NTI, 128])
    B_ = []
    for s in range(NCH):
        d = {}
        d["A"] = [c.sb("A%d_%d" % (s, i), [128, 512]) for i in range(2)]
        d["Q"] = [c.sb("Q%d_%d" % (s, i), [128, 512]) for i in range(2)]
        d["IV"] = [c.sb("IV%d_%d" % (s, i), [128, 4, 128], BF16) for i in range(2)]
        d["GG"] = [c.sb("GG%d_%d" % (s, i), [128, 512]) for i in range(2)]
        for nm in ("L", "KK", "BFW", "BB", "BR", "EQ", "RC"):
            d[nm] = c.sb("%s%d" % (nm, s), [128, 512])
        for nm in ("QE", "KD", "QI", "kdT"):
            d[nm] = [c.sb("%s%d_%d" % (nm, s, i), [128, 512], BF16) for i in range(2)]
        d["KDEC"] = c.sb("KDEC%d" % s, [128, 512], BF16)
        d["KDZ"] = [[c.sb("KDZ%d_%d_%d" % (s, i, j), [128, 512], BF16) for j in range(2)] for i in range(2)]
        d["dS"] = [c.sb("dS%d_%d" % (s, i), [128, 4]) for i in range(2)]; d["nref"] = c.sb("nref%d" % s, [128, 4])
        d["attT"] = c.sb("attT%d" % s, [128, 128], BF16)
        d["S32"] = c.sb("S32_%d" % s, [128, 128]); d["Sbf"] = c.sb("Sbf%d" % s, [128, 128], BF16)
        d["SG"] = c.sb("SG%d" % s, [128, 2, 128])
        d["osum"] = c.sb("osum%d" % s, [128, 128]); d["obst"] = [c.sb("obst%d_%d" % (s, i), [128, 128]) for i in range(2)]
        d["sj2"] = c.sb("sj2_%d" % s, [128, 128]); d["ssh"] = c.sb("ssh%d" % s, [128, 4])
        for i in range(2):
            for j in range(2):
                c.vec("memset", r=[], w=["KDZ%d_c%d" % (j, s)], ap=d["KDZ"][i][j][:], constant=0.0)
        B_.append(d)

    def chain(s, hh, dr):
        d = B_[s]
        K = lambda n: "%s_c%d" % (n, s)
        L, KK, BFW, BB, BR, EQ, RC = d["L"], d["KK"], d["BFW"], d["BB"], d["BR"], d["EQ"], d["RC"]
        KDEC = d["KDEC"]
        attT, S32, Sbf, SG = d["attT"], d["S32"], d["Sbf"], d["SG"]
        osum, obst, sj2, ssh = d["osum"], d["obst"], d["sj2"], d["ssh"]
        col = dr * 4 + hh
        lbc, omlc = lb8[:, col:col + 1], oml8[:, col:col + 1]
        if dr == 0:
            c.vec("memset", r=[], w=[K("S32")], ap=S32[:], constant=0.0)
            c.vec("memset", r=[], w=[K("Sbf")], ap=Sbf[:], constant=0.0)
        else:
            c.P.dma("sync", c.slot("SG%d" % s), SG[:], ex_out.rearrange("(r h p) v -> p r h v", r=2, h=4)[:, :, hh, :],
                    r=["ex_out"], w=[K("SG")])
            c.vec("tensor_scalar", r=[K("SG"), "sel"], w=[K("S32")], out=S32[:], in0=SG[:, 0, :], scalar1=sel[:, 0:1], scalar2=None,
                  op0=ALU.mult)
            c.vec("scalar_tensor_tensor", r=[K("SG"), "sel", K("S32")], w=[K("S32")], out=S32[:], in0=SG[:, 1, :], scalar=sel[:, 1:2],
                  in1=S32[:], op0=ALU.mult, op1=ALU.add)
            c.act(Sbf[:], S32[:], AF.Copy, r=[K("S32")], w=[K("Sbf")])
        yield
        tbs = list(range(TOK // 512)) if dr == 0 else list(range(TOK // 512 - 1, -1, -1))
        nob = [0]

        def pre(ntb, tb):
            p2 = ntb % 2
            P2 = lambda n: K("%s%d" % (n, p2))
            ak, qk, ik, gk = P2("A"), P2("Q"), P2("IV"), P2("GG")
            a_, q_, IVt, GGt = d["A"][p2], d["Q"][p2], d["IV"][p2], d["GG"][p2]
            QE, KD, QI, kdT, dS, KDZ = d["QE"][p2], d["KD"][p2], d["QI"][p2], d["kdT"][p2], d["dS"][p2], d["KDZ"][dr][p2]
            tbs_ = slice(tb * 512, (tb + 1) * 512)
            c.load(ak, a_[:], y_fm[512 + dr * 512 + hh * 128:512 + dr * 512 + (hh + 1) * 128, tbs_])
            c.load(qk, q_[:], y_fm[hh * 128:(hh + 1) * 128, tbs_])
            c.load(ik, IVt[:], y_tm[tbs_, 1024 + hh * 128:1024 + (hh + 1) * 128].rearrange("(t p) c -> p t c", p=128), eng="gpsimd")
            if dr == 1:
                c.load(gk, v3(GGt), y_tm[tbs_, 1536 + hh * 128:1536 + (hh + 1) * 128].rearrange("(t p) c -> p t c", p=128))
                c.act(RC[:], GGt[:], AF.Exp, r=[gk], w=[K("RC")], scale=-1.0)
                c.act(RC[:], RC[:], AF.Ln, r=[K("RC"), "onesb"], w=[K("RC")], bias=onesb[:, 0:1])
                c.act(RC[:], RC[:], AF.Exp, r=[K("RC")], w=[K("RC")], scale=-1.0)
                yield
                c.vec("tensor_tensor", r=[K("RC"), gk], w=[gk], out=GGt[:], in0=GGt[:], in1=RC[:], op=ALU.mult)
                yield
            c.act(a_[:], a_[:], AF.Exp, r=[ak], w=[ak], scale=-1.0)
            yield
            c.act(L[:], a_[:], AF.Ln, r=[ak, "lb8", "onesb"], w=[K("L")], scale=lbc, bias=onesb[:, 0:1])
            c.act(RC[:], a_[:], AF.Ln, r=[ak, "onesb"], w=[K("RC")], bias=onesb[:, 0:1])
            yield
            c.vec("tensor_tensor", r=[K("L"), K("RC")], w=[K("L")], out=L[:], in0=L[:], in1=RC[:], op=ALU.subtract)
            c.act(RC[:], RC[:], AF.Exp, r=[K("RC"), K("L")], w=[K("RC")], scale=-1.0)
            yield
            c.vec("scalar_tensor_tensor", r=[ak, "oml8", K("RC")], w=[K("KK")], out=KK[:], in0=a_[:], scalar=omlc, in1=RC[:],
                  op0=ALU.mult, op1=ALU.mult)
            yield
            c.vec("tensor_tensor_scan", r=[K("L"), "segm"], w=[K("BFW")], out=BFW[:], data0=segm[:], data1=L[:], initial=0.0,
                  op0=ALU.mult, op1=ALU.add)
            yield
            if dr == 0:
                Bt, bkey, ri, li = BFW, K("BFW"), 63, 127
            else:
                c.vec("tensor_tensor", r=[K("L"), K("BFW")], w=[K("L")], out=L[:], in0=L[:], in1=BFW[:], op=ALU.subtract)
                c.vec("tensor_tensor", r=[K("L"), K("BFW")], w=[K("BB")], out=v3(BB), in0=v3(L),
                      in1=v3(BFW)[:, :, 127:128].to_broadcast([128, 4, 128]), op=ALU.add)
                Bt, bkey, ri, li = BB, K("BB"), 64, 0
            B3 = v3(Bt)
            c.vec("tensor_scalar", r=[bkey], w=[K("nref")], out=d["nref"][:].rearrange("p (t o) -> p t o", o=1),
                  in0=B3[:, :, ri:ri + 1], scalar1=-1.0, scalar2=None, op0=ALU.mult)
            yield
            for t in range(4):
                c.act(EQ[:, t * 128:(t + 1) * 128], Bt[:, t * 128:(t + 1) * 128], AF.Exp, r=[bkey, K("nref")], w=[K("EQ")],
                      bias=d["nref"][:, t:t + 1])
            yield
            c.vec("tensor_tensor", r=[qk, K("EQ")], w=[P2("QE")], out=QE[:], in0=q_[:], in1=EQ[:], op=ALU.mult)
            for t in range(4):
                c.act(BR[:, t * 128:(t + 1) * 128], Bt[:, t * 128:(t + 1) * 128], AF.Exp, r=[bkey], w=[K("BR")], scale=-1.0,
                      bias=B3[:, t, ri:ri + 1])
            yield
            c.vec("tensor_tensor", r=[K("KK"), K("BR")], w=[P2("KD")], out=KD[:], in0=KK[:], in1=BR[:], op=ALU.mult)
            hsl = slice(0, 64) if dr == 0 else slice(64, 128)
            c.vec("tensor_tensor", r=[K("KK"), K("BR")], w=[P2("KDZ")], out=v3(KDZ)[:, :, hsl], in0=v3(KK)[:, :, hsl],
                  in1=v3(BR)[:, :, hsl], op=ALU.mult)
            c.act(EQ[:], Bt[:], AF.Exp, r=[bkey, P2("QE")], w=[K("EQ")])
            yield
            c.vec("tensor_tensor", r=[qk, K("EQ")], w=[P2("QI")], out=QI[:], in0=q_[:], in1=EQ[:], op=ALU.mult)
            for t in range(4):
                c.act(BR[:, t * 128:(t + 1) * 128], Bt[:, t * 128:(t + 1) * 128], AF.Exp, r=[bkey, P2("KD"), P2("KDZ")],
                      w=[K("BR")], scale=-1.0, bias=B3[:, t, li:li + 1])
            c.act(dS[:].rearrange("p (t o) -> p t o", o=1), B3[:, :, li:li + 1], AF.Exp, r=[bkey], w=[P2("dS")])
            yield
            c.vec("tensor_tensor", r=[K("KK"), K("BR")], w=[K("KDEC")], out=KDEC[:], in0=KK[:], in1=BR[:], op=ALU.mult)
            yield

            def trk(e):
                for t in range(4):
                    ins = e.transpose(pkt[s][:, t * 128:(t + 1) * 128], KDEC[:, t * 128:(t + 1) * 128], idb[:])
                return ins

            c.pe(trk, r=[K("KDEC"), "idb"], w=[K("pkt")])
            yield
            c.act(kdT[:], pkt[s][:, 0:512], AF.Copy, r=[K("pkt")], w=[P2("kdT")])
            yield

        def tiles(ntb, tb):
            p2 = ntb % 2
            P2 = lambda n: K("%s%d" % (n, p2))
            ik, gk = P2("IV"), P2("GG")
            IVt, GGt = d["IV"][p2], d["GG"][p2]
            QE, KD, QI, kdT, dS, KDZ = d["QE"][p2], d["KD"][p2], d["QI"][p2], d["kdT"][p2], d["dS"][p2], d["KDZ"][dr][p2]
            tts = range(4) if dr == 0 else range(3, -1, -1)
            for tt in tts:
                ti = tb * 4 + tt
                oi = hh * NTI + ti
                kA, kB = (KDZ, KD) if dr == 0 else (KD, KDZ)

                def mma(e, tt=tt, kA=kA, kB=kB, QE=QE):
                    e.matmul(pa[s][:, 0:64], lhsT=kA[:, tt * 128:(tt + 1) * 128], rhs=QE[:, tt * 128:tt * 128 + 64],
                             start=True, stop=True)
                    return e.matmul(pa[s][:, 64:128], lhsT=kB[:, tt * 128:(tt + 1) * 128],
                                    rhs=QE[:, tt * 128 + 64:(tt + 1) * 128], start=True, stop=True)

                c.pe(mma, r=[P2("KD"), P2("KDZ"), P2("QE")], w=[K("pa")])
                c.pe(lambda e, tt=tt, IVt=IVt, kdT=kdT: e.matmul(pu[s][:, 0:128], lhsT=kdT[:, tt * 128:(tt + 1) * 128],
                                                                 rhs=IVt[:, tt, :], start=True, stop=True),
                     r=[P2("kdT"), ik], w=[K("pu")])
                yield
                c.vec("tensor_tensor", r=[K("pa"), "mF", "mB"], w=[K("attT")], out=attT[:], in0=pa[s][:, 0:128],
                      in1=(mF if dr == 0 else mB)[:], op=ALU.mult)
                yield

                def mmo(e, tt=tt, IVt=IVt, QI=QI):
                    e.matmul(po[s][:, 0:128], lhsT=QI[:, tt * 128:(tt + 1) * 128], rhs=Sbf[:], start=True, stop=False)
                    return e.matmul(po[s][:, 0:128], lhsT=attT[:], rhs=IVt[:, tt, :], start=False, stop=True)

                c.pe(mmo, r=[P2("QI"), K("Sbf"), K("attT"), ik], w=[K("po")])
                yield
                c.vec("scalar_tensor_tensor", r=[K("S32"), P2("dS"), K("pu")], w=[K("S32")], out=S32[:], in0=S32[:],
                      scalar=dS[:, tt:tt + 1], in1=pu[s][:, 0:128], op0=ALU.mult, op1=ALU.add)
                yield
                c.act(Sbf[:], S32[:], AF.Copy, r=[K("S32")], w=[K("Sbf")])
                if dr == 0:
                    c.act(OF[:, oi, :], po[s][:, 0:128], AF.Copy, r=[K("po")], w=["OF%d" % oi])
                    yield
                else:
                    ob_ = nob[0] % 2
                    nob[0] += 1
                    c.vec("tensor_tensor", r=[K("po"), "OF%d" % oi], w=[K("osum")], out=osum[:],
                          in0=po[s][:, 0:128], in1=OF[:, oi, :], op=ALU.add)
                    yield
                    c.act(sj2[:], osum[:], AF.Square, r=[K("osum")], w=[K("sj2"), K("ssh")], accum_out=ssh[:, 0:1])
                    yield
                    c.act(ssh[:, 0:1], ssh[:, 0:1], AF.Ln, r=[K("ssh"), "epsb"], w=[K("ssh")], scale=1.0 / 128, bias=epsb[:, 0:1])
                    c.act(ssh[:, 0:1], ssh[:, 0:1], AF.Exp, r=[K("ssh")], w=[K("ssh")], scale=-0.5)
                    yield
                    c.vec("scalar_tensor_tensor", r=[K("osum"), K("ssh"), "hng"], w=[K("osum")], out=osum[:],
                          in0=osum[:], scalar=ssh[:, 0:1], in1=hng[:, hh * 128:(hh + 1) * 128], op0=ALU.mult, op1=ALU.mult)
                    c.vec("tensor_tensor", r=[K("osum"), gk], w=[K("obst%d" % ob_)], out=obst[ob_][:], in0=osum[:],
                          in1=GGt[:, tt * 128:(tt + 1) * 128], op=ALU.mult)
                    c.P.dma("sync", c.slot("st_obst%d_%d" % (s, ob_)), cat_tm[ti * 128:(ti + 1) * 128, 512 + hh * 128:512 + (hh + 1) * 128],
                            obst[ob_][:], r=[K("obst%d" % ob_)])
                    yield

        for _ in pre(0, tbs[0]):
            yield
        for n_ in range(len(tbs)):
            subs = [tiles(n_, tbs[n_])]
            if n_ + 1 < len(tbs):
                subs.append(pre(n_ + 1, tbs[n_ + 1]))
            live = [True] * len(subs)
            while any(live):
                for k_ in range(len(subs)):
                    if live[k_]:
                        try:
                            next(subs[k_])
                            yield
                        except StopIteration:
                            live[k_] = False
        if dr == 0:
            c.P.dma("sync", c.slot("st_S32_%d" % s), ex_in[hh * 128:(hh + 1) * 128, :], S32[:], r=[K("S32")], w=["ex_in%d" % hh])
        yield

    for dr in range(2):
        if dr == 1:
            slot_ag = c.P.slot()

            def agfn(e, slot_ag=slot_ag):
                return e.collective_compute("AllGather", ALU.bypass, replica_groups=GROUPS, ins=[ex_in], outs=[ex_out]).then_inc(slot_ag.sem)

            c.P.add("gpsimd", agfn, r=["ex_in%d" % h for h in range(4)], w=["ex_out"], slot=slot_ag, dval=1)
        for h0 in range(0, 4, NCH):
            gens = [chain(s, h0 + s, dr) for s in range(NCH)]
            alive = [True] * NCH
            while any(alive):
                for s in range(NCH):
                    if alive[s]:
                        try:
                            next(gens[s])
                        except StopIteration:
                            alive[s] = False
    c.stage_end()
    c.stage_end()


def f_modd(c, idb, cst, y_tm, y_fm, cat_tm, cat_fm, dd, lname):
    sel = cst["sel"]
    zx_in = c.dint("zx_in" + lname, [128, 4]); zx_out = c.dint("zx_out" + lname, [256, 4])
    kx_in = c.dint("kx_in" + lname, [128, 4 * TOK], BF16); kx_out = c.dint("kx_out" + lname, [256, 4 * TOK], BF16)
    vx_in = c.dint("vx_in" + lname, [TOK, 512], BF16); vx_out = c.dint("vx_out" + lname, [2 * TOK, 512], BF16)
    NKT = 2 * NTI
    c.stage_begin()
    cw = c.sb("cw", [128, 12]); c.load_const("cw", cw[:], dd["cw"])
    gv = c.sb("gv", [128, 4]); c.load_const("gv", gv[:], dd["gv"])
    cosT = c.sb("cosT", [128, TOK]); c.load_const("cosT", cosT[:], cst["cosT"])
    sinT = c.sb("sinT", [128, TOK]); c.load_const("sinT", sinT[:], cst["sinT"])
    lamv = c.sb("lamv", [128, 4, 64]); c.load_const("lamv", lamv[:], dd["lamv"])
    lconst = c.sb("lconst", [128, 2]); c.load_const("lconst", lconst[:], dd["lconst"])
    subg = c.sb("subg", [128, 128]); c.load_const("subg", subg[:], dd["subg"])
    bones = c.sb("bones", [128, 128]); c.load_const("bones", bones[:], cst["bones"])
    c.const_done()
    epsb = c.sb("epsb", [128, 1]); c.vec("memset", r=[], w=["epsb"], ap=epsb[:], constant=EPS)
    lj = c.sb("lj", [128, 64]); ls = c.sb("ls", [128, 2]); nlam = c.sb("nlam", [128, 1])
    for j in range(2):
        c.vec("tensor_tensor", r=["lamv"], w=["lj"], out=lj[:], in0=lamv[:, 2 * j, :], in1=lamv[:, 2 * j + 1, :], op=ALU.mult)
        c.vec("tensor_reduce", r=["lj"], w=["ls%d" % j], out=ls[:, j:j + 1], in_=lj[:], axis=AX.X, op=ALU.add)
    c.act(ls[:], ls[:], AF.Exp, r=["ls0", "ls1"], w=["ls"])
    c.vec("tensor_tensor", r=["ls"], w=["nlam"], out=nlam[:], in0=ls[:, 1:2], in1=ls[:, 0:1], op=ALU.subtract)
    c.vec("tensor_tensor", r=["nlam", "lconst"], w=["nlam"], out=nlam[:], in0=nlam[:], in1=lconst[:, 0:1], op=ALU.subtract)
    c.vec("tensor_scalar", r=["subg", "lconst"], w=["subg"], out=subg[:], in0=subg[:], scalar1=lconst[:, 1:2], scalar2=None,
          op0=ALU.mult)
    z4 = c.sb("z4", [128, 4, TOK + 2])
    xt = [c.sb("axt%d" % i, [128, 512]) for i in range(2)]
    xp = [c.sb("axp%d" % i, [128, 512]) for i in range(2)]
    sq = c.sb("sq", [128, 512]); rr = c.sb("rr", [128, 512]); t1 = c.sb("t1", [128, 512]); t2 = c.sb("t2", [128, 512])
    pss = c.ps("pss", [128, 512])
    pS = [c.ps("pS%d" % i, [128, 1024]) for i in range(2)]
    pacc = c.ps("pacc", [128, 3, 512])
    nbc = [0]

    def qk_prep(rows0, rowsp0, hh, dst, dkey, gi, nblk):
        for blk in range(nblk):
            b = nbc[0] % 2
            nbc[0] += 1
            bs_ = slice(blk * 512, (blk + 1) * 512)
            xk, pk = "axt%d" % b, "axp%d" % b
            c.load(xk, xt[b][:], y_fm[rows0 + hh * 128:rows0 + (hh + 1) * 128, bs_])
            c.load(pk, xp[b][:], y_fm[rowsp0 + hh * 128:rowsp0 + (hh + 1) * 128, bs_])
            c.act(sq[:], xt[b][:], AF.Square, r=[xk], w=["sq"])
            c.pe(lambda e: e.matmul(pss[:], lhsT=bones[:], rhs=sq[:], start=True, stop=True), r=["sq", "bones"], w=["pss"])
            c.act(rr[:], pss[:], AF.Ln, r=["pss", "epsb"], w=["rr"], scale=1.0 / 64, bias=epsb[:, 0:1])
            c.act(rr[:], rr[:], AF.Exp, r=["rr"], w=["rr"], scale=-0.5)
            c.vec("scalar_tensor_tensor", r=[xk, "gv", "cosT"], w=["t1"], out=t1[:], in0=xt[b][:], scalar=gv[:, gi:gi + 1],
                  in1=cosT[:, bs_], op0=ALU.mult, op1=ALU.mult)
            c.vec("scalar_tensor_tensor", r=[pk, "gv", "sinT"], w=["t2"], out=t2[:], in0=xp[b][:], scalar=gv[:, gi + 1:gi + 2],
                  in1=sinT[:, bs_], op0=ALU.mult, op1=ALU.mult)
            c.vec("tensor_tensor", r=["t1", "t2"], w=["t1"], out=t1[:], in0=t1[:], in1=t2[:], op=ALU.add)
            c.vec("tensor_tensor", r=["t1", "rr"], w=[dkey], out=dst(blk), in0=t1[:], in1=rr[:], op=ALU.mult)

    c.stage_begin()
    hin = c.sb("hin", [128, TOK]); cg = c.sb("cg", [128, TOK])
    Kown = c.sb("Kown", [128, 4, TOK], BF16)
    vbf = c.sb("vbf", [128, NTI, 512], BF16)
    c.vec("memset", r=[], w=["z4"], ap=z4[:], constant=0.0)
    for g in range(4):
        c.load("hin", hin[:], y_fm[g * 128:(g + 1) * 128, :])
        c.load("cg", cg[:], y_fm[1024 + g * 128:1024 + (g + 1) * 128, :])
        c.vec("tensor_tensor", r=["hin", "cg", "z4"], w=["z4_%d" % g], out=z4[:, g, 1:TOK + 1], in0=cg[:], in1=hin[:], op=ALU.mult)
    zc = c.sb("zc", [128, 4])
    c.vec("tensor_copy", r=["z4_%d" % g for g in range(4)], w=["zc"], out=zc[:], in_=z4[:, :, TOK])
    c.P.dma("sync", c.slot("st_zx"), zx_in, zc[:], r=["zc"], w=["zx_in"])
    allgather(c, zx_in, zx_out, "zx_in", "zx_out")
    c.load("vbf", vbf[:], y_tm[:, 0:512].rearrange("(t p) c -> p t c", p=128), eng="gpsimd")
    c.P.dma("sync", c.slot("st_vx"), vx_in.rearrange("(t p) c -> p t c", p=128), vbf[:], r=["vbf"], w=["vx_in"])
    allgather(c, vx_in, vx_out, "vx_in", "vx_out")
    for hh in range(4):
        qk_prep(2048, 3072, hh, lambda blk, hh=hh: Kown[:, hh, blk * 512:(blk + 1) * 512], "Kown", 2, TOK // 512)
    c.P.dma("sync", c.slot("st_kx"), kx_in, Kown[:].rearrange("p h t -> p (h t)"), r=["Kown"], w=["kx_in"])
    allgather(c, kx_in, kx_out, "kx_in", "kx_out")
    c.stage_end()
    c.stage_begin()
    bg = c.sb("bg", [128, TOK]); yy = c.sb("yy", [128, TOK]); ZG = c.sb("ZG", [128, 2, 4]); zh = c.sb("zh", [128, 4])
    c.P.dma("sync", c.slot("ZG"), ZG[:], zx_out.rearrange("(r p) g -> p r g", r=2), r=["zx_out"], w=["ZG"])
    c.vec("tensor_scalar", r=["ZG", "sel"], w=["zh"], out=zh[:], in0=ZG[:, 0, :], scalar1=sel[:, 0:1], scalar2=None, op0=ALU.mult)
    c.vec("scalar_tensor_tensor", r=["ZG", "sel", "zh"], w=["zh"], out=zh[:], in0=ZG[:, 1, :], scalar=sel[:, 1:2], in1=zh[:],
          op0=ALU.mult, op1=ALU.add)
    c.vec("tensor_copy", r=["zh"], w=["z4h"], out=z4[:, :, TOK + 1], in_=zh[:])
    for g in range(4):
        c.load("bg", bg[:], y_fm[512 + g * 128:512 + (g + 1) * 128, :])
        c.vec("tensor_scalar", r=["z4h", "cw", "yy"], w=["yy"], out=yy[:], in0=z4[:, g, 1:TOK + 1], scalar1=cw[:, 3 * g + 1:3 * g + 2],
              scalar2=None, op0=ALU.mult)
        c.vec("scalar_tensor_tensor", r=["cw", "yy"], w=["yy"], out=yy[:], in0=z4[:, g, 0:TOK], scalar=cw[:, 3 * g:3 * g + 1],
              in1=yy[:], op0=ALU.mult, op1=ALU.add)
        c.vec("scalar_tensor_tensor", r=["cw", "yy"], w=["yy"], out=yy[:], in0=z4[:, g, 2:TOK + 2],
              scalar=cw[:, 3 * g + 2:3 * g + 3], in1=yy[:], op0=ALU.mult, op1=ALU.add)
        c.vec("tensor_tensor", r=["yy", "bg"], w=["yy"], out=yy[:], in0=yy[:], in1=bg[:], op=ALU.mult)
        c.P.dma("sync", c.slot("st_yy"), cat_fm[g * 128:(g + 1) * 128, :], yy[:], r=["yy"])
    c.stage_end()
    c.stage_begin()
    Kr = [c.sb("Kr%d" % i, [128, 2 * TOK], BF16) for i in range(2)]
    Qr = [c.sb("Qr%d" % i, [128, TOK], BF16) for i in range(2)]
    Vaug = [c.sb("Vaug%d" % i, [128, NKT, 132], BF16) for i in range(2)]
    pT = [c.sb("pTs%d" % i, [128, 1024], BF16) for i in range(2)]
    rc = c.sb("rc", [128, 8]); o1 = [c.sb("o1_%d" % i, [128, 128]) for i in range(4)]
    odt = [c.sb("odt%d" % i, [128, 128]) for i in range(2)]
    sj = c.sb("sj", [128, 128]); ssd = c.sb("ssd", [128, 4])

    def head_loads(hh):
        hp = hh % 2
        for r_ in range(2):
            c.P.dma("sync", c.slot("Kr%d_%d" % (hp, r_)), Kr[hp][:, r_ * TOK:(r_ + 1) * TOK],
                    kx_out[r_ * 128:(r_ + 1) * 128, hh * TOK:(hh + 1) * TOK], r=["kx_out"], w=["Kr%d_%d" % (hp, r_)])
        c.P.dma("sync", c.slot("Vaug%d" % hp), Vaug[hp][:, :, 0:128],
                vx_out[:, hh * 128:(hh + 1) * 128].rearrange("(t p) c -> p t c", p=128), r=["vx_out"], w=["Vaug%d" % hp])
        c.vec("memset", r=[], w=["Vones%d" % hp], ap=Vaug[hp][:, :, 128:129], constant=1.0)

    def q_block(hh, blk):
        hp = hh % 2
        b = nbc[0] % 2
        nbc[0] += 1
        bs_ = slice(blk * 512, (blk + 1) * 512)
        xk, pk = "axt%d" % b, "axp%d" % b
        c.load(xk, xt[b][:], y_fm[1536 + hh * 128:1536 + (hh + 1) * 128, bs_])
        c.load(pk, xp[b][:], y_fm[2560 + hh * 128:2560 + (hh + 1) * 128, bs_])
        c.vec("tensor_tensor", r=[xk], w=["sq"], out=sq[:], in0=xt[b][:], in1=xt[b][:], op=ALU.mult)
        c.pe(lambda e: e.matmul(pss[:], lhsT=bones[:], rhs=sq[:], start=True, stop=True), r=["sq", "bones"], w=["pss"])
        c.act(rr[:], pss[:], AF.Ln, r=["pss", "epsb"], w=["rr"], scale=1.0 / 64, bias=epsb[:, 0:1])
        c.act(rr[:], rr[:], AF.Exp, r=["rr"], w=["rr"], scale=-0.5)
        c.vec("scalar_tensor_tensor", r=[xk, "gv", "cosT"], w=["t1"], out=t1[:], in0=xt[b][:], scalar=gv[:, 0:1],
              in1=cosT[:, bs_], op0=ALU.mult, op1=ALU.mult)
        c.vec("scalar_tensor_tensor", r=[pk, "gv", "sinT"], w=["t2"], out=t2[:], in0=xp[b][:], scalar=gv[:, 1:2],
              in1=sinT[:, bs_], op0=ALU.mult, op1=ALU.mult)
        c.vec("tensor_tensor", r=["t1", "t2"], w=["t1"], out=t1[:], in0=t1[:], in1=t2[:], op=ALU.add)
        c.vec("tensor_tensor", r=["t1", "rr"], w=["Qr%d_%d" % (hp, blk)], out=Qr[hp][:, bs_], in0=t1[:], in1=rr[:], op=ALU.mult)

    def sc_exp(hh, qb, kt, sb_):
        hp = hh % 2

        def mms(e):
            e.matmul(pS[sb_][:, 0:512], lhsT=Kr[hp][0:64, kt * 128:(kt + 1) * 128], rhs=Qr[hp][0:64, qb * 512:(qb + 1) * 512],
                     start=True, stop=True)
            return e.matmul(pS[sb_][:, 512:1024], lhsT=Kr[hp][64:128, kt * 128:(kt + 1) * 128],
                            rhs=Qr[hp][64:128, qb * 512:(qb + 1) * 512], start=True, stop=True)

        c.pe(mms, r=["Kr%d_%d" % (hp, kt // NTI), "Qr%d_%d" % (hp, qb)], w=["pS%d" % sb_])
        c.act(pT[sb_][:], pS[sb_][:], AF.Exp, r=["pS%d" % sb_], w=["pTs%d" % sb_], scale=0.125)

    def epi1():
        for a in range(8):
            bank, off = a // 3, (a % 3) * 132
            c.vec("reciprocal", r=["pacc"], w=["rc%d" % a], out=rc[:, a:a + 1], in_=pacc[:, bank, off + 128:off + 129])
        c.vec("tensor_scalar", r=["rc%d" % a for a in range(4, 8)] + ["nlam"], w=["rc%d" % a for a in range(4, 8)],
              out=rc[:, 4:8], in0=rc[:, 4:8], scalar1=nlam[:, 0:1], scalar2=None, op0=ALU.mult)
        for qs in range(4):
            a1, a2 = qs, 4 + qs
            c.vec("tensor_scalar", r=["pacc", "rc%d" % a1], w=["o1_%d" % qs], out=o1[qs][:],
                  in0=pacc[:, a1 // 3, (a1 % 3) * 132:(a1 % 3) * 132 + 128], scalar1=rc[:, a1:a1 + 1], scalar2=None, op0=ALU.mult)
            c.vec("scalar_tensor_tensor", r=["pacc", "rc%d" % a2, "o1_%d" % qs], w=["o1_%d" % qs], out=o1[qs][:],
                  in0=pacc[:, a2 // 3, (a2 % 3) * 132:(a2 % 3) * 132 + 128], scalar=rc[:, a2:a2 + 1], in1=o1[qs][:],
                  op0=ALU.mult, op1=ALU.add)

    def epi2(hh, qb, qs):
        e_ = qs % 2
        c.vec("tensor_tensor", r=["o1_%d" % qs], w=["sj"], out=sj[:], in0=o1[qs][:], in1=o1[qs][:], op=ALU.mult)
        c.vec("tensor_reduce", r=["sj"], w=["ssd"], out=ssd[:, 0:1], in_=sj[:], axis=AX.X, op=ALU.add)
        c.act(ssd[:, 0:1], ssd[:, 0:1], AF.Ln, r=["ssd", "epsb"], w=["ssd"], scale=1.0 / 128, bias=epsb[:, 0:1])
        c.act(ssd[:, 0:1], ssd[:, 0:1], AF.Exp, r=["ssd"], w=["ssd"], scale=-0.5)
        c.vec("scalar_tensor_tensor", r=["o1_%d" % qs, "ssd", "subg"], w=["odt%d" % e_], out=odt[e_][:], in0=o1[qs][:],
              scalar=ssd[:, 0:1], in1=subg[:], op0=ALU.mult, op1=ALU.mult)
        ti = qb * 4 + qs
        c.P.dma("sync", c.slot("st_odt%d" % e_), cat_tm[ti * 128:(ti + 1) * 128, 512 + hh * 128:512 + (hh + 1) * 128],
                odt[e_][:], r=["odt%d" % e_])

    its = [(hh, qb, kt) for hh in range(4) for qb in range(TOK // 512) for kt in range(NKT)]
    pending = {}
    head_loads(0)
    for blk in range(TOK // 512):
        q_block(0, blk)
    sc_exp(its[0][0], its[0][1], its[0][2], 0)
    for ii, (hh, qb, kt) in enumerate(its):
        sb_ = ii % 2
        if qb == 0 and kt == 0 and hh + 1 < 4:
            head_loads(hh + 1)
            for blk in range(TOK // 512):
                pending.setdefault(ii + 16 + 24 * blk, []).append(lambda hh=hh, blk=blk: q_block(hh + 1, blk))
        if ii + 1 < len(its):
            sc_exp(its[ii + 1][0], its[ii + 1][1], its[ii + 1][2], (ii + 1) % 2)

        def mmv(e, sb_=sb_, kt=kt, hp=hh % 2):
            for a in range(8):
                bank, off = a // 3, (a % 3) * 132
                ins = e.matmul(pacc[:, bank, off:off + 129], lhsT=pT[sb_][:, a * 128:(a + 1) * 128], rhs=Vaug[hp][:, kt, 0:129],
                               start=(kt == 0 and a % 3 == 0), stop=(kt == NKT - 1), skip_group_check=True)
            return ins

        c.pe(mmv, r=["pTs%d" % sb_, "Vaug%d" % (hh % 2), "Vones%d" % (hh % 2)], w=["pacc"])
        if kt == NKT - 1:
            epi1()
            for qs in range(4):
                pending.setdefault(ii + 2 + qs, []).append(lambda hh=hh, qb=qb, qs=qs: epi2(hh, qb, qs))
        for f in pending.pop(ii, []):
            f()
    for k in sorted(pending):
        for f in pending[k]:
            f()
    c.stage_end()
    c.stage_end()


EVEN_TM = [0, 512, 1536, 2048]
EVEN_FM = list(range(1024, 1536, 128)) + list(range(2560, 3584, 128))
ODD_TM = [2560]
ODD_FM = list(range(0, 2560, 128)) + list(range(3072, 4096, 128))


def build_fused(nlayers=4):
    c = Ctx()
    x = c.din("x", [TOK, D]); xo = c.dout("xo", [TOK, D])
    cst_d = {k: c.din(k, shp) for k, shp in (("ident", [128, 128]), ("maskF", [128, 128]), ("maskB", [128, 128]),
                                             ("segm", [128, 512]), ("sel", [128, 2]), ("bones", [128, 128]),
                                             ("cosT", [128, TOK]), ("sinT", [128, TOK]))}
    L = []
    for l in range(nlayers):
        d = dict(gmix=c.din("gmix%d" % l, [128, 8]), gmlp=c.din("gmlp%d" % l, [128, 8]),
                 w_in=c.din("w_in%d" % l, [D, 3584 if l % 2 == 0 else 4096]), w_out=c.din("w_out%d" % l, [D, D]),
                 w1=c.din("w1_%d" % l, [D, 4 * D]), w2=c.din("w2_%d" % l, [4 * D, D]))
        if l % 2 == 0:
            d.update(sgug=c.din("sgug%d" % l, [128, 512]), wsT=c.din("wsT%d" % l, [128, 4, 128]), bs=c.din("bs%d" % l, [128, 4]),
                     lbl=c.din("lbl%d" % l, [128, 16]), lbm=c.din("lbm%d" % l, [128, 1]), hng=c.din("hng%d" % l, [128, 512]))
        else:
            d.update(cw=c.din("cw%d" % l, [128, 12]), gv=c.din("gv%d" % l, [128, 4]), lamv=c.din("lamv%d" % l, [128, 4, 64]),
                     lconst=c.din("lconst%d" % l, [128, 2]), subg=c.din("subg%d" % l, [128, 128]))
        L.append(d)
    y_tm = c.dint("y_tm", [TOK, 2048]); y_fm = c.dint("y_fm", [3584, TOK])
    cat_tm = c.dint("cat_tm", [TOK, D]); cat_fm = c.dint("cat_fm", [512, TOK])
    xres = c.sb("xres", [128, NTI, D])
    idb = c.sb("idb", [128, 128], BF16); c.load_const("idb", idb[:], cst_d["ident"], eng="gpsimd")
    mF = c.sb("mF", [128, 128]); c.load_const("mF", mF[:], cst_d["maskF"])
    mB = c.sb("mB", [128, 128]); c.load_const("mB", mB[:], cst_d["maskB"])
    segm = c.sb("segm", [128, 512]); c.load_const("segm", segm[:], cst_d["segm"])
    sel = c.sb("sel", [128, 2]); c.load_const("sel", sel[:], cst_d["sel"])
    cst = dict(mF=mF, mB=mB, segm=segm, sel=sel, bones=cst_d["bones"], cosT=cst_d["cosT"], sinT=cst_d["sinT"])
    c.const_done()
    xkeys = ["xres%d" % i for i in range(NTI)]
    c.P.dma("sync", c.slot("xres"), xres[:], x.rearrange("(t p) d -> p t d", p=128), w=xkeys)
    for l in range(nlayers):
        d = L[l]
        if l % 2 == 0:
            f_proj(c, xres, idb, d["w_in"], d["gmix"], 3584, EVEN_TM, EVEN_FM, y_tm, y_fm)
            f_meven(c, idb, cst, y_tm, y_fm, cat_tm, d, "_%d" % l)
            f_outmlp(c, xres, idb, cat_tm, cat_fm, 0, d["w_out"], d["gmlp"], d["w1"], d["w2"])
        else:
            f_proj(c, xres, idb, d["w_in"], d["gmix"], 4096, ODD_TM, ODD_FM, y_tm, y_fm)
            f_modd(c, idb, cst, y_tm, y_fm, cat_tm, cat_fm, d, "_%d" % l)
            f_outmlp(c, xres, idb, cat_tm, cat_fm, 4, d["w_out"], d["gmlp"], d["w1"], d["w2"])
    c.P.store("sync", c.slot("st_xres"), xo.rearrange("(t p) d -> p t d", p=128), xres[:], r=xkeys)
    return c.finish()


def _perm_cols64(w):
    n = w.shape[1]
    idx = np.arange(n).reshape(n // 64, 2, 32)[:, ::-1, :].reshape(n)
    return w[:, idx]


def fused_inputs(inp, nlayers=4):
    maps = []
    inv = 1.0 / (10000.0 ** (np.arange(0, 64, 2, dtype=np.float32) / 64.0))
    for c_ in range(NCORES):
        b, r = c_ // 2, c_ % 2
        xs = inp["x"][b]
        xl = xs[0:TOK] if r == 0 else xs[::-1][0:TOK]
        pos = np.arange(TOK, dtype=np.float32) if r == 0 else (SEQ - 1 - np.arange(TOK)).astype(np.float32)
        ang = pos[:, None] * inv[None, :]
        cos, sin = np.cos(ang).astype(np.float32).T, np.sin(ang).astype(np.float32).T
        m = dict(x=np.ascontiguousarray(xl), ident=_IDENT, maskF=_MASKF, maskB=_MASKB, segm=_SEGM,
                 sel=np.ascontiguousarray(np.broadcast_to(np.array([[0.0, 1.0]] if r == 0 else [[1.0, 0.0]], np.float32), (128, 2))),
                 bones=_BONES, cosT=np.ascontiguousarray(np.concatenate([cos] * 4, 0)),
                 sinT=np.ascontiguousarray(np.concatenate([-sin, sin, -sin, sin], 0)))
        for l in range(nlayers):
            m["gmix%d" % l] = _gT(inp["norm_mix_g"][l]); m["gmlp%d" % l] = _gT(inp["norm_mlp_g"][l])
            m["w1_%d" % l] = np.ascontiguousarray(inp["mlp_w1"][l]); m["w2_%d" % l] = np.ascontiguousarray(inp["mlp_w2"][l])
            if l % 2 == 0:
                e = l // 2
                w = inp["w_in_even"][e]
                if r == 1:
                    w = np.concatenate([w[:, :2560], w[:, 3072:3584], w[:, 2560:3072]], 1)
                m["w_in%d" % l] = np.ascontiguousarray(w)
                m["w_out%d" % l] = np.ascontiguousarray(inp["w_out_even"][e])
                m["sgug%d" % l] = _bc(inp["sgu_norm_g"][e])
                wsT = np.transpose(inp["sgu_w"][e], (2, 0, 1))
                bs = inp["sgu_b"][e].T
                if r == 1:
                    wsT = wsT[::-1, :, ::-1]
                    bs = bs[::-1]
                m["wsT%d" % l] = np.ascontiguousarray(wsT); m["bs%d" % l] = np.ascontiguousarray(bs)
                lbl = np.zeros((128, 16), np.float32)
                for dr in range(2):
                    src = dr if r == 0 else 1 - dr
                    for hh in range(4):
                        for le in range(2):
                            lbl[:, (dr * 4 + hh) * 2 + le] = inp["hgrn_lb_logits"][src, le, hh * 128:(hh + 1) * 128]
                m["lbl%d" % l] = lbl
                m["lbm%d" % l] = np.full((128, 1), float(e), np.float32)
                m["hng%d" % l] = _bc(inp["hgrn_norm_g"][e])
            else:
                o = l // 2
                w = inp["w_in_odd"][o]
                m["w_in%d" % l] = np.ascontiguousarray(np.concatenate([w, _perm_cols64(w[:, 1536:2048]), _perm_cols64(w[:, 2048:2560])], 1))
                m["w_out%d" % l] = np.ascontiguousarray(inp["w_out_odd"][o])
                cwm = inp["conv_w"][o] if r == 0 else inp["conv_w"][o][::-1]
                m["cw%d" % l] = np.ascontiguousarray(np.concatenate([cwm[:, g * 128:(g + 1) * 128].T for g in range(4)], 1))
                qg, kg = np.tile(inp["q_norm_g"][o], 2), np.tile(inp["k_norm_g"][o], 2)
                m["gv%d" % l] = np.ascontiguousarray(np.stack([qg, _perm64(qg), kg, _perm64(kg)], 1).astype(np.float32))
                m["lamv%d" % l] = np.ascontiguousarray(np.stack([_bc(inp[k][o]) for k in ("lambda_q1", "lambda_k1", "lambda_q2", "lambda_k2")], 1))
                lam_init = 0.8 - 0.6 * math.exp(-0.3 * l)
                m["lconst%d" % l] = np.ascontiguousarray(np.stack([np.full(128, lam_init, np.float32), np.full(128, 1.0 - lam_init, np.float32)], 1))
                m["subg%d" % l] = _bc(inp["diff_norm_g"][o])
        maps.append(m)
    return maps


def run_fused(inp, nlayers=4):
    nc = _prog(("F", nlayers), build_fused, nlayers)
    r = _run(nc, fused_inputs(inp, nlayers))
    out = np.empty((4, SEQ, D), np.float32)
    for c_ in range(NCORES):
        b, rr = c_ // 2, c_ % 2
        if rr == 0:
            out[b, 0:TOK] = r[c_]["xo"]
        else:
            out[b, TOK:SEQ] = r[c_]["xo"][::-1]
    return out


def kernel_unfused(**inputs):
    return kernel_12(**inputs)


kernel_12 = kernel


def kernel(**inputs):
    inp = {k: np.asarray(v) for k, v in inputs.items()}
    return run_fused(inp, 4)
```
